# Optimizing a Trainium2 kernel written in Bass

```python
import math, functools
import jax, jax.numpy as jnp
from jax import lax
import numpy as np

D_MODEL = 1024
BATCH = 4
SEQ = 4096
DEPTH = 1
DEC_BATCH = 32
DEC_SEQ = 4
PAST_LEN = 16384
PAGE_SIZE = 128

RW_HEAD = 64
RW_HEADS = (D_MODEL // 2) // RW_HEAD
RW_WIDTH = RW_HEADS * RW_HEAD
W_LORA = 64
A_LORA = 64
G_LORA = 128
RW_GN_EPS = 64e-5
RW_SHIFT = 3 * RW_WIDTH + W_LORA + A_LORA + G_LORA

NSA_HEAD = 64
NSA_HEADS = (D_MODEL // 2) // NSA_HEAD
NSA_KV = 2
NSA_HPG = NSA_HEADS // NSA_KV
NSA_WIDTH = NSA_HEADS * NSA_HEAD
NSA_KV_COLS = NSA_KV * NSA_HEAD
NSA_SCALE = NSA_HEAD ** -0.5
CMP_STRIDE = 16
CMP_LEN = 2 * CMP_STRIDE
CMP_HID = 128
SEL_BLOCK = 64
SEL_TOPK = 16
WINDOW = 512
Q_BLOCK = 128

D_FF = 2816
N_IN = RW_SHIFT + NSA_WIDTH + 6 * NSA_KV_COLS + 3 * NSA_HEADS + 2 * D_MODEL
ALPHA = (2 * DEPTH) ** 0.25
BETA = (8 * DEPTH) ** -0.25
LN_EPS = 1e-5
NEG_INF = -1e30
BIG = 1e9

kernel_name = 'hybrid_rwkv7_nsa_macaron_deepnorm_step'


def layer_norm(x, g, b):
    xf = x.astype(jnp.float32)
    mu = xf.mean(-1, keepdims=True)
    var = jnp.square(xf - mu).mean(-1, keepdims=True)
    return ((xf - mu) * lax.rsqrt(var + LN_EPS) * g + b).astype(x.dtype)


def swiglu(u, w_gate, w_up, w_down):
    return (jax.nn.silu(u @ w_gate) * (u @ w_up)) @ w_down


def masked_softmax(s, mask):
    s = jnp.where(mask, s, NEG_INF)
    e = jnp.where(mask, jnp.exp(s - s.max(axis=-1, keepdims=True)), 0.0)
    return e / jnp.maximum(e.sum(axis=-1, keepdims=True), 1e-30)


def rwkv7_time_mix(p, prev, s0, mu, w0, w2, a0, a2, g2, k_k, k_a, r_k, ln_w, ln_b):
    B, T, _ = p.shape
    f32 = jnp.float32
    p_prev = jnp.concatenate([prev[:, None].astype(p.dtype), p[:, :-1]], axis=1)
    xs = p + (p_prev - p) * mu
    r, k, v, wd, ad, gd = jnp.split(xs, [RW_WIDTH, 2 * RW_WIDTH, 3 * RW_WIDTH,
                                         3 * RW_WIDTH + W_LORA, 3 * RW_WIDTH + W_LORA + A_LORA], axis=-1)
    heads = lambda t: t.reshape(B, T, RW_HEADS, RW_HEAD).astype(f32)
    log_w = -jax.nn.softplus(-(w0 + jnp.tanh(wd) @ w2).astype(f32)) - 0.5
    decay = jnp.exp(-jnp.exp(log_w))
    a = jax.nn.sigmoid(a0 + ad @ a2)
    g = jax.nn.sigmoid(gd) @ g2
    kk = heads(k * k_k)
    kk = kk * lax.rsqrt(jnp.maximum(jnp.sum(kk * kk, -1, keepdims=True), 1e-24))
    k = k * (1 + (a - 1) * k_a)
    r_h, k_h, v_h, a_h, w_h = heads(r), heads(k), heads(v), heads(a), heads(decay)

    def step(S, inp):
        r_t, w_t, k_t, v_t, kk_t, a_t = inp
        sa = jnp.einsum('bhij,bhj->bhi', S, -kk_t)
        S = S * w_t[:, :, None, :] + sa[..., None] * (kk_t * a_t)[:, :, None, :] + v_t[..., None] * k_t[:, :, None, :]
        return S, jnp.einsum('bhij,bhj->bhi', S, r_t)

    seq_first = lambda t: jnp.swapaxes(t, 0, 1)
    s_final, y = lax.scan(step, s0.astype(f32), tuple(seq_first(t) for t in (r_h, w_h, k_h, v_h, kk, a_h)))
    y = seq_first(y)
    mean = y.mean(-1, keepdims=True)
    var = jnp.square(y - mean).mean(-1, keepdims=True)
    y = ((y - mean) * lax.rsqrt(var + RW_GN_EPS)).reshape(B, T, RW_WIDTH) * ln_w + ln_b
    bonus = (jnp.sum(r_h * k_h * r_k, -1, keepdims=True) * v_h).reshape(B, T, RW_WIDTH)
    out = ((y + bonus) * g).astype(p.dtype)
    return out, p[:, -1], s_final


def compress_blocks(x, pe, w1, w2):
    B, L = x.shape[:2]
    n_cmp = (L - CMP_LEN) // CMP_STRIDE + 1
    ch = x[:, :(n_cmp + 1) * CMP_STRIDE].reshape(B, n_cmp + 1, CMP_STRIDE, NSA_KV, NSA_HEAD)
    h_lo = jnp.einsum('bnjgd,jde->bnge', ch, w1[:CMP_STRIDE])
    h_hi = jnp.einsum('bnjgd,jde->bnge', ch, w1[CMP_STRIDE:])
    h = jax.nn.gelu(h_lo[:, :-1] + h_hi[:, 1:] + jnp.einsum('jd,jde->e', pe, w1))
    return jnp.einsum('bnge,ed->bngd', h, w2)


def to_blocks(x):
    B, L = x.shape[:2]
    n_sel = -(-L // SEL_BLOCK)
    x = jnp.pad(x, ((0, 0), (0, n_sel * SEL_BLOCK - L), (0, 0), (0, 0)))
    return x.reshape(B, n_sel, SEL_BLOCK, NSA_KV, NSA_HEAD).transpose(0, 3, 1, 2, 4)


def block_importance(p, n_sel):
    R = SEL_BLOCK // CMP_STRIDE
    n_cmp = p.shape[-1]
    pad = [(0, 0)] * (p.ndim - 1) + [(1, R * (n_sel + 1) - 1 - n_cmp)]
    pp = jnp.pad(p, pad).reshape(p.shape[:-1] + (n_sel + 1, R))
    return pp[..., :-1, :].sum(-1) + pp[..., 1:, 0]


def nsa_attend(q, gates, q_pos, kc, vc, ksb, vsb, kw, vw, w_pos):
    B, Qb = q.shape[:2]
    f32 = jnp.float32
    qg = q.reshape(B, Qb, NSA_KV, NSA_HPG, NSA_HEAD) * NSA_SCALE
    cmp_end = jnp.arange(kc.shape[1]) * CMP_STRIDE + (CMP_LEN - 1)
    s = jnp.einsum('bqghd,bngd->bghqn', qg, kc, preferred_element_type=f32)
    p_c = masked_softmax(s, cmp_end[None, :] <= q_pos[:, None])
    o_c = jnp.einsum('bghqn,bngd->bqghd', p_c.astype(vc.dtype), vc)
    n_sel = ksb.shape[2]
    score = block_importance(p_c.sum(axis=2), n_sel)
    cur = (q_pos // SEL_BLOCK)[:, None]
    j = jnp.arange(n_sel)[None, :]
    forced = (j == 0) | (j == cur) | (j == cur - 1)
    score = jnp.where(j <= cur, jnp.where(forced, BIG, score), -BIG)
    _, idx = lax.top_k(score, min(SEL_TOPK, n_sel))
    bi = jnp.arange(B)[:, None, None, None]
    gi = jnp.arange(NSA_KV)[None, :, None, None]
    n_keys = idx.shape[-1] * SEL_BLOCK
    kb = ksb[bi, gi, idx].reshape(B, NSA_KV, Qb, n_keys, NSA_HEAD)
    vb = vsb[bi, gi, idx].reshape(B, NSA_KV, Qb, n_keys, NSA_HEAD)
    k_pos = (idx[..., None] * SEL_BLOCK + jnp.arange(SEL_BLOCK)).reshape(B, NSA_KV, 1, Qb, n_keys)
    s = jnp.einsum('bqghd,bgqld->bghql', qg, kb, preferred_element_type=f32)
    p_s = masked_softmax(s, k_pos <= q_pos[:, None])
    o_s = jnp.einsum('bghql,bgqld->bqghd', p_s.astype(vb.dtype), vb)
    s = jnp.einsum('bqghd,blgd->bghql', qg, kw, preferred_element_type=f32)
    dist = q_pos[:, None] - w_pos[None, :]
    p_w = masked_softmax(s, (dist >= 0) & (dist < WINDOW) & (w_pos[None, :] >= 0))
    o_w = jnp.einsum('bghql,blgd->bqghd', p_w.astype(vw.dtype), vw)
    gt = gates.reshape(B, Qb, NSA_KV, NSA_HPG, 3)
    o = gt[..., 0:1] * o_c + gt[..., 1:2] * o_s + gt[..., 2:3] * o_w
    return o.reshape(B, Qb, NSA_WIDTH)


def nsa_prompt(q, gates, kvc, kvs, kvw, lp):
    B, T = q.shape[:2]
    kc = compress_blocks(kvc[:, :, 0], lp['phi_pe'][0], lp['phi_w1'][0], lp['phi_w2'][0])
    vc = compress_blocks(kvc[:, :, 1], lp['phi_pe'][1], lp['phi_w1'][1], lp['phi_w2'][1])
    ksb, vsb = to_blocks(kvs[:, :, 0]), to_blocks(kvs[:, :, 1])
    qb = min(Q_BLOCK, T)
    nb = T // qb
    kvw_pad = jnp.pad(kvw, ((0, 0), (WINDOW, 0), (0, 0), (0, 0), (0, 0)))

    def one_block(args):
        i, q_i, g_i = args
        t0 = i * qb
        kvw_i = lax.dynamic_slice_in_dim(kvw_pad, t0, WINDOW + qb, axis=1)
        return nsa_attend(q_i, g_i, t0 + jnp.arange(qb), kc, vc, ksb, vsb,
                          kvw_i[:, :, 0], kvw_i[:, :, 1], t0 - WINDOW + jnp.arange(WINDOW + qb))

    blocks = lambda t: jnp.swapaxes(t.reshape((B, nb, qb) + t.shape[2:]), 0, 1)
    out = lax.map(one_block, (jnp.arange(nb), blocks(q), blocks(gates)))
    y = jnp.swapaxes(out, 0, 1).reshape(B, T, NSA_WIDTH)
    return y, kvw[:, T - min(WINDOW, T):]


def nsa_sample(q, gates, kvc, kvs, kvw, lp, past_kvc, past_kvs, win_buf, past_len):
    B, T = q.shape[:2]
    kvc_all = jnp.concatenate([past_kvc.astype(kvc.dtype), kvc], axis=1)
    kvs_all = jnp.concatenate([past_kvs.astype(kvs.dtype), kvs], axis=1)
    kc = compress_blocks(kvc_all[:, :, 0], lp['phi_pe'][0], lp['phi_w1'][0], lp['phi_w2'][0])
    vc = compress_blocks(kvc_all[:, :, 1], lp['phi_pe'][1], lp['phi_w1'][1], lp['phi_w2'][1])
    ksb, vsb = to_blocks(kvs_all[:, :, 0]), to_blocks(kvs_all[:, :, 1])
    n_buf = win_buf.shape[1]
    win_all = jnp.concatenate([win_buf.astype(kvw.dtype), kvw], axis=1)
    w_pos = past_len - n_buf + jnp.arange(n_buf + T)
    y = nsa_attend(q, gates, past_len + jnp.arange(T), kc, vc, ksb, vsb,
                   win_all[:, :, 0], win_all[:, :, 1], w_pos)
    return y, win_all[:, T:]


def group_layer(x, c, lp, rw_prev, rw_wkv, nsa_fn):
    B, T, _ = x.shape
    mod = (jax.nn.silu(c) @ lp['w_ada'] + lp['b_ada']).reshape(B, 9, 1, D_MODEL)
    u = x * (1 + mod[:, 1]) + mod[:, 0]
    f = swiglu(u, lp['ffn_w_gate'][0], lp['ffn_w_up'][0], lp['ffn_w_down'][0])
    x = layer_norm(ALPHA * x + 0.5 * (1 + mod[:, 2]) * f, lp['ln_g'][0], lp['ln_b'][0])
    u = x * (1 + mod[:, 4]) + mod[:, 3]
    proj = u @ lp['w_in'] + lp['b_in']
    c1 = RW_SHIFT
    c2 = c1 + NSA_WIDTH
    c3 = c2 + 6 * NSA_KV_COLS
    c4 = c3 + 3 * NSA_HEADS
    p_rw, p_q, p_kv, p_gate, p_merge = jnp.split(proj, [c1, c2, c3, c4], axis=-1)
    y_a, new_prev, new_wkv = rwkv7_time_mix(p_rw, rw_prev, rw_wkv, lp['rw_mu'], lp['rw_w0'], lp['rw_w2'],
                                            lp['rw_a0'], lp['rw_a2'], lp['rw_g2'], lp['rw_k_k'],
                                            lp['rw_k_a'], lp['rw_r_k'], lp['rw_ln_w'], lp['rw_ln_b'])
    kv = p_kv.reshape(B, T, 3, 2, NSA_KV, NSA_HEAD)
    y_b, new_win = nsa_fn(p_q.reshape(B, T, NSA_HEADS, NSA_HEAD),
                          jax.nn.sigmoid(p_gate).reshape(B, T, NSA_HEADS, 3),
                          kv[:, :, 0], kv[:, :, 1], kv[:, :, 2], lp)
    g_a, g_b = jnp.split(jax.nn.sigmoid(p_merge), 2, axis=-1)
    m = (g_a * (y_a @ lp['w_out_a']) + g_b * (y_b @ lp['w_out_b'])) @ lp['w_o']
    x = layer_norm(ALPHA * x + (1 + mod[:, 5]) * m, lp['ln_g'][1], lp['ln_b'][1])
    u = x * (1 + mod[:, 7]) + mod[:, 6]
    f = swiglu(u, lp['ffn_w_gate'][1], lp['ffn_w_up'][1], lp['ffn_w_down'][1])
    x = layer_norm(ALPHA * x + 0.5 * (1 + mod[:, 8]) * f, lp['ln_g'][2], lp['ln_b'][2])
    return x, (kv[:, :, 0], kv[:, :, 1], new_win, new_wkv, new_prev)


def setup_inputs(seed: int = 0) -> dict:
    key = jax.random.key(seed)
    ks = jax.random.split(key, 40)
    nrm = lambda k, shape, scale: scale * jax.random.normal(k, shape, jnp.float32)
    L = DEPTH
    n_pages = PAST_LEN // PAGE_SIZE
    n_used = DEC_BATCH * n_pages
    n_phys = n_used + max(1, n_used // 4)
    win_buf = min(WINDOW, PAST_LEN)
    page_table = jax.random.permutation(ks[0], n_phys)[:n_used].reshape(DEC_BATCH, n_pages).astype(jnp.int32)
    return {
        'x_prompt': nrm(ks[1], (BATCH, SEQ, D_MODEL), 1.0),
        'x_sample': nrm(ks[2], (DEC_BATCH, DEC_SEQ, D_MODEL), 1.0),
        'c_prompt': nrm(ks[3], (BATCH, D_MODEL), 1.0),
        'c_sample': nrm(ks[4], (DEC_BATCH, D_MODEL), 1.0),
        'cache_kv_cmp': nrm(ks[5], (L, n_phys, PAGE_SIZE, 2, NSA_KV, NSA_HEAD), 1.0),
        'cache_kv_sel': nrm(ks[6], (L, n_phys, PAGE_SIZE, 2, NSA_KV, NSA_HEAD), 1.0),
        'state_kv_win': nrm(ks[7], (L, DEC_BATCH, win_buf, 2, NSA_KV, NSA_HEAD), 1.0),
        'state_wkv': nrm(ks[8], (L, DEC_BATCH, RW_HEADS, RW_HEAD, RW_HEAD), 0.3),
        'state_shift': nrm(ks[9], (L, DEC_BATCH, RW_SHIFT), 1.0),
        'page_table': page_table,
        'w_ada': nrm(ks[10], (L, D_MODEL, 9 * D_MODEL), 0.2 * D_MODEL ** -0.5),
        'b_ada': nrm(ks[11], (L, 9 * D_MODEL), 0.02),
        'ln_g': 1.0 + nrm(ks[12], (L, 3, D_MODEL), 0.05),
        'ln_b': nrm(ks[13], (L, 3, D_MODEL), 0.02),
        'ffn_w_gate': nrm(ks[14], (L, 2, D_MODEL, D_FF), D_MODEL ** -0.5),
        'ffn_w_up': nrm(ks[15], (L, 2, D_MODEL, D_FF), D_MODEL ** -0.5),
        'ffn_w_down': nrm(ks[16], (L, 2, D_FF, D_MODEL), BETA * D_FF ** -0.5),
        'w_in': nrm(ks[17], (L, D_MODEL, N_IN), D_MODEL ** -0.5),
        'b_in': nrm(ks[18], (L, N_IN), 0.02),
        'rw_mu': jax.random.uniform(ks[19], (L, RW_SHIFT), jnp.float32),
        'rw_w0': -1.0 + nrm(ks[20], (L, RW_WIDTH), 0.5),
        'rw_w2': nrm(ks[21], (L, W_LORA, RW_WIDTH), 0.5 * W_LORA ** -0.5),
        'rw_a0': nrm(ks[22], (L, RW_WIDTH), 0.5),
        'rw_a2': nrm(ks[23], (L, A_LORA, RW_WIDTH), 0.5 * A_LORA ** -0.5),
        'rw_g2': nrm(ks[24], (L, G_LORA, RW_WIDTH), G_LORA ** -0.5),
        'rw_k_k': 0.85 + nrm(ks[25], (L, RW_WIDTH), 0.05),
        'rw_k_a': 1.0 + nrm(ks[26], (L, RW_WIDTH), 0.05),
        'rw_r_k': nrm(ks[27], (L, RW_HEADS, RW_HEAD), 0.1),
        'rw_ln_w': 1.0 + nrm(ks[28], (L, RW_WIDTH), 0.05),
        'rw_ln_b': nrm(ks[29], (L, RW_WIDTH), 0.02),
        'nsa_phi_pe': nrm(ks[30], (L, 2, CMP_LEN, NSA_HEAD), 0.1),
        'nsa_phi_w1': nrm(ks[31], (L, 2, CMP_LEN, NSA_HEAD, CMP_HID), (CMP_LEN * NSA_HEAD) ** -0.5),
        'nsa_phi_w2': nrm(ks[32], (L, 2, CMP_HID, NSA_HEAD), CMP_HID ** -0.5),
        'w_out_a': nrm(ks[33], (L, RW_WIDTH, D_MODEL), BETA * RW_WIDTH ** -0.5),
        'w_out_b': nrm(ks[34], (L, NSA_WIDTH, D_MODEL), BETA * NSA_WIDTH ** -0.5),
        'w_o': nrm(ks[35], (L, D_MODEL, D_MODEL), BETA * D_MODEL ** -0.5),
    }


def reference(x_prompt, x_sample, c_prompt, c_sample, cache_kv_cmp, cache_kv_sel, state_kv_win,
              state_wkv, state_shift, page_table, w_ada, b_ada, ln_g, ln_b, ffn_w_gate, ffn_w_up,
              ffn_w_down, w_in, b_in, rw_mu, rw_w0, rw_w2, rw_a0, rw_a2, rw_g2, rw_k_k, rw_k_a,
              rw_r_k, rw_ln_w, rw_ln_b, nsa_phi_pe, nsa_phi_w1, nsa_phi_w2, w_out_a, w_out_b, w_o):
    bp = x_prompt.shape[0]
    bd = x_sample.shape[0]
    past_len = page_table.shape[1] * cache_kv_cmp.shape[2]
    y_prompt, y_sample = x_prompt, x_sample
    st_prompt, st_sample = [], []
    for l in range(DEPTH):
        lp = dict(w_ada=w_ada[l], b_ada=b_ada[l], ln_g=ln_g[l], ln_b=ln_b[l],
                  ffn_w_gate=ffn_w_gate[l], ffn_w_up=ffn_w_up[l], ffn_w_down=ffn_w_down[l],
                  w_in=w_in[l], b_in=b_in[l], rw_mu=rw_mu[l], rw_w0=rw_w0[l], rw_w2=rw_w2[l],
                  rw_a0=rw_a0[l], rw_a2=rw_a2[l], rw_g2=rw_g2[l], rw_k_k=rw_k_k[l], rw_k_a=rw_k_a[l],
                  rw_r_k=rw_r_k[l], rw_ln_w=rw_ln_w[l], rw_ln_b=rw_ln_b[l],
                  phi_pe=nsa_phi_pe[l], phi_w1=nsa_phi_w1[l], phi_w2=nsa_phi_w2[l],
                  w_out_a=w_out_a[l], w_out_b=w_out_b[l], w_o=w_o[l])
        y_prompt, st_p = group_layer(y_prompt, c_prompt, lp,
                                     jnp.zeros((bp, RW_SHIFT), x_prompt.dtype),
                                     jnp.zeros((bp, RW_HEADS, RW_HEAD, RW_HEAD), jnp.float32),
                                     nsa_prompt)
        past_kvc = cache_kv_cmp[l, page_table].reshape(bd, past_len, 2, NSA_KV, NSA_HEAD)
        past_kvs = cache_kv_sel[l, page_table].reshape(bd, past_len, 2, NSA_KV, NSA_HEAD)
        nsa_s = functools.partial(nsa_sample, past_kvc=past_kvc, past_kvs=past_kvs,
                                  win_buf=state_kv_win[l], past_len=past_len)
        y_sample, st_s = group_layer(y_sample, c_sample, lp, state_shift[l], state_wkv[l], nsa_s)
        st_prompt.append(st_p)
        st_sample.append(st_s)
    new_kv_cmp_prompt = jnp.stack([s[0] for s in st_prompt])
    new_kv_sel_prompt = jnp.stack([s[1] for s in st_prompt])
    new_kv_win_prompt = jnp.stack([s[2] for s in st_prompt])
    new_wkv_prompt = jnp.stack([s[3] for s in st_prompt])
    new_shift_prompt = jnp.stack([s[4] for s in st_prompt])
    new_kv_cmp_sample = jnp.stack([s[0] for s in st_sample])
    new_kv_sel_sample = jnp.stack([s[1] for s in st_sample])
    new_kv_win_sample = jnp.stack([s[2] for s in st_sample])
    new_wkv_sample = jnp.stack([s[3] for s in st_sample])
    new_shift_sample = jnp.stack([s[4] for s in st_sample])
    return (y_prompt, y_sample, new_kv_cmp_prompt, new_kv_sel_prompt, new_kv_win_prompt,
            new_wkv_prompt, new_shift_prompt, new_kv_cmp_sample, new_kv_sel_sample,
            new_kv_win_sample, new_wkv_sample, new_shift_sample)
```

```python
import contextlib
import numpy as np
import concourse.bass as bass
import concourse.mybir as mybir
from concourse.bass_utils import run_bass_kernel_spmd

F32 = mybir.dt.float32
BF16 = mybir.dt.bfloat16
I32 = mybir.dt.int32
AF = mybir.ActivationFunctionType
ALU = mybir.AluOpType
AX = mybir.AxisListType

D = 1024
DC = 8
DFF = 2816
FC = 22
SEQ = 4096
NS = 4
ST = 4
TT = SEQ + NS * ST
NSEQ = 1 + NS
RW_SHIFT = 1792
N_IN = 5144
ALPHA = 2 ** 0.25
LN_EPS = 1e-5
NT = 256
DEBUG = False
SEQ_SCAN = SEQ
SEQ_NSA = SEQ
DO_SAMPLE = True
NPG_DBG = 128
NS_DBG = NS

ENGS = ("pe", "act", "dve", "pool", "sp")


class _Rec:
    def __getattr__(self, name):
        def f(*a, **k):
            self.call = (name, a, k)
            return self
        return f


class Prog:
    def __init__(self, nc, n_dma_sems=32):
        self.nc = nc
        self.ops = {e: [] for e in ENGS}
        self.cnt = {e: 0 for e in ENGS}
        self.last_w = {}
        self.readers = {}
        self.seen = {e: {} for e in ENGS}
        self.n_dma = n_dma_sems
        self.dma_i = 0
        self.dma_cnt = [0] * n_dma_sems
        self.out_tokens = []

    def _deps(self, reads, writes):
        deps = []
        for k in reads:
            t = self.last_w.get(k)
            if t is not None:
                deps.append(t)
        for k in writes:
            t = self.last_w.get(k)
            if t is not None:
                deps.append(t)
            deps.extend(self.readers.get(k, ()))
        return deps

    def _commit(self, tok, reads, writes):
        for k in reads:
            self.readers.setdefault(k, []).append(tok)
        for k in writes:
            self.last_w[k] = tok
            self.readers[k] = []

    def _waits(self, eng, deps, skip_own=False):
        need = {}
        for (s, v) in deps:
            if skip_own and s == ("c", eng):
                continue
            if self.seen[eng].get(s, 0) >= v:
                continue
            if need.get(s, 0) < v:
                need[s] = v
        for s, v in need.items():
            self.seen[eng][s] = v
        return list(need.items())

    def op(self, eng, fn, reads=(), writes=(), skip_own=False):
        rec = _Rec()
        fn(rec)
        fn = rec.call
        deps = self._deps(reads, writes)
        waits = self._waits(eng, deps, skip_own)
        self.cnt[eng] += 1
        tok = (("c", eng), self.cnt[eng])
        self.ops[eng].append(("c", fn, waits, tok))
        self._commit(tok, reads, writes)
        return tok

    def dma(self, eng, out, in_, reads=(), writes=(), is_output=False, **kw):
        deps = self._deps(reads, writes)
        si = self.dma_i % self.n_dma
        self.dma_i += 1
        sem = ("d", si)
        prev = self.dma_cnt[si]
        if prev > 0:
            deps.append((sem, 16 * prev))
        waits = self._waits(eng, deps)
        self.dma_cnt[si] += 1
        tok = (sem, 16 * self.dma_cnt[si])
        self.ops[eng].append(("d", (out, in_, kw), waits, tok))
        self._commit(tok, reads, writes)
        if is_output:
            self.out_tokens.append(tok)
        return tok

    def emit(self):
        nc = self.nc
        with contextlib.ExitStack() as es:
            sems = {}
            for e in ENGS:
                sems[("c", e)] = es.enter_context(nc.semaphore("c_" + e))
            for i in range(self.n_dma):
                sems[("d", i)] = es.enter_context(nc.semaphore("d_%d" % i))
            final = {}
            for (s, v) in self.out_tokens:
                final[s] = max(final.get(s, 0), v)
            block = es.enter_context(nc.Block())
            engobj = {"pe": "tensor", "act": "scalar", "dve": "vector", "pool": "gpsimd", "sp": "sync"}

            def run(e, eng):
                for kind, payload, waits, tok in self.ops[e]:
                    for (s, v) in waits:
                        eng.wait_ge(sems[s], v)
                    if kind == "c":
                        name, a, k = payload
                        getattr(eng, name)(*a, **k).then_inc(sems[tok[0]], 1)
                    else:
                        out, in_, kw = payload
                        if "gather_idx" in kw:
                            eng.indirect_dma_start(out=out, out_offset=None, in_=in_,
                                                   in_offset=bass.IndirectOffsetOnAxis(ap=kw["gather_idx"], axis=0),
                                                   element_offset=kw.get("element_offset", 0)
                                                   ).then_inc(sems[tok[0]], 16)
                        else:
                            eng.dma_start(out=out, in_=in_, **kw).then_inc(sems[tok[0]], 16)
                if e == "sp":
                    for s, v in final.items():
                        eng.wait_ge(sems[s], v)

            for e in ENGS:
                getattr(block, engobj[e])(lambda eng, e=e: run(e, eng))


def tiles():
    out = []
    for i in range(SEQ // NT):
        out.append((i * NT, NT, [(0, NT, 0)]))
    out.append((SEQ, NS * ST, [(s * ST, (s + 1) * ST, 1 + s) for s in range(NS)]))
    return out


def build():
    nc = bass.Bass("TRN2", target_bir_lowering=False)
    P = Prog(nc)
    es = contextlib.ExitStack()

    def din(name, shape, dt=F32):
        return nc.dram_tensor(name, list(shape), dt, kind="ExternalInput").ap()

    def dout(name, shape, dt=F32):
        return nc.dram_tensor(name, list(shape), dt, kind="ExternalOutput").ap()

    def dscr(name, shape, dt=F32):
        return nc.dram_tensor(name, list(shape), dt, kind="Internal").ap()

    def sb(name, shape, dt=F32):
        return es.enter_context(nc.sbuf_tensor(name, list(shape), dt))

    xT = din("xT", [D, TT])
    cT = din("cT", [128, DC, NSEQ])
    w_ada = din("w_ada", [D, 9 * D])
    b_ada = din("b_ada", [128, 72])
    ln_g = din("ln_g", [128, 3, DC])
    ln_b = din("ln_b", [128, 3, DC])
    w_gate = din("w_gate", [2, D, DFF])
    w_up = din("w_up", [2, D, DFF])
    w_down = din("w_down", [2, DFF, D])
    w_in = din("w_in", [D, N_IN])
    b_in_bc = din("b_in_bc", [128, N_IN])
    win_state = din("win_state", [NS, 512, 256])

    rwp = din("rwp", [128, 15, 2])
    rwq = din("rwq", [128, 4, 7])
    w2_d = din("rw_w2", [64, 512])
    a2_d = din("rw_a2", [64, 512])
    g2_d = din("rw_g2", [128, 512])
    shs = din("shs", [128, 15, NS])
    wkv0 = din("wkv0", [NS, 8, 64, 64])
    blk1_d = din("blk1", [128, 128])
    istk_d = din("istk", [128, 64])

    nsab = din("nsab", [64, 16])
    phi_w1 = din("phi_w1", [2, 32, 64, 128])
    phi_w2 = din("phi_w2", [2, 128, 64])
    peT_d = din("peT", [64, 2, 32])
    tri_d = din("tri", [128, 128])
    cvalT_d = din("cvalT", [128, 2, 128])
    jidx_d = din("jidx", [128, 64])
    curb_d = din("curb", [128, 1])
    eall_d = din("eall", [64, SEQ])
    bimp_d = din("bimp", [128, 2, 64])
    idx16_d = din("idx16", [128, 16])
    ident_d = din("ident", [128, 128])
    bm_d = din("bm", [128, 16])
    w_out_a = din("w_out_a", [512, D])
    w_out_b = din("w_out_b", [512, D])
    w_o = din("w_o", [D, D])

    NPHYS = 5120
    cache_cmp = din("cache_cmp", [NPHYS * 256, 128])
    cache_sel = din("cache_sel", [NPHYS * 256, 128])
    ptab = din("ptab", [1, NS * 128], I32)
    bimps_d = din("bimps", [128, 8, 257])
    e2_d = din("e2", [128, 8192])
    hsel_d = din("hsel", [16, 4])
    pcol_d = din("pcol", [128, 1])
    jidx2_d = din("jidx2", [128, 264])
    tri16_d = din("tri16", [128, 16])

    yT = dout("yT", [D, TT])
    o_kvc = dout("o_kvc", [TT, 256])
    o_kvs = dout("o_kvs", [TT, 256])
    o_kvw_p = dout("o_kvw_p", [512, 256])
    o_kvw_s = dout("o_kvw_s", [NS, 512, 256])
    o_shift = dout("o_shift", [NSEQ, RW_SHIFT])
    o_wkv = dout("o_wkv", [NSEQ, 8, 64, 64])

    x1T = (dout if DEBUG else dscr)("x1T", [D, TT])
    x2T = (dout if DEBUG else dscr)("x2T", [D, TT])

    RWN = ("nkk", "w", "b", "kp", "r", "v", "g", "bonus")
    scr = {k: (dout if DEBUG else dscr)("scr_" + k, [128, 4, TT]) for k in RWN}
    y_fm = (dout if DEBUG else dscr)("y_fm", [128, 4, TT])
    yb_fm = (dout if DEBUG else dscr)("yb_fm", [128, 4, TT])
    nsaT = dscr("nsaT", [64, 16, TT])
    kv_tm = dscr("kv_tm", [TT, 768])
    gates_tm = dscr("gates_tm", [TT, 24])

    arena = sb("arena", [128, 3 * DC * DFF], BF16)
    stage = [sb("stage%d" % i, [128, DFF], F32) for i in range(2)]
    xt = [sb("xt%d" % i, [128, DC, NT], F32) for i in range(2)]
    ub = sb("ub", [128, DC, NT], BF16)
    hT = sb("hT", [128, FC, NT], BF16)
    tmpa = [sb("tmpa%d" % i, [128, NT], F32) for i in range(2)]
    tmpb = [sb("tmpb%d" % i, [128, NT], F32) for i in range(2)]
    lnt = {k: sb("ln_" + k, [128, NT], F32) for k in ("mu", "musq", "var", "rstd")}
    ones = sb("ones", [128, 128], F32)
    modS = sb("modS", [128, NSEQ, 9, DC], F32)
    ct_sb = sb("ct_sb", [128, DC, NSEQ], F32)
    bada_sb = sb("bada_sb", [128, 72], F32)
    lng_sb = sb("lng_sb", [128, 3, DC], F32)
    lnb_sb = sb("lnb_sb", [128, 3, DC], F32)
    def carve(off_bytes, ncols):
        a = off_bytes // 2
        return arena[:, a:a + 2 * ncols].bitcast(F32)

    CV0 = DC * 3096 * 2
    kvt = [carve(CV0 + i * 792 * 4, 792) for i in range(2)]
    rwt = carve(CV0 + 2 * 792 * 4, RW_SHIFT)
    bias_tm = carve(CV0 + 2 * 792 * 4 + RW_SHIFT * 4, 792 + RW_SHIFT)

    rwp_sb = sb("rwp_sb", [128, 15, 2])
    rwq_sb = sb("rwq_sb", [128, 4, 7])
    w2_sb = sb("w2_sb", [128, 512])
    a2_sb = sb("a2_sb", [128, 512])
    g2_sb = sb("g2_sb", [128, 512])
    nsab_sb = sb("nsab_sb", [64, 16])
    bm_sb = sb("bm_sb", [128, 16])
    blk1 = sb("blk1_sb", [128, 128])
    istk = sb("istk_sb", [128, 64])
    ps = [es.enter_context(nc.psum_tensor("ps%d" % i, [128, 512], F32)) for i in range(8)]

    P.op("pool", lambda e: e.memset(ones[:], 1.0), writes=["ones"])
    P.dma("sp", ct_sb[:], cT, writes=["ct"])
    P.dma("sp", bada_sb[:], b_ada, writes=["bada"])
    P.dma("sp", lng_sb[:], ln_g, writes=["lng"])
    P.dma("sp", lnb_sb[:], ln_b, writes=["lnb"])
    P.dma("sp", rwp_sb[:], rwp, writes=["rwc"])
    P.dma("sp", rwq_sb[:], rwq, writes=["rwc"])
    P.dma("sp", w2_sb[0:64, :], w2_d, writes=["rwc"])
    P.dma("sp", a2_sb[0:64, :], a2_d, writes=["rwc"])
    P.dma("sp", g2_sb[:], g2_d, writes=["rwc"])
    P.dma("sp", nsab_sb[:], nsab, writes=["rwc"])
    P.dma("sp", bm_sb[:], bm_d, writes=["rwc"])
    P.dma("sp", blk1[:], blk1_d, writes=["rwc"])
    P.dma("sp", istk[:], istk_d, writes=["rwc"])

    P.op("act", lambda e: e.activation(out=ct_sb[:], in_=ct_sb[:], func=AF.Silu), reads=["ct"], writes=["ct"])
    wada_v = w_ada.rearrange("(c p) n -> p c n", p=128)
    mod_ps = ps[0]
    GW = 256
    for gidx in range(9216 // GW):
        st = stage[gidx % 2]
        sk = "stage%d" % (gidx % 2)
        stv = st[:, 0:DC * GW].rearrange("p (c n) -> p c n", c=DC)
        P.dma("sp", stv, wada_v[:, :, gidx * GW:(gidx + 1) * GW], writes=[sk])
        for j in range(GW // 128):
            ch = gidx * (GW // 128) + j
            for c in range(DC):
                P.op("pe", lambda e, stv=stv, j=j, ch=ch, c=c: e.matmul(
                    mod_ps[:, ch * NSEQ:(ch + 1) * NSEQ], lhsT=stv[:, c, j * 128:(j + 1) * 128], rhs=ct_sb[:, c, :],
                    start=(c == 0), stop=(c == DC - 1)),
                    reads=[sk, "ct"], writes=["ps0"], skip_own=True)
    P.op("dve", lambda e: e.tensor_tensor(
        out=modS[:].rearrange("p s k c -> p (k c) s"),
        in0=mod_ps[:, 0:72 * NSEQ].rearrange("p (ch s) -> p ch s", s=NSEQ),
        in1=bada_sb[:].unsqueeze(2).to_broadcast([128, 72, NSEQ]), op=ALU.add),
        reads=["ps0", "bada"], writes=["modS"])
    for k, (mulv, addv) in {1: (1.0, 1.0), 4: (1.0, 1.0), 7: (1.0, 1.0), 2: (0.5, 0.5), 8: (0.5, 0.5), 5: (1.0, 1.0)}.items():
        P.op("dve", lambda e, k=k, mulv=mulv, addv=addv: e.tensor_scalar(
            out=modS[:, :, k, :], in0=modS[:, :, k, :], scalar1=mulv, scalar2=addv, op0=ALU.mult, op1=ALU.add),
            reads=["modS"], writes=["modS"])

    cast_rr = [0]

    gate_sb = sb("gate_sb", [128, 4], F32)

    def arena_gate():
        for gi_, eng in enumerate(("pool", "dve", "act")):
            if eng == "act":
                P.op(eng, lambda e, gi_=gi_: e.memzero(gate_sb[:, gi_:gi_ + 1]), writes=["arena", ("gate", gi_)])
            else:
                P.op(eng, lambda e, gi_=gi_: e.memset(gate_sb[:, gi_:gi_ + 1], 0.0), writes=["arena", ("gate", gi_)])

    def load_weight_bf16(dst_ap, src_ap, ncols, dkey):
        i = cast_rr[0]
        cast_rr[0] += 1
        st = stage[i % 2]
        sk = "stage%d" % (i % 2)
        P.dma("sp", st[:, 0:ncols], src_ap, writes=[sk])
        eng = ("pool", "dve", "act")[i % 3]
        if eng == "act":
            P.op("act", lambda e: e.copy(out=dst_ap, in_=st[:, 0:ncols]), reads=[sk], writes=[dkey])
        else:
            P.op(eng, lambda e: e.tensor_copy(out=dst_ap, in_=st[:, 0:ncols]), reads=[sk], writes=[dkey])

    def layer_norm_tile(z, zk, n, gi, out, outk):
        s1, s2 = ps[6], ps[7]
        for c in range(DC):
            tq = tmpa[c % 2]
            P.op("act", lambda e, c=c, tq=tq: e.activation(out=tq[:, 0:n], in_=z[:, c, 0:n], func=AF.Square),
                 reads=[zk], writes=["tmpa%d" % (c % 2)])
            P.op("pe", lambda e, c=c: e.matmul(s1[:, 0:n], lhsT=ones[:], rhs=z[:, c, 0:n], start=(c == 0), stop=(c == DC - 1)),
                 reads=[zk, "ones"], writes=["ps6"], skip_own=True)
            P.op("pe", lambda e, c=c, tq=tq: e.matmul(s2[:, 0:n], lhsT=ones[:], rhs=tq[:, 0:n], start=(c == 0), stop=(c == DC - 1)),
                 reads=["tmpa%d" % (c % 2), "ones"], writes=["ps7"], skip_own=True)
        mu, musq, var, rstd = lnt["mu"], lnt["musq"], lnt["var"], lnt["rstd"]
        P.op("act", lambda e: e.mul(out=mu[:, 0:n], in_=s1[:, 0:n], mul=1.0 / D), reads=["ps6"], writes=["ln_mu"])
        P.op("dve", lambda e: e.tensor_tensor(out=musq[:, 0:n], in0=mu[:, 0:n], in1=mu[:, 0:n], op=ALU.mult),
             reads=["ln_mu"], writes=["ln_musq"])
        P.op("dve", lambda e: e.scalar_tensor_tensor(out=var[:, 0:n], in0=s2[:, 0:n], scalar=1.0 / D, in1=musq[:, 0:n],
                                                     op0=ALU.mult, op1=ALU.subtract),
             reads=["ps7", "ln_musq"], writes=["ln_var"])
        P.op("dve", lambda e: e.tensor_scalar(out=var[:, 0:n], in0=var[:, 0:n], scalar1=LN_EPS, scalar2=None,
                                              op0=ALU.add), reads=["ln_var"], writes=["ln_var"])
        P.op("act", lambda e: e.sqrt(out=var[:, 0:n], in_=var[:, 0:n]), reads=["ln_var"], writes=["ln_var"])
        P.op("dve", lambda e: e.reciprocal(out=rstd[:, 0:n], in_=var[:, 0:n]), reads=["ln_var"], writes=["ln_rstd"])
        for c in range(DC):
            ta = tmpb[c % 2]
            tk = "tmpb%d" % (c % 2)
            P.op("dve", lambda e, c=c, ta=ta: e.tensor_tensor(out=ta[:, 0:n], in0=z[:, c, 0:n], in1=mu[:, 0:n], op=ALU.subtract),
                 reads=[zk, "ln_mu"], writes=[tk])
            P.op("pool", lambda e, c=c, ta=ta: e.tensor_tensor(out=ta[:, 0:n], in0=ta[:, 0:n], in1=rstd[:, 0:n], op=ALU.mult),
                 reads=[tk, "ln_rstd"], writes=[tk])
            P.op("act", lambda e, c=c, ta=ta: e.activation(out=out[:, c, 0:n], in_=ta[:, 0:n], func=AF.Identity,
                                                           scale=lng_sb[:, gi, c:c + 1], bias=lnb_sb[:, gi, c:c + 1]),
                 reads=[tk, "lng", "lnb"], writes=[outk])

    def modulate(x, xk, n, segs, kshift, dst, dk):
        for c in range(DC):
            for (lo, hi, s) in segs:
                P.op("act", lambda e, c=c, lo=lo, hi=hi, s=s: e.activation(
                    out=dst[:, c, lo:hi], in_=x[:, c, lo:hi], func=AF.Identity,
                    scale=modS[:, s, kshift + 1, c:c + 1], bias=modS[:, s, kshift, c:c + 1]),
                    reads=[xk, "modS"], writes=[dk])

    def ffn_phase(fi, src, dst, gi, kmod, dst_is_output):
        wg = arena[:, 0:DC * DFF].rearrange("p (c f) -> p c f", c=DC)
        wu = arena[:, DC * DFF:2 * DC * DFF].rearrange("p (c f) -> p c f", c=DC)
        wd = arena[:, 2 * DC * DFF:3 * DC * DFF].rearrange("p (f d) -> p f d", f=FC)
        arena_gate()
        for c in range(DC):
            load_weight_bf16(wg[:, c, :], w_gate[fi, c * 128:(c + 1) * 128, :], DFF, ("wg", fi, c))
            load_weight_bf16(wu[:, c, :], w_up[fi, c * 128:(c + 1) * 128, :], DFF, ("wu", fi, c))
        for f2 in range(FC // 2):
            i = cast_rr[0]
            cast_rr[0] += 1
            st = stage[i % 2]
            sk = "stage%d" % (i % 2)
            P.dma("sp", st[:, 0:2 * D].rearrange("p (a d) -> p a d", a=2),
                  w_down[fi, f2 * 256:(f2 + 1) * 256, :].rearrange("(a p) d -> p a d", p=128), writes=[sk])
            eng = ("pool", "dve")[i % 2]
            P.op(eng, lambda e, st=st, f2=f2: e.tensor_copy(
                out=wd[:, 2 * f2:2 * f2 + 2, :], in_=st[:, 0:2 * D].rearrange("p (a d) -> p a d", a=2)),
                reads=[sk], writes=[("wd", fi, f2)])
        src_v = src.rearrange("(c p) n -> p c n", p=128)
        dst_v = dst.rearrange("(c p) n -> p c n", p=128)
        tl = tiles()
        P.dma("sp", xt[0][:, :, 0:tl[0][1]], src_v[:, :, tl[0][0]:tl[0][0] + tl[0][1]], writes=["xt0"])
        for ti, (c0, n, segs) in enumerate(tl):
            x = xt[ti % 2]
            xk = "xt%d" % (ti % 2)
            if ti + 1 < len(tl):
                c0n, nn, _ = tl[ti + 1]
                P.dma("sp", xt[(ti + 1) % 2][:, :, 0:nn], src_v[:, :, c0n:c0n + nn], writes=["xt%d" % ((ti + 1) % 2)])
            modulate(x, xk, n, segs, kmod, ub, "ub")
            P.op("pool", lambda e, x=x, n=n: e.tensor_scalar(out=x[:, :, 0:n], in0=x[:, :, 0:n], scalar1=ALPHA, scalar2=None,
                                                            op0=ALU.mult), reads=[xk], writes=[xk])
            for f in range(FC):
                pg, pu = ps[(2 * f) % 4], ps[(2 * f + 1) % 4]
                kg, ku = "ps%d" % ((2 * f) % 4), "ps%d" % ((2 * f + 1) % 4)
                for c in range(DC):
                    P.op("pe", lambda e, c=c, f=f, pg=pg: e.matmul(pg[:, 0:n], lhsT=wg[:, c, f * 128:(f + 1) * 128], rhs=ub[:, c, 0:n],
                                                                 start=(c == 0), stop=(c == DC - 1)),
                         reads=["arena", ("wg", fi, c), "ub"], writes=[kg], skip_own=True)
                for c in range(DC):
                    P.op("pe", lambda e, c=c, f=f, pu=pu: e.matmul(pu[:, 0:n], lhsT=wu[:, c, f * 128:(f + 1) * 128], rhs=ub[:, c, 0:n],
                                                                 start=(c == 0), stop=(c == DC - 1)),
                         reads=["arena", ("wu", fi, c), "ub"], writes=[ku], skip_own=True)
                tq = tmpa[f % 2]
                tk = "tmpa%d" % (f % 2)
                P.op("act", lambda e, pg=pg, tq=tq: e.activation(out=tq[:, 0:n], in_=pg[:, 0:n], func=AF.Silu),
                     reads=[kg], writes=[tk])
                P.op("dve", lambda e, pu=pu, tq=tq, f=f: e.tensor_tensor(out=hT[:, f, 0:n], in0=tq[:, 0:n], in1=pu[:, 0:n], op=ALU.mult),
                     reads=[tk, ku], writes=["hT"])
            for m in range(DC):
                py = ps[4 + (m % 2)]
                ky = "ps%d" % (4 + (m % 2))
                for f in range(FC):
                    P.op("pe", lambda e, m=m, f=f, py=py: e.matmul(py[:, 0:n], lhsT=wd[:, f, m * 128:(m + 1) * 128], rhs=hT[:, f, 0:n],
                                                                 start=(f == 0), stop=(f == FC - 1)),
                         reads=["arena", ("wd", fi, f // 2), "hT"], writes=[ky], skip_own=True)
                for (lo, hi, s) in segs:
                    P.op("dve", lambda e, m=m, py=py, lo=lo, hi=hi, s=s, x=x: e.scalar_tensor_tensor(
                        out=x[:, m, lo:hi], in0=py[:, lo:hi], scalar=modS[:, s, kmod + 2, m:m + 1], in1=x[:, m, lo:hi],
                        op0=ALU.mult, op1=ALU.add),
                        reads=[ky, xk, "modS"], writes=[xk])
            layer_norm_tile(x, xk, n, gi, x, xk)
            P.dma("pool", dst_v[:, :, c0:c0 + n], x[:, :, 0:n], reads=[xk], writes=["dram_" + dst.name], is_output=dst_is_output)

    ffn_phase(0, xT, x1T, 0, 0, False)

    NC1 = 3096
    win_sb = arena[:, 0:DC * NC1].rearrange("p (c f) -> p c f", c=DC)
    arena_gate()
    for c in range(DC):
        for hi_, (a, b) in enumerate(((0, 1548), (1548, 3096))):
            load_weight_bf16(win_sb[:, c, a:b], w_in[c * 128:(c + 1) * 128, a:b], b - a, ("win", c, hi_))
    P.dma("sp", bias_tm[:, 0:792], b_in_bc[:, 2304:3096], reads=[("gate", 0), "arena"], writes=["bias_tm"])
    P.dma("sp", bias_tm[:, 792:792 + RW_SHIFT], b_in_bc[:, 0:RW_SHIFT], reads=[("gate", 0), "arena"], writes=["bias_tm"])
    CV1 = CV0 + (2 * 792 + RW_SHIFT + 792 + RW_SHIFT) * 4
    pbuf = carve(CV1, 15 * 260)
    xsb = carve(CV1 + 15 * 260 * 4, 15 * NT).rearrange("p (c n) -> p c n", c=15)
    DER0 = CV1 + 15 * 260 * 4 + 15 * NT * 4
    DERN = ("w", "a", "g", "kk", "nkk", "kp", "b", "bonus", "t1", "t2", "tw", "sgd")
    der = {k: carve(DER0 + i * NT * 4, NT) for i, k in enumerate(DERN)}
    CHT = [(j * 128, 128) for j in range(12)] + [(1536, 64), (1600, 64), (1664, 128)]
    GK = [("gate", 0), ("gate", 1), ("gate", 2), "arena"]

    def rw_pre(ti, c0, n, segs):
        S_ = len(segs)
        L = n // S_
        pb = pbuf[:, 0:15 * S_ * (L + 1)].rearrange("p (c s l) -> p c s l", c=15, s=S_)
        if S_ > 1:
            P.dma("sp", pb[:, :, :, 0], shs, reads=GK, writes=["pbuf"], allow_slow_non_contiguous=True)
        for ci, (col0, w) in enumerate(CHT):
            pp = ps[2 + ci % 2]
            pk = "ps%d" % (2 + ci % 2)
            for c in range(DC):
                P.op("pe", lambda e: e.matmul(pp[0:w, 0:n], lhsT=win_sb[:, c, col0:col0 + w], rhs=ub[:, c, 0:n],
                                              start=(c == 0), stop=(c == DC - 1)),
                     reads=["arena", ("win", c, 0), ("win", c, 1), "ub"], writes=[pk], skip_own=True)
            P.op("act", lambda e: e.activation(out=pb[0:w, ci, :, 1:L + 1], in_=pp[0:w, 0:n].rearrange("p (s l) -> p s l", s=S_),
                                               func=AF.Identity, bias=rwp_sb[0:w, ci, 0:1], scale=1.0),
                 reads=[pk, "rwc"] + GK, writes=["pbuf"])
        xs4 = xsb[:, :, 0:n].rearrange("p c (s l) -> p c s l", s=S_)
        P.op("dve", lambda e: e.tensor_tensor(out=xs4, in0=pb[:, :, :, 0:L], in1=pb[:, :, :, 1:L + 1], op=ALU.subtract),
             reads=["pbuf"] + GK, writes=["xs"])
        P.op("dve", lambda e: e.tensor_tensor(out=xsb[:, :, 0:n], in0=xsb[:, :, 0:n],
                                              in1=rwp_sb[:, :, 1:2].to_broadcast([128, 15, n]), op=ALU.mult),
             reads=["xs", "rwc"] + GK, writes=["xs"])
        P.op("dve", lambda e: e.tensor_tensor(out=xs4, in0=xs4, in1=pb[:, :, :, 1:L + 1], op=ALU.add),
             reads=["xs", "pbuf"] + GK, writes=["xs"])
        if S_ == 1:
            P.op("dve", lambda e: e.tensor_copy(out=pb[:, :, :, 0:1], in_=pb[:, :, :, L:L + 1]), reads=["pbuf"] + GK, writes=["pbuf"])
        tw, sgd = der["tw"], der["sgd"]
        P.op("act", lambda e: e.activation(out=tw[0:64, 0:n], in_=xsb[0:64, 12, 0:n], func=AF.Tanh), reads=["xs"] + GK, writes=["tw"])
        P.op("act", lambda e: e.activation(out=sgd[:, 0:n], in_=xsb[:, 14, 0:n], func=AF.Sigmoid), reads=["xs"] + GK, writes=["sgd"])
        NEG = -float(np.exp(-0.5))
        for j in range(4):
            jc = slice(j * 128, (j + 1) * 128)
            q = lambda k: rwq_sb[:, j, k:k + 1]
            r_j, k_j, v_j = xsb[:, j, 0:n], xsb[:, 4 + j, 0:n], xsb[:, 8 + j, 0:n]
            dw, da, dg, dkk, dnkk, dkp, db, dbo, t1, t2 = (der[k][:, 0:n] for k in ("w", "a", "g", "kk", "nkk", "kp", "b", "bonus", "t1", "t2"))
            p4, p5 = ps[4][:, 0:n], ps[5][:, 0:n]
            P.op("pe", lambda e: e.matmul(p4, lhsT=w2_sb[0:64, jc], rhs=tw[0:64, 0:n], start=True, stop=True),
                 reads=["rwc", "tw"] + GK, writes=["ps4"])
            P.op("act", lambda e: e.activation(out=dw, in_=p4, func=AF.Sigmoid, bias=q(0), scale=1.0), reads=["ps4", "rwc"] + GK, writes=["d_w"])
            P.op("act", lambda e: e.activation(out=dw, in_=dw, func=AF.Exp, scale=NEG), reads=["d_w"] + GK, writes=["d_w"])
            P.op("pe", lambda e: e.matmul(p5, lhsT=a2_sb[0:64, jc], rhs=xsb[0:64, 13, 0:n], start=True, stop=True),
                 reads=["rwc", "xs"] + GK, writes=["ps5"])
            P.op("act", lambda e: e.activation(out=da, in_=p5, func=AF.Sigmoid, bias=q(1), scale=1.0), reads=["ps5", "rwc"] + GK, writes=["d_a"])
            P.op("pe", lambda e: e.matmul(p4, lhsT=g2_sb[:, jc], rhs=sgd[:, 0:n], start=True, stop=True),
                 reads=["rwc", "sgd"] + GK, writes=["ps4"])
            P.op("act", lambda e: e.copy(out=dg, in_=p4), reads=["ps4"] + GK, writes=["d_g"])
            P.op("dve", lambda e: e.tensor_scalar(out=dkk, in0=k_j, scalar1=q(2), scalar2=None, op0=ALU.mult), reads=["xs", "rwc"] + GK, writes=["d_kk"])
            P.op("act", lambda e: e.activation(out=t1, in_=dkk, func=AF.Square), reads=["d_kk"] + GK, writes=["d_t1"])
            P.op("pe", lambda e: e.matmul(p5, lhsT=blk1[:], rhs=t1, start=True, stop=True), reads=["rwc", "d_t1"] + GK, writes=["ps5"])
            P.op("dve", lambda e: e.tensor_scalar(out=t2, in0=p5, scalar1=1e-24, scalar2=None, op0=ALU.max), reads=["ps5"] + GK, writes=["d_t2"])
            P.op("act", lambda e: e.sqrt(out=t2, in_=t2), reads=["d_t2"] + GK, writes=["d_t2"])
            P.op("dve", lambda e: e.reciprocal(out=t2, in_=t2), reads=["d_t2"] + GK, writes=["d_t2"])
            P.op("dve", lambda e: e.scalar_tensor_tensor(out=dnkk, in0=dkk, scalar=-1.0, in1=t2, op0=ALU.mult, op1=ALU.mult),
                 reads=["d_kk", "d_t2"] + GK, writes=["d_nkk"])
            P.op("dve", lambda e: e.tensor_scalar(out=t1, in0=da, scalar1=-1.0, scalar2=q(3), op0=ALU.add, op1=ALU.mult),
                 reads=["d_a", "rwc", "d_t1"] + GK, writes=["d_t1"])
            P.op("dve", lambda e: e.scalar_tensor_tensor(out=dkp, in0=t1, scalar=1.0, in1=k_j, op0=ALU.add, op1=ALU.mult),
                 reads=["d_t1", "xs"] + GK, writes=["d_kp"])
            P.op("dve", lambda e: e.scalar_tensor_tensor(out=db, in0=dnkk, scalar=-1.0, in1=da, op0=ALU.mult, op1=ALU.mult),
                 reads=["d_nkk", "d_a"] + GK, writes=["d_b"])
            P.op("dve", lambda e: e.scalar_tensor_tensor(out=t1, in0=r_j, scalar=q(4), in1=dkp, op0=ALU.mult, op1=ALU.mult),
                 reads=["xs", "rwc", "d_kp", "d_t1"] + GK, writes=["d_t1"])
            P.op("pe", lambda e: e.matmul(p4, lhsT=blk1[:], rhs=t1, start=True, stop=True), reads=["rwc", "d_t1"] + GK, writes=["ps4"])
            P.op("dve", lambda e: e.tensor_tensor(out=dbo, in0=p4, in1=v_j, op=ALU.mult), reads=["ps4", "xs"] + GK, writes=["d_bonus"])
            for nm, src_, key in (("nkk", dnkk, "d_nkk"), ("w", dw, "d_w"), ("b", db, "d_b"), ("kp", dkp, "d_kp"),
                                  ("g", dg, "d_g"), ("bonus", dbo, "d_bonus"), ("r", r_j, "xs"), ("v", v_j, "xs")):
                P.dma("pool", scr[nm][:, j, c0:c0 + n], src_, reads=[key] + GK, writes=["dram_scr"])

    nst = carve(DER0 + len(DERN) * NT * 4, 16 * NT).rearrange("p (c n) -> p c n", c=16)
    NCH = [1792 + 64 * h for h in range(8)] + [2304 + br * 256 + g * 64 for br in range(3) for g in range(2)] \
        + [2304 + 128 + g * 64 for g in range(2)]

    def nsa_pre(c0, n):
        for ci, col0 in enumerate(NCH):
            pp = ps[4 + ci % 2]
            pk = "ps%d" % (4 + ci % 2)
            for c in range(DC):
                P.op("pe", lambda e: e.matmul(pp[0:64, 0:n], lhsT=win_sb[:, c, col0:col0 + 64], rhs=ub[:, c, 0:n],
                                              start=(c == 0), stop=(c == DC - 1)),
                     reads=["arena", ("win", c, 0), ("win", c, 1), "ub"], writes=[pk], skip_own=True)
            P.op("act", lambda e: e.activation(out=nst[0:64, ci, 0:n], in_=pp[0:64, 0:n], func=AF.Identity,
                                               bias=nsab_sb[0:64, ci:ci + 1], scale=(0.125 if ci < 8 else 1.0)),
                 reads=[pk, "rwc"] + GK, writes=["nst"])
        P.dma("pool", nsaT[:, :, c0:c0 + n], nst[0:64, :, 0:n], reads=["nst"] + GK, writes=["dram_nsaT"])

    P.op("pool", lambda e: e.memset(pbuf, 0.0), reads=GK, writes=["pbuf"])
    P.op("pool", lambda e: e.memset(xsb, 0.0), reads=GK, writes=["xs"])
    x1_v = x1T.rearrange("(c p) n -> p c n", p=128)
    tl = tiles()
    for ti, (c0, n, segs) in enumerate(tl):
        x = xt[ti % 2]
        xk = "xt%d" % (ti % 2)
        P.dma("sp", x[:, :, 0:n], x1_v[:, :, c0:c0 + n], reads=["dram_x1T"], writes=[xk])
        modulate(x, xk, n, segs, 3, ub, "ub")
        rw_pre(ti, c0, n, segs)
        nsa_pre(c0, n)
        nblk = max(1, n // 128)
        for tb in range(nblk):
            m = min(128, n)
            t0 = c0 + tb * 128
            kv = kvt[tb % 2]
            kk_ = "kvt%d" % (tb % 2)
            for (ca, cb, pi) in ((0, 512, 0), (512, 792, 1)):
                pp = ps[pi]
                for c in range(DC):
                    P.op("pe", lambda e, c=c, tb=tb, m=m, ca=ca, cb=cb, pp=pp: e.matmul(
                        pp[0:m, 0:cb - ca], lhsT=ub[:, c, tb * 128:tb * 128 + m], rhs=win_sb[:, c, 2304 + ca:2304 + cb],
                        start=(c == 0), stop=(c == DC - 1)),
                        reads=["arena", ("win", c, 1), "ub"], writes=["ps%d" % pi], skip_own=True)
                P.op("dve", lambda e, m=m, ca=ca, cb=cb, pp=pp, kv=kv: e.tensor_tensor(
                    out=kv[0:m, ca:cb], in0=pp[0:m, 0:cb - ca], in1=bias_tm[0:m, ca:cb], op=ALU.add),
                    reads=["ps%d" % pi, "bias_tm", "arena"], writes=[kk_])
            P.dma("pool", kv_tm[t0:t0 + m, :], kv[0:m, 0:768], reads=[kk_, "arena"], writes=["dram_kvtm"])
            P.op("act", lambda e: e.activation(out=kv[0:m, 768:792], in_=kv[0:m, 768:792], func=AF.Sigmoid), reads=[kk_, "arena"], writes=[kk_])
            P.dma("pool", gates_tm[t0:t0 + m, :], kv[0:m, 768:792], reads=[kk_, "arena"], writes=["dram_gates"])
            P.dma("pool", o_kvc[t0:t0 + m, :], kv[0:m, 0:256], reads=[kk_, "arena"], is_output=True)
            P.dma("pool", o_kvs[t0:t0 + m, :], kv[0:m, 256:512], reads=[kk_, "arena"], is_output=True)
            if t0 >= SEQ - 512 and t0 < SEQ:
                P.dma("pool", o_kvw_p[t0 - (SEQ - 512):t0 - (SEQ - 512) + m, :], kv[0:m, 512:768], reads=[kk_, "arena"], is_output=True)
            if t0 >= SEQ:
                for s in range(NS):
                    P.dma("pool", o_kvw_s[s, 512 - ST:512, :], kv[s * ST:(s + 1) * ST, 512:768], reads=[kk_, "arena"], is_output=True)
            last_blk = (t0 + m == SEQ) or (t0 >= SEQ)
            if last_blk:
                for q4 in range(4):
                    ca, cb = q4 * 448, (q4 + 1) * 448
                    pp = ps[2 + (q4 % 2)]
                    pk = "ps%d" % (2 + (q4 % 2))
                    for c in range(DC):
                        P.op("pe", lambda e, c=c, tb=tb, m=m, ca=ca, cb=cb, pp=pp: e.matmul(
                            pp[0:m, 0:448], lhsT=ub[:, c, tb * 128:tb * 128 + m], rhs=win_sb[:, c, ca:cb],
                            start=(c == 0), stop=(c == DC - 1)),
                            reads=["arena", ("win", c, 0), ("win", c, 1), "ub"], writes=[pk], skip_own=True)
                    P.op("dve", lambda e, m=m, ca=ca, cb=cb, pp=pp: e.tensor_tensor(
                        out=rwt[0:m, ca:cb], in0=pp[0:m, 0:448], in1=bias_tm[0:m, 792 + ca:792 + cb], op=ALU.add),
                        reads=[pk, "bias_tm", "arena"], writes=["rwt"])
                if t0 < SEQ:
                    P.dma("pool", o_shift[0:1, :], rwt[127:128, :], reads=["rwt", "arena"], is_output=True)
                else:
                    for s in range(NS):
                        P.dma("pool", o_shift[1 + s:2 + s, :], rwt[s * ST + ST - 1:s * ST + ST, :], reads=["rwt", "arena"], is_output=True)
    for s in range(NS):
        P.dma("sp", o_kvw_s[s, 0:512 - ST, :], win_state[s, ST:512, :], is_output=True)

    arena_gate()
    TC = 8
    BN = ("nkk", "w", "b", "kp", "r")
    off = [0]

    def cv(ncols):
        v = carve(off[0], ncols)
        off[0] += ncols * 4
        return v

    xB = {k: [cv(TC * 256) for _ in range(2)] for k in BN}
    bld = [cv(TC * 256) for _ in range(2)]
    vkb = [cv(TC * 256) for _ in range(2)]
    xin = {k: [cv(4 * TC).rearrange("p (h t) -> p h t", h=4) for _ in range(2)] for k in BN + ("v",)}
    Sst, S2, m1, m2, m3 = (cv(256) for _ in range(5))
    sa = cv(4)
    ybuf = [cv(4 * 256).rearrange("p (h t) -> p h t", h=4) for _ in range(2)]
    v3 = lambda ap: ap.rearrange("p (h j) -> p h j", h=4)
    ycnt = [0]
    bcnt = [0]

    def scan_seq(seq_i, col0, T):
        if seq_i == 0:
            P.op("pool", lambda e: e.memset(Sst, 0.0), reads=GK, writes=["S"])
        else:
            src = wkv0[seq_i - 1].rearrange("(hf hp) i j -> hp i hf j", hp=2)
            for hp in range(2):
                P.dma("sp", v3(Sst)[hp * 64:(hp + 1) * 64], src[hp], reads=GK, writes=["S"])
        yb_i = ycnt[0] % 2
        ycnt[0] += 1
        yb, ybk = ybuf[yb_i], "ybuf%d" % yb_i
        ycol0 = col0
        for t0 in range(0, T, TC):
            tc = min(TC, T - t0)
            bi = bcnt[0] % 2
            bcnt[0] += 1
            for k in BN + ("v",):
                P.dma("sp", xin[k][bi][:, :, 0:tc], scr[k][:, :, col0 + t0:col0 + t0 + tc], reads=["dram_scr"] + GK, writes=[("xin", k, bi)])
            for ki, k in enumerate(BN):
                bl, blk_ = bld[ki % 2], "bld%d" % (ki % 2)
                P.op("pool", lambda e: e.tensor_tensor(
                    out=bl[:, 0:tc * 256].rearrange("p (t h j) -> p t h j", t=tc, h=4),
                    in0=istk[:].unsqueeze(1).unsqueeze(1).to_broadcast([128, tc, 4, 64]),
                    in1=xin[k][bi][:, :, 0:tc].rearrange("p h t -> p t h").unsqueeze(3).to_broadcast([128, tc, 4, 64]),
                    op=ALU.mult), reads=[("xin", k, bi), "rwc"] + GK, writes=[blk_])
                for t2 in range(0, tc, 2):
                    w_ = min(2, tc - t2) * 256
                    pi = (t2 // 2) % 4
                    P.op("pe", lambda e: e.matmul(ps[pi][:, 0:w_], lhsT=blk1[:], rhs=bl[:, t2 * 256:t2 * 256 + w_], start=True, stop=True),
                         reads=[blk_, "rwc"] + GK, writes=["ps%d" % pi])
                    P.op("act", lambda e: e.copy(out=xB[k][bi][:, t2 * 256:t2 * 256 + w_], in_=ps[pi][:, 0:w_]),
                         reads=["ps%d" % pi] + GK, writes=[("xB", k, bi)])
            P.op("pool", lambda e: e.tensor_tensor(
                out=vkb[bi][:, 0:tc * 256].rearrange("p (t h j) -> p t h j", t=tc, h=4),
                in0=xB["kp"][bi][:, 0:tc * 256].rearrange("p (t h j) -> p t h j", t=tc, h=4),
                in1=xin["v"][bi][:, :, 0:tc].rearrange("p h t -> p t h").unsqueeze(3).to_broadcast([128, tc, 4, 64]),
                op=ALU.mult), reads=[("xB", "kp", bi), ("xin", "v", bi)] + GK, writes=[("vk", bi)])
            for tl_ in range(tc):
                sl = slice(tl_ * 256, (tl_ + 1) * 256)
                t = t0 + tl_
                P.op("dve", lambda e: e.tensor_tensor(out=m1, in0=Sst, in1=xB["nkk"][bi][:, sl], op=ALU.mult),
                     reads=["S", ("xB", "nkk", bi)] + GK, writes=["m1"])
                P.op("dve", lambda e: e.tensor_reduce(out=sa, in_=v3(m1), axis=AX.X, op=ALU.add), reads=["m1"] + GK, writes=["sa"])
                P.op("pool", lambda e: e.tensor_tensor(out=S2, in0=Sst, in1=xB["w"][bi][:, sl], op=ALU.mult),
                     reads=["S", ("xB", "w", bi)] + GK, writes=["S2"])
                P.op("pool", lambda e: e.tensor_tensor(out=S2, in0=S2, in1=vkb[bi][:, sl], op=ALU.add),
                     reads=["S2", ("vk", bi)] + GK, writes=["S2"])
                P.op("dve", lambda e: e.tensor_tensor(out=v3(m2), in0=v3(xB["b"][bi][:, sl]), in1=sa.unsqueeze(2).to_broadcast([128, 4, 64]),
                                                      op=ALU.mult), reads=["sa", ("xB", "b", bi)] + GK, writes=["m2"])
                P.op("dve", lambda e: e.tensor_tensor(out=Sst, in0=S2, in1=m2, op=ALU.add), reads=["S2", "m2"] + GK, writes=["S"])
                P.op("pool", lambda e: e.tensor_tensor(out=m3, in0=Sst, in1=xB["r"][bi][:, sl], op=ALU.mult),
                     reads=["S", ("xB", "r", bi)] + GK, writes=["m3"])
                yc = (t - (ycol0 - col0))
                P.op("dve", lambda e: e.tensor_reduce(out=yb[:, :, yc], in_=v3(m3), axis=AX.X, op=ALU.add), reads=["m3"] + GK, writes=[ybk])
                if yc == 255 or t == T - 1:
                    P.dma("sp", y_fm[:, :, ycol0:ycol0 + yc + 1], yb[:, :, 0:yc + 1], reads=[ybk] + GK, writes=["dram_yfm"])
                    ycol0 += yc + 1
                    yb_i = ycnt[0] % 2
                    ycnt[0] += 1
                    yb, ybk = ybuf[yb_i], "ybuf%d" % yb_i
        dst = o_wkv[seq_i].rearrange("(hf hp) i j -> hp i hf j", hp=2)
        for hp in range(2):
            P.dma("sp", dst[hp], v3(Sst)[hp * 64:(hp + 1) * 64], reads=["S"] + GK, is_output=True)

    scan_seq(0, 0, SEQ_SCAN)
    for s_ in range(NS):
        scan_seq(1 + s_, SEQ + s_ * ST, ST)

    arena_gate()
    off[0] = 0
    ktile = [cv(SEQ) for _ in range(3)]
    vaug = [cv(32 * 65).rearrange("p (c f) -> p c f", c=32) for _ in range(2)]
    vcaug = [cv(2 * 129).rearrange("p (c f) -> p c f", c=2) for _ in range(2)]
    kcT = [cv(256) for _ in range(2)]
    w1_sb = cv(32 * 128).rearrange("p (j e) -> p j e", j=32)
    w2a_sb = cv(64)
    peT_sb = cv(64).rearrange("p (k j) -> p k j", k=2)
    hx, ht_, hs_ = cv(256), cv(256), cv(256)
    tri_sb, ident_sb = cv(128), cv(128)
    cvalT_sb = cv(256).rearrange("p (c q) -> p c q", c=2)
    jidx_sb, f0_sb, eall_sb, idx16_sb, curb_sb = cv(64), cv(64), cv(SEQ), cv(16), cv(1)
    ones_c = cv(512)
    qs4, qsq, e_sb = cv(512), cv(512), [cv(512), cv(512)]
    mrow = cv(3 * 512).rearrange("p (b n) -> p b n", b=3)
    kmx = cv(8)
    mask_sb = [cv(128), cv(128)]
    gat = cv(24)
    ob = cv(256).rearrange("p (h d) -> p h d", h=4)
    imp, Am, Fm, F2m, NFm, nfm, nf2m, selm = (cv(64) for _ in range(8))
    t16, oh16 = cv(16), cv(16)
    cur_, curm1, nF_, thr_, rc_ = (cv(1) for _ in range(5))
    rc4 = cv(4)
    selT_sb = cv(128)
    ybT = cv(2 * 128).rearrange("p (c q) -> p c q", c=2)
    pe_sb = cv(1)
    assert off[0] <= 135168, off[0]

    def A(eng, fn, reads=(), writes=(), **kw):
        return P.op(eng, fn, reads=list(reads) + GK, writes=writes, **kw)

    def Dm(out, in_, reads=(), writes=(), **kw):
        return P.dma("sp", out, in_, reads=list(reads) + GK, writes=writes, **kw)

    for dst_, src_ in ((tri_sb, tri_d), (ident_sb, ident_d), (cvalT_sb, cvalT_d), (jidx_sb, jidx_d), (eall_sb[0:64, :], eall_d),
                       (idx16_sb, idx16_d), (curb_sb, curb_d)):
        Dm(dst_, src_, writes=["ncst"])
    A("pool", lambda e: e.memset(ones_c, 1.0), writes=["ncst"])
    A("dve", lambda e: e.tensor_scalar(out=f0_sb, in0=jidx_sb, scalar1=0.0, scalar2=None, op0=ALU.is_equal), reads=["ncst"], writes=["ncst2"])
    for g in range(2):
        A("pool", lambda e: e.memset(vcaug[g][:, :, 64:65], 1.0), writes=[("vcaug", g)])
        Dm(vcaug[g][:, :, 65:129], bimp_d, writes=[("vcaug", g)])

    for kvi in range(2):
        Dm(w1_sb[0:64], phi_w1[kvi].rearrange("j d e -> d j e"), writes=["w1"])
        Dm(w2a_sb, phi_w2[kvi], writes=["w2a"])
        if kvi == 0:
            Dm(peT_sb[0:64], peT_d, writes=["peT"])
        for j in range(32):
            A("pe", lambda e: e.matmul(ps[3][:, 0:1], lhsT=w1_sb[0:64, j, :], rhs=peT_sb[0:64, kvi, j:j + 1], start=(j == 0), stop=(j == 31)),
              reads=["w1", "peT"], writes=["ps3"], skip_own=True)
        A("act", lambda e: e.copy(out=pe_sb, in_=ps[3][:, 0:1]), reads=["ps3"], writes=["pe_sb"])
        for g in range(2):
            xc = ktile[0]
            Dm(xc[0:64, :], nsaT[:, (8 + g) if kvi == 0 else (14 + g), 0:SEQ], reads=["dram_nsaT"], writes=["kt0"])
            for j in range(32):
                A("pe", lambda e: e.matmul(ps[0][:, 0:255], lhsT=w1_sb[0:64, j, :], rhs=xc[0:64, j:j + 16 * 254 + 1:16],
                                           start=(j == 0), stop=(j == 31)), reads=["w1", "kt0"], writes=["ps0"], skip_own=True)
            A("act", lambda e: e.activation(out=hx[:, 0:255], in_=ps[0][:, 0:255], func=AF.Identity, bias=pe_sb[:, 0:1], scale=1.0),
              reads=["ps0", "pe_sb"], writes=["hx"])
            A("dve", lambda e: e.tensor_tensor(out=ht_[:, 0:255], in0=hx[:, 0:255], in1=hx[:, 0:255], op=ALU.mult), reads=["hx"], writes=["ht"])
            A("dve", lambda e: e.tensor_scalar(out=ht_[:, 0:255], in0=ht_[:, 0:255], scalar1=0.044715, scalar2=1.0, op0=ALU.mult, op1=ALU.add),
              reads=["ht"], writes=["ht"])
            A("dve", lambda e: e.tensor_tensor(out=ht_[:, 0:255], in0=ht_[:, 0:255], in1=hx[:, 0:255], op=ALU.mult), reads=["ht", "hx"], writes=["ht"])
            A("act", lambda e: e.activation(out=hs_[:, 0:255], in_=ht_[:, 0:255], func=AF.Sigmoid, scale=1.5957691216057308), reads=["ht"], writes=["hs"])
            A("dve", lambda e: e.tensor_tensor(out=hx[:, 0:255], in0=hx[:, 0:255], in1=hs_[:, 0:255], op=ALU.mult), reads=["hx", "hs"], writes=["hx"])
            if kvi == 0:
                A("pe", lambda e: e.matmul(ps[1][0:64, 0:255], lhsT=w2a_sb[:, 0:64], rhs=hx[:, 0:255], start=True, stop=True),
                  reads=["w2a", "hx"], writes=["ps1"])
                A("act", lambda e: e.copy(out=kcT[g][0:64, 0:255], in_=ps[1][0:64, 0:255]), reads=["ps1"], writes=[("kcT", g)])
            else:
                for ch, nk in ((0, 128), (1, 127)):
                    A("pe", lambda e: e.matmul(ps[1][0:nk, 0:64], lhsT=hx[:, ch * 128:ch * 128 + nk], rhs=w2a_sb[:, 0:64], start=True, stop=True),
                      reads=["w2a", "hx"], writes=["ps1"])
                    A("act", lambda e: e.copy(out=vcaug[g][0:nk, ch, 0:64], in_=ps[1][0:nk, 0:64]), reads=["ps1"], writes=[("vcaug", g)])

    def key_max(kt, nkeys, slot, ktk="kt*"):
        nchunk = (nkeys + 511) // 512
        for c in range(nchunk):
            w_ = min(512, nkeys - c * 512)
            A("act", lambda e: e.activation(out=qsq[0:64, 0:w_], in_=kt[0:64, c * 512:c * 512 + w_], func=AF.Square), reads=[ktk], writes=["qsq"])
            A("pe", lambda e: e.matmul(ps[3][0:1, 0:w_], lhsT=ones_c[0:64, 0:1], rhs=qsq[0:64, 0:w_], start=True, stop=True),
              reads=["qsq", "ncst"], writes=["ps3"])
            A("dve", lambda e: e.tensor_reduce(out=t16[0:1, c:c + 1], in_=ps[3][0:1, 0:w_], axis=AX.X, op=ALU.max), reads=["ps3"], writes=["t16"])
        A("dve", lambda e: e.tensor_reduce(out=kmx[0:1, slot:slot + 1], in_=t16[0:1, 0:nchunk], axis=AX.X, op=ALU.max), reads=["t16"], writes=["kmx"])
        A("act", lambda e: e.sqrt(out=kmx[0:1, slot:slot + 1], in_=kmx[0:1, slot:slot + 1]), reads=["kmx"], writes=["kmx"])
        A("dve", lambda e: e.tensor_scalar(out=kmx[0:1, slot:slot + 1], in0=kmx[0:1, slot:slot + 1], scalar1=-1.0, scalar2=None, op0=ALU.mult),
          reads=["kmx"], writes=["kmx"])

    ecnt = [0]

    def attend(kt_chunk, nk, br, mask_ap, mask_key, vaug_chunk, W, first, last, ktk="kt*", vk="vaug*"):
        i = ecnt[0] % 2
        ecnt[0] += 1
        sc, sk = ps[i], "ps%d" % i
        es_, ek = e_sb[i], "e%d" % i
        A("pe", lambda e: e.matmul(sc[0:nk, 0:512], lhsT=kt_chunk, rhs=qs4[0:64, :], start=True, stop=False),
          reads=[ktk, "qs4"], writes=[sk], skip_own=True)
        A("pe", lambda e: e.matmul(sc[0:nk, 0:512], lhsT=ones_c[0:1, 0:nk], rhs=mrow[0:1, br, :], start=False, stop=True),
          reads=["mrow", "ncst"], writes=[sk], skip_own=True)
        A("act", lambda e: e.activation(out=es_[0:nk, :], in_=sc[0:nk, 0:512], func=AF.Exp), reads=[sk], writes=[ek])
        if mask_ap is not None:
            A("dve", lambda e: e.tensor_tensor(out=es_[0:nk, :].rearrange("p (h q) -> p h q", h=4), in0=es_[0:nk, :].rearrange("p (h q) -> p h q", h=4),
                                               in1=mask_ap.unsqueeze(1).to_broadcast([nk, 4, 128]), op=ALU.mult),
              reads=[ek, mask_key], writes=[ek])
        for h in range(4):
            A("pe", lambda e: e.matmul(ps[4 + h][:, 0:W], lhsT=es_[0:nk, h * 128:(h + 1) * 128], rhs=vaug_chunk, start=first, stop=last),
              reads=[ek, vk], writes=["ps%d" % (4 + h)], skip_own=True)

    def finish_branch(g, br, first_branch):
        for h in range(4):
            o = ps[4 + h]
            ok = "ps%d" % (4 + h)
            A("dve", lambda e: e.tensor_scalar(out=rc_, in0=o[:, 64:65], scalar1=1e-30, scalar2=None, op0=ALU.max), reads=[ok], writes=["rc"])
            A("dve", lambda e: e.reciprocal(out=rc_, in_=rc_), reads=["rc"], writes=["rc"])
            if br == 0:
                if h == 0:
                    A("dve", lambda e: e.tensor_scalar(out=imp, in0=o[:, 65:129], scalar1=rc_[:, 0:1], scalar2=None, op0=ALU.mult),
                      reads=[ok, "rc"], writes=["imp"])
                else:
                    A("dve", lambda e: e.scalar_tensor_tensor(out=imp, in0=o[:, 65:129], scalar=rc_[:, 0:1], in1=imp, op0=ALU.mult, op1=ALU.add),
                      reads=[ok, "rc", "imp"], writes=["imp"])
            gi_ = (4 * g + h) * 3 + br
            A("dve", lambda e: e.tensor_tensor(out=rc_, in0=rc_, in1=gat[:, gi_:gi_ + 1], op=ALU.mult), reads=["rc", "gat"], writes=["rc"])
            if first_branch:
                A("dve", lambda e: e.tensor_scalar(out=ob[:, h, :], in0=o[:, 0:64], scalar1=rc_[:, 0:1], scalar2=None, op0=ALU.mult),
                  reads=[ok, "rc"], writes=["ob"])
            else:
                A("dve", lambda e: e.scalar_tensor_tensor(out=ob[:, h, :], in0=o[:, 0:64], scalar=rc_[:, 0:1], in1=ob[:, h, :], op0=ALU.mult, op1=ALU.add),
                  reads=[ok, "rc", "ob"], writes=["ob"])

    def select_blocks(curval):
        A("dve", lambda e: e.tensor_scalar(out=cur_, in0=curb_sb, scalar1=float(curval), scalar2=None, op0=ALU.add), reads=["ncst"], writes=["cur"])
        A("dve", lambda e: e.tensor_scalar(out=curm1, in0=cur_, scalar1=-1.0, scalar2=None, op0=ALU.add), reads=["cur"], writes=["curm1"])
        A("dve", lambda e: e.tensor_scalar(out=Am, in0=jidx_sb, scalar1=cur_[:, 0:1], scalar2=None, op0=ALU.is_le), reads=["cur", "ncst"], writes=["Am"])
        A("dve", lambda e: e.tensor_scalar(out=Fm, in0=jidx_sb, scalar1=cur_[:, 0:1], scalar2=None, op0=ALU.is_equal), reads=["cur", "ncst"], writes=["Fm"])
        A("dve", lambda e: e.tensor_scalar(out=F2m, in0=jidx_sb, scalar1=curm1[:, 0:1], scalar2=None, op0=ALU.is_equal), reads=["curm1", "ncst"], writes=["F2m"])
        A("dve", lambda e: e.tensor_tensor(out=Fm, in0=Fm, in1=F2m, op=ALU.max), reads=["Fm", "F2m"], writes=["Fm"])
        A("dve", lambda e: e.tensor_tensor(out=Fm, in0=Fm, in1=f0_sb, op=ALU.max), reads=["Fm", "ncst2"], writes=["Fm"])
        A("dve", lambda e: e.tensor_tensor(out=NFm, in0=Am, in1=Fm, op=ALU.subtract), reads=["Am", "Fm"], writes=["NFm"])
        A("dve", lambda e: e.scalar_tensor_tensor(out=nfm, in0=imp, scalar=1.0, in1=NFm, op0=ALU.add, op1=ALU.mult), reads=["imp", "NFm"], writes=["nfm"])
        A("dve", lambda e: e.tensor_scalar(out=nfm, in0=nfm, scalar1=-1.0, scalar2=None, op0=ALU.add), reads=["nfm"], writes=["nfm"])
        A("dve", lambda e: e.tensor_reduce(out=nF_, in_=Fm, axis=AX.X, op=ALU.add), reads=["Fm"], writes=["nF"])
        A("dve", lambda e: e.tensor_scalar(out=nF_, in0=nF_, scalar1=-1.0, scalar2=15.0, op0=ALU.mult, op1=ALU.add), reads=["nF"], writes=["nF"])
        A("dve", lambda e: e.tensor_scalar(out=oh16, in0=idx16_sb, scalar1=nF_[:, 0:1], scalar2=None, op0=ALU.is_equal), reads=["nF", "ncst"], writes=["oh16"])
        A("dve", lambda e: e.max(out=t16[:, 0:8], in_=nfm), reads=["nfm"], writes=["t16"])
        A("dve", lambda e: e.match_replace(out=nf2m, in_to_replace=t16[:, 0:8], in_values=nfm, imm_value=-2.0), reads=["nfm", "t16"], writes=["nf2m"])
        A("dve", lambda e: e.max(out=t16[:, 8:16], in_=nf2m), reads=["nf2m"], writes=["t16"])
        A("dve", lambda e: e.tensor_tensor(out=t16, in0=t16, in1=oh16, op=ALU.mult), reads=["t16", "oh16"], writes=["t16"])
        A("dve", lambda e: e.tensor_reduce(out=thr_, in_=t16, axis=AX.X, op=ALU.add), reads=["t16"], writes=["thr"])
        A("dve", lambda e: e.tensor_scalar(out=selm, in0=nfm, scalar1=thr_[:, 0:1], scalar2=None, op0=ALU.is_ge), reads=["nfm", "thr"], writes=["selm"])
        A("dve", lambda e: e.tensor_tensor(out=selm, in0=selm, in1=NFm, op=ALU.mult), reads=["selm", "NFm"], writes=["selm"])
        A("dve", lambda e: e.tensor_tensor(out=selm, in0=selm, in1=Fm, op=ALU.add), reads=["selm", "Fm"], writes=["selm"])
        A("pe", lambda e: e.transpose(out=ps[3][0:64, 0:128], in_=selm, identity=ident_sb), reads=["selm", "ncst"], writes=["ps3"])
        A("act", lambda e: e.copy(out=selT_sb[0:64, :], in_=ps[3][0:64, 0:128]), reads=["ps3"], writes=["selT"])

    mcnt = [0]

    def sel_mask(chunk, diag):
        i = mcnt[0] % 2
        mcnt[0] += 1
        A("pe", lambda e: e.matmul(ps[2][:, 0:128], lhsT=eall_sb[0:64, chunk * 128:(chunk + 1) * 128], rhs=selT_sb[0:64, :], start=True, stop=True),
          reads=["selT", "ncst"], writes=["ps2"])
        if diag:
            A("dve", lambda e: e.tensor_tensor(out=mask_sb[i], in0=ps[2][:, 0:128], in1=tri_sb, op=ALU.mult), reads=["ps2", "ncst"], writes=[("mask", i)])
        else:
            A("act", lambda e: e.copy(out=mask_sb[i], in_=ps[2][:, 0:128]), reads=["ps2"], writes=[("mask", i)])
        return mask_sb[i], ("mask", i)

    ntri_sb = hs_[:, 0:128]
    A("dve", lambda e: e.tensor_scalar(out=ntri_sb, in0=tri_sb, scalar1=-1.0, scalar2=1.0, op0=ALU.mult, op1=ALU.add), reads=["ncst", "hs"], writes=["ntri"])

    NQB = SEQ_NSA // 128
    A("pool", lambda e: e.memset(qsq, 0.0), writes=["qsq", "qsq0"])
    Dm(yb_fm[:, :, SEQ:TT], ybT[:, :, 0:NS * ST].rearrange("p c q -> p (c q)")[:, 0:4 * NS * ST].rearrange("p (c q) -> p c q", c=4)
       if False else qsq[:, 0:4 * NS * ST].rearrange("p (c q) -> p c q", c=4), reads=["qsq0"], writes=["dram_ybfm"])
    for g in range(2):
        Dm(ktile[1][0:64, :], nsaT[:, 10 + g, 0:SEQ], reads=["dram_nsaT"], writes=["kt*"])
        Dm(ktile[2][0:64, :], nsaT[:, 12 + g, 0:SEQ], reads=["dram_nsaT"], writes=["kt*"])
        for bi_, col in ((0, 256 + 128 + g * 64), (1, 512 + 128 + g * 64)):
            Dm(vaug[bi_][:, :, 0:64], kv_tm[0:SEQ, col:col + 64].rearrange("(c p) f -> p c f", p=128), reads=["dram_kvtm"], writes=["vaug*"])
            A("pool", lambda e: e.memset(vaug[bi_][:, :, 64:65], 1.0), writes=["vaug*"])
        key_max(kcT[g], 255, 0, ("kcT", g))
        key_max(ktile[1], SEQ, 1)
        key_max(ktile[2], SEQ, 2)
        for qb in range(NQB):
            q0 = qb * 128
            Dm(qs4[0:64, :].rearrange("p (h q) -> p h q", h=4), nsaT[:, 4 * g:4 * g + 4, q0:q0 + 128], reads=["dram_nsaT"], writes=["qs4"])
            Dm(gat, gates_tm[q0:q0 + 128, :], reads=["dram_gates"], writes=["gat"])
            A("act", lambda e: e.activation(out=qsq[0:64, :], in_=qs4[0:64, :], func=AF.Square), reads=["qs4"], writes=["qsq"])
            A("pe", lambda e: e.matmul(ps[3][0:1, 0:512], lhsT=ones_c[0:64, 0:1], rhs=qsq[0:64, :], start=True, stop=True),
              reads=["qsq", "ncst"], writes=["ps3"])
            A("act", lambda e: e.sqrt(out=mrow[0:1, 0, :], in_=ps[3][0:1, 0:512]), reads=["ps3"], writes=["mrow"])
            for br in (2, 1, 0):
                A("dve", lambda e: e.tensor_scalar(out=mrow[0:1, br, :], in0=mrow[0:1, 0, :], scalar1=kmx[0:1, br:br + 1], scalar2=None, op0=ALU.mult),
                  reads=["mrow", "kmx"], writes=["mrow"])
            for ch, nk in ((0, 128), (1, 127)):
                i = mcnt[0] % 2
                mcnt[0] += 1
                A("dve", lambda e: e.tensor_scalar(out=mask_sb[i][0:nk, :], in0=cvalT_sb[0:nk, ch, :], scalar1=float(q0), scalar2=None, op0=ALU.is_le),
                  reads=["ncst"], writes=[("mask", i)])
                attend(kcT[g][0:64, ch * 128:ch * 128 + nk], nk, 0, mask_sb[i][0:nk, :], ("mask", i), vcaug[g][0:nk, ch, :], 129, ch == 0, ch == 1, ktk=("kcT", g), vk=("vcaug", g))
            finish_branch(g, 0, True)
            select_blocks(2 * qb)
            for ch in range(qb + 1):
                mk, mkk = sel_mask(ch, ch == qb)
                attend(ktile[1][0:64, ch * 128:(ch + 1) * 128], 128, 1, mk, mkk, vaug[0][:, ch, :], 65, ch == 0, ch == qb)
            finish_branch(g, 1, False)
            lo = max(0, qb - 4)
            for ch in range(lo, qb + 1):
                if ch == qb:
                    mk, mkk = tri_sb, "ncst"
                elif ch == qb - 4:
                    mk, mkk = ntri_sb, "ntri"
                else:
                    mk, mkk = None, None
                attend(ktile[2][0:64, ch * 128:(ch + 1) * 128], 128, 2, mk, mkk, vaug[1][:, ch, :], 65, ch == lo, ch == qb)
            finish_branch(g, 2, False)
            for c2 in range(2):
                A("pe", lambda e: e.transpose(out=ps[3][:, 0:128], in_=ob[:, 2 * c2:2 * c2 + 2, :].rearrange("p h d -> p (h d)"), identity=ident_sb),
                  reads=["ob", "ncst"], writes=["ps3"])
                A("act", lambda e: e.copy(out=ybT[:, c2, :], in_=ps[3][:, 0:128]), reads=["ps3"], writes=["ybT"])
            Dm(yb_fm[:, 2 * g:2 * g + 2, q0:q0 + 128], ybT, reads=["ybT"], writes=["dram_ybfm"])

    if DO_SAMPLE:
        arena_gate()
        off[0] = 0
        PAST = 16384
        NPG = NPG_DBG
        GK = GK + ["xt0", "xt1"]
        xoff = [0, 0]

        def cvx(i, ncols):
            v = xt[i][:].rearrange("p c n -> p (c n)")[:, xoff[i]:xoff[i] + ncols]
            xoff[i] += ncols
            assert xoff[i] <= DC * NT
            return v
        xTs = cv(PAST + 4)
        regB = cv(8192)
        w1pad = regB.rearrange("p (g j e) -> p g j e", g=2, j=32)
        pg = [cv(128) for _ in range(3)]
        vpg = [cv(2 * 65).rearrange("p (g f) -> p g f", g=2) for _ in range(2)]
        ptf, idxf = cv(NS * 128), cv(NS * 128)
        idx_i = [cv(NS * 128).bitcast(I32) for _ in range(2)]
        pt_i = cv(NS * 128).bitcast(I32)
        pcol_sb, hsel_sb = cv(1), cv(4)
        tri16, ntri16, ident_s, ones_s = cv(16), cv(16), cv(128), cv(512)
        onesg = cv(2)
        jidx2, f02 = cv(264), cv(264)
        qg = [cv(16) for _ in range(2)]
        qpad = [cv(16) for _ in range(2)]
        qsqs, qn_s = cv(16), cv(16)
        mrow_s = cv(6 * 16).rearrange("p (b n) -> p b n", b=6)
        kmx_s = cv(8)
        es2 = [cv(16), cv(16)]
        msk2 = [cv(16), cv(16)]
        gat_s = [cv(3), cv(3)]
        oacc = [cv(64), cv(64)]
        o322 = cv(322)
        imp2, Am2, Fm2, F2m2, NFm2, nfm2, nf2m2 = (cvx(1, 264) for _ in range(7))
        selm2 = cv(264)
        t16b, oh16b, idx16_s = cv(16), cv(16), cv(16)
        cur2, curm2, nF2, thr2, rc2, zero1 = (cv(1) for _ in range(6))
        selT2 = [cv(2 * 16).rearrange("p (k q) -> p k q", k=2) for _ in range(2)]
        vnew = [cv(2 * 65).rearrange("p (g f) -> p g f", g=2) for _ in range(2)]
        wv = cv(4 * 2 * 65).rearrange("p (c g f) -> p c g f", c=4, g=2)
        wk = cv(4 * 256).rearrange("p (c f) -> p c f", c=4)
        kTw = cv(516)
        obT = cv(16)
        sqt = cvx(0, 512)
        hx2, ht2, hs2 = cvx(0, 512), cvx(0, 512), cvx(0, 512)
        pe2, w2b = cv(1), cv(64)
        peT2 = cv(64).rearrange("p (k j) -> p k j", k=2)
        assert off[0] <= 135168, off[0]
        vcs = [stage[g_][:, 0:8 * 322].rearrange("p (c f) -> p c f", c=8) for g_ in range(2)]
        kcs = hT[:].rearrange("p f n -> p (f n)").bitcast(F32)[:, 0:2048].rearrange("p (g n) -> p g n", g=2)
        SK = ["stage0", "stage1", "hT"]

        def Gq(out, cache, half, col, reads=(), writes=()):
            return P.dma("pool", out, cache, reads=list(reads) + GK, writes=writes, gather_idx=idx_i[half][:, col:col + 1])

        for dst_, src_ in ((tri16, tri16_d), (ident_s, ident_d), (pcol_sb, pcol_d), (hsel_sb[0:16, :], hsel_d), (jidx2, jidx2_d), (idx16_s, idx16_d)):
            Dm(dst_, src_, writes=["scst"])
        A("pool", lambda e: e.memset(ones_s, 1.0), writes=["scst"])
        A("pool", lambda e: e.memset(zero1, 0.0), writes=["scst"])
        A("pool", lambda e: e.memset(onesg, 0.0), writes=["onesg"])
        A("pool", lambda e: e.memset(onesg[0:64, 0:1], 1.0), reads=["onesg"], writes=["onesg"])
        A("pool", lambda e: e.memset(onesg[64:128, 1:2], 1.0), reads=["onesg"], writes=["onesg"])
        A("dve", lambda e: e.tensor_scalar(out=ntri16, in0=tri16, scalar1=-1.0, scalar2=1.0, op0=ALU.mult, op1=ALU.add), reads=["scst"], writes=["scst2"])
        A("dve", lambda e: e.tensor_scalar(out=f02, in0=jidx2, scalar1=0.0, scalar2=None, op0=ALU.is_equal), reads=["scst"], writes=["scst2"])
        for g in range(2):
            A("pool", lambda e: e.memset(vcs[g][:, :, 64:65], 1.0), reads=SK, writes=[("vcs", g)])
            Dm(vcs[g][:, :, 65:322], bimps_d, reads=SK, writes=[("vcs", g)])
            A("pool", lambda e: e.memset(vpg[g][:, :, 64:65], 1.0), writes=[("vpg", g)])
            A("pool", lambda e: e.memset(vnew[g][:, :, 64:65], 1.0), writes=["vnew"])
            A("pool", lambda e: e.memset(qpad[g], 0.0), writes=["qpad"])
        A("pool", lambda e: e.memset(wv[:, :, :, 64:65], 1.0), writes=["wv"])
        Dm(pt_i, ptab.partition_broadcast(128), writes=["pt"])
        A("dve", lambda e: e.tensor_copy(out=ptf, in_=pt_i), reads=["pt"], writes=["ptf"])
        A("dve", lambda e: e.tensor_scalar(out=idxf, in0=ptf, scalar1=128.0, scalar2=pcol_sb[:, 0:1], op0=ALU.mult, op1=ALU.add), reads=["ptf", "scst"], writes=["idxf"])
        A("dve", lambda e: e.tensor_scalar(out=idxf, in0=idxf, scalar1=2.0, scalar2=None, op0=ALU.mult), reads=["idxf"], writes=["idxf"])
        A("dve", lambda e: e.tensor_copy(out=idx_i[0], in_=idxf), reads=["idxf"], writes=["idx"])
        A("dve", lambda e: e.tensor_scalar(out=ptf, in0=idxf, scalar1=1.0, scalar2=None, op0=ALU.add), reads=["idxf", "ptf"], writes=["ptf"])
        A("dve", lambda e: e.tensor_copy(out=idx_i[1], in_=ptf), reads=["ptf"], writes=["idx"])
        Dm(peT2[0:64], peT_d, writes=["peT2"])

        pcnt = [0]

        def gather_T(cache, s_, j, half, dst, dst_cols, dkey):
            i = pcnt[0] % 3
            pcnt[0] += 1
            Gq(pg[i], cache, half, s_ * 128 + j, reads=["idx"], writes=[("pg", i)])
            pb_ = ps[6 + i % 2]
            pbk = "ps%d" % (6 + i % 2)
            A("pe", lambda e: e.transpose(out=pb_[:, 0:128], in_=pg[i], identity=ident_s), reads=[("pg", i), "scst"], writes=[pbk])
            if i % 2 == 0:
                A("act", lambda e: e.copy(out=dst[:, dst_cols], in_=pb_[:, 0:128]), reads=[pbk], writes=[dkey])
            else:
                A("dve", lambda e: e.tensor_copy(out=dst[:, dst_cols], in_=pb_[:, 0:128]), reads=[pbk], writes=[dkey])

        def gelu_to(hx_, n_):
            A("dve", lambda e: e.tensor_tensor(out=ht2[:, 0:n_], in0=hx_, in1=hx_, op=ALU.mult), reads=["hx2"], writes=["ht2"])
            A("dve", lambda e: e.tensor_scalar(out=ht2[:, 0:n_], in0=ht2[:, 0:n_], scalar1=0.044715, scalar2=1.0, op0=ALU.mult, op1=ALU.add), reads=["ht2"], writes=["ht2"])
            A("dve", lambda e: e.tensor_tensor(out=ht2[:, 0:n_], in0=ht2[:, 0:n_], in1=hx_, op=ALU.mult), reads=["ht2", "hx2"], writes=["ht2"])
            A("act", lambda e: e.activation(out=hs2[:, 0:n_], in_=ht2[:, 0:n_], func=AF.Sigmoid, scale=1.5957691216057308), reads=["ht2"], writes=["hs2"])
            A("dve", lambda e: e.tensor_tensor(out=hx_, in0=hx_, in1=hs2[:, 0:n_], op=ALU.mult), reads=["hx2", "hs2"], writes=["hx2"])

        def key_max_s(kt, K_, ncols, slot, ktk, lhs1):
            nchunk = (ncols + 511) // 512
            for c in range(nchunk):
                w_ = min(512, ncols - c * 512)
                A("act", lambda e: e.activation(out=sqt[0:K_, 0:w_], in_=kt[0:K_, c * 512:c * 512 + w_], func=AF.Square), reads=[ktk], writes=["sqt"])
                A("pe", lambda e: e.matmul(ps[3][0:1, 0:w_], lhsT=lhs1, rhs=sqt[0:K_, 0:w_], start=True, stop=True), reads=["sqt", "scst", "onesg"], writes=["ps3"])
                if c == 0:
                    A("dve", lambda e: e.tensor_reduce(out=kmx_s[0:1, slot:slot + 1], in_=ps[3][0:1, 0:w_], axis=AX.X, op=ALU.max), reads=["ps3"], writes=["kmxs"])
                else:
                    A("dve", lambda e: e.tensor_reduce(out=kmx_s[0:1, 7:8], in_=ps[3][0:1, 0:w_], axis=AX.X, op=ALU.max), reads=["ps3"], writes=["kmxs"])
                    A("dve", lambda e: e.tensor_tensor(out=kmx_s[0:1, slot:slot + 1], in0=kmx_s[0:1, slot:slot + 1], in1=kmx_s[0:1, 7:8], op=ALU.max),
                      reads=["kmxs"], writes=["kmxs"])
            A("act", lambda e: e.sqrt(out=kmx_s[0:1, slot:slot + 1], in_=kmx_s[0:1, slot:slot + 1]), reads=["kmxs"], writes=["kmxs"])
            A("dve", lambda e: e.tensor_scalar(out=mrow_s[0:1, slot, :], in0=qn_s[0:1, :] if False else mrow_s[0:1, slot, :], scalar1=1.0, scalar2=None, op0=ALU.mult),
              reads=["mrows"], writes=["mrows"]) if False else None

        def set_mrow(slot, g):
            A("act", lambda e: e.activation(out=qsqs[0:64, :], in_=qg[g][0:64, :], func=AF.Square), reads=["qg"], writes=["qsqs"])
            A("pe", lambda e: e.matmul(ps[3][0:1, 0:16], lhsT=ones_s[0:64, 0:1], rhs=qsqs[0:64, :], start=True, stop=True), reads=["qsqs", "scst"], writes=["ps3"])
            A("act", lambda e: e.sqrt(out=qn_s[0:1, :], in_=ps[3][0:1, 0:16]), reads=["ps3"], writes=["qn"])
            A("dve", lambda e: e.tensor_scalar(out=mrow_s[0:1, slot, :], in0=qn_s[0:1, :], scalar1=kmx_s[0:1, slot:slot + 1], scalar2=-1.0, op0=ALU.mult, op1=ALU.mult),
              reads=["qn", "kmxs"], writes=["mrows"])

        e2cnt = [0]

        def attend_s(kt_chunk, K_, nk, qtile, slot, mask_ap, mask_key, v_chunk, W, g, first, last, ktk, vk):
            i = e2cnt[0] % 2
            e2cnt[0] += 1
            sc, sk = ps[i], "ps%d" % i
            es_, ek = es2[i], "es%d" % i
            A("pe", lambda e: e.matmul(sc[0:nk, 0:16], lhsT=kt_chunk, rhs=qtile[0:K_, :], start=True, stop=False), reads=[ktk, "qg", "qpad"], writes=[sk], skip_own=True)
            A("pe", lambda e: e.matmul(sc[0:nk, 0:16], lhsT=ones_s[0:1, 0:nk], rhs=mrow_s[0:1, slot, :], start=False, stop=True), reads=["mrows", "scst"], writes=[sk], skip_own=True)
            A("act", lambda e: e.activation(out=es_[0:nk, :], in_=sc[0:nk, 0:16], func=AF.Exp), reads=[sk], writes=[ek])
            if mask_ap is not None:
                A("dve", lambda e: e.tensor_tensor(out=es_[0:nk, :], in0=es_[0:nk, :], in1=mask_ap, op=ALU.mult), reads=[ek, mask_key], writes=[ek])
            A("pe", lambda e: e.matmul(ps[4 + g][0:16, 0:W], lhsT=es_[0:nk, :], rhs=v_chunk, start=first, stop=last), reads=[ek, vk], writes=["ps%d" % (4 + g)], skip_own=True)

        def finish_s(g, br, first_branch):
            o = ps[4 + g]
            ok = "ps%d" % (4 + g)
            A("dve", lambda e: e.tensor_scalar(out=rc2[0:16], in0=o[0:16, 64:65], scalar1=1e-30, scalar2=None, op0=ALU.max), reads=[ok], writes=["rc2"])
            A("dve", lambda e: e.reciprocal(out=rc2[0:16], in_=rc2[0:16]), reads=["rc2"], writes=["rc2"])
            if br == 0:
                A("dve", lambda e: e.tensor_scalar(out=o322[0:16, 0:257], in0=o[0:16, 65:322], scalar1=rc2[0:16, 0:1], scalar2=None, op0=ALU.mult), reads=[ok, "rc2"], writes=["o322"])
                A("pe", lambda e: e.matmul(ps[3][0:4, 0:257], lhsT=hsel_sb[0:16, 0:4], rhs=o322[0:16, 0:257], start=True, stop=True), reads=["o322", "scst"], writes=["ps3"])
                A("act", lambda e: e.copy(out=imp2[0:4, 0:257], in_=ps[3][0:4, 0:257]), reads=["ps3"], writes=["imp2"])
            A("dve", lambda e: e.tensor_tensor(out=rc2[0:16], in0=rc2[0:16], in1=gat_s[g][0:16, br:br + 1], op=ALU.mult), reads=["rc2", "gats"], writes=["rc2"])
            if first_branch:
                A("dve", lambda e: e.tensor_scalar(out=oacc[g][0:16, :], in0=o[0:16, 0:64], scalar1=rc2[0:16, 0:1], scalar2=None, op0=ALU.mult), reads=[ok, "rc2"], writes=[("oacc", g)])
            else:
                A("dve", lambda e: e.scalar_tensor_tensor(out=oacc[g][0:16, :], in0=o[0:16, 0:64], scalar=rc2[0:16, 0:1], in1=oacc[g][0:16, :], op0=ALU.mult, op1=ALU.add),
                  reads=[ok, "rc2", ("oacc", g)], writes=[("oacc", g)])

        def select_s(g):
            R4 = slice(0, 4)
            NB = 257
            v = lambda t_: t_[R4, 0:NB]
            A("dve", lambda e: e.tensor_scalar(out=cur2[R4], in0=zero1[R4], scalar1=256.0, scalar2=None, op0=ALU.add), reads=["scst"], writes=["cur2"])
            A("dve", lambda e: e.tensor_scalar(out=curm2[R4], in0=zero1[R4], scalar1=255.0, scalar2=None, op0=ALU.add), reads=["scst"], writes=["curm2"])
            A("dve", lambda e: e.tensor_scalar(out=v(Am2), in0=v(jidx2), scalar1=cur2[R4, 0:1], scalar2=None, op0=ALU.is_le), reads=["cur2", "scst"], writes=["Am2"])
            A("dve", lambda e: e.tensor_scalar(out=v(Fm2), in0=v(jidx2), scalar1=cur2[R4, 0:1], scalar2=None, op0=ALU.is_equal), reads=["cur2", "scst"], writes=["Fm2"])
            A("dve", lambda e: e.tensor_scalar(out=v(F2m2), in0=v(jidx2), scalar1=curm2[R4, 0:1], scalar2=None, op0=ALU.is_equal), reads=["curm2", "scst"], writes=["F2m2"])
            A("dve", lambda e: e.tensor_tensor(out=v(Fm2), in0=v(Fm2), in1=v(F2m2), op=ALU.max), reads=["Fm2", "F2m2"], writes=["Fm2"])
            A("dve", lambda e: e.tensor_tensor(out=v(Fm2), in0=v(Fm2), in1=v(f02), op=ALU.max), reads=["Fm2", "scst2"], writes=["Fm2"])
            A("dve", lambda e: e.tensor_tensor(out=v(NFm2), in0=v(Am2), in1=v(Fm2), op=ALU.subtract), reads=["Am2", "Fm2"], writes=["NFm2"])
            A("dve", lambda e: e.scalar_tensor_tensor(out=v(nfm2), in0=v(imp2), scalar=1.0, in1=v(NFm2), op0=ALU.add, op1=ALU.mult), reads=["imp2", "NFm2"], writes=["nfm2"])
            A("dve", lambda e: e.tensor_scalar(out=v(nfm2), in0=v(nfm2), scalar1=-1.0, scalar2=None, op0=ALU.add), reads=["nfm2"], writes=["nfm2"])
            A("dve", lambda e: e.tensor_reduce(out=nF2[R4], in_=v(Fm2), axis=AX.X, op=ALU.add), reads=["Fm2"], writes=["nF2"])
            A("dve", lambda e: e.tensor_scalar(out=nF2[R4], in0=nF2[R4], scalar1=-1.0, scalar2=15.0, op0=ALU.mult, op1=ALU.add), reads=["nF2"], writes=["nF2"])
            A("dve", lambda e: e.tensor_scalar(out=oh16b[R4], in0=idx16_s[R4], scalar1=nF2[R4, 0:1], scalar2=None, op0=ALU.is_equal), reads=["nF2", "scst"], writes=["oh16b"])
            A("dve", lambda e: e.max(out=t16b[R4, 0:8], in_=v(nfm2)), reads=["nfm2"], writes=["t16b"])
            A("dve", lambda e: e.match_replace(out=v(nf2m2), in_to_replace=t16b[R4, 0:8], in_values=v(nfm2), imm_value=-2.0), reads=["nfm2", "t16b"], writes=["nf2m2"])
            A("dve", lambda e: e.max(out=t16b[R4, 8:16], in_=v(nf2m2)), reads=["nf2m2"], writes=["t16b"])
            A("dve", lambda e: e.tensor_tensor(out=t16b[R4], in0=t16b[R4], in1=oh16b[R4], op=ALU.mult), reads=["t16b", "oh16b"], writes=["t16b"])
            A("dve", lambda e: e.tensor_reduce(out=thr2[R4], in_=t16b[R4], axis=AX.X, op=ALU.add), reads=["t16b"], writes=["thr2"])
            A("dve", lambda e: e.tensor_scalar(out=v(selm2), in0=v(nfm2), scalar1=thr2[R4, 0:1], scalar2=None, op0=ALU.is_ge), reads=["nfm2", "thr2"], writes=["selm2"])
            A("dve", lambda e: e.tensor_tensor(out=v(selm2), in0=v(selm2), in1=v(NFm2), op=ALU.mult), reads=["selm2", "NFm2"], writes=["selm2"])
            A("dve", lambda e: e.tensor_tensor(out=v(selm2), in0=v(selm2), in1=v(Fm2), op=ALU.add), reads=["selm2", "Fm2"], writes=["selm2"])
            for kch in range(2):
                A("pe", lambda e: e.transpose(out=ps[3][0:128, 0:4], in_=selm2[R4, kch * 128:kch * 128 + 128], identity=ident_s[0:4, 0:4]), reads=["selm2", "scst"], writes=["ps3"])
                for h in range(4):
                    A("act", lambda e: e.copy(out=selT2[g][:, kch, h * 4:(h + 1) * 4], in_=ps[3][0:128, 0:4]), reads=["ps3"], writes=[("selT2", g)])

        def dbg_dump(name, ap, key):
            if DEBUG:
                t = dout("dbg_" + name, list(ap.shape))
                Dm(t, ap, reads=[key], is_output=True)

        for s_ in range(NS_DBG):
            col_s = SEQ + s_ * ST
            for g in range(2):
                Dm(qg[g][0:64, :].rearrange("p (h q) -> p h q", h=4), nsaT[:, 4 * g:4 * g + 4, col_s:col_s + ST], reads=["dram_nsaT"], writes=["qg"])
                Dm(qpad[g][g * 64:(g + 1) * 64, :].rearrange("p (h q) -> p h q", h=4), nsaT[:, 4 * g:4 * g + 4, col_s:col_s + ST], reads=["dram_nsaT"], writes=["qpad"])
                for h in range(4):
                    Dm(gat_s[g][h * 4:(h + 1) * 4, :], gates_tm[col_s:col_s + ST, (4 * g + h) * 3:(4 * g + h) * 3 + 3], reads=["dram_gates"], writes=["gats"])
            for kvi in range(2):
                for g in range(2):
                    A("pool", lambda e: e.memset(w1pad[:, g], 0.0), writes=["regB"])
                    Dm(w1pad[g * 64:(g + 1) * 64, g], phi_w1[kvi].rearrange("j d e -> d j e"), reads=["regB"], writes=["regB"])
                Dm(w2b, phi_w2[kvi], writes=["w2b"])
                for j in range(32):
                    A("pe", lambda e: e.matmul(ps[3][:, 0:1], lhsT=w1pad[0:64, 0, j, :], rhs=peT2[0:64, kvi, j:j + 1], start=(j == 0), stop=(j == 31)),
                      reads=["regB", "peT2"], writes=["ps3"], skip_own=True)
                A("act", lambda e: e.copy(out=pe2, in_=ps[3][:, 0:1]), reads=["ps3"], writes=["pe2"])
                for j in range(NPG):
                    gather_T(cache_cmp, s_, j, kvi, xTs, slice(j * 128, (j + 1) * 128), "xTs")
                for g in range(2):
                    for (n0, ncol) in ((0, 512), (512, 511)):
                        for j in range(32):
                            A("pe", lambda e: e.matmul(ps[2][:, 0:ncol], lhsT=w1pad[:, g, j, :], rhs=xTs[:, 16 * n0 + j:16 * n0 + j + 16 * (ncol - 1) + 1:16],
                                                       start=(j == 0), stop=(j == 31)), reads=["regB", "xTs"], writes=["ps2"], skip_own=True)
                        A("act", lambda e: e.activation(out=hx2[:, 0:ncol], in_=ps[2][:, 0:ncol], func=AF.Identity, bias=pe2[:, 0:1], scale=1.0), reads=["ps2", "pe2"], writes=["hx2"])
                        gelu_to(hx2[:, 0:ncol], ncol)
                        if kvi == 0:
                            A("pe", lambda e: e.matmul(ps[3][0:64, 0:ncol], lhsT=w2b[:, 0:64], rhs=hx2[:, 0:ncol], start=True, stop=True), reads=["w2b", "hx2"], writes=["ps3"])
                            A("act", lambda e: e.copy(out=kcs[0:64, g, n0:n0 + ncol], in_=ps[3][0:64, 0:ncol]), reads=["ps3"] + SK, writes=[("kcs", g)])
                        else:
                            for c4 in range(4):
                                nk = min(128, ncol - c4 * 128)
                                A("pe", lambda e: e.matmul(ps[3][0:nk, 0:64], lhsT=hx2[:, c4 * 128:c4 * 128 + nk], rhs=w2b[:, 0:64], start=True, stop=True), reads=["w2b", "hx2"], writes=["ps3"])
                                A("act", lambda e: e.copy(out=vcs[g][0:nk, n0 // 128 + c4, 0:64], in_=ps[3][0:nk, 0:64]), reads=["ps3"] + SK, writes=[("vcs", g)])
            for g in range(2):
                key_max_s(kcs[:, g, :], 64, 1023, g, ("kcs", g), ones_s[0:64, 0:1])
                set_mrow(g, g)
                for ch in range(8):
                    nk = 128 if ch < 7 else 127
                    attend_s(kcs[0:64, g, ch * 128:ch * 128 + nk], 64, nk, qg[g], g, None, None, vcs[g][0:nk, ch, :], 322, g, ch == 0, ch == 7, ("kcs", g), ("vcs", g))
                finish_s(g, 0, True)
                select_s(g)
            Dm(regB, e2_d, reads=["regB"], writes=["regB"])
            for j in range(NPG):
                gather_T(cache_sel, s_, j, 0, xTs, slice(j * 128, (j + 1) * 128), "xTs")
            for g in range(2):
                Dm(xTs[g * 64:(g + 1) * 64, PAST:PAST + ST], nsaT[:, 10 + g, col_s:col_s + ST], reads=["dram_nsaT"], writes=["xTs"])
            Dm(vnew[0][0:ST, :, 0:64], kv_tm[col_s:col_s + ST, 384:512].rearrange("r (g f) -> r g f", g=2), reads=["dram_kvtm"], writes=["vnew"])
            Dm(vnew[1][0:ST, :, 0:64], kv_tm[col_s:col_s + ST, 640:768].rearrange("r (g f) -> r g f", g=2), reads=["dram_kvtm"], writes=["vnew"])
            for g in range(2):
                key_max_s(xTs, 128, PAST + ST, 2 + g, "xTs", onesg[:, g:g + 1])
                set_mrow(2 + g, g)
            for j in range(NPG):
                i = pcnt[0] % 3
                pcnt[0] += 1
                vi = j % 2
                Gq(pg[i], cache_sel, 1, s_ * 128 + j, reads=["idx"], writes=[("pg", i)])
                A("act", lambda e: e.copy(out=vpg[vi][:, :, 0:64], in_=pg[i].rearrange("p (g f) -> p g f", g=2)), reads=[("pg", i)], writes=[("vpg", vi)])
                kch, cl = j // 64, j % 64
                for g in range(2):
                    mi = e2cnt[0] % 2
                    A("pe", lambda e: e.matmul(ps[2][:, 0:16], lhsT=regB[:, cl * 128:(cl + 1) * 128], rhs=selT2[g][:, kch, :], start=True, stop=True),
                      reads=["regB", ("selT2", g)], writes=["ps2"])
                    A("dve", lambda e: e.tensor_copy(out=msk2[mi], in_=ps[2][:, 0:16]), reads=["ps2"], writes=[("msk2", mi)])
                    attend_s(xTs[:, j * 128:(j + 1) * 128], 128, 128, qpad[g], 2 + g, msk2[mi], ("msk2", mi), vpg[vi][:, g, :], 65, g, j == 0, False, "xTs", ("vpg", vi))
            if s_ == 0:
                dbg_dump("selm2", selm2[0:4, 0:257], "selm2")
                dbg_dump("imp2", imp2[0:4, 0:257], "imp2")
                dbg_dump("kmx", kmx_s[0:1, :], "kmxs")
                dbg_dump("mrow", mrow_s[0:1, :, :], "mrows")
                dbg_dump("selT2", selT2[1][:, :, :], ("selT2", 1))
                dbg_dump("msk", msk2[0][:, :], ("msk2", 0))
                dbg_dump("xTs", xTs[:, PAST - 256:PAST + ST], "xTs")
            for g in range(2):
                attend_s(xTs[:, PAST:PAST + ST], 128, ST, qpad[g], 2 + g, tri16[0:ST, :], "scst", vnew[0][0:ST, g, :], 65, g, NPG == 0, True, "xTs", "vnew")
                finish_s(g, 1, False)
            Dm(wk, win_state[s_].rearrange("(c p) f -> p c f", p=128), writes=["wk"])
            for c4 in range(4):
                A("act", lambda e: e.copy(out=wv[:, c4, :, 0:64], in_=wk[:, c4, 128:256].rearrange("p (g f) -> p g f", g=2)), reads=["wk"], writes=["wv"])
                pb_ = ps[6 + c4 % 2]
                pbk = "ps%d" % (6 + c4 % 2)
                A("pe", lambda e: e.transpose(out=pb_[:, 0:128], in_=wk[:, c4, 0:128], identity=ident_s), reads=["wk", "scst"], writes=[pbk])
                A("dve", lambda e: e.tensor_copy(out=kTw[:, c4 * 128:(c4 + 1) * 128], in_=pb_[:, 0:128]), reads=[pbk], writes=["kTw"])
            for g in range(2):
                Dm(kTw[g * 64:(g + 1) * 64, 512:512 + ST], nsaT[:, 12 + g, col_s:col_s + ST], reads=["dram_nsaT"], writes=["kTw"])
            for g in range(2):
                key_max_s(kTw, 128, 512 + ST, 4 + g, "kTw", onesg[:, g:g + 1])
                set_mrow(4 + g, g)
                for c4 in range(4):
                    attend_s(kTw[:, c4 * 128:(c4 + 1) * 128], 128, 128, qpad[g], 4 + g, ntri16 if c4 == 0 else None, "scst2", wv[:, c4, g, :], 65, g, c4 == 0, False, "kTw", "wv")
                attend_s(kTw[:, 512:512 + ST], 128, ST, qpad[g], 4 + g, tri16[0:ST, :], "scst", vnew[1][0:ST, g, :], 65, g, False, True, "kTw", "vnew")
                finish_s(g, 2, False)
                A("pe", lambda e: e.transpose(out=ps[3][0:64, 0:16], in_=oacc[g][0:16, :], identity=ident_s[0:16, 0:16]), reads=[("oacc", g), "scst"], writes=["ps3"])
                A("act", lambda e: e.copy(out=obT[0:64, :], in_=ps[3][0:64, 0:16]), reads=["ps3"], writes=["obT"])
                for h in range(4):
                    hh = 4 * g + h
                    Dm(yb_fm[(hh % 2) * 64:(hh % 2) * 64 + 64, hh // 2, col_s:col_s + ST], obT[0:64, h * 4:(h + 1) * 4], reads=["obT"], writes=["dram_ybfm"])

    arena_gate()
    wm = arena[:, 0:DC * 2048].rearrange("p (c f) -> p c f", c=DC)
    woa = arena[:, 16384:16384 + 4096].rearrange("p (c f) -> p c f", c=4)
    wob = arena[:, 20480:20480 + 4096].rearrange("p (c f) -> p c f", c=4)
    wo = arena[:, 24576:24576 + 8192].rearrange("p (c f) -> p c f", c=DC)
    for c in range(DC):
        load_weight_bf16(wm[:, c, :], w_in[c * 128:(c + 1) * 128, 3096:5144], 2048, ("wm", c))
        load_weight_bf16(wo[:, c, :], w_o[c * 128:(c + 1) * 128, :], 1024, ("wo", c))
    for c in range(4):
        load_weight_bf16(woa[:, c, :], w_out_a[c * 128:(c + 1) * 128, :], 1024, ("woa", c))
        load_weight_bf16(wob[:, c, :], w_out_b[c * 128:(c + 1) * 128, :], 1024, ("wob", c))
    off[0] = 32768 * 2
    yt_, gt_, bt_, ybt_ = (cv(4 * NT).rearrange("p (c n) -> p c n", c=4) for _ in range(4))
    yab = arena[:, off[0] // 2:off[0] // 2 + 4 * NT].rearrange("p (c n) -> p c n", c=4)
    ybb = arena[:, off[0] // 2 + 4 * NT:off[0] // 2 + 8 * NT].rearrange("p (c n) -> p c n", c=4)
    mmb = arena[:, off[0] // 2 + 8 * NT:off[0] // 2 + 16 * NT].rearrange("p (c n) -> p c n", c=8)
    off[0] += 16 * NT * 2
    gn1, gn2, gn3, gab, gbb = (cv(NT) for _ in range(5))
    x2_v = x2T.rearrange("(c p) n -> p c n", p=128)
    for ti, (c0, n, segs) in enumerate(tl):
        x = xt[ti % 2]
        xk = "xt%d" % (ti % 2)
        P.dma("sp", x[:, :, 0:n], x1_v[:, :, c0:c0 + n], reads=["dram_x1T"], writes=[xk])
        Dm(yt_[:, :, 0:n], y_fm[:, :, c0:c0 + n], reads=["dram_yfm"], writes=["yt"])
        Dm(gt_[:, :, 0:n], scr["g"][:, :, c0:c0 + n], reads=["dram_scr"], writes=["gt"])
        Dm(bt_[:, :, 0:n], scr["bonus"][:, :, c0:c0 + n], reads=["dram_scr"], writes=["bt"])
        Dm(ybt_[:, :, 0:n], yb_fm[:, :, c0:c0 + n], reads=["dram_ybfm"], writes=["ybt"])
        modulate(x, xk, n, segs, 3, ub, "ub")
        P.op("pool", lambda e: e.tensor_scalar(out=x[:, :, 0:n], in0=x[:, :, 0:n], scalar1=ALPHA, scalar2=None, op0=ALU.mult),
             reads=[xk], writes=[xk])
        for j in range(4):
            yj = yt_[:, j, 0:n]
            A("act", lambda e: e.activation(out=gn1[:, 0:n], in_=yj, func=AF.Square), reads=["yt"], writes=["gn1"])
            A("pe", lambda e: e.matmul(ps[0][:, 0:n], lhsT=blk1[:], rhs=yj, start=True, stop=True), reads=["yt", "rwc"], writes=["ps0"])
            A("pe", lambda e: e.matmul(ps[1][:, 0:n], lhsT=blk1[:], rhs=gn1[:, 0:n], start=True, stop=True), reads=["gn1", "rwc"], writes=["ps1"])
            A("act", lambda e: e.mul(out=gn2[:, 0:n], in_=ps[0][:, 0:n], mul=1.0 / 64), reads=["ps0"], writes=["gn2"])
            A("dve", lambda e: e.tensor_tensor(out=gn3[:, 0:n], in0=gn2[:, 0:n], in1=gn2[:, 0:n], op=ALU.mult), reads=["gn2"], writes=["gn3"])
            A("dve", lambda e: e.scalar_tensor_tensor(out=gn3[:, 0:n], in0=ps[1][:, 0:n], scalar=1.0 / 64, in1=gn3[:, 0:n], op0=ALU.mult, op1=ALU.subtract),
              reads=["ps1", "gn3"], writes=["gn3"])
            A("dve", lambda e: e.tensor_scalar(out=gn3[:, 0:n], in0=gn3[:, 0:n], scalar1=64e-5, scalar2=None, op0=ALU.add), reads=["gn3"], writes=["gn3"])
            A("act", lambda e: e.sqrt(out=gn3[:, 0:n], in_=gn3[:, 0:n]), reads=["gn3"], writes=["gn3"])
            A("dve", lambda e: e.reciprocal(out=gn3[:, 0:n], in_=gn3[:, 0:n]), reads=["gn3"], writes=["gn3"])
            A("dve", lambda e: e.tensor_tensor(out=gn1[:, 0:n], in0=yj, in1=gn2[:, 0:n], op=ALU.subtract), reads=["yt", "gn2", "gn1"], writes=["gn1"])
            A("dve", lambda e: e.tensor_tensor(out=gn1[:, 0:n], in0=gn1[:, 0:n], in1=gn3[:, 0:n], op=ALU.mult), reads=["gn1", "gn3"], writes=["gn1"])
            A("act", lambda e: e.activation(out=gn1[:, 0:n], in_=gn1[:, 0:n], func=AF.Identity, scale=rwq_sb[:, j, 5:6], bias=rwq_sb[:, j, 6:7]),
              reads=["gn1", "rwc"], writes=["gn1"])
            A("dve", lambda e: e.tensor_tensor(out=gn1[:, 0:n], in0=gn1[:, 0:n], in1=bt_[:, j, 0:n], op=ALU.add), reads=["gn1", "bt"], writes=["gn1"])
            A("dve", lambda e: e.tensor_tensor(out=yab[:, j, 0:n], in0=gn1[:, 0:n], in1=gt_[:, j, 0:n], op=ALU.mult), reads=["gn1", "gt"], writes=["yab"])
            A("act", lambda e: e.copy(out=ybb[:, j, 0:n], in_=ybt_[:, j, 0:n]), reads=["ybt"], writes=["ybb"])
        for m in range(DC):
            mc = slice(m * 128, (m + 1) * 128)
            for j in range(4):
                A("pe", lambda e: e.matmul(ps[0][:, 0:n], lhsT=woa[:, j, mc], rhs=yab[:, j, 0:n], start=(j == 0), stop=(j == 3)),
                  reads=["yab", ("woa", j), "arena"], writes=["ps0"], skip_own=True)
            for j in range(4):
                A("pe", lambda e: e.matmul(ps[1][:, 0:n], lhsT=wob[:, j, mc], rhs=ybb[:, j, 0:n], start=(j == 0), stop=(j == 3)),
                  reads=["ybb", ("wob", j), "arena"], writes=["ps1"], skip_own=True)
            for c in range(DC):
                A("pe", lambda e: e.matmul(ps[2][:, 0:n], lhsT=wm[:, c, mc], rhs=ub[:, c, 0:n], start=(c == 0), stop=(c == DC - 1)),
                  reads=["ub", ("wm", c), "arena"], writes=["ps2"], skip_own=True)
            for c in range(DC):
                A("pe", lambda e: e.matmul(ps[3][:, 0:n], lhsT=wm[:, c, 1024 + m * 128:1024 + (m + 1) * 128], rhs=ub[:, c, 0:n], start=(c == 0), stop=(c == DC - 1)),
                  reads=["ub", ("wm", c), "arena"], writes=["ps3"], skip_own=True)
            A("act", lambda e: e.activation(out=gab[:, 0:n], in_=ps[2][:, 0:n], func=AF.Sigmoid, bias=bm_sb[:, m:m + 1], scale=1.0), reads=["ps2", "rwc"], writes=["gab"])
            A("act", lambda e: e.activation(out=gbb[:, 0:n], in_=ps[3][:, 0:n], func=AF.Sigmoid, bias=bm_sb[:, 8 + m:9 + m], scale=1.0), reads=["ps3", "rwc"], writes=["gbb"])
            A("dve", lambda e: e.tensor_tensor(out=gab[:, 0:n], in0=gab[:, 0:n], in1=ps[0][:, 0:n], op=ALU.mult), reads=["gab", "ps0"], writes=["gab"])
            A("dve", lambda e: e.tensor_tensor(out=gbb[:, 0:n], in0=gbb[:, 0:n], in1=ps[1][:, 0:n], op=ALU.mult), reads=["gbb", "ps1"], writes=["gbb"])
            A("dve", lambda e: e.tensor_tensor(out=mmb[:, m, 0:n], in0=gab[:, 0:n], in1=gbb[:, 0:n], op=ALU.add), reads=["gab", "gbb"], writes=["mmb"])
        for m in range(DC):
            py = ps[4 + (m % 2)]
            ky = "ps%d" % (4 + (m % 2))
            for c in range(DC):
                A("pe", lambda e: e.matmul(py[:, 0:n], lhsT=wo[:, c, m * 128:(m + 1) * 128], rhs=mmb[:, c, 0:n], start=(c == 0), stop=(c == DC - 1)),
                  reads=["mmb", ("wo", c), "arena"], writes=[ky], skip_own=True)
            for (lo, hi, s_) in segs:
                P.op("dve", lambda e: e.scalar_tensor_tensor(out=x[:, m, lo:hi], in0=py[:, lo:hi], scalar=modS[:, s_, 5, m:m + 1], in1=x[:, m, lo:hi],
                                                             op0=ALU.mult, op1=ALU.add), reads=[ky, xk, "modS"], writes=[xk])
        layer_norm_tile(x, xk, n, 1, x, xk)
        P.dma("pool", x2_v[:, :, c0:c0 + n], x[:, :, 0:n], reads=[xk], writes=["dram_x2T"])

    ffn_phase(1, x2T, yT, 2, 6, True)

    P.emit()
    es.close()
    return nc


_NC_CACHE = {}


def kernel(**inp):
    f = lambda a: np.ascontiguousarray(np.asarray(a), dtype=np.float32)
    x_prompt, x_sample = f(inp["x_prompt"]), f(inp["x_sample"])
    c_prompt, c_sample = f(inp["c_prompt"]), f(inp["c_sample"])
    w_ada = f(inp["w_ada"])[0]
    b_ada = f(inp["b_ada"])[0].reshape(72, 128).T.copy()
    ln_g = f(inp["ln_g"])[0].reshape(3, DC, 128).transpose(2, 0, 1).copy()
    ln_b = f(inp["ln_b"])[0].reshape(3, DC, 128).transpose(2, 0, 1).copy()
    w_gate, w_up, w_down = f(inp["ffn_w_gate"])[0], f(inp["ffn_w_up"])[0], f(inp["ffn_w_down"])[0]
    w_in = f(inp["w_in"])[0]
    b_in_bc = np.ascontiguousarray(np.broadcast_to(f(inp["b_in"])[0][None, :], (128, N_IN)))
    state_kv_win = f(inp["state_kv_win"])[0].reshape(32, 512, 256)

    CHT = [(j * 128, 128) for j in range(12)] + [(1536, 64), (1600, 64), (1664, 128)]

    def chunked(vec):
        vec = np.asarray(vec)
        lead = vec.shape[:-1]
        out = np.zeros((128, 15) + lead, np.float32)
        for ci, (c0_, w_) in enumerate(CHT):
            out[:w_, ci] = np.moveaxis(vec[..., c0_:c0_ + w_], -1, 0)
        return out

    b_in = f(inp["b_in"])[0]
    rwp = np.ascontiguousarray(np.stack([chunked(b_in[:RW_SHIFT]), chunked(f(inp["rw_mu"])[0])], axis=-1))
    q7 = [f(inp[k])[0].reshape(512) for k in ("rw_w0", "rw_a0", "rw_k_k", "rw_k_a", "rw_r_k", "rw_ln_w", "rw_ln_b")]
    rwq = np.ascontiguousarray(np.stack([v.reshape(4, 128).T for v in q7], axis=-1))
    rw_w2, rw_a2, rw_g2 = f(inp["rw_w2"])[0], f(inp["rw_a2"])[0], f(inp["rw_g2"])[0]
    state_shift = f(inp["state_shift"])[0]
    state_wkv = f(inp["state_wkv"])[0]
    pidx = np.arange(128)
    blk1 = (pidx[:, None] // 64 == pidx[None, :] // 64).astype(np.float32)
    istk = (pidx[:, None] % 64 == np.arange(64)[None, :]).astype(np.float32)

    nsab_cols = [1792 + 64 * h for h in range(8)] + [2304 + br * 256 + g * 64 for br in range(3) for g in range(2)] \
        + [2304 + 128 + g * 64 for g in range(2)]
    nsab = np.ascontiguousarray(np.stack([b_in[c0_:c0_ + 64] * (0.125 if ci < 8 else 1.0) for ci, c0_ in enumerate(nsab_cols)], axis=1))
    phi_w1, phi_w2 = f(inp["nsa_phi_w1"])[0], f(inp["nsa_phi_w2"])[0]
    peT = np.ascontiguousarray(f(inp["nsa_phi_pe"])[0].transpose(2, 0, 1))
    tri = (pidx[:, None] <= pidx[None, :]).astype(np.float32)
    nn = np.arange(256).reshape(2, 128)
    cvalT = np.ascontiguousarray((16.0 * nn.T[:, :, None] + 31.0 - pidx[None, None, :]).astype(np.float32))
    jidx = np.ascontiguousarray(np.broadcast_to(np.arange(64, dtype=np.float32)[None, :], (128, 64)))
    curb = (pidx[:, None] >= 64).astype(np.float32)
    eall = (np.arange(SEQ)[None, :] // 64 == np.arange(64)[:, None]).astype(np.float32)
    nidx = np.arange(256)[:, None]
    jj = np.arange(64)[None, :]
    bimp_full = ((nidx >= 4 * jj - 1) & (nidx <= 4 * jj + 3) & (nidx < 255)).astype(np.float32)
    bimp = np.ascontiguousarray(bimp_full.reshape(2, 128, 64).transpose(1, 0, 2))
    idx16 = np.ascontiguousarray(np.broadcast_to(np.arange(16, dtype=np.float32)[None, :], (128, 16)))
    ident = np.eye(128, dtype=np.float32)
    bm = np.ascontiguousarray(b_in[3096:5144].reshape(16, 128).T)
    w_out_a, w_out_b, w_o = f(inp["w_out_a"])[0], f(inp["w_out_b"])[0], f(inp["w_o"])[0]

    cache_cmp = f(inp["cache_kv_cmp"])[0].reshape(-1, 128)
    cache_sel = f(inp["cache_kv_sel"])[0].reshape(-1, 128)
    page_table = np.ascontiguousarray(np.asarray(inp["page_table"]), dtype=np.int32)
    n1 = np.arange(1024)[:, None]
    j1 = np.arange(257)[None, :]
    bimps_full = ((n1 >= 4 * j1 - 1) & (n1 <= 4 * j1 + 3) & (n1 < 1023)).astype(np.float32)
    bimps = np.ascontiguousarray(bimps_full.reshape(8, 128, 257).transpose(1, 0, 2))
    bb = np.arange(128)[:, None]
    kk8 = np.arange(8192)[None, :]
    e2 = (bb == 2 * (kk8 // 128) + (kk8 % 128) // 64).astype(np.float32)
    hsel = (np.arange(16)[:, None] % 4 == np.arange(4)[None, :]).astype(np.float32)
    pcol = pidx[:, None].astype(np.float32)
    jidx2 = np.ascontiguousarray(np.broadcast_to(np.arange(264, dtype=np.float32)[None, :], (128, 264)))
    tri16 = (pidx[:, None] <= (np.arange(16)[None, :] % 4)).astype(np.float32)

    if "nc" not in _NC_CACHE:
        _NC_CACHE["nc"] = build()
    nc = _NC_CACHE["nc"]

    in_maps = []
    for i in range(8):
        b = i // 2
        xs = x_sample[4 * i:4 * i + 4].reshape(NS * ST, D)
        xT = np.ascontiguousarray(np.concatenate([x_prompt[b], xs], axis=0).T)
        cv = np.concatenate([c_prompt[b:b + 1], c_sample[4 * i:4 * i + 4]], axis=0)
        cT = np.ascontiguousarray(cv.T.reshape(DC, 128, NSEQ).transpose(1, 0, 2))
        in_maps.append(dict(xT=xT, cT=cT, w_ada=w_ada, b_ada=b_ada, ln_g=ln_g, ln_b=ln_b, w_gate=w_gate, w_up=w_up,
                            w_down=w_down, w_in=w_in, b_in_bc=b_in_bc,
                            win_state=np.ascontiguousarray(state_kv_win[4 * i:4 * i + 4]),
                            rwp=rwp, rwq=rwq, rw_w2=rw_w2, rw_a2=rw_a2, rw_g2=rw_g2,
                            shs=np.ascontiguousarray(chunked(state_shift[4 * i:4 * i + 4])),
                            wkv0=np.ascontiguousarray(state_wkv[4 * i:4 * i + 4]), blk1=blk1, istk=istk,
                            nsab=nsab, phi_w1=phi_w1, phi_w2=phi_w2, peT=peT, tri=tri, cvalT=cvalT, jidx=jidx, curb=curb,
                            eall=eall, bimp=bimp, idx16=idx16, ident=ident, bm=bm, w_out_a=w_out_a, w_out_b=w_out_b, w_o=w_o,
                            cache_cmp=cache_cmp, cache_sel=cache_sel,
                            ptab=np.ascontiguousarray(page_table[4 * i:4 * i + 4].reshape(1, NS * 128)),
                            bimps=bimps, e2=e2, hsel=hsel, pcol=pcol, jidx2=jidx2, tri16=tri16))
    res = run_bass_kernel_spmd(nc, in_maps, core_ids=list(range(8)))
    R = res.results

    y_prompt = np.stack([R[2 * b]["yT"][:, :SEQ].T for b in range(4)])
    y_sample = np.concatenate([R[i]["yT"][:, SEQ:].T.reshape(NS, ST, D) for i in range(8)])
    kvshape = lambda a: a.reshape(a.shape[0], 2, 2, 64)
    kvc_p = np.stack([kvshape(R[2 * b]["o_kvc"][:SEQ]) for b in range(4)])[None]
    kvs_p = np.stack([kvshape(R[2 * b]["o_kvs"][:SEQ]) for b in range(4)])[None]
    kvw_p = np.stack([kvshape(R[2 * b]["o_kvw_p"]) for b in range(4)])[None]
    wkv_p = np.stack([R[2 * b]["o_wkv"][0] for b in range(4)])[None]
    sh_p = np.stack([R[2 * b]["o_shift"][0] for b in range(4)])[None]
    kvc_s = np.concatenate([R[i]["o_kvc"][SEQ:].reshape(NS, ST, 2, 2, 64) for i in range(8)])[None]
    kvs_s = np.concatenate([R[i]["o_kvs"][SEQ:].reshape(NS, ST, 2, 2, 64) for i in range(8)])[None]
    kvw_s = np.concatenate([R[i]["o_kvw_s"].reshape(NS, 512, 2, 2, 64) for i in range(8)])[None]
    wkv_s = np.concatenate([R[i]["o_wkv"][1:] for i in range(8)])[None]
    sh_s = np.concatenate([R[i]["o_shift"][1:] for i in range(8)])[None]
    outs = (y_prompt, y_sample, kvc_p, kvs_p, kvw_p, wkv_p, sh_p, kvc_s, kvs_s, kvw_s, wkv_s, sh_s)
    return tuple(np.ascontiguousarray(o, dtype=np.float32) for o in outs)
```

```python
import contextlib
import numpy as np
import concourse.bass as bass
import concourse.mybir as mybir
from concourse.bass_utils import run_bass_kernel_spmd

F32 = mybir.dt.float32
BF16 = mybir.dt.bfloat16
I32 = mybir.dt.int32
AF = mybir.ActivationFunctionType
ALU = mybir.AluOpType
AX = mybir.AxisListType

D = 1024
DC = 8
DFF = 2816
FC = 22
SEQ = 4096
NS = 4
ST = 4
TT = SEQ + NS * ST
NSEQ = 1 + NS
RW_SHIFT = 1792
N_IN = 5144
ALPHA = 2 ** 0.25
LN_EPS = 1e-5
NT = 256
DEBUG = False
SEQ_SCAN = SEQ
SEQ_NSA = SEQ
DO_SAMPLE = True
NPG_DBG = 128
NS_DBG = NS

ENGS = ("pe", "act", "dve", "pool", "sp")


class _Rec:
    def __getattr__(self, name):
        def f(*a, **k):
            self.call = (name, a, k)
            return self
        return f


class Prog:
    def __init__(self, nc, n_dma_sems=32):
        self.nc = nc
        self.ops = {e: [] for e in ENGS}
        self.cnt = {e: 0 for e in ENGS}
        self.last_w = {}
        self.readers = {}
        self.seen = {e: {} for e in ENGS}
        self.n_dma = n_dma_sems
        self.dma_i = 0
        self.dma_cnt = [0] * n_dma_sems
        self.out_tokens = []

    def _deps(self, reads, writes):
        deps = []
        for k in reads:
            t = self.last_w.get(k)
            if t is not None:
                deps.append(t)
        for k in writes:
            t = self.last_w.get(k)
            if t is not None:
                deps.append(t)
            deps.extend(self.readers.get(k, ()))
        return deps

    def _commit(self, tok, reads, writes):
        for k in reads:
            self.readers.setdefault(k, []).append(tok)
        for k in writes:
            self.last_w[k] = tok
            self.readers[k] = []

    def _waits(self, eng, deps, skip_own=False):
        need = {}
        for (s, v) in deps:
            if skip_own and s == ("c", eng):
                continue
            if self.seen[eng].get(s, 0) >= v:
                continue
            if need.get(s, 0) < v:
                need[s] = v
        for s, v in need.items():
            self.seen[eng][s] = v
        return list(need.items())

    def op(self, eng, fn, reads=(), writes=(), skip_own=False):
        rec = _Rec()
        fn(rec)
        fn = rec.call
        deps = self._deps(reads, writes)
        waits = self._waits(eng, deps, skip_own)
        self.cnt[eng] += 1
        tok = (("c", eng), self.cnt[eng])
        self.ops[eng].append(("c", fn, waits, tok))
        self._commit(tok, reads, writes)
        return tok

    def dma(self, eng, out, in_, reads=(), writes=(), is_output=False, **kw):
        deps = self._deps(reads, writes)
        si = self.dma_i % self.n_dma
        self.dma_i += 1
        sem = ("d", si)
        prev = self.dma_cnt[si]
        if prev > 0:
            deps.append((sem, 16 * prev))
        waits = self._waits(eng, deps)
        self.dma_cnt[si] += 1
        tok = (sem, 16 * self.dma_cnt[si])
        self.ops[eng].append(("d", (out, in_, kw), waits, tok))
        self._commit(tok, reads, writes)
        if is_output:
            self.out_tokens.append(tok)
        return tok

    def emit(self):
        nc = self.nc
        with contextlib.ExitStack() as es:
            sems = {}
            for e in ENGS:
                sems[("c", e)] = es.enter_context(nc.semaphore("c_" + e))
            for i in range(self.n_dma):
                sems[("d", i)] = es.enter_context(nc.semaphore("d_%d" % i))
            final = {}
            for (s, v) in self.out_tokens:
                final[s] = max(final.get(s, 0), v)
            block = es.enter_context(nc.Block())
            engobj = {"pe": "tensor", "act": "scalar", "dve": "vector", "pool": "gpsimd", "sp": "sync"}

            def run(e, eng):
                for kind, payload, waits, tok in self.ops[e]:
                    for (s, v) in waits:
                        eng.wait_ge(sems[s], v)
                    if kind == "c":
                        name, a, k = payload
                        getattr(eng, name)(*a, **k).then_inc(sems[tok[0]], 1)
                    else:
                        out, in_, kw = payload
                        if "gather_idx" in kw:
                            eng.indirect_dma_start(out=out, out_offset=None, in_=in_,
                                                   in_offset=bass.IndirectOffsetOnAxis(ap=kw["gather_idx"], axis=0),
                                                   element_offset=kw.get("element_offset", 0)
                                                   ).then_inc(sems[tok[0]], 16)
                        else:
                            eng.dma_start(out=out, in_=in_, **kw).then_inc(sems[tok[0]], 16)
                if e == "sp":
                    for s, v in final.items():
                        eng.wait_ge(sems[s], v)

            for e in ENGS:
                getattr(block, engobj[e])(lambda eng, e=e: run(e, eng))


def tiles():
    out = []
    for i in range(SEQ // NT):
        out.append((i * NT, NT, [(0, NT, 0)]))
    out.append((SEQ, NS * ST, [(s * ST, (s + 1) * ST, 1 + s) for s in range(NS)]))
    return out


def build():
    nc = bass.Bass("TRN2", target_bir_lowering=False)
    P = Prog(nc)
    es = contextlib.ExitStack()

    def din(name, shape, dt=F32):
        return nc.dram_tensor(name, list(shape), dt, kind="ExternalInput").ap()

    def dout(name, shape, dt=F32):
        return nc.dram_tensor(name, list(shape), dt, kind="ExternalOutput").ap()

    def dscr(name, shape, dt=F32):
        return nc.dram_tensor(name, list(shape), dt, kind="Internal").ap()

    def sb(name, shape, dt=F32):
        return es.enter_context(nc.sbuf_tensor(name, list(shape), dt))

    xT = din("xT", [D, TT])
    cT = din("cT", [128, DC, NSEQ])
    w_ada = din("w_ada", [D, 9 * D])
    b_ada = din("b_ada", [128, 72])
    ln_g = din("ln_g", [128, 3, DC])
    ln_b = din("ln_b", [128, 3, DC])
    w_gate = din("w_gate", [2, D, DFF])
    w_up = din("w_up", [2, D, DFF])
    w_down = din("w_down", [2, DFF, D])
    w_in = din("w_in", [D, N_IN])
    b_in_bc = din("b_in_bc", [128, N_IN])
    win_state = din("win_state", [NS, 512, 256])

    rwp = din("rwp", [128, 15, 2])
    rwq = din("rwq", [128, 4, 7])
    w2_d = din("rw_w2", [64, 512])
    a2_d = din("rw_a2", [64, 512])
    g2_d = din("rw_g2", [128, 512])
    shs = din("shs", [128, 15, NS])
    wkv0 = din("wkv0", [NS, 8, 64, 64])
    blk1_d = din("blk1", [128, 128])
    istk_d = din("istk", [128, 64])

    nsab = din("nsab", [64, 16])
    phi_w1 = din("phi_w1", [2, 32, 64, 128])
    phi_w2 = din("phi_w2", [2, 128, 64])
    peT_d = din("peT", [64, 2, 32])
    tri_d = din("tri", [128, 128])
    cvalT_d = din("cvalT", [128, 2, 128])
    jidx_d = din("jidx", [128, 64])
    curb_d = din("curb", [128, 1])
    eall_d = din("eall", [64, SEQ])
    bimp_d = din("bimp", [128, 2, 64])
    idx16_d = din("idx16", [128, 16])
    ident_d = din("ident", [128, 128])
    bm_d = din("bm", [128, 16])
    w_out_a = din("w_out_a", [512, D])
    w_out_b = din("w_out_b", [512, D])
    w_o = din("w_o", [D, D])

    NPHYS = 5120
    cache_cmp = din("cache_cmp", [NPHYS * 256, 128])
    cache_sel = din("cache_sel", [NPHYS * 256, 128])
    ptab = din("ptab", [1, NS * 128], I32)
    bimps_d = din("bimps", [128, 8, 257])
    e2_d = din("e2", [128, 8192])
    hsel_d = din("hsel", [16, 4])
    pcol_d = din("pcol", [128, 1])
    jidx2_d = din("jidx2", [128, 264])
    tri16_d = din("tri16", [128, 16])

    yT = dout("yT", [D, TT])
    o_kvc = dout("o_kvc", [TT, 256])
    o_kvs = dout("o_kvs", [TT, 256])
    o_kvw_p = dout("o_kvw_p", [512, 256])
    o_kvw_s = dout("o_kvw_s", [NS, 512, 256])
    o_shift = dout("o_shift", [NSEQ, RW_SHIFT])
    o_wkv = dout("o_wkv", [NSEQ, 8, 64, 64])

    x1T = (dout if DEBUG else dscr)("x1T", [D, TT])
    x2T = (dout if DEBUG else dscr)("x2T", [D, TT])

    RWN = ("nkk", "w", "b", "kp", "r", "v", "g", "bonus")
    scr = {k: (dout if DEBUG else dscr)("scr_" + k, [128, 4, TT]) for k in RWN}
    y_fm = (dout if DEBUG else dscr)("y_fm", [128, 4, TT])
    yb_fm = (dout if DEBUG else dscr)("yb_fm", [128, 4, TT])
    nsaT = dscr("nsaT", [64, 16, TT])
    kv_tm = dscr("kv_tm", [TT, 768])
    gates_tm = dscr("gates_tm", [TT, 24])

    arena = sb("arena", [128, 3 * DC * DFF], BF16)
    stage = [sb("stage%d" % i, [128, DFF], F32) for i in range(2)]
    xt = [sb("xt%d" % i, [128, DC, NT], F32) for i in range(2)]
    ub = sb("ub", [128, DC, NT], BF16)
    hT = sb("hT", [128, FC, NT], BF16)
    tmpa = [sb("tmpa%d" % i, [128, NT], F32) for i in range(2)]
    tmpb = [sb("tmpb%d" % i, [128, NT], F32) for i in range(2)]
    lnt = {k: sb("ln_" + k, [128, NT], F32) for k in ("mu", "musq", "var", "rstd")}
    ones = sb("ones", [128, 128], F32)
    modS = sb("modS", [128, NSEQ, 9, DC], F32)
    ct_sb = sb("ct_sb", [128, DC, NSEQ], F32)
    bada_sb = sb("bada_sb", [128, 72], F32)
    lng_sb = sb("lng_sb", [128, 3, DC], F32)
    lnb_sb = sb("lnb_sb", [128, 3, DC], F32)
    def carve(off_bytes, ncols):
        a = off_bytes // 2
        return arena[:, a:a + 2 * ncols].bitcast(F32)

    CV0 = DC * 3096 * 2
    kvt = [carve(CV0 + i * 792 * 4, 792) for i in range(2)]
    rwt = carve(CV0 + 2 * 792 * 4, RW_SHIFT)
    bias_tm = carve(CV0 + 2 * 792 * 4 + RW_SHIFT * 4, 792 + RW_SHIFT)

    rwp_sb = sb("rwp_sb", [128, 15, 2])
    rwq_sb = sb("rwq_sb", [128, 4, 7])
    w2_sb = sb("w2_sb", [128, 512])
    a2_sb = sb("a2_sb", [128, 512])
    g2_sb = sb("g2_sb", [128, 512])
    nsab_sb = sb("nsab_sb", [64, 16])
    bm_sb = sb("bm_sb", [128, 16])
    blk1 = sb("blk1_sb", [128, 128])
    istk = sb("istk_sb", [128, 64])
    ps = [es.enter_context(nc.psum_tensor("ps%d" % i, [128, 512], F32)) for i in range(8)]

    P.op("pool", lambda e: e.memset(ones[:], 1.0), writes=["ones"])
    P.dma("sp", ct_sb[:], cT, writes=["ct"])
    P.dma("sp", bada_sb[:], b_ada, writes=["bada"])
    P.dma("sp", lng_sb[:], ln_g, writes=["lng"])
    P.dma("sp", lnb_sb[:], ln_b, writes=["lnb"])
    P.dma("sp", rwp_sb[:], rwp, writes=["rwc"])
    P.dma("sp", rwq_sb[:], rwq, writes=["rwc"])
    P.dma("sp", w2_sb[0:64, :], w2_d, writes=["rwc"])
    P.dma("sp", a2_sb[0:64, :], a2_d, writes=["rwc"])
    P.dma("sp", g2_sb[:], g2_d, writes=["rwc"])
    P.dma("sp", nsab_sb[:], nsab, writes=["rwc"])
    P.dma("sp", bm_sb[:], bm_d, writes=["rwc"])
    P.dma("sp", blk1[:], blk1_d, writes=["rwc"])
    P.dma("sp", istk[:], istk_d, writes=["rwc"])

    P.op("act", lambda e: e.activation(out=ct_sb[:], in_=ct_sb[:], func=AF.Silu), reads=["ct"], writes=["ct"])
    wada_v = w_ada.rearrange("(c p) n -> p c n", p=128)
    mod_ps = ps[0]
    GW = 256
    for gidx in range(9216 // GW):
        st = stage[gidx % 2]
        sk = "stage%d" % (gidx % 2)
        stv = st[:, 0:DC * GW].rearrange("p (c n) -> p c n", c=DC)
        P.dma("sp", stv, wada_v[:, :, gidx * GW:(gidx + 1) * GW], writes=[sk])
        for j in range(GW // 128):
            ch = gidx * (GW // 128) + j
            for c in range(DC):
                P.op("pe", lambda e, stv=stv, j=j, ch=ch, c=c: e.matmul(
                    mod_ps[:, ch * NSEQ:(ch + 1) * NSEQ], lhsT=stv[:, c, j * 128:(j + 1) * 128], rhs=ct_sb[:, c, :],
                    start=(c == 0), stop=(c == DC - 1)),
                    reads=[sk, "ct"], writes=["ps0"], skip_own=True)
    P.op("dve", lambda e: e.tensor_tensor(
        out=modS[:].rearrange("p s k c -> p (k c) s"),
        in0=mod_ps[:, 0:72 * NSEQ].rearrange("p (ch s) -> p ch s", s=NSEQ),
        in1=bada_sb[:].unsqueeze(2).to_broadcast([128, 72, NSEQ]), op=ALU.add),
        reads=["ps0", "bada"], writes=["modS"])
    for k, (mulv, addv) in {1: (1.0, 1.0), 4: (1.0, 1.0), 7: (1.0, 1.0), 2: (0.5, 0.5), 8: (0.5, 0.5), 5: (1.0, 1.0)}.items():
        P.op("dve", lambda e, k=k, mulv=mulv, addv=addv: e.tensor_scalar(
            out=modS[:, :, k, :], in0=modS[:, :, k, :], scalar1=mulv, scalar2=addv, op0=ALU.mult, op1=ALU.add),
            reads=["modS"], writes=["modS"])

    cast_rr = [0]

    gate_sb = sb("gate_sb", [128, 4], F32)

    def arena_gate():
        for gi_, eng in enumerate(("pool", "dve", "act")):
            if eng == "act":
                P.op(eng, lambda e, gi_=gi_: e.memzero(gate_sb[:, gi_:gi_ + 1]), writes=["arena", "hT", "stage0", "stage1", "xt0", "xt1", ("gate", gi_)])
            else:
                P.op(eng, lambda e, gi_=gi_: e.memset(gate_sb[:, gi_:gi_ + 1], 0.0), writes=["arena", "hT", "stage0", "stage1", "xt0", "xt1", ("gate", gi_)])

    def load_weight_bf16(dst_ap, src_ap, ncols, dkey):
        i = cast_rr[0]
        cast_rr[0] += 1
        st = stage[i % 2]
        sk = "stage%d" % (i % 2)
        P.dma("sp", st[:, 0:ncols], src_ap, writes=[sk])
        eng = ("pool", "dve", "act")[i % 3]
        if eng == "act":
            P.op("act", lambda e: e.copy(out=dst_ap, in_=st[:, 0:ncols]), reads=[sk], writes=[dkey])
        else:
            P.op(eng, lambda e: e.tensor_copy(out=dst_ap, in_=st[:, 0:ncols]), reads=[sk], writes=[dkey])

    def layer_norm_tile(z, zk, n, gi, out, outk):
        s1, s2 = ps[6], ps[7]
        for c in range(DC):
            tq = tmpa[c % 2]
            P.op("act", lambda e, c=c, tq=tq: e.activation(out=tq[:, 0:n], in_=z[:, c, 0:n], func=AF.Square),
                 reads=[zk], writes=["tmpa%d" % (c % 2)])
            P.op("pe", lambda e, c=c: e.matmul(s1[:, 0:n], lhsT=ones[:], rhs=z[:, c, 0:n], start=(c == 0), stop=(c == DC - 1)),
                 reads=[zk, "ones"], writes=["ps6"], skip_own=True)
            P.op("pe", lambda e, c=c, tq=tq: e.matmul(s2[:, 0:n], lhsT=ones[:], rhs=tq[:, 0:n], start=(c == 0), stop=(c == DC - 1)),
                 reads=["tmpa%d" % (c % 2), "ones"], writes=["ps7"], skip_own=True)
        mu, musq, var, rstd = lnt["mu"], lnt["musq"], lnt["var"], lnt["rstd"]
        P.op("act", lambda e: e.mul(out=mu[:, 0:n], in_=s1[:, 0:n], mul=1.0 / D), reads=["ps6"], writes=["ln_mu"])
        P.op("dve", lambda e: e.tensor_tensor(out=musq[:, 0:n], in0=mu[:, 0:n], in1=mu[:, 0:n], op=ALU.mult),
             reads=["ln_mu"], writes=["ln_musq"])
        P.op("dve", lambda e: e.scalar_tensor_tensor(out=var[:, 0:n], in0=s2[:, 0:n], scalar=1.0 / D, in1=musq[:, 0:n],
                                                     op0=ALU.mult, op1=ALU.subtract),
             reads=["ps7", "ln_musq"], writes=["ln_var"])
        P.op("dve", lambda e: e.tensor_scalar(out=var[:, 0:n], in0=var[:, 0:n], scalar1=LN_EPS, scalar2=None,
                                              op0=ALU.add), reads=["ln_var"], writes=["ln_var"])
        P.op("act", lambda e: e.sqrt(out=var[:, 0:n], in_=var[:, 0:n]), reads=["ln_var"], writes=["ln_var"])
        P.op("dve", lambda e: e.reciprocal(out=rstd[:, 0:n], in_=var[:, 0:n]), reads=["ln_var"], writes=["ln_rstd"])
        for c in range(DC):
            ta = tmpb[c % 2]
            tk = "tmpb%d" % (c % 2)
            P.op("dve", lambda e, c=c, ta=ta: e.tensor_tensor(out=ta[:, 0:n], in0=z[:, c, 0:n], in1=mu[:, 0:n], op=ALU.subtract),
                 reads=[zk, "ln_mu"], writes=[tk])
            P.op("pool", lambda e, c=c, ta=ta: e.tensor_tensor(out=ta[:, 0:n], in0=ta[:, 0:n], in1=rstd[:, 0:n], op=ALU.mult),
                 reads=[tk, "ln_rstd"], writes=[tk])
            P.op("act", lambda e, c=c, ta=ta: e.activation(out=out[:, c, 0:n], in_=ta[:, 0:n], func=AF.Identity,
                                                           scale=lng_sb[:, gi, c:c + 1], bias=lnb_sb[:, gi, c:c + 1]),
                 reads=[tk, "lng", "lnb"], writes=[outk])

    def modulate(x, xk, n, segs, kshift, dst, dk):
        for c in range(DC):
            for (lo, hi, s) in segs:
                P.op("act", lambda e, c=c, lo=lo, hi=hi, s=s: e.activation(
                    out=dst[:, c, lo:hi], in_=x[:, c, lo:hi], func=AF.Identity,
                    scale=modS[:, s, kshift + 1, c:c + 1], bias=modS[:, s, kshift, c:c + 1]),
                    reads=[xk, "modS"], writes=[dk])

    def ffn_phase(fi, src, dst, gi, kmod, dst_is_output):
        wg = arena[:, 0:DC * DFF].rearrange("p (c f) -> p c f", c=DC)
        wu = arena[:, DC * DFF:2 * DC * DFF].rearrange("p (c f) -> p c f", c=DC)
        wd = arena[:, 2 * DC * DFF:3 * DC * DFF].rearrange("p (f d) -> p f d", f=FC)
        arena_gate()
        for c in range(DC):
            load_weight_bf16(wg[:, c, :], w_gate[fi, c * 128:(c + 1) * 128, :], DFF, ("wg", fi, c))
            load_weight_bf16(wu[:, c, :], w_up[fi, c * 128:(c + 1) * 128, :], DFF, ("wu", fi, c))
        for f2 in range(FC // 2):
            i = cast_rr[0]
            cast_rr[0] += 1
            st = stage[i % 2]
            sk = "stage%d" % (i % 2)
            P.dma("sp", st[:, 0:2 * D].rearrange("p (a d) -> p a d", a=2),
                  w_down[fi, f2 * 256:(f2 + 1) * 256, :].rearrange("(a p) d -> p a d", p=128), writes=[sk])
            eng = ("pool", "dve")[i % 2]
            P.op(eng, lambda e, st=st, f2=f2: e.tensor_copy(
                out=wd[:, 2 * f2:2 * f2 + 2, :], in_=st[:, 0:2 * D].rearrange("p (a d) -> p a d", a=2)),
                reads=[sk], writes=[("wd", fi, f2)])
        src_v = src.rearrange("(c p) n -> p c n", p=128)
        dst_v = dst.rearrange("(c p) n -> p c n", p=128)
        tl = tiles()
        P.dma("sp", xt[0][:, :, 0:tl[0][1]], src_v[:, :, tl[0][0]:tl[0][0] + tl[0][1]], writes=["xt0"])
        for ti, (c0, n, segs) in enumerate(tl):
            x = xt[ti % 2]
            xk = "xt%d" % (ti % 2)
            if ti + 1 < len(tl):
                c0n, nn, _ = tl[ti + 1]
                P.dma("sp", xt[(ti + 1) % 2][:, :, 0:nn], src_v[:, :, c0n:c0n + nn], writes=["xt%d" % ((ti + 1) % 2)])
            modulate(x, xk, n, segs, kmod, ub, "ub")
            P.op("pool", lambda e, x=x, n=n: e.tensor_scalar(out=x[:, :, 0:n], in0=x[:, :, 0:n], scalar1=ALPHA, scalar2=None,
                                                            op0=ALU.mult), reads=[xk], writes=[xk])
            for f in range(FC):
                pg, pu = ps[(2 * f) % 4], ps[(2 * f + 1) % 4]
                kg, ku = "ps%d" % ((2 * f) % 4), "ps%d" % ((2 * f + 1) % 4)
                for c in range(DC):
                    P.op("pe", lambda e, c=c, f=f, pg=pg: e.matmul(pg[:, 0:n], lhsT=wg[:, c, f * 128:(f + 1) * 128], rhs=ub[:, c, 0:n],
                                                                 start=(c == 0), stop=(c == DC - 1)),
                         reads=["arena", ("wg", fi, c), "ub"], writes=[kg], skip_own=True)
                for c in range(DC):
                    P.op("pe", lambda e, c=c, f=f, pu=pu: e.matmul(pu[:, 0:n], lhsT=wu[:, c, f * 128:(f + 1) * 128], rhs=ub[:, c, 0:n],
                                                                 start=(c == 0), stop=(c == DC - 1)),
                         reads=["arena", ("wu", fi, c), "ub"], writes=[ku], skip_own=True)
                tq = tmpa[f % 2]
                tk = "tmpa%d" % (f % 2)
                P.op("act", lambda e, pg=pg, tq=tq: e.activation(out=tq[:, 0:n], in_=pg[:, 0:n], func=AF.Silu),
                     reads=[kg], writes=[tk])
                P.op("dve", lambda e, pu=pu, tq=tq, f=f: e.tensor_tensor(out=hT[:, f, 0:n], in0=tq[:, 0:n], in1=pu[:, 0:n], op=ALU.mult),
                     reads=[tk, ku], writes=["hT"])
            for m in range(DC):
                py = ps[4 + (m % 2)]
                ky = "ps%d" % (4 + (m % 2))
                for f in range(FC):
                    P.op("pe", lambda e, m=m, f=f, py=py: e.matmul(py[:, 0:n], lhsT=wd[:, f, m * 128:(m + 1) * 128], rhs=hT[:, f, 0:n],
                                                                 start=(f == 0), stop=(f == FC - 1)),
                         reads=["arena", ("wd", fi, f // 2), "hT"], writes=[ky], skip_own=True)
                for (lo, hi, s) in segs:
                    P.op("dve", lambda e, m=m, py=py, lo=lo, hi=hi, s=s, x=x: e.scalar_tensor_tensor(
                        out=x[:, m, lo:hi], in0=py[:, lo:hi], scalar=modS[:, s, kmod + 2, m:m + 1], in1=x[:, m, lo:hi],
                        op0=ALU.mult, op1=ALU.add),
                        reads=[ky, xk, "modS"], writes=[xk])
            layer_norm_tile(x, xk, n, gi, x, xk)
            P.dma("pool", dst_v[:, :, c0:c0 + n], x[:, :, 0:n], reads=[xk], writes=["dram_" + dst.name], is_output=dst_is_output)

    ffn_phase(0, xT, x1T, 0, 0, False)

    NC1 = 3096
    win_sb = arena[:, 0:DC * NC1].rearrange("p (c f) -> p c f", c=DC)
    arena_gate()
    for c in range(DC):
        for hi_, (a, b) in enumerate(((0, 1548), (1548, 3096))):
            load_weight_bf16(win_sb[:, c, a:b], w_in[c * 128:(c + 1) * 128, a:b], b - a, ("win", c, hi_))
    P.dma("sp", bias_tm[:, 0:792], b_in_bc[:, 2304:3096], reads=[("gate", 0), "arena"], writes=["bias_tm"])
    P.dma("sp", bias_tm[:, 792:792 + RW_SHIFT], b_in_bc[:, 0:RW_SHIFT], reads=[("gate", 0), "arena"], writes=["bias_tm"])
    CV1 = CV0 + (2 * 792 + RW_SHIFT + 792 + RW_SHIFT) * 4
    pbuf = carve(CV1, 15 * 260)
    xsb = carve(CV1 + 15 * 260 * 4, 15 * NT).rearrange("p (c n) -> p c n", c=15)
    DER0 = CV1 + 15 * 260 * 4 + 15 * NT * 4
    DERN = ("w", "a", "g", "kk", "nkk", "kp", "b", "bonus", "t1", "t2", "tw", "sgd")
    der = {k: carve(DER0 + i * NT * 4, NT) for i, k in enumerate(DERN)}
    CHT = [(j * 128, 128) for j in range(12)] + [(1536, 64), (1600, 64), (1664, 128)]
    GK = [("gate", 0), ("gate", 1), ("gate", 2), "arena"]

    def rw_pre(ti, c0, n, segs):
        S_ = len(segs)
        L = n // S_
        pb = pbuf[:, 0:15 * S_ * (L + 1)].rearrange("p (c s l) -> p c s l", c=15, s=S_)
        if S_ > 1:
            P.dma("sp", pb[:, :, :, 0], shs, reads=GK, writes=["pbuf"], allow_slow_non_contiguous=True)
        for ci, (col0, w) in enumerate(CHT):
            pp = ps[2 + ci % 2]
            pk = "ps%d" % (2 + ci % 2)
            for c in range(DC):
                P.op("pe", lambda e: e.matmul(pp[0:w, 0:n], lhsT=win_sb[:, c, col0:col0 + w], rhs=ub[:, c, 0:n],
                                              start=(c == 0), stop=(c == DC - 1)),
                     reads=["arena", ("win", c, 0), ("win", c, 1), "ub"], writes=[pk], skip_own=True)
            P.op("act", lambda e: e.activation(out=pb[0:w, ci, :, 1:L + 1], in_=pp[0:w, 0:n].rearrange("p (s l) -> p s l", s=S_),
                                               func=AF.Identity, bias=rwp_sb[0:w, ci, 0:1], scale=1.0),
                 reads=[pk, "rwc"] + GK, writes=["pbuf"])
        xs4 = xsb[:, :, 0:n].rearrange("p c (s l) -> p c s l", s=S_)
        P.op("dve", lambda e: e.tensor_tensor(out=xs4, in0=pb[:, :, :, 0:L], in1=pb[:, :, :, 1:L + 1], op=ALU.subtract),
             reads=["pbuf"] + GK, writes=["xs"])
        P.op("dve", lambda e: e.tensor_tensor(out=xsb[:, :, 0:n], in0=xsb[:, :, 0:n],
                                              in1=rwp_sb[:, :, 1:2].to_broadcast([128, 15, n]), op=ALU.mult),
             reads=["xs", "rwc"] + GK, writes=["xs"])
        P.op("dve", lambda e: e.tensor_tensor(out=xs4, in0=xs4, in1=pb[:, :, :, 1:L + 1], op=ALU.add),
             reads=["xs", "pbuf"] + GK, writes=["xs"])
        if S_ == 1:
            P.op("dve", lambda e: e.tensor_copy(out=pb[:, :, :, 0:1], in_=pb[:, :, :, L:L + 1]), reads=["pbuf"] + GK, writes=["pbuf"])
        tw, sgd = der["tw"], der["sgd"]
        P.op("act", lambda e: e.activation(out=tw[0:64, 0:n], in_=xsb[0:64, 12, 0:n], func=AF.Tanh), reads=["xs"] + GK, writes=["tw"])
        P.op("act", lambda e: e.activation(out=sgd[:, 0:n], in_=xsb[:, 14, 0:n], func=AF.Sigmoid), reads=["xs"] + GK, writes=["sgd"])
        NEG = -float(np.exp(-0.5))
        for j in range(4):
            jc = slice(j * 128, (j + 1) * 128)
            q = lambda k: rwq_sb[:, j, k:k + 1]
            r_j, k_j, v_j = xsb[:, j, 0:n], xsb[:, 4 + j, 0:n], xsb[:, 8 + j, 0:n]
            dw, da, dg, dkk, dnkk, dkp, db, dbo, t1, t2 = (der[k][:, 0:n] for k in ("w", "a", "g", "kk", "nkk", "kp", "b", "bonus", "t1", "t2"))
            p4, p5 = ps[4][:, 0:n], ps[5][:, 0:n]
            P.op("pe", lambda e: e.matmul(p4, lhsT=w2_sb[0:64, jc], rhs=tw[0:64, 0:n], start=True, stop=True),
                 reads=["rwc", "tw"] + GK, writes=["ps4"])
            P.op("act", lambda e: e.activation(out=dw, in_=p4, func=AF.Sigmoid, bias=q(0), scale=1.0), reads=["ps4", "rwc"] + GK, writes=["d_w"])
            P.op("act", lambda e: e.activation(out=dw, in_=dw, func=AF.Exp, scale=NEG), reads=["d_w"] + GK, writes=["d_w"])
            P.op("pe", lambda e: e.matmul(p5, lhsT=a2_sb[0:64, jc], rhs=xsb[0:64, 13, 0:n], start=True, stop=True),
                 reads=["rwc", "xs"] + GK, writes=["ps5"])
            P.op("act", lambda e: e.activation(out=da, in_=p5, func=AF.Sigmoid, bias=q(1), scale=1.0), reads=["ps5", "rwc"] + GK, writes=["d_a"])
            P.op("pe", lambda e: e.matmul(p4, lhsT=g2_sb[:, jc], rhs=sgd[:, 0:n], start=True, stop=True),
                 reads=["rwc", "sgd"] + GK, writes=["ps4"])
            P.op("act", lambda e: e.copy(out=dg, in_=p4), reads=["ps4"] + GK, writes=["d_g"])
            P.op("dve", lambda e: e.tensor_scalar(out=dkk, in0=k_j, scalar1=q(2), scalar2=None, op0=ALU.mult), reads=["xs", "rwc"] + GK, writes=["d_kk"])
            P.op("act", lambda e: e.activation(out=t1, in_=dkk, func=AF.Square), reads=["d_kk"] + GK, writes=["d_t1"])
            P.op("pe", lambda e: e.matmul(p5, lhsT=blk1[:], rhs=t1, start=True, stop=True), reads=["rwc", "d_t1"] + GK, writes=["ps5"])
            P.op("dve", lambda e: e.tensor_scalar(out=t2, in0=p5, scalar1=1e-24, scalar2=None, op0=ALU.max), reads=["ps5"] + GK, writes=["d_t2"])
            P.op("act", lambda e: e.sqrt(out=t2, in_=t2), reads=["d_t2"] + GK, writes=["d_t2"])
            P.op("dve", lambda e: e.reciprocal(out=t2, in_=t2), reads=["d_t2"] + GK, writes=["d_t2"])
            P.op("dve", lambda e: e.scalar_tensor_tensor(out=dnkk, in0=dkk, scalar=-1.0, in1=t2, op0=ALU.mult, op1=ALU.mult),
                 reads=["d_kk", "d_t2"] + GK, writes=["d_nkk"])
            P.op("dve", lambda e: e.tensor_scalar(out=t1, in0=da, scalar1=-1.0, scalar2=q(3), op0=ALU.add, op1=ALU.mult),
                 reads=["d_a", "rwc", "d_t1"] + GK, writes=["d_t1"])
            P.op("dve", lambda e: e.scalar_tensor_tensor(out=dkp, in0=t1, scalar=1.0, in1=k_j, op0=ALU.add, op1=ALU.mult),
                 reads=["d_t1", "xs"] + GK, writes=["d_kp"])
            P.op("dve", lambda e: e.scalar_tensor_tensor(out=db, in0=dnkk, scalar=-1.0, in1=da, op0=ALU.mult, op1=ALU.mult),
                 reads=["d_nkk", "d_a"] + GK, writes=["d_b"])
            P.op("dve", lambda e: e.scalar_tensor_tensor(out=t1, in0=r_j, scalar=q(4), in1=dkp, op0=ALU.mult, op1=ALU.mult),
                 reads=["xs", "rwc", "d_kp", "d_t1"] + GK, writes=["d_t1"])
            P.op("pe", lambda e: e.matmul(p4, lhsT=blk1[:], rhs=t1, start=True, stop=True), reads=["rwc", "d_t1"] + GK, writes=["ps4"])
            P.op("dve", lambda e: e.tensor_tensor(out=dbo, in0=p4, in1=v_j, op=ALU.mult), reads=["ps4", "xs"] + GK, writes=["d_bonus"])
            for nm, src_, key in (("nkk", dnkk, "d_nkk"), ("w", dw, "d_w"), ("b", db, "d_b"), ("kp", dkp, "d_kp"),
                                  ("g", dg, "d_g"), ("bonus", dbo, "d_bonus"), ("r", r_j, "xs"), ("v", v_j, "xs")):
                P.dma("pool", scr[nm][:, j, c0:c0 + n], src_, reads=[key] + GK, writes=["dram_scr"])

    nst = carve(DER0 + len(DERN) * NT * 4, 16 * NT).rearrange("p (c n) -> p c n", c=16)
    NCH = [1792 + 64 * h for h in range(8)] + [2304 + br * 256 + g * 64 for br in range(3) for g in range(2)] \
        + [2304 + 128 + g * 64 for g in range(2)]

    def nsa_pre(c0, n):
        for ci, col0 in enumerate(NCH):
            pp = ps[4 + ci % 2]
            pk = "ps%d" % (4 + ci % 2)
            for c in range(DC):
                P.op("pe", lambda e: e.matmul(pp[0:64, 0:n], lhsT=win_sb[:, c, col0:col0 + 64], rhs=ub[:, c, 0:n],
                                              start=(c == 0), stop=(c == DC - 1)),
                     reads=["arena", ("win", c, 0), ("win", c, 1), "ub"], writes=[pk], skip_own=True)
            P.op("act", lambda e: e.activation(out=nst[0:64, ci, 0:n], in_=pp[0:64, 0:n], func=AF.Identity,
                                               bias=nsab_sb[0:64, ci:ci + 1], scale=(0.125 if ci < 8 else 1.0)),
                 reads=[pk, "rwc"] + GK, writes=["nst"])
        P.dma("pool", nsaT[:, :, c0:c0 + n], nst[0:64, :, 0:n], reads=["nst"] + GK, writes=["dram_nsaT"])

    P.op("pool", lambda e: e.memset(pbuf, 0.0), reads=GK, writes=["pbuf"])
    P.op("pool", lambda e: e.memset(xsb, 0.0), reads=GK, writes=["xs"])
    x1_v = x1T.rearrange("(c p) n -> p c n", p=128)
    tl = tiles()
    for ti, (c0, n, segs) in enumerate(tl):
        x = xt[ti % 2]
        xk = "xt%d" % (ti % 2)
        P.dma("sp", x[:, :, 0:n], x1_v[:, :, c0:c0 + n], reads=["dram_x1T"], writes=[xk])
        modulate(x, xk, n, segs, 3, ub, "ub")
        rw_pre(ti, c0, n, segs)
        nsa_pre(c0, n)
        nblk = max(1, n // 128)
        for tb in range(nblk):
            m = min(128, n)
            t0 = c0 + tb * 128
            kv = kvt[tb % 2]
            kk_ = "kvt%d" % (tb % 2)
            for (ca, cb, pi) in ((0, 512, 0), (512, 792, 1)):
                pp = ps[pi]
                for c in range(DC):
                    P.op("pe", lambda e, c=c, tb=tb, m=m, ca=ca, cb=cb, pp=pp: e.matmul(
                        pp[0:m, 0:cb - ca], lhsT=ub[:, c, tb * 128:tb * 128 + m], rhs=win_sb[:, c, 2304 + ca:2304 + cb],
                        start=(c == 0), stop=(c == DC - 1)),
                        reads=["arena", ("win", c, 1), "ub"], writes=["ps%d" % pi], skip_own=True)
                P.op("dve", lambda e, m=m, ca=ca, cb=cb, pp=pp, kv=kv: e.tensor_tensor(
                    out=kv[0:m, ca:cb], in0=pp[0:m, 0:cb - ca], in1=bias_tm[0:m, ca:cb], op=ALU.add),
                    reads=["ps%d" % pi, "bias_tm", "arena"], writes=[kk_])
            P.dma("pool", kv_tm[t0:t0 + m, :], kv[0:m, 0:768], reads=[kk_, "arena"], writes=["dram_kvtm"])
            P.op("act", lambda e: e.activation(out=kv[0:m, 768:792], in_=kv[0:m, 768:792], func=AF.Sigmoid), reads=[kk_, "arena"], writes=[kk_])
            P.dma("pool", gates_tm[t0:t0 + m, :], kv[0:m, 768:792], reads=[kk_, "arena"], writes=["dram_gates"])
            P.dma("pool", o_kvc[t0:t0 + m, :], kv[0:m, 0:256], reads=[kk_, "arena"], is_output=True)
            P.dma("pool", o_kvs[t0:t0 + m, :], kv[0:m, 256:512], reads=[kk_, "arena"], is_output=True)
            if t0 >= SEQ - 512 and t0 < SEQ:
                P.dma("pool", o_kvw_p[t0 - (SEQ - 512):t0 - (SEQ - 512) + m, :], kv[0:m, 512:768], reads=[kk_, "arena"], is_output=True)
            if t0 >= SEQ:
                for s in range(NS):
                    P.dma("pool", o_kvw_s[s, 512 - ST:512, :], kv[s * ST:(s + 1) * ST, 512:768], reads=[kk_, "arena"], is_output=True)
            last_blk = (t0 + m == SEQ) or (t0 >= SEQ)
            if last_blk:
                for q4 in range(4):
                    ca, cb = q4 * 448, (q4 + 1) * 448
                    pp = ps[2 + (q4 % 2)]
                    pk = "ps%d" % (2 + (q4 % 2))
                    for c in range(DC):
                        P.op("pe", lambda e, c=c, tb=tb, m=m, ca=ca, cb=cb, pp=pp: e.matmul(
                            pp[0:m, 0:448], lhsT=ub[:, c, tb * 128:tb * 128 + m], rhs=win_sb[:, c, ca:cb],
                            start=(c == 0), stop=(c == DC - 1)),
                            reads=["arena", ("win", c, 0), ("win", c, 1), "ub"], writes=[pk], skip_own=True)
                    P.op("dve", lambda e, m=m, ca=ca, cb=cb, pp=pp: e.tensor_tensor(
                        out=rwt[0:m, ca:cb], in0=pp[0:m, 0:448], in1=bias_tm[0:m, 792 + ca:792 + cb], op=ALU.add),
                        reads=[pk, "bias_tm", "arena"], writes=["rwt"])
                if t0 < SEQ:
                    P.dma("pool", o_shift[0:1, :], rwt[127:128, :], reads=["rwt", "arena"], is_output=True)
                else:
                    for s in range(NS):
                        P.dma("pool", o_shift[1 + s:2 + s, :], rwt[s * ST + ST - 1:s * ST + ST, :], reads=["rwt", "arena"], is_output=True)
    for s in range(NS):
        P.dma("sp", o_kvw_s[s, 0:512 - ST, :], win_state[s, ST:512, :], is_output=True)

    arena_gate()
    TC = 8
    BN = ("nkk", "w", "b", "kp", "r")
    off = [0]

    def cv(ncols):
        v = carve(off[0], ncols)
        off[0] += ncols * 4
        return v

    xB = {k: [cv(TC * 256) for _ in range(2)] for k in BN}
    bld = [cv(TC * 256) for _ in range(2)]
    m3buf = cv(TC * 256)
    Shist = [cv(TC * 256) for _ in range(2)]
    xin = {k: [cv(4 * TC).rearrange("p (h t) -> p h t", h=4) for _ in range(2)] for k in BN + ("v",)}
    Sst, Sw, m1 = (cv(256) for _ in range(3))
    sa = cv(4)
    hT32 = hT[:].rearrange("p f n -> p (f n)").bitcast(F32)
    ybuf = [hT32[:, i_ * 1024:(i_ + 1) * 1024].rearrange("p (h t) -> p h t", h=4) for i_ in range(2)]
    v3 = lambda ap: ap.rearrange("p (h j) -> p h j", h=4)
    ycnt = [0]
    bcnt = [0]
    BLD_ENG = {"nkk": "dve", "w": "dve", "b": "pool", "kp": "pool", "r": "pool"}

    def scan_seq(seq_i, col0, T):
        if seq_i == 0:
            P.op("pool", lambda e: e.memset(Sst, 0.0), reads=GK, writes=["S0"])
        else:
            src = wkv0[seq_i - 1].rearrange("(hf hp) i j -> hp i hf j", hp=2)
            for hp in range(2):
                P.dma("sp", v3(Sst)[hp * 64:(hp + 1) * 64], src[hp], reads=GK, writes=["S0"])
        Sprev, Spk = Sst, "S0"
        yb_i = ycnt[0] % 2
        ycnt[0] += 1
        yb, ybk = ybuf[yb_i], "ybuf%d" % yb_i
        ycol0 = col0
        for t0 in range(0, T, TC):
            tc = min(TC, T - t0)
            bi = bcnt[0] % 2
            bcnt[0] += 1
            for k in BN + ("v",):
                P.dma("sp", xin[k][bi][:, :, 0:tc], scr[k][:, :, col0 + t0:col0 + t0 + tc], reads=["dram_scr"] + GK, writes=[("xin", k, bi)])
            for ki, k in enumerate(BN):
                bl, blk_ = bld[ki % 2], "bld%d" % (ki % 2)
                P.op(BLD_ENG[k], lambda e: e.tensor_tensor(
                    out=bl[:, 0:tc * 256].rearrange("p (t h j) -> p t h j", t=tc, h=4),
                    in0=istk[:].unsqueeze(1).unsqueeze(1).to_broadcast([128, tc, 4, 64]),
                    in1=xin[k][bi][:, :, 0:tc].rearrange("p h t -> p t h").unsqueeze(3).to_broadcast([128, tc, 4, 64]),
                    op=ALU.mult), reads=[("xin", k, bi), "rwc"] + GK, writes=[blk_])
                for t2 in range(0, tc, 2):
                    w_ = min(2, tc - t2) * 256
                    pi = (t2 // 2) % 4
                    P.op("pe", lambda e: e.matmul(ps[pi][:, 0:w_], lhsT=blk1[:], rhs=bl[:, t2 * 256:t2 * 256 + w_], start=True, stop=True),
                         reads=[blk_, "rwc"] + GK, writes=["ps%d" % pi])
                    P.op("act", lambda e: e.copy(out=xB[k][bi][:, t2 * 256:t2 * 256 + w_], in_=ps[pi][:, 0:w_]),
                         reads=["ps%d" % pi] + GK, writes=[("xB", k, bi)])
            Sh, Shk = Shist[bi], ("Sh", bi)
            for tl_ in range(tc):
                sl = slice(tl_ * 256, (tl_ + 1) * 256)
                P.op("dve", lambda e: e.tensor_tensor(out=m1, in0=Sprev, in1=xB["nkk"][bi][:, sl], op=ALU.mult),
                     reads=[Spk, ("xB", "nkk", bi)] + GK, writes=["m1"])
                P.op("dve", lambda e: e.tensor_reduce(out=sa, in_=v3(m1), axis=AX.X, op=ALU.add), reads=["m1"] + GK, writes=["sa"])
                P.op("pool", lambda e: e.tensor_tensor(out=Sw, in0=Sprev, in1=xB["w"][bi][:, sl], op=ALU.mult),
                     reads=[Spk, ("xB", "w", bi)] + GK, writes=["Sw"])
                for hf in range(4):
                    hs = slice(hf * 64, (hf + 1) * 64)
                    P.op("dve", lambda e: e.scalar_tensor_tensor(out=Sw[:, hs], in0=xB["kp"][bi][:, sl][:, hs], scalar=xin["v"][bi][:, hf, tl_:tl_ + 1],
                                                                  in1=Sw[:, hs], op0=ALU.mult, op1=ALU.add),
                         reads=["Sw", ("xB", "kp", bi), ("xin", "v", bi)] + GK, writes=["Sw"])
                for hf in range(4):
                    hs = slice(hf * 64, (hf + 1) * 64)
                    P.op("dve", lambda e: e.scalar_tensor_tensor(out=Sh[:, sl][:, hs], in0=xB["b"][bi][:, sl][:, hs], scalar=sa[:, hf:hf + 1],
                                                                 in1=Sw[:, hs], op0=ALU.mult, op1=ALU.add),
                         reads=["Sw", "sa", ("xB", "b", bi)] + GK, writes=[Shk])
                Sprev, Spk = Sh[:, sl], Shk
            yc = t0 - (ycol0 - col0)
            P.op("pool", lambda e: e.tensor_tensor(out=m3buf[:, 0:tc * 256], in0=Sh[:, 0:tc * 256], in1=xB["r"][bi][:, 0:tc * 256], op=ALU.mult),
                 reads=[Shk, ("xB", "r", bi)] + GK, writes=["m3"])
            P.op("dve", lambda e: e.tensor_reduce(out=yb[:, :, yc:yc + tc].rearrange("p h t -> p t h"),
                                                  in_=m3buf[:, 0:tc * 256].rearrange("p (t h j) -> p t h j", t=tc, h=4), axis=AX.X, op=ALU.add),
                 reads=["m3", "hT"] + GK, writes=[ybk])
            if yc + tc == 256 or t0 + tc == T:
                P.dma("sp", y_fm[:, :, ycol0:ycol0 + yc + tc], yb[:, :, 0:yc + tc], reads=[ybk, "hT"] + GK, writes=["dram_yfm"])
                ycol0 += yc + tc
                yb_i = ycnt[0] % 2
                ycnt[0] += 1
                yb, ybk = ybuf[yb_i], "ybuf%d" % yb_i
        dst = o_wkv[seq_i].rearrange("(hf hp) i j -> hp i hf j", hp=2)
        for hp in range(2):
            P.dma("sp", dst[hp], v3(Sprev)[hp * 64:(hp + 1) * 64], reads=[Spk] + GK, is_output=True)

    scan_seq(0, 0, SEQ_SCAN)
    for s_ in range(NS):
        scan_seq(1 + s_, SEQ + s_ * ST, ST)

    arena_gate()
    off[0] = 0
    ktile = [cv(SEQ) for _ in range(3)]
    vaug = [cv(32 * 65).rearrange("p (c f) -> p c f", c=32) for _ in range(2)]
    vcaug = [cv(2 * 129).rearrange("p (c f) -> p c f", c=2) for _ in range(2)]
    kcT = [cv(256) for _ in range(2)]
    w1_sb = cv(32 * 128).rearrange("p (j e) -> p j e", j=32)
    w2a_sb = cv(64)
    peT_sb = cv(64).rearrange("p (k j) -> p k j", k=2)
    hx, ht_, hs_ = cv(256), cv(256), cv(256)
    tri_sb, ident_sb = cv(128), cv(128)
    cvalT_sb = cv(256).rearrange("p (c q) -> p c q", c=2)
    jidx_sb, f0_sb, idx16_sb, curb_sb = cv(64), cv(64), cv(16), cv(1)
    eall_sb = ktile[0]
    ones_c = cv(512)
    qs4, qsq, e_sb = cv(512), cv(512), [cv(512), cv(512)]
    mrow = cv(3 * 512).rearrange("p (b n) -> p b n", b=3)
    kmx = cv(8)
    mask_sb = [cv(128), cv(128)]
    gat = cv(24)
    ob = cv(256).rearrange("p (h d) -> p h d", h=4)
    imp, Am, Fm, F2m, NFm, nfm, nf2m, selm = (cv(64) for _ in range(8))
    t16, oh16 = cv(16), cv(16)
    cur_, curm1, nF_, thr_, rc_ = (cv(1) for _ in range(5))
    rc4 = cv(4)
    selT_sb = cv(128)
    ybT = cv(2 * 128).rearrange("p (c q) -> p c q", c=2)
    pe_sb = cv(1)

    def cvh(ncols):
        a_ = off[0] // 2
        off[0] += ncols * 2
        return arena[:, a_:a_ + ncols]

    GK = GK + ["xt0", "xt1", "hT", "stage0", "stage1"]
    kb16 = [xt[i_][:].rearrange("p c n -> p (c n)").bitcast(BF16) for i_ in range(2)]
    v16 = [stage[i_][:].bitcast(BF16)[:, 0:32 * 65].rearrange("p (c f) -> p c f", c=32) for i_ in range(2)]
    eall16 = hT[:].rearrange("p f n -> p (f n)")[:, 0:SEQ]
    q16, ones16 = cvh(512), cvh(128)
    mrow16 = cvh(2 * 512).rearrange("p (b n) -> p b n", b=2)
    e16 = [cvh(512), cvh(512)]
    mask16 = [cvh(128), cvh(128)]
    tri16p, ntri16p, selT16 = cvh(128), cvh(128), cvh(128)
    assert off[0] <= 135168, off[0]

    def A(eng, fn, reads=(), writes=(), **kw):
        return P.op(eng, fn, reads=list(reads) + GK, writes=writes, **kw)

    def Dm(out, in_, reads=(), writes=(), **kw):
        return P.dma("sp", out, in_, reads=list(reads) + GK, writes=writes, **kw)

    for dst_, src_ in ((tri_sb, tri_d), (ident_sb, ident_d), (cvalT_sb, cvalT_d), (jidx_sb, jidx_d), (eall_sb[0:64, :], eall_d),
                       (idx16_sb, idx16_d), (curb_sb, curb_d)):
        Dm(dst_, src_, writes=["ncst", "kt0"] if dst_ is not tri_sb and False else ["ncst"])
    A("pool", lambda e: e.memset(ones_c, 1.0), writes=["ncst"])
    A("act", lambda e: e.copy(out=eall16[0:64, :], in_=eall_sb[0:64, :]), reads=["ncst"], writes=["ncst16", "kt0"])
    A("pool", lambda e: e.memset(ones16, 1.0), writes=["ncst16"])
    A("dve", lambda e: e.tensor_copy(out=tri16p, in_=tri_sb), reads=["ncst"], writes=["ncst16"])
    A("dve", lambda e: e.tensor_scalar(out=f0_sb, in0=jidx_sb, scalar1=0.0, scalar2=None, op0=ALU.is_equal), reads=["ncst"], writes=["ncst2"])
    for g in range(2):
        A("pool", lambda e: e.memset(vcaug[g][:, :, 64:65], 1.0), writes=[("vcaug", g)])
        Dm(vcaug[g][:, :, 65:129], bimp_d, writes=[("vcaug", g)])

    for kvi in range(2):
        Dm(w1_sb[0:64], phi_w1[kvi].rearrange("j d e -> d j e"), writes=["w1"])
        Dm(w2a_sb, phi_w2[kvi], writes=["w2a"])
        if kvi == 0:
            Dm(peT_sb[0:64], peT_d, writes=["peT"])
        for j in range(32):
            A("pe", lambda e: e.matmul(ps[3][:, 0:1], lhsT=w1_sb[0:64, j, :], rhs=peT_sb[0:64, kvi, j:j + 1], start=(j == 0), stop=(j == 31)),
              reads=["w1", "peT"], writes=["ps3"], skip_own=True)
        A("act", lambda e: e.copy(out=pe_sb, in_=ps[3][:, 0:1]), reads=["ps3"], writes=["pe_sb"])
        for g in range(2):
            xc = ktile[0]
            Dm(xc[0:64, :], nsaT[:, (8 + g) if kvi == 0 else (14 + g), 0:SEQ], reads=["dram_nsaT"], writes=["kt0"])
            for j in range(32):
                A("pe", lambda e: e.matmul(ps[0][:, 0:255], lhsT=w1_sb[0:64, j, :], rhs=xc[0:64, j:j + 16 * 254 + 1:16],
                                           start=(j == 0), stop=(j == 31)), reads=["w1", "kt0"], writes=["ps0"], skip_own=True)
            A("act", lambda e: e.activation(out=hx[:, 0:255], in_=ps[0][:, 0:255], func=AF.Identity, bias=pe_sb[:, 0:1], scale=1.0),
              reads=["ps0", "pe_sb"], writes=["hx"])
            A("dve", lambda e: e.tensor_tensor(out=ht_[:, 0:255], in0=hx[:, 0:255], in1=hx[:, 0:255], op=ALU.mult), reads=["hx"], writes=["ht"])
            A("dve", lambda e: e.tensor_scalar(out=ht_[:, 0:255], in0=ht_[:, 0:255], scalar1=0.044715, scalar2=1.0, op0=ALU.mult, op1=ALU.add),
              reads=["ht"], writes=["ht"])
            A("dve", lambda e: e.tensor_tensor(out=ht_[:, 0:255], in0=ht_[:, 0:255], in1=hx[:, 0:255], op=ALU.mult), reads=["ht", "hx"], writes=["ht"])
            A("act", lambda e: e.activation(out=hs_[:, 0:255], in_=ht_[:, 0:255], func=AF.Sigmoid, scale=1.5957691216057308), reads=["ht"], writes=["hs"])
            A("dve", lambda e: e.tensor_tensor(out=hx[:, 0:255], in0=hx[:, 0:255], in1=hs_[:, 0:255], op=ALU.mult), reads=["hx", "hs"], writes=["hx"])
            if kvi == 0:
                A("pe", lambda e: e.matmul(ps[1][0:64, 0:255], lhsT=w2a_sb[:, 0:64], rhs=hx[:, 0:255], start=True, stop=True),
                  reads=["w2a", "hx"], writes=["ps1"])
                A("act", lambda e: e.copy(out=kcT[g][0:64, 0:255], in_=ps[1][0:64, 0:255]), reads=["ps1"], writes=[("kcT", g)])
            else:
                for ch, nk in ((0, 128), (1, 127)):
                    A("pe", lambda e: e.matmul(ps[1][0:nk, 0:64], lhsT=hx[:, ch * 128:ch * 128 + nk], rhs=w2a_sb[:, 0:64], start=True, stop=True),
                      reads=["w2a", "hx"], writes=["ps1"])
                    A("act", lambda e: e.copy(out=vcaug[g][0:nk, ch, 0:64], in_=ps[1][0:nk, 0:64]), reads=["ps1"], writes=[("vcaug", g)])

    def key_max(kt, nkeys, slot, ktk="kt*"):
        nchunk = (nkeys + 511) // 512
        for c in range(nchunk):
            w_ = min(512, nkeys - c * 512)
            A("act", lambda e: e.activation(out=qsq[0:64, 0:w_], in_=kt[0:64, c * 512:c * 512 + w_], func=AF.Square), reads=[ktk], writes=["qsq"])
            A("pe", lambda e: e.matmul(ps[3][0:1, 0:w_], lhsT=ones_c[0:64, 0:1], rhs=qsq[0:64, 0:w_], start=True, stop=True),
              reads=["qsq", "ncst"], writes=["ps3"])
            A("dve", lambda e: e.tensor_reduce(out=t16[0:1, c:c + 1], in_=ps[3][0:1, 0:w_], axis=AX.X, op=ALU.max), reads=["ps3"], writes=["t16"])
        A("dve", lambda e: e.tensor_reduce(out=kmx[0:1, slot:slot + 1], in_=t16[0:1, 0:nchunk], axis=AX.X, op=ALU.max), reads=["t16"], writes=["kmx"])
        A("act", lambda e: e.sqrt(out=kmx[0:1, slot:slot + 1], in_=kmx[0:1, slot:slot + 1]), reads=["kmx"], writes=["kmx"])
        A("dve", lambda e: e.tensor_scalar(out=kmx[0:1, slot:slot + 1], in0=kmx[0:1, slot:slot + 1], scalar1=-1.0, scalar2=None, op0=ALU.mult),
          reads=["kmx"], writes=["kmx"])

    ecnt = [0]

    def attend(kt_chunk, nk, br, mask_ap, mask_key, vaug_chunk, W, first, last, ktk="kt*", vk="vaug*", lowp=False):
        i = ecnt[0] % 2
        ecnt[0] += 1
        sc, sk = ps[i], "ps%d" % i
        es_, ek = (e16[i], "e16_%d" % i) if lowp else (e_sb[i], "e%d" % i)
        if lowp:
            A("pe", lambda e: e.matmul(sc[0:nk, 0:512], lhsT=kt_chunk, rhs=q16[0:64, :], start=True, stop=False),
              reads=[ktk, "q16"], writes=[sk], skip_own=True)
            A("pe", lambda e: e.matmul(sc[0:nk, 0:512], lhsT=ones16[0:1, 0:nk], rhs=mrow16[0:1, br - 1, :], start=False, stop=True),
              reads=["mrow16", "ncst16"], writes=[sk], skip_own=True)
        else:
            A("pe", lambda e: e.matmul(sc[0:nk, 0:512], lhsT=kt_chunk, rhs=qs4[0:64, :], start=True, stop=False),
              reads=[ktk, "qs4"], writes=[sk], skip_own=True)
            A("pe", lambda e: e.matmul(sc[0:nk, 0:512], lhsT=ones_c[0:1, 0:nk], rhs=mrow[0:1, br, :], start=False, stop=True),
              reads=["mrow", "ncst"], writes=[sk], skip_own=True)
        A("act", lambda e: e.activation(out=es_[0:nk, :], in_=sc[0:nk, 0:512], func=AF.Exp), reads=[sk], writes=[ek])
        if mask_ap is not None:
            A("dve", lambda e: e.tensor_tensor(out=es_[0:nk, :].rearrange("p (h q) -> p h q", h=4), in0=es_[0:nk, :].rearrange("p (h q) -> p h q", h=4),
                                               in1=mask_ap.unsqueeze(1).to_broadcast([nk, 4, 128]), op=ALU.mult),
              reads=[ek, mask_key], writes=[ek])
        for h in range(4):
            A("pe", lambda e: e.matmul(ps[4 + h][:, 0:W], lhsT=es_[0:nk, h * 128:(h + 1) * 128], rhs=vaug_chunk, start=first, stop=last),
              reads=[ek, vk], writes=["ps%d" % (4 + h)], skip_own=True)

    def finish_branch(g, br, first_branch):
        for h in range(4):
            o = ps[4 + h]
            ok = "ps%d" % (4 + h)
            A("dve", lambda e: e.tensor_scalar(out=rc_, in0=o[:, 64:65], scalar1=1e-30, scalar2=None, op0=ALU.max), reads=[ok], writes=["rc"])
            A("dve", lambda e: e.reciprocal(out=rc_, in_=rc_), reads=["rc"], writes=["rc"])
            if br == 0:
                if h == 0:
                    A("dve", lambda e: e.tensor_scalar(out=imp, in0=o[:, 65:129], scalar1=rc_[:, 0:1], scalar2=None, op0=ALU.mult),
                      reads=[ok, "rc"], writes=["imp"])
                else:
                    A("dve", lambda e: e.scalar_tensor_tensor(out=imp, in0=o[:, 65:129], scalar=rc_[:, 0:1], in1=imp, op0=ALU.mult, op1=ALU.add),
                      reads=[ok, "rc", "imp"], writes=["imp"])
            gi_ = (4 * g + h) * 3 + br
            A("dve", lambda e: e.tensor_tensor(out=rc_, in0=rc_, in1=gat[:, gi_:gi_ + 1], op=ALU.mult), reads=["rc", "gat"], writes=["rc"])
            if first_branch:
                A("dve", lambda e: e.tensor_scalar(out=ob[:, h, :], in0=o[:, 0:64], scalar1=rc_[:, 0:1], scalar2=None, op0=ALU.mult),
                  reads=[ok, "rc"], writes=["ob"])
            else:
                A("dve", lambda e: e.scalar_tensor_tensor(out=ob[:, h, :], in0=o[:, 0:64], scalar=rc_[:, 0:1], in1=ob[:, h, :], op0=ALU.mult, op1=ALU.add),
                  reads=[ok, "rc", "ob"], writes=["ob"])

    def select_blocks(curval):
        A("dve", lambda e: e.tensor_scalar(out=cur_, in0=curb_sb, scalar1=float(curval), scalar2=None, op0=ALU.add), reads=["ncst"], writes=["cur"])
        A("dve", lambda e: e.tensor_scalar(out=curm1, in0=cur_, scalar1=-1.0, scalar2=None, op0=ALU.add), reads=["cur"], writes=["curm1"])
        A("dve", lambda e: e.tensor_scalar(out=Am, in0=jidx_sb, scalar1=cur_[:, 0:1], scalar2=None, op0=ALU.is_le), reads=["cur", "ncst"], writes=["Am"])
        A("dve", lambda e: e.tensor_scalar(out=Fm, in0=jidx_sb, scalar1=cur_[:, 0:1], scalar2=None, op0=ALU.is_equal), reads=["cur", "ncst"], writes=["Fm"])
        A("dve", lambda e: e.tensor_scalar(out=F2m, in0=jidx_sb, scalar1=curm1[:, 0:1], scalar2=None, op0=ALU.is_equal), reads=["curm1", "ncst"], writes=["F2m"])
        A("dve", lambda e: e.tensor_tensor(out=Fm, in0=Fm, in1=F2m, op=ALU.max), reads=["Fm", "F2m"], writes=["Fm"])
        A("dve", lambda e: e.tensor_tensor(out=Fm, in0=Fm, in1=f0_sb, op=ALU.max), reads=["Fm", "ncst2"], writes=["Fm"])
        A("dve", lambda e: e.tensor_tensor(out=NFm, in0=Am, in1=Fm, op=ALU.subtract), reads=["Am", "Fm"], writes=["NFm"])
        A("dve", lambda e: e.scalar_tensor_tensor(out=nfm, in0=imp, scalar=1.0, in1=NFm, op0=ALU.add, op1=ALU.mult), reads=["imp", "NFm"], writes=["nfm"])
        A("dve", lambda e: e.tensor_scalar(out=nfm, in0=nfm, scalar1=-1.0, scalar2=None, op0=ALU.add), reads=["nfm"], writes=["nfm"])
        A("dve", lambda e: e.tensor_reduce(out=nF_, in_=Fm, axis=AX.X, op=ALU.add), reads=["Fm"], writes=["nF"])
        A("dve", lambda e: e.tensor_scalar(out=nF_, in0=nF_, scalar1=-1.0, scalar2=15.0, op0=ALU.mult, op1=ALU.add), reads=["nF"], writes=["nF"])
        A("dve", lambda e: e.tensor_scalar(out=oh16, in0=idx16_sb, scalar1=nF_[:, 0:1], scalar2=None, op0=ALU.is_equal), reads=["nF", "ncst"], writes=["oh16"])
        A("dve", lambda e: e.max(out=t16[:, 0:8], in_=nfm), reads=["nfm"], writes=["t16"])
        A("dve", lambda e: e.match_replace(out=nf2m, in_to_replace=t16[:, 0:8], in_values=nfm, imm_value=-2.0), reads=["nfm", "t16"], writes=["nf2m"])
        A("dve", lambda e: e.max(out=t16[:, 8:16], in_=nf2m), reads=["nf2m"], writes=["t16"])
        A("dve", lambda e: e.tensor_tensor(out=t16, in0=t16, in1=oh16, op=ALU.mult), reads=["t16", "oh16"], writes=["t16"])
        A("dve", lambda e: e.tensor_reduce(out=thr_, in_=t16, axis=AX.X, op=ALU.add), reads=["t16"], writes=["thr"])
        A("dve", lambda e: e.tensor_scalar(out=selm, in0=nfm, scalar1=thr_[:, 0:1], scalar2=None, op0=ALU.is_ge), reads=["nfm", "thr"], writes=["selm"])
        A("dve", lambda e: e.tensor_tensor(out=selm, in0=selm, in1=NFm, op=ALU.mult), reads=["selm", "NFm"], writes=["selm"])
        A("dve", lambda e: e.tensor_tensor(out=selm, in0=selm, in1=Fm, op=ALU.add), reads=["selm", "Fm"], writes=["selm"])
        A("pe", lambda e: e.transpose(out=ps[3][0:64, 0:128], in_=selm, identity=ident_sb), reads=["selm", "ncst"], writes=["ps3"])
        A("act", lambda e: e.copy(out=selT16[0:64, :], in_=ps[3][0:64, 0:128]), reads=["ps3"], writes=["selT"])

    mcnt = [0]

    def sel_mask(chunk, diag):
        i = mcnt[0] % 2
        mcnt[0] += 1
        A("pe", lambda e: e.matmul(ps[2][:, 0:128], lhsT=eall16[0:64, chunk * 128:(chunk + 1) * 128], rhs=selT16[0:64, :], start=True, stop=True),
          reads=["selT", "ncst16"], writes=["ps2"])
        if diag:
            A("dve", lambda e: e.tensor_tensor(out=mask16[i], in0=ps[2][:, 0:128], in1=tri_sb, op=ALU.mult), reads=["ps2", "ncst"], writes=[("mask16", i)])
        else:
            A("act", lambda e: e.copy(out=mask16[i], in_=ps[2][:, 0:128]), reads=["ps2"], writes=[("mask16", i)])
        return mask16[i], ("mask16", i)

    ntri_sb = hs_[:, 0:128]
    A("dve", lambda e: e.tensor_scalar(out=ntri_sb, in0=tri_sb, scalar1=-1.0, scalar2=1.0, op0=ALU.mult, op1=ALU.add), reads=["ncst", "hs"], writes=["ntri"])
    A("dve", lambda e: e.tensor_copy(out=ntri16p, in_=ntri_sb), reads=["ntri"], writes=["ntri16"])

    NQB = SEQ_NSA // 128
    A("pool", lambda e: e.memset(qsq, 0.0), writes=["qsq", "qsq0"])
    Dm(yb_fm[:, :, SEQ:TT], ybT[:, :, 0:NS * ST].rearrange("p c q -> p (c q)")[:, 0:4 * NS * ST].rearrange("p (c q) -> p c q", c=4)
       if False else qsq[:, 0:4 * NS * ST].rearrange("p (c q) -> p c q", c=4), reads=["qsq0"], writes=["dram_ybfm"])
    for g in range(2):
        Dm(ktile[1][0:64, :], nsaT[:, 10 + g, 0:SEQ], reads=["dram_nsaT"], writes=["kt*"])
        Dm(ktile[2][0:64, :], nsaT[:, 12 + g, 0:SEQ], reads=["dram_nsaT"], writes=["kt*"])
        for bi_, col in ((0, 256 + 128 + g * 64), (1, 512 + 128 + g * 64)):
            Dm(vaug[bi_][:, :, 0:64], kv_tm[0:SEQ, col:col + 64].rearrange("(c p) f -> p c f", p=128), reads=["dram_kvtm"], writes=["vaug*"])
            A("pool", lambda e: e.memset(vaug[bi_][:, :, 64:65], 1.0), writes=["vaug*"])
        key_max(kcT[g], 255, 0, ("kcT", g))
        key_max(ktile[1], SEQ, 1)
        key_max(ktile[2], SEQ, 2)
        A("act", lambda e: e.copy(out=kb16[0][0:64, :], in_=ktile[1][0:64, :]), reads=["kt*"], writes=["kb16"])
        A("dve", lambda e: e.tensor_copy(out=kb16[1][0:64, :], in_=ktile[2][0:64, :]), reads=["kt*"], writes=["kb16"])
        A("pool", lambda e: e.tensor_copy(out=v16[0], in_=vaug[0]), reads=["vaug*"], writes=["v16"])
        A("pool", lambda e: e.tensor_copy(out=v16[1], in_=vaug[1]), reads=["vaug*"], writes=["v16"])
        for qb in range(NQB):
            q0 = qb * 128
            Dm(qs4[0:64, :].rearrange("p (h q) -> p h q", h=4), nsaT[:, 4 * g:4 * g + 4, q0:q0 + 128], reads=["dram_nsaT"], writes=["qs4"])
            Dm(gat, gates_tm[q0:q0 + 128, :], reads=["dram_gates"], writes=["gat"])
            A("act", lambda e: e.activation(out=qsq[0:64, :], in_=qs4[0:64, :], func=AF.Square), reads=["qs4"], writes=["qsq"])
            A("pe", lambda e: e.matmul(ps[3][0:1, 0:512], lhsT=ones_c[0:64, 0:1], rhs=qsq[0:64, :], start=True, stop=True),
              reads=["qsq", "ncst"], writes=["ps3"])
            A("act", lambda e: e.sqrt(out=mrow[0:1, 0, :], in_=ps[3][0:1, 0:512]), reads=["ps3"], writes=["mrow"])
            for br in (2, 1, 0):
                A("dve", lambda e: e.tensor_scalar(out=mrow[0:1, br, :], in0=mrow[0:1, 0, :], scalar1=kmx[0:1, br:br + 1], scalar2=None, op0=ALU.mult),
                  reads=["mrow", "kmx"], writes=["mrow"])
            A("dve", lambda e: e.tensor_copy(out=mrow16[0:1, :, :], in_=mrow[0:1, 1:3, :]), reads=["mrow"], writes=["mrow16"])
            A("dve", lambda e: e.tensor_copy(out=q16[0:64, :], in_=qs4[0:64, :]), reads=["qs4"], writes=["q16"])
            for ch, nk in ((0, 128), (1, 127)):
                i = mcnt[0] % 2
                mcnt[0] += 1
                A("dve", lambda e: e.tensor_scalar(out=mask_sb[i][0:nk, :], in0=cvalT_sb[0:nk, ch, :], scalar1=float(q0), scalar2=None, op0=ALU.is_le),
                  reads=["ncst"], writes=[("mask", i)])
                attend(kcT[g][0:64, ch * 128:ch * 128 + nk], nk, 0, mask_sb[i][0:nk, :], ("mask", i), vcaug[g][0:nk, ch, :], 129, ch == 0, ch == 1, ktk=("kcT", g), vk=("vcaug", g))
            finish_branch(g, 0, True)
            select_blocks(2 * qb)
            for ch in range(qb + 1):
                mk, mkk = sel_mask(ch, ch == qb)
                attend(kb16[0][0:64, ch * 128:(ch + 1) * 128], 128, 1, mk, mkk, v16[0][:, ch, :], 65, ch == 0, ch == qb, ktk="kb16", vk="v16", lowp=True)
            finish_branch(g, 1, False)
            lo = max(0, qb - 4)
            for ch in range(lo, qb + 1):
                if ch == qb:
                    mk, mkk = tri16p, "ncst16"
                elif ch == qb - 4:
                    mk, mkk = ntri16p, "ntri16"
                else:
                    mk, mkk = None, None
                attend(kb16[1][0:64, ch * 128:(ch + 1) * 128], 128, 2, mk, mkk, v16[1][:, ch, :], 65, ch == lo, ch == qb, ktk="kb16", vk="v16", lowp=True)
            finish_branch(g, 2, False)
            for c2 in range(2):
                A("pe", lambda e: e.transpose(out=ps[3][:, 0:128], in_=ob[:, 2 * c2:2 * c2 + 2, :].rearrange("p h d -> p (h d)"), identity=ident_sb),
                  reads=["ob", "ncst"], writes=["ps3"])
                A("act", lambda e: e.copy(out=ybT[:, c2, :], in_=ps[3][:, 0:128]), reads=["ps3"], writes=["ybT"])
            Dm(yb_fm[:, 2 * g:2 * g + 2, q0:q0 + 128], ybT, reads=["ybT"], writes=["dram_ybfm"])

    if DO_SAMPLE:
        arena_gate()
        off[0] = 0
        PAST = 16384
        NPG = NPG_DBG
        GK = GK + ["xt0", "xt1"]
        xoff = [0, 0]

        def cvx(i, ncols):
            v = xt[i][:].rearrange("p c n -> p (c n)")[:, xoff[i]:xoff[i] + ncols]
            xoff[i] += ncols
            assert xoff[i] <= DC * NT
            return v
        xTs = cv(PAST + 4)
        regB = cv(8192)
        w1pad = regB.rearrange("p (g j e) -> p g j e", g=2, j=32)
        pg = [cv(128) for _ in range(3)]
        vpg = [cv(2 * 65).rearrange("p (g f) -> p g f", g=2) for _ in range(2)]
        ptf, idxf = cv(NS * 128), cv(NS * 128)
        idx_i = [cv(NS * 128).bitcast(I32) for _ in range(2)]
        pt_i = cv(NS * 128).bitcast(I32)
        pcol_sb, hsel_sb = cv(1), cv(4)
        tri16, ntri16, ident_s, ones_s = cv(16), cv(16), cv(128), cv(512)
        onesg = cv(2)
        jidx2, f02 = cv(264), cv(264)
        qg = [cv(16) for _ in range(2)]
        qpad = [cv(16) for _ in range(2)]
        qsqs, qn_s = cv(16), cv(16)
        mrow_s = cv(6 * 16).rearrange("p (b n) -> p b n", b=6)
        kmx_s = cv(8)
        es2 = [cv(16), cv(16)]
        msk2 = [cv(16), cv(16)]
        gat_s = [cv(3), cv(3)]
        oacc = [cv(64), cv(64)]
        o322 = cv(322)
        imp2, Am2, Fm2, F2m2, NFm2, nfm2, nf2m2 = (cvx(1, 264) for _ in range(7))
        selm2 = cv(264)
        t16b, oh16b, idx16_s = cv(16), cv(16), cv(16)
        cur2, curm2, nF2, thr2, rc2, zero1 = (cv(1) for _ in range(6))
        selT2 = [cv(2 * 16).rearrange("p (k q) -> p k q", k=2) for _ in range(2)]
        vnew = [cv(2 * 65).rearrange("p (g f) -> p g f", g=2) for _ in range(2)]
        wv = cv(4 * 2 * 65).rearrange("p (c g f) -> p c g f", c=4, g=2)
        wk = cv(4 * 256).rearrange("p (c f) -> p c f", c=4)
        kTw = cv(516)
        obT = cv(16)
        sqt = cvx(0, 512)
        hx2, ht2, hs2 = cvx(0, 512), cvx(0, 512), cvx(0, 512)
        pe2, w2b = cv(1), cv(64)
        peT2 = cv(64).rearrange("p (k j) -> p k j", k=2)
        assert off[0] <= 135168, off[0]
        vcs = [stage[g_][:, 0:8 * 322].rearrange("p (c f) -> p c f", c=8) for g_ in range(2)]
        kcs = hT[:].rearrange("p f n -> p (f n)").bitcast(F32)[:, 0:2048].rearrange("p (g n) -> p g n", g=2)
        SK = ["stage0", "stage1", "hT"]

        def Gq(out, cache, half, col, reads=(), writes=()):
            return P.dma("pool", out, cache, reads=list(reads) + GK, writes=writes, gather_idx=idx_i[half][:, col:col + 1])

        for dst_, src_ in ((tri16, tri16_d), (ident_s, ident_d), (pcol_sb, pcol_d), (hsel_sb[0:16, :], hsel_d), (jidx2, jidx2_d), (idx16_s, idx16_d)):
            Dm(dst_, src_, writes=["scst"])
        A("pool", lambda e: e.memset(ones_s, 1.0), writes=["scst"])
        A("pool", lambda e: e.memset(zero1, 0.0), writes=["scst"])
        A("pool", lambda e: e.memset(onesg, 0.0), writes=["onesg"])
        A("pool", lambda e: e.memset(onesg[0:64, 0:1], 1.0), reads=["onesg"], writes=["onesg"])
        A("pool", lambda e: e.memset(onesg[64:128, 1:2], 1.0), reads=["onesg"], writes=["onesg"])
        A("dve", lambda e: e.tensor_scalar(out=ntri16, in0=tri16, scalar1=-1.0, scalar2=1.0, op0=ALU.mult, op1=ALU.add), reads=["scst"], writes=["scst2"])
        A("dve", lambda e: e.tensor_scalar(out=f02, in0=jidx2, scalar1=0.0, scalar2=None, op0=ALU.is_equal), reads=["scst"], writes=["scst2"])
        for g in range(2):
            A("pool", lambda e: e.memset(vcs[g][:, :, 64:65], 1.0), reads=SK, writes=[("vcs", g)])
            Dm(vcs[g][:, :, 65:322], bimps_d, reads=SK, writes=[("vcs", g)])
            A("pool", lambda e: e.memset(vpg[g][:, :, 64:65], 1.0), writes=[("vpg", g)])
            A("pool", lambda e: e.memset(vnew[g][:, :, 64:65], 1.0), writes=["vnew"])
            A("pool", lambda e: e.memset(qpad[g], 0.0), writes=["qpad"])
        A("pool", lambda e: e.memset(wv[:, :, :, 64:65], 1.0), writes=["wv"])
        Dm(pt_i, ptab.partition_broadcast(128), writes=["pt"])
        A("dve", lambda e: e.tensor_copy(out=ptf, in_=pt_i), reads=["pt"], writes=["ptf"])
        A("dve", lambda e: e.tensor_scalar(out=idxf, in0=ptf, scalar1=128.0, scalar2=pcol_sb[:, 0:1], op0=ALU.mult, op1=ALU.add), reads=["ptf", "scst"], writes=["idxf"])
        A("dve", lambda e: e.tensor_scalar(out=idxf, in0=idxf, scalar1=2.0, scalar2=None, op0=ALU.mult), reads=["idxf"], writes=["idxf"])
        A("dve", lambda e: e.tensor_copy(out=idx_i[0], in_=idxf), reads=["idxf"], writes=["idx"])
        A("dve", lambda e: e.tensor_scalar(out=ptf, in0=idxf, scalar1=1.0, scalar2=None, op0=ALU.add), reads=["idxf", "ptf"], writes=["ptf"])
        A("dve", lambda e: e.tensor_copy(out=idx_i[1], in_=ptf), reads=["ptf"], writes=["idx"])
        Dm(peT2[0:64], peT_d, writes=["peT2"])

        pcnt = [0]

        def gather_T(cache, s_, j, half, dst, dst_cols, dkey):
            i = pcnt[0] % 3
            pcnt[0] += 1
            Gq(pg[i], cache, half, s_ * 128 + j, reads=["idx"], writes=[("pg", i)])
            pb_ = ps[6 + i % 2]
            pbk = "ps%d" % (6 + i % 2)
            A("pe", lambda e: e.transpose(out=pb_[:, 0:128], in_=pg[i], identity=ident_s), reads=[("pg", i), "scst"], writes=[pbk])
            if i % 2 == 0:
                A("act", lambda e: e.copy(out=dst[:, dst_cols], in_=pb_[:, 0:128]), reads=[pbk], writes=[dkey])
            else:
                A("dve", lambda e: e.tensor_copy(out=dst[:, dst_cols], in_=pb_[:, 0:128]), reads=[pbk], writes=[dkey])

        def gelu_to(hx_, n_):
            A("dve", lambda e: e.tensor_tensor(out=ht2[:, 0:n_], in0=hx_, in1=hx_, op=ALU.mult), reads=["hx2"], writes=["ht2"])
            A("dve", lambda e: e.tensor_scalar(out=ht2[:, 0:n_], in0=ht2[:, 0:n_], scalar1=0.044715, scalar2=1.0, op0=ALU.mult, op1=ALU.add), reads=["ht2"], writes=["ht2"])
            A("dve", lambda e: e.tensor_tensor(out=ht2[:, 0:n_], in0=ht2[:, 0:n_], in1=hx_, op=ALU.mult), reads=["ht2", "hx2"], writes=["ht2"])
            A("act", lambda e: e.activation(out=hs2[:, 0:n_], in_=ht2[:, 0:n_], func=AF.Sigmoid, scale=1.5957691216057308), reads=["ht2"], writes=["hs2"])
            A("dve", lambda e: e.tensor_tensor(out=hx_, in0=hx_, in1=hs2[:, 0:n_], op=ALU.mult), reads=["hx2", "hs2"], writes=["hx2"])

        def key_max_s(kt, K_, ncols, slot, ktk, lhs1):
            nchunk = (ncols + 511) // 512
            for c in range(nchunk):
                w_ = min(512, ncols - c * 512)
                A("act", lambda e: e.activation(out=sqt[0:K_, 0:w_], in_=kt[0:K_, c * 512:c * 512 + w_], func=AF.Square), reads=[ktk], writes=["sqt"])
                A("pe", lambda e: e.matmul(ps[3][0:1, 0:w_], lhsT=lhs1, rhs=sqt[0:K_, 0:w_], start=True, stop=True), reads=["sqt", "scst", "onesg"], writes=["ps3"])
                if c == 0:
                    A("dve", lambda e: e.tensor_reduce(out=kmx_s[0:1, slot:slot + 1], in_=ps[3][0:1, 0:w_], axis=AX.X, op=ALU.max), reads=["ps3"], writes=["kmxs"])
                else:
                    A("dve", lambda e: e.tensor_reduce(out=kmx_s[0:1, 7:8], in_=ps[3][0:1, 0:w_], axis=AX.X, op=ALU.max), reads=["ps3"], writes=["kmxs"])
                    A("dve", lambda e: e.tensor_tensor(out=kmx_s[0:1, slot:slot + 1], in0=kmx_s[0:1, slot:slot + 1], in1=kmx_s[0:1, 7:8], op=ALU.max),
                      reads=["kmxs"], writes=["kmxs"])
            A("act", lambda e: e.sqrt(out=kmx_s[0:1, slot:slot + 1], in_=kmx_s[0:1, slot:slot + 1]), reads=["kmxs"], writes=["kmxs"])
            A("dve", lambda e: e.tensor_scalar(out=mrow_s[0:1, slot, :], in0=qn_s[0:1, :] if False else mrow_s[0:1, slot, :], scalar1=1.0, scalar2=None, op0=ALU.mult),
              reads=["mrows"], writes=["mrows"]) if False else None

        def set_mrow(slot, g):
            A("act", lambda e: e.activation(out=qsqs[0:64, :], in_=qg[g][0:64, :], func=AF.Square), reads=["qg"], writes=["qsqs"])
            A("pe", lambda e: e.matmul(ps[3][0:1, 0:16], lhsT=ones_s[0:64, 0:1], rhs=qsqs[0:64, :], start=True, stop=True), reads=["qsqs", "scst"], writes=["ps3"])
            A("act", lambda e: e.sqrt(out=qn_s[0:1, :], in_=ps[3][0:1, 0:16]), reads=["ps3"], writes=["qn"])
            A("dve", lambda e: e.tensor_scalar(out=mrow_s[0:1, slot, :], in0=qn_s[0:1, :], scalar1=kmx_s[0:1, slot:slot + 1], scalar2=-1.0, op0=ALU.mult, op1=ALU.mult),
              reads=["qn", "kmxs"], writes=["mrows"])

        e2cnt = [0]

        def attend_s(kt_chunk, K_, nk, qtile, slot, mask_ap, mask_key, v_chunk, W, g, first, last, ktk, vk):
            i = e2cnt[0] % 2
            e2cnt[0] += 1
            sc, sk = ps[i], "ps%d" % i
            es_, ek = es2[i], "es%d" % i
            A("pe", lambda e: e.matmul(sc[0:nk, 0:16], lhsT=kt_chunk, rhs=qtile[0:K_, :], start=True, stop=False), reads=[ktk, "qg", "qpad"], writes=[sk], skip_own=True)
            A("pe", lambda e: e.matmul(sc[0:nk, 0:16], lhsT=ones_s[0:1, 0:nk], rhs=mrow_s[0:1, slot, :], start=False, stop=True), reads=["mrows", "scst"], writes=[sk], skip_own=True)
            A("act", lambda e: e.activation(out=es_[0:nk, :], in_=sc[0:nk, 0:16], func=AF.Exp), reads=[sk], writes=[ek])
            if mask_ap is not None:
                A("dve", lambda e: e.tensor_tensor(out=es_[0:nk, :], in0=es_[0:nk, :], in1=mask_ap, op=ALU.mult), reads=[ek, mask_key], writes=[ek])
            A("pe", lambda e: e.matmul(ps[4 + g][0:16, 0:W], lhsT=es_[0:nk, :], rhs=v_chunk, start=first, stop=last), reads=[ek, vk], writes=["ps%d" % (4 + g)], skip_own=True)

        def finish_s(g, br, first_branch):
            o = ps[4 + g]
            ok = "ps%d" % (4 + g)
            A("dve", lambda e: e.tensor_scalar(out=rc2[0:16], in0=o[0:16, 64:65], scalar1=1e-30, scalar2=None, op0=ALU.max), reads=[ok], writes=["rc2"])
            A("dve", lambda e: e.reciprocal(out=rc2[0:16], in_=rc2[0:16]), reads=["rc2"], writes=["rc2"])
            if br == 0:
                A("dve", lambda e: e.tensor_scalar(out=o322[0:16, 0:257], in0=o[0:16, 65:322], scalar1=rc2[0:16, 0:1], scalar2=None, op0=ALU.mult), reads=[ok, "rc2"], writes=["o322"])
                A("pe", lambda e: e.matmul(ps[3][0:4, 0:257], lhsT=hsel_sb[0:16, 0:4], rhs=o322[0:16, 0:257], start=True, stop=True), reads=["o322", "scst"], writes=["ps3"])
                A("act", lambda e: e.copy(out=imp2[0:4, 0:257], in_=ps[3][0:4, 0:257]), reads=["ps3"], writes=["imp2"])
            A("dve", lambda e: e.tensor_tensor(out=rc2[0:16], in0=rc2[0:16], in1=gat_s[g][0:16, br:br + 1], op=ALU.mult), reads=["rc2", "gats"], writes=["rc2"])
            if first_branch:
                A("dve", lambda e: e.tensor_scalar(out=oacc[g][0:16, :], in0=o[0:16, 0:64], scalar1=rc2[0:16, 0:1], scalar2=None, op0=ALU.mult), reads=[ok, "rc2"], writes=[("oacc", g)])
            else:
                A("dve", lambda e: e.scalar_tensor_tensor(out=oacc[g][0:16, :], in0=o[0:16, 0:64], scalar=rc2[0:16, 0:1], in1=oacc[g][0:16, :], op0=ALU.mult, op1=ALU.add),
                  reads=[ok, "rc2", ("oacc", g)], writes=[("oacc", g)])

        def select_s(g):
            R4 = slice(0, 4)
            NB = 257
            v = lambda t_: t_[R4, 0:NB]
            A("dve", lambda e: e.tensor_scalar(out=cur2[R4], in0=zero1[R4], scalar1=256.0, scalar2=None, op0=ALU.add), reads=["scst"], writes=["cur2"])
            A("dve", lambda e: e.tensor_scalar(out=curm2[R4], in0=zero1[R4], scalar1=255.0, scalar2=None, op0=ALU.add), reads=["scst"], writes=["curm2"])
            A("dve", lambda e: e.tensor_scalar(out=v(Am2), in0=v(jidx2), scalar1=cur2[R4, 0:1], scalar2=None, op0=ALU.is_le), reads=["cur2", "scst"], writes=["Am2"])
            A("dve", lambda e: e.tensor_scalar(out=v(Fm2), in0=v(jidx2), scalar1=cur2[R4, 0:1], scalar2=None, op0=ALU.is_equal), reads=["cur2", "scst"], writes=["Fm2"])
            A("dve", lambda e: e.tensor_scalar(out=v(F2m2), in0=v(jidx2), scalar1=curm2[R4, 0:1], scalar2=None, op0=ALU.is_equal), reads=["curm2", "scst"], writes=["F2m2"])
            A("dve", lambda e: e.tensor_tensor(out=v(Fm2), in0=v(Fm2), in1=v(F2m2), op=ALU.max), reads=["Fm2", "F2m2"], writes=["Fm2"])
            A("dve", lambda e: e.tensor_tensor(out=v(Fm2), in0=v(Fm2), in1=v(f02), op=ALU.max), reads=["Fm2", "scst2"], writes=["Fm2"])
            A("dve", lambda e: e.tensor_tensor(out=v(NFm2), in0=v(Am2), in1=v(Fm2), op=ALU.subtract), reads=["Am2", "Fm2"], writes=["NFm2"])
            A("dve", lambda e: e.scalar_tensor_tensor(out=v(nfm2), in0=v(imp2), scalar=1.0, in1=v(NFm2), op0=ALU.add, op1=ALU.mult), reads=["imp2", "NFm2"], writes=["nfm2"])
            A("dve", lambda e: e.tensor_scalar(out=v(nfm2), in0=v(nfm2), scalar1=-1.0, scalar2=None, op0=ALU.add), reads=["nfm2"], writes=["nfm2"])
            A("dve", lambda e: e.tensor_reduce(out=nF2[R4], in_=v(Fm2), axis=AX.X, op=ALU.add), reads=["Fm2"], writes=["nF2"])
            A("dve", lambda e: e.tensor_scalar(out=nF2[R4], in0=nF2[R4], scalar1=-1.0, scalar2=15.0, op0=ALU.mult, op1=ALU.add), reads=["nF2"], writes=["nF2"])
            A("dve", lambda e: e.tensor_scalar(out=oh16b[R4], in0=idx16_s[R4], scalar1=nF2[R4, 0:1], scalar2=None, op0=ALU.is_equal), reads=["nF2", "scst"], writes=["oh16b"])
            A("dve", lambda e: e.max(out=t16b[R4, 0:8], in_=v(nfm2)), reads=["nfm2"], writes=["t16b"])
            A("dve", lambda e: e.match_replace(out=v(nf2m2), in_to_replace=t16b[R4, 0:8], in_values=v(nfm2), imm_value=-2.0), reads=["nfm2", "t16b"], writes=["nf2m2"])
            A("dve", lambda e: e.max(out=t16b[R4, 8:16], in_=v(nf2m2)), reads=["nf2m2"], writes=["t16b"])
            A("dve", lambda e: e.tensor_tensor(out=t16b[R4], in0=t16b[R4], in1=oh16b[R4], op=ALU.mult), reads=["t16b", "oh16b"], writes=["t16b"])
            A("dve", lambda e: e.tensor_reduce(out=thr2[R4], in_=t16b[R4], axis=AX.X, op=ALU.add), reads=["t16b"], writes=["thr2"])
            A("dve", lambda e: e.tensor_scalar(out=v(selm2), in0=v(nfm2), scalar1=thr2[R4, 0:1], scalar2=None, op0=ALU.is_ge), reads=["nfm2", "thr2"], writes=["selm2"])
            A("dve", lambda e: e.tensor_tensor(out=v(selm2), in0=v(selm2), in1=v(NFm2), op=ALU.mult), reads=["selm2", "NFm2"], writes=["selm2"])
            A("dve", lambda e: e.tensor_tensor(out=v(selm2), in0=v(selm2), in1=v(Fm2), op=ALU.add), reads=["selm2", "Fm2"], writes=["selm2"])
            for kch in range(2):
                A("pe", lambda e: e.transpose(out=ps[3][0:128, 0:4], in_=selm2[R4, kch * 128:kch * 128 + 128], identity=ident_s[0:4, 0:4]), reads=["selm2", "scst"], writes=["ps3"])
                for h in range(4):
                    A("act", lambda e: e.copy(out=selT2[g][:, kch, h * 4:(h + 1) * 4], in_=ps[3][0:128, 0:4]), reads=["ps3"], writes=[("selT2", g)])

        def dbg_dump(name, ap, key):
            if DEBUG:
                t = dout("dbg_" + name, list(ap.shape))
                Dm(t, ap, reads=[key], is_output=True)

        for s_ in range(NS_DBG):
            col_s = SEQ + s_ * ST
            for g in range(2):
                Dm(qg[g][0:64, :].rearrange("p (h q) -> p h q", h=4), nsaT[:, 4 * g:4 * g + 4, col_s:col_s + ST], reads=["dram_nsaT"], writes=["qg"])
                Dm(qpad[g][g * 64:(g + 1) * 64, :].rearrange("p (h q) -> p h q", h=4), nsaT[:, 4 * g:4 * g + 4, col_s:col_s + ST], reads=["dram_nsaT"], writes=["qpad"])
                for h in range(4):
                    Dm(gat_s[g][h * 4:(h + 1) * 4, :], gates_tm[col_s:col_s + ST, (4 * g + h) * 3:(4 * g + h) * 3 + 3], reads=["dram_gates"], writes=["gats"])
            for kvi in range(2):
                for g in range(2):
                    A("pool", lambda e: e.memset(w1pad[:, g], 0.0), writes=["regB"])
                    Dm(w1pad[g * 64:(g + 1) * 64, g], phi_w1[kvi].rearrange("j d e -> d j e"), reads=["regB"], writes=["regB"])
                Dm(w2b, phi_w2[kvi], writes=["w2b"])
                for j in range(32):
                    A("pe", lambda e: e.matmul(ps[3][:, 0:1], lhsT=w1pad[0:64, 0, j, :], rhs=peT2[0:64, kvi, j:j + 1], start=(j == 0), stop=(j == 31)),
                      reads=["regB", "peT2"], writes=["ps3"], skip_own=True)
                A("act", lambda e: e.copy(out=pe2, in_=ps[3][:, 0:1]), reads=["ps3"], writes=["pe2"])
                for j in range(NPG):
                    gather_T(cache_cmp, s_, j, kvi, xTs, slice(j * 128, (j + 1) * 128), "xTs")
                for g in range(2):
                    for (n0, ncol) in ((0, 512), (512, 511)):
                        for j in range(32):
                            A("pe", lambda e: e.matmul(ps[2][:, 0:ncol], lhsT=w1pad[:, g, j, :], rhs=xTs[:, 16 * n0 + j:16 * n0 + j + 16 * (ncol - 1) + 1:16],
                                                       start=(j == 0), stop=(j == 31)), reads=["regB", "xTs"], writes=["ps2"], skip_own=True)
                        A("act", lambda e: e.activation(out=hx2[:, 0:ncol], in_=ps[2][:, 0:ncol], func=AF.Identity, bias=pe2[:, 0:1], scale=1.0), reads=["ps2", "pe2"], writes=["hx2"])
                        gelu_to(hx2[:, 0:ncol], ncol)
                        if kvi == 0:
                            A("pe", lambda e: e.matmul(ps[3][0:64, 0:ncol], lhsT=w2b[:, 0:64], rhs=hx2[:, 0:ncol], start=True, stop=True), reads=["w2b", "hx2"], writes=["ps3"])
                            A("act", lambda e: e.copy(out=kcs[0:64, g, n0:n0 + ncol], in_=ps[3][0:64, 0:ncol]), reads=["ps3"] + SK, writes=[("kcs", g)])
                        else:
                            for c4 in range(4):
                                nk = min(128, ncol - c4 * 128)
                                A("pe", lambda e: e.matmul(ps[3][0:nk, 0:64], lhsT=hx2[:, c4 * 128:c4 * 128 + nk], rhs=w2b[:, 0:64], start=True, stop=True), reads=["w2b", "hx2"], writes=["ps3"])
                                A("act", lambda e: e.copy(out=vcs[g][0:nk, n0 // 128 + c4, 0:64], in_=ps[3][0:nk, 0:64]), reads=["ps3"] + SK, writes=[("vcs", g)])
            for g in range(2):
                key_max_s(kcs[:, g, :], 64, 1023, g, ("kcs", g), ones_s[0:64, 0:1])
                set_mrow(g, g)
                for ch in range(8):
                    nk = 128 if ch < 7 else 127
                    attend_s(kcs[0:64, g, ch * 128:ch * 128 + nk], 64, nk, qg[g], g, None, None, vcs[g][0:nk, ch, :], 322, g, ch == 0, ch == 7, ("kcs", g), ("vcs", g))
                finish_s(g, 0, True)
                select_s(g)
            Dm(regB, e2_d, reads=["regB"], writes=["regB"])
            for j in range(NPG):
                gather_T(cache_sel, s_, j, 0, xTs, slice(j * 128, (j + 1) * 128), "xTs")
            for g in range(2):
                Dm(xTs[g * 64:(g + 1) * 64, PAST:PAST + ST], nsaT[:, 10 + g, col_s:col_s + ST], reads=["dram_nsaT"], writes=["xTs"])
            Dm(vnew[0][0:ST, :, 0:64], kv_tm[col_s:col_s + ST, 384:512].rearrange("r (g f) -> r g f", g=2), reads=["dram_kvtm"], writes=["vnew"])
            Dm(vnew[1][0:ST, :, 0:64], kv_tm[col_s:col_s + ST, 640:768].rearrange("r (g f) -> r g f", g=2), reads=["dram_kvtm"], writes=["vnew"])
            for g in range(2):
                key_max_s(xTs, 128, PAST + ST, 2 + g, "xTs", onesg[:, g:g + 1])
                set_mrow(2 + g, g)
            for j in range(NPG):
                i = pcnt[0] % 3
                pcnt[0] += 1
                vi = j % 2
                Gq(pg[i], cache_sel, 1, s_ * 128 + j, reads=["idx"], writes=[("pg", i)])
                A("act", lambda e: e.copy(out=vpg[vi][:, :, 0:64], in_=pg[i].rearrange("p (g f) -> p g f", g=2)), reads=[("pg", i)], writes=[("vpg", vi)])
                kch, cl = j // 64, j % 64
                for g in range(2):
                    mi = e2cnt[0] % 2
                    A("pe", lambda e: e.matmul(ps[2][:, 0:16], lhsT=regB[:, cl * 128:(cl + 1) * 128], rhs=selT2[g][:, kch, :], start=True, stop=True),
                      reads=["regB", ("selT2", g)], writes=["ps2"])
                    A("dve", lambda e: e.tensor_copy(out=msk2[mi], in_=ps[2][:, 0:16]), reads=["ps2"], writes=[("msk2", mi)])
                    attend_s(xTs[:, j * 128:(j + 1) * 128], 128, 128, qpad[g], 2 + g, msk2[mi], ("msk2", mi), vpg[vi][:, g, :], 65, g, j == 0, False, "xTs", ("vpg", vi))
            if s_ == 0:
                dbg_dump("selm2", selm2[0:4, 0:257], "selm2")
                dbg_dump("imp2", imp2[0:4, 0:257], "imp2")
                dbg_dump("kmx", kmx_s[0:1, :], "kmxs")
                dbg_dump("mrow", mrow_s[0:1, :, :], "mrows")
                dbg_dump("selT2", selT2[1][:, :, :], ("selT2", 1))
                dbg_dump("msk", msk2[0][:, :], ("msk2", 0))
                dbg_dump("xTs", xTs[:, PAST - 256:PAST + ST], "xTs")
            for g in range(2):
                attend_s(xTs[:, PAST:PAST + ST], 128, ST, qpad[g], 2 + g, tri16[0:ST, :], "scst", vnew[0][0:ST, g, :], 65, g, NPG == 0, True, "xTs", "vnew")
                finish_s(g, 1, False)
            Dm(wk, win_state[s_].rearrange("(c p) f -> p c f", p=128), writes=["wk"])
            for c4 in range(4):
                A("act", lambda e: e.copy(out=wv[:, c4, :, 0:64], in_=wk[:, c4, 128:256].rearrange("p (g f) -> p g f", g=2)), reads=["wk"], writes=["wv"])
                pb_ = ps[6 + c4 % 2]
                pbk = "ps%d" % (6 + c4 % 2)
                A("pe", lambda e: e.transpose(out=pb_[:, 0:128], in_=wk[:, c4, 0:128], identity=ident_s), reads=["wk", "scst"], writes=[pbk])
                A("dve", lambda e: e.tensor_copy(out=kTw[:, c4 * 128:(c4 + 1) * 128], in_=pb_[:, 0:128]), reads=[pbk], writes=["kTw"])
            for g in range(2):
                Dm(kTw[g * 64:(g + 1) * 64, 512:512 + ST], nsaT[:, 12 + g, col_s:col_s + ST], reads=["dram_nsaT"], writes=["kTw"])
            for g in range(2):
                key_max_s(kTw, 128, 512 + ST, 4 + g, "kTw", onesg[:, g:g + 1])
                set_mrow(4 + g, g)
                for c4 in range(4):
                    attend_s(kTw[:, c4 * 128:(c4 + 1) * 128], 128, 128, qpad[g], 4 + g, ntri16 if c4 == 0 else None, "scst2", wv[:, c4, g, :], 65, g, c4 == 0, False, "kTw", "wv")
                attend_s(kTw[:, 512:512 + ST], 128, ST, qpad[g], 4 + g, tri16[0:ST, :], "scst", vnew[1][0:ST, g, :], 65, g, False, True, "kTw", "vnew")
                finish_s(g, 2, False)
                A("pe", lambda e: e.transpose(out=ps[3][0:64, 0:16], in_=oacc[g][0:16, :], identity=ident_s[0:16, 0:16]), reads=[("oacc", g), "scst"], writes=["ps3"])
                A("act", lambda e: e.copy(out=obT[0:64, :], in_=ps[3][0:64, 0:16]), reads=["ps3"], writes=["obT"])
                for h in range(4):
                    hh = 4 * g + h
                    Dm(yb_fm[(hh % 2) * 64:(hh % 2) * 64 + 64, hh // 2, col_s:col_s + ST], obT[0:64, h * 4:(h + 1) * 4], reads=["obT"], writes=["dram_ybfm"])

    arena_gate()
    wm = arena[:, 0:DC * 2048].rearrange("p (c f) -> p c f", c=DC)
    woa = arena[:, 16384:16384 + 4096].rearrange("p (c f) -> p c f", c=4)
    wob = arena[:, 20480:20480 + 4096].rearrange("p (c f) -> p c f", c=4)
    wo = arena[:, 24576:24576 + 8192].rearrange("p (c f) -> p c f", c=DC)
    for c in range(DC):
        load_weight_bf16(wm[:, c, :], w_in[c * 128:(c + 1) * 128, 3096:5144], 2048, ("wm", c))
        load_weight_bf16(wo[:, c, :], w_o[c * 128:(c + 1) * 128, :], 1024, ("wo", c))
    for c in range(4):
        load_weight_bf16(woa[:, c, :], w_out_a[c * 128:(c + 1) * 128, :], 1024, ("woa", c))
        load_weight_bf16(wob[:, c, :], w_out_b[c * 128:(c + 1) * 128, :], 1024, ("wob", c))
    off[0] = 32768 * 2
    yt_, gt_, bt_, ybt_ = (cv(4 * NT).rearrange("p (c n) -> p c n", c=4) for _ in range(4))
    yab = arena[:, off[0] // 2:off[0] // 2 + 4 * NT].rearrange("p (c n) -> p c n", c=4)
    ybb = arena[:, off[0] // 2 + 4 * NT:off[0] // 2 + 8 * NT].rearrange("p (c n) -> p c n", c=4)
    mmb = arena[:, off[0] // 2 + 8 * NT:off[0] // 2 + 16 * NT].rearrange("p (c n) -> p c n", c=8)
    off[0] += 16 * NT * 2
    gn1, gn2, gn3, gab, gbb = (cv(NT) for _ in range(5))
    x2_v = x2T.rearrange("(c p) n -> p c n", p=128)
    for ti, (c0, n, segs) in enumerate(tl):
        x = xt[ti % 2]
        xk = "xt%d" % (ti % 2)
        P.dma("sp", x[:, :, 0:n], x1_v[:, :, c0:c0 + n], reads=["dram_x1T"], writes=[xk])
        Dm(yt_[:, :, 0:n], y_fm[:, :, c0:c0 + n], reads=["dram_yfm"], writes=["yt"])
        Dm(gt_[:, :, 0:n], scr["g"][:, :, c0:c0 + n], reads=["dram_scr"], writes=["gt"])
        Dm(bt_[:, :, 0:n], scr["bonus"][:, :, c0:c0 + n], reads=["dram_scr"], writes=["bt"])
        Dm(ybt_[:, :, 0:n], yb_fm[:, :, c0:c0 + n], reads=["dram_ybfm"], writes=["ybt"])
        modulate(x, xk, n, segs, 3, ub, "ub")
        P.op("pool", lambda e: e.tensor_scalar(out=x[:, :, 0:n], in0=x[:, :, 0:n], scalar1=ALPHA, scalar2=None, op0=ALU.mult),
             reads=[xk], writes=[xk])
        for j in range(4):
            yj = yt_[:, j, 0:n]
            A("act", lambda e: e.activation(out=gn1[:, 0:n], in_=yj, func=AF.Square), reads=["yt"], writes=["gn1"])
            A("pe", lambda e: e.matmul(ps[0][:, 0:n], lhsT=blk1[:], rhs=yj, start=True, stop=True), reads=["yt", "rwc"], writes=["ps0"])
            A("pe", lambda e: e.matmul(ps[1][:, 0:n], lhsT=blk1[:], rhs=gn1[:, 0:n], start=True, stop=True), reads=["gn1", "rwc"], writes=["ps1"])
            A("act", lambda e: e.mul(out=gn2[:, 0:n], in_=ps[0][:, 0:n], mul=1.0 / 64), reads=["ps0"], writes=["gn2"])
            A("dve", lambda e: e.tensor_tensor(out=gn3[:, 0:n], in0=gn2[:, 0:n], in1=gn2[:, 0:n], op=ALU.mult), reads=["gn2"], writes=["gn3"])
            A("dve", lambda e: e.scalar_tensor_tensor(out=gn3[:, 0:n], in0=ps[1][:, 0:n], scalar=1.0 / 64, in1=gn3[:, 0:n], op0=ALU.mult, op1=ALU.subtract),
              reads=["ps1", "gn3"], writes=["gn3"])
            A("dve", lambda e: e.tensor_scalar(out=gn3[:, 0:n], in0=gn3[:, 0:n], scalar1=64e-5, scalar2=None, op0=ALU.add), reads=["gn3"], writes=["gn3"])
            A("act", lambda e: e.sqrt(out=gn3[:, 0:n], in_=gn3[:, 0:n]), reads=["gn3"], writes=["gn3"])
            A("dve", lambda e: e.reciprocal(out=gn3[:, 0:n], in_=gn3[:, 0:n]), reads=["gn3"], writes=["gn3"])
            A("dve", lambda e: e.tensor_tensor(out=gn1[:, 0:n], in0=yj, in1=gn2[:, 0:n], op=ALU.subtract), reads=["yt", "gn2", "gn1"], writes=["gn1"])
            A("dve", lambda e: e.tensor_tensor(out=gn1[:, 0:n], in0=gn1[:, 0:n], in1=gn3[:, 0:n], op=ALU.mult), reads=["gn1", "gn3"], writes=["gn1"])
            A("act", lambda e: e.activation(out=gn1[:, 0:n], in_=gn1[:, 0:n], func=AF.Identity, scale=rwq_sb[:, j, 5:6], bias=rwq_sb[:, j, 6:7]),
              reads=["gn1", "rwc"], writes=["gn1"])
            A("dve", lambda e: e.tensor_tensor(out=gn1[:, 0:n], in0=gn1[:, 0:n], in1=bt_[:, j, 0:n], op=ALU.add), reads=["gn1", "bt"], writes=["gn1"])
            A("dve", lambda e: e.tensor_tensor(out=yab[:, j, 0:n], in0=gn1[:, 0:n], in1=gt_[:, j, 0:n], op=ALU.mult), reads=["gn1", "gt"], writes=["yab"])
            A("act", lambda e: e.copy(out=ybb[:, j, 0:n], in_=ybt_[:, j, 0:n]), reads=["ybt"], writes=["ybb"])
        for m in range(DC):
            mc = slice(m * 128, (m + 1) * 128)
            for j in range(4):
                A("pe", lambda e: e.matmul(ps[0][:, 0:n], lhsT=woa[:, j, mc], rhs=yab[:, j, 0:n], start=(j == 0), stop=(j == 3)),
                  reads=["yab", ("woa", j), "arena"], writes=["ps0"], skip_own=True)
            for j in range(4):
                A("pe", lambda e: e.matmul(ps[1][:, 0:n], lhsT=wob[:, j, mc], rhs=ybb[:, j, 0:n], start=(j == 0), stop=(j == 3)),
                  reads=["ybb", ("wob", j), "arena"], writes=["ps1"], skip_own=True)
            for c in range(DC):
                A("pe", lambda e: e.matmul(ps[2][:, 0:n], lhsT=wm[:, c, mc], rhs=ub[:, c, 0:n], start=(c == 0), stop=(c == DC - 1)),
                  reads=["ub", ("wm", c), "arena"], writes=["ps2"], skip_own=True)
            for c in range(DC):
                A("pe", lambda e: e.matmul(ps[3][:, 0:n], lhsT=wm[:, c, 1024 + m * 128:1024 + (m + 1) * 128], rhs=ub[:, c, 0:n], start=(c == 0), stop=(c == DC - 1)),
                  reads=["ub", ("wm", c), "arena"], writes=["ps3"], skip_own=True)
            A("act", lambda e: e.activation(out=gab[:, 0:n], in_=ps[2][:, 0:n], func=AF.Sigmoid, bias=bm_sb[:, m:m + 1], scale=1.0), reads=["ps2", "rwc"], writes=["gab"])
            A("act", lambda e: e.activation(out=gbb[:, 0:n], in_=ps[3][:, 0:n], func=AF.Sigmoid, bias=bm_sb[:, 8 + m:9 + m], scale=1.0), reads=["ps3", "rwc"], writes=["gbb"])
            A("dve", lambda e: e.tensor_tensor(out=gab[:, 0:n], in0=gab[:, 0:n], in1=ps[0][:, 0:n], op=ALU.mult), reads=["gab", "ps0"], writes=["gab"])
            A("dve", lambda e: e.tensor_tensor(out=gbb[:, 0:n], in0=gbb[:, 0:n], in1=ps[1][:, 0:n], op=ALU.mult), reads=["gbb", "ps1"], writes=["gbb"])
            A("dve", lambda e: e.tensor_tensor(out=mmb[:, m, 0:n], in0=gab[:, 0:n], in1=gbb[:, 0:n], op=ALU.add), reads=["gab", "gbb"], writes=["mmb"])
        for m in range(DC):
            py = ps[4 + (m % 2)]
            ky = "ps%d" % (4 + (m % 2))
            for c in range(DC):
                A("pe", lambda e: e.matmul(py[:, 0:n], lhsT=wo[:, c, m * 128:(m + 1) * 128], rhs=mmb[:, c, 0:n], start=(c == 0), stop=(c == DC - 1)),
                  reads=["mmb", ("wo", c), "arena"], writes=[ky], skip_own=True)
            for (lo, hi, s_) in segs:
                P.op("dve", lambda e: e.scalar_tensor_tensor(out=x[:, m, lo:hi], in0=py[:, lo:hi], scalar=modS[:, s_, 5, m:m + 1], in1=x[:, m, lo:hi],
                                                             op0=ALU.mult, op1=ALU.add), reads=[ky, xk, "modS"], writes=[xk])
        layer_norm_tile(x, xk, n, 1, x, xk)
        P.dma("pool", x2_v[:, :, c0:c0 + n], x[:, :, 0:n], reads=[xk], writes=["dram_x2T"])

    ffn_phase(1, x2T, yT, 2, 6, True)

    P.emit()
    es.close()
    return nc


_NC_CACHE = {}


def kernel(**inp):
    f = lambda a: np.ascontiguousarray(np.asarray(a), dtype=np.float32)
    x_prompt, x_sample = f(inp["x_prompt"]), f(inp["x_sample"])
    c_prompt, c_sample = f(inp["c_prompt"]), f(inp["c_sample"])
    w_ada = f(inp["w_ada"])[0]
    b_ada = f(inp["b_ada"])[0].reshape(72, 128).T.copy()
    ln_g = f(inp["ln_g"])[0].reshape(3, DC, 128).transpose(2, 0, 1).copy()
    ln_b = f(inp["ln_b"])[0].reshape(3, DC, 128).transpose(2, 0, 1).copy()
    w_gate, w_up, w_down = f(inp["ffn_w_gate"])[0], f(inp["ffn_w_up"])[0], f(inp["ffn_w_down"])[0]
    w_in = f(inp["w_in"])[0]
    b_in_bc = np.ascontiguousarray(np.broadcast_to(f(inp["b_in"])[0][None, :], (128, N_IN)))
    state_kv_win = f(inp["state_kv_win"])[0].reshape(32, 512, 256)

    CHT = [(j * 128, 128) for j in range(12)] + [(1536, 64), (1600, 64), (1664, 128)]

    def chunked(vec):
        vec = np.asarray(vec)
        lead = vec.shape[:-1]
        out = np.zeros((128, 15) + lead, np.float32)
        for ci, (c0_, w_) in enumerate(CHT):
            out[:w_, ci] = np.moveaxis(vec[..., c0_:c0_ + w_], -1, 0)
        return out

    b_in = f(inp["b_in"])[0]
    rwp = np.ascontiguousarray(np.stack([chunked(b_in[:RW_SHIFT]), chunked(f(inp["rw_mu"])[0])], axis=-1))
    q7 = [f(inp[k])[0].reshape(512) for k in ("rw_w0", "rw_a0", "rw_k_k", "rw_k_a", "rw_r_k", "rw_ln_w", "rw_ln_b")]
    rwq = np.ascontiguousarray(np.stack([v.reshape(4, 128).T for v in q7], axis=-1))
    rw_w2, rw_a2, rw_g2 = f(inp["rw_w2"])[0], f(inp["rw_a2"])[0], f(inp["rw_g2"])[0]
    state_shift = f(inp["state_shift"])[0]
    state_wkv = f(inp["state_wkv"])[0]
    pidx = np.arange(128)
    blk1 = (pidx[:, None] // 64 == pidx[None, :] // 64).astype(np.float32)
    istk = (pidx[:, None] % 64 == np.arange(64)[None, :]).astype(np.float32)

    nsab_cols = [1792 + 64 * h for h in range(8)] + [2304 + br * 256 + g * 64 for br in range(3) for g in range(2)] \
        + [2304 + 128 + g * 64 for g in range(2)]
    nsab = np.ascontiguousarray(np.stack([b_in[c0_:c0_ + 64] * (0.125 if ci < 8 else 1.0) for ci, c0_ in enumerate(nsab_cols)], axis=1))
    phi_w1, phi_w2 = f(inp["nsa_phi_w1"])[0], f(inp["nsa_phi_w2"])[0]
    peT = np.ascontiguousarray(f(inp["nsa_phi_pe"])[0].transpose(2, 0, 1))
    tri = (pidx[:, None] <= pidx[None, :]).astype(np.float32)
    nn = np.arange(256).reshape(2, 128)
    cvalT = np.ascontiguousarray((16.0 * nn.T[:, :, None] + 31.0 - pidx[None, None, :]).astype(np.float32))
    jidx = np.ascontiguousarray(np.broadcast_to(np.arange(64, dtype=np.float32)[None, :], (128, 64)))
    curb = (pidx[:, None] >= 64).astype(np.float32)
    eall = (np.arange(SEQ)[None, :] // 64 == np.arange(64)[:, None]).astype(np.float32)
    nidx = np.arange(256)[:, None]
    jj = np.arange(64)[None, :]
    bimp_full = ((nidx >= 4 * jj - 1) & (nidx <= 4 * jj + 3) & (nidx < 255)).astype(np.float32)
    bimp = np.ascontiguousarray(bimp_full.reshape(2, 128, 64).transpose(1, 0, 2))
    idx16 = np.ascontiguousarray(np.broadcast_to(np.arange(16, dtype=np.float32)[None, :], (128, 16)))
    ident = np.eye(128, dtype=np.float32)
    bm = np.ascontiguousarray(b_in[3096:5144].reshape(16, 128).T)
    w_out_a, w_out_b, w_o = f(inp["w_out_a"])[0], f(inp["w_out_b"])[0], f(inp["w_o"])[0]

    cache_cmp = f(inp["cache_kv_cmp"])[0].reshape(-1, 128)
    cache_sel = f(inp["cache_kv_sel"])[0].reshape(-1, 128)
    page_table = np.ascontiguousarray(np.asarray(inp["page_table"]), dtype=np.int32)
    n1 = np.arange(1024)[:, None]
    j1 = np.arange(257)[None, :]
    bimps_full = ((n1 >= 4 * j1 - 1) & (n1 <= 4 * j1 + 3) & (n1 < 1023)).astype(np.float32)
    bimps = np.ascontiguousarray(bimps_full.reshape(8, 128, 257).transpose(1, 0, 2))
    bb = np.arange(128)[:, None]
    kk8 = np.arange(8192)[None, :]
    e2 = (bb == 2 * (kk8 // 128) + (kk8 % 128) // 64).astype(np.float32)
    hsel = (np.arange(16)[:, None] % 4 == np.arange(4)[None, :]).astype(np.float32)
    pcol = pidx[:, None].astype(np.float32)
    jidx2 = np.ascontiguousarray(np.broadcast_to(np.arange(264, dtype=np.float32)[None, :], (128, 264)))
    tri16 = (pidx[:, None] <= (np.arange(16)[None, :] % 4)).astype(np.float32)

    if "nc" not in _NC_CACHE:
        _NC_CACHE["nc"] = build()
    nc = _NC_CACHE["nc"]

    in_maps = []
    for i in range(8):
        b = i // 2
        xs = x_sample[4 * i:4 * i + 4].reshape(NS * ST, D)
        xT = np.ascontiguousarray(np.concatenate([x_prompt[b], xs], axis=0).T)
        cv = np.concatenate([c_prompt[b:b + 1], c_sample[4 * i:4 * i + 4]], axis=0)
        cT = np.ascontiguousarray(cv.T.reshape(DC, 128, NSEQ).transpose(1, 0, 2))
        in_maps.append(dict(xT=xT, cT=cT, w_ada=w_ada, b_ada=b_ada, ln_g=ln_g, ln_b=ln_b, w_gate=w_gate, w_up=w_up,
                            w_down=w_down, w_in=w_in, b_in_bc=b_in_bc,
                            win_state=np.ascontiguousarray(state_kv_win[4 * i:4 * i + 4]),
                            rwp=rwp, rwq=rwq, rw_w2=rw_w2, rw_a2=rw_a2, rw_g2=rw_g2,
                            shs=np.ascontiguousarray(chunked(state_shift[4 * i:4 * i + 4])),
                            wkv0=np.ascontiguousarray(state_wkv[4 * i:4 * i + 4]), blk1=blk1, istk=istk,
                            nsab=nsab, phi_w1=phi_w1, phi_w2=phi_w2, peT=peT, tri=tri, cvalT=cvalT, jidx=jidx, curb=curb,
                            eall=eall, bimp=bimp, idx16=idx16, ident=ident, bm=bm, w_out_a=w_out_a, w_out_b=w_out_b, w_o=w_o,
                            cache_cmp=cache_cmp, cache_sel=cache_sel,
                            ptab=np.ascontiguousarray(page_table[4 * i:4 * i + 4].reshape(1, NS * 128)),
                            bimps=bimps, e2=e2, hsel=hsel, pcol=pcol, jidx2=jidx2, tri16=tri16))
    res = run_bass_kernel_spmd(nc, in_maps, core_ids=list(range(8)))
    R = res.results

    y_prompt = np.stack([R[2 * b]["yT"][:, :SEQ].T for b in range(4)])
    y_sample = np.concatenate([R[i]["yT"][:, SEQ:].T.reshape(NS, ST, D) for i in range(8)])
    kvshape = lambda a: a.reshape(a.shape[0], 2, 2, 64)
    kvc_p = np.stack([kvshape(R[2 * b]["o_kvc"][:SEQ]) for b in range(4)])[None]
    kvs_p = np.stack([kvshape(R[2 * b]["o_kvs"][:SEQ]) for b in range(4)])[None]
    kvw_p = np.stack([kvshape(R[2 * b]["o_kvw_p"]) for b in range(4)])[None]
    wkv_p = np.stack([R[2 * b]["o_wkv"][0] for b in range(4)])[None]
    sh_p = np.stack([R[2 * b]["o_shift"][0] for b in range(4)])[None]
    kvc_s = np.concatenate([R[i]["o_kvc"][SEQ:].reshape(NS, ST, 2, 2, 64) for i in range(8)])[None]
    kvs_s = np.concatenate([R[i]["o_kvs"][SEQ:].reshape(NS, ST, 2, 2, 64) for i in range(8)])[None]
    kvw_s = np.concatenate([R[i]["o_kvw_s"].reshape(NS, 512, 2, 2, 64) for i in range(8)])[None]
    wkv_s = np.concatenate([R[i]["o_wkv"][1:] for i in range(8)])[None]
    sh_s = np.concatenate([R[i]["o_shift"][1:] for i in range(8)])[None]
    outs = (y_prompt, y_sample, kvc_p, kvs_p, kvw_p, wkv_p, sh_p, kvc_s, kvs_s, kvw_s, wkv_s, sh_s)
    return tuple(np.ascontiguousarray(o, dtype=np.float32) for o in outs)
```

```python
import contextlib
import numpy as np
import concourse.bass as bass
import concourse.mybir as mybir
from concourse.bass_utils import run_bass_kernel_spmd

F32 = mybir.dt.float32
BF16 = mybir.dt.bfloat16
I32 = mybir.dt.int32
AF = mybir.ActivationFunctionType
ALU = mybir.AluOpType
AX = mybir.AxisListType

D = 1024
DC = 8
DFF = 2816
FC = 22
SEQ = 4096
NS = 4
ST = 4
TT = SEQ + NS * ST
NSEQ = 1 + NS
RW_SHIFT = 1792
N_IN = 5144
ALPHA = 2 ** 0.25
LN_EPS = 1e-5
NT = 256
DEBUG = False
SEQ_SCAN = SEQ
SEQ_NSA = SEQ
DO_SAMPLE = True
NPG_DBG = 128
NS_DBG = NS

ENGS = ("pe", "act", "dve", "pool", "sp")


class _Rec:
    def __getattr__(self, name):
        def f(*a, **k):
            self.call = (name, a, k)
            return self
        return f


class Prog:
    def __init__(self, nc, n_dma_sems=32):
        self.nc = nc
        self.ops = {e: [] for e in ENGS}
        self.cnt = {e: 0 for e in ENGS}
        self.last_w = {}
        self.readers = {}
        self.seen = {e: {} for e in ENGS}
        self.n_dma = n_dma_sems
        self.dma_i = 0
        self.dma_cnt = [0] * n_dma_sems
        self.out_tokens = []

    def _deps(self, reads, writes):
        deps = []
        for k in reads:
            t = self.last_w.get(k)
            if t is not None:
                deps.append(t)
        for k in writes:
            t = self.last_w.get(k)
            if t is not None:
                deps.append(t)
            deps.extend(self.readers.get(k, ()))
        return deps

    def _commit(self, tok, reads, writes):
        for k in reads:
            self.readers.setdefault(k, []).append(tok)
        for k in writes:
            self.last_w[k] = tok
            self.readers[k] = []

    def _waits(self, eng, deps, skip_own=False):
        need = {}
        for (s, v) in deps:
            if skip_own and s == ("c", eng):
                continue
            if self.seen[eng].get(s, 0) >= v:
                continue
            if need.get(s, 0) < v:
                need[s] = v
        for s, v in need.items():
            self.seen[eng][s] = v
        return list(need.items())

    def op(self, eng, fn, reads=(), writes=(), skip_own=False):
        rec = _Rec()
        fn(rec)
        fn = rec.call
        deps = self._deps(reads, writes)
        waits = self._waits(eng, deps, skip_own)
        self.cnt[eng] += 1
        tok = (("c", eng), self.cnt[eng])
        self.ops[eng].append(("c", fn, waits, tok))
        self._commit(tok, reads, writes)
        return tok

    def dma(self, eng, out, in_, reads=(), writes=(), is_output=False, **kw):
        deps = self._deps(reads, writes)
        si = self.dma_i % self.n_dma
        self.dma_i += 1
        sem = ("d", si)
        prev = self.dma_cnt[si]
        if prev > 0:
            deps.append((sem, 16 * prev))
        waits = self._waits(eng, deps)
        self.dma_cnt[si] += 1
        tok = (sem, 16 * self.dma_cnt[si])
        self.ops[eng].append(("d", (out, in_, kw), waits, tok))
        self._commit(tok, reads, writes)
        if is_output:
            self.out_tokens.append(tok)
        return tok

    def emit(self):
        nc = self.nc
        with contextlib.ExitStack() as es:
            sems = {}
            for e in ENGS:
                sems[("c", e)] = es.enter_context(nc.semaphore("c_" + e))
            for i in range(self.n_dma):
                sems[("d", i)] = es.enter_context(nc.semaphore("d_%d" % i))
            final = {}
            for (s, v) in self.out_tokens:
                final[s] = max(final.get(s, 0), v)
            block = es.enter_context(nc.Block())
            engobj = {"pe": "tensor", "act": "scalar", "dve": "vector", "pool": "gpsimd", "sp": "sync"}

            def run(e, eng):
                for kind, payload, waits, tok in self.ops[e]:
                    for (s, v) in waits:
                        eng.wait_ge(sems[s], v)
                    if kind == "c":
                        name, a, k = payload
                        getattr(eng, name)(*a, **k).then_inc(sems[tok[0]], 1)
                    else:
                        out, in_, kw = payload
                        if "gather_idx" in kw:
                            eng.indirect_dma_start(out=out, out_offset=None, in_=in_,
                                                   in_offset=bass.IndirectOffsetOnAxis(ap=kw["gather_idx"], axis=0),
                                                   element_offset=kw.get("element_offset", 0)
                                                   ).then_inc(sems[tok[0]], 16)
                        else:
                            eng.dma_start(out=out, in_=in_, **kw).then_inc(sems[tok[0]], 16)
                if e == "sp":
                    for s, v in final.items():
                        eng.wait_ge(sems[s], v)

            for e in ENGS:
                getattr(block, engobj[e])(lambda eng, e=e: run(e, eng))


def tiles():
    out = []
    for i in range(SEQ // NT):
        out.append((i * NT, NT, [(0, NT, 0)]))
    out.append((SEQ, NS * ST, [(s * ST, (s + 1) * ST, 1 + s) for s in range(NS)]))
    return out


def build():
    nc = bass.Bass("TRN2", target_bir_lowering=False)
    P = Prog(nc)
    es = contextlib.ExitStack()

    def din(name, shape, dt=F32):
        return nc.dram_tensor(name, list(shape), dt, kind="ExternalInput").ap()

    def dout(name, shape, dt=F32):
        return nc.dram_tensor(name, list(shape), dt, kind="ExternalOutput").ap()

    def dscr(name, shape, dt=F32):
        return nc.dram_tensor(name, list(shape), dt, kind="Internal").ap()

    def sb(name, shape, dt=F32):
        return es.enter_context(nc.sbuf_tensor(name, list(shape), dt))

    xT = din("xT", [D, TT])
    cT = din("cT", [128, DC, NSEQ])
    w_ada = din("w_ada", [D, 9 * D])
    b_ada = din("b_ada", [128, 72])
    ln_g = din("ln_g", [128, 3, DC])
    ln_b = din("ln_b", [128, 3, DC])
    w_gate = din("w_gate", [2, D, DFF])
    w_up = din("w_up", [2, D, DFF])
    w_down = din("w_down", [2, DFF, D])
    w_in = din("w_in", [D, N_IN])
    b_in_bc = din("b_in_bc", [128, N_IN])
    win_state = din("win_state", [NS, 512, 256])

    rwp = din("rwp", [128, 15, 2])
    rwq = din("rwq", [128, 4, 7])
    w2_d = din("rw_w2", [64, 512])
    a2_d = din("rw_a2", [64, 512])
    g2_d = din("rw_g2", [128, 512])
    shs = din("shs", [128, 15, NS])
    wkv0 = din("wkv0", [NS, 8, 64, 64])
    blk1_d = din("blk1", [128, 128])
    istk_d = din("istk", [128, 64])

    nsab = din("nsab", [64, 16])
    phi_w1 = din("phi_w1", [2, 32, 64, 128])
    phi_w2 = din("phi_w2", [2, 128, 64])
    peT_d = din("peT", [64, 2, 32])
    tri_d = din("tri", [128, 128])
    cvalT_d = din("cvalT", [128, 2, 128])
    jidx_d = din("jidx", [128, 64])
    curb_d = din("curb", [128, 1])
    eall_d = din("eall", [64, SEQ])
    bimp_d = din("bimp", [128, 2, 64])
    idx16_d = din("idx16", [128, 16])
    ident_d = din("ident", [128, 128])
    bm_d = din("bm", [128, 16])
    w_out_a = din("w_out_a", [512, D])
    w_out_b = din("w_out_b", [512, D])
    w_o = din("w_o", [D, D])

    NPHYS = 5120
    cache_cmp = din("cache_cmp", [NPHYS * 256, 128])
    cache_sel = din("cache_sel", [NPHYS * 256, 128])
    ptab = din("ptab", [1, NS * 128], I32)
    bimps_d = din("bimps", [128, 8, 257])
    e2_d = din("e2", [128, 8192])
    hsel_d = din("hsel", [16, 4])
    pcol_d = din("pcol", [128, 1])
    jidx2_d = din("jidx2", [128, 264])
    tri16_d = din("tri16", [128, 16])

    yT = dout("yT", [D, TT])
    o_kvc = dout("o_kvc", [TT, 256])
    o_kvs = dout("o_kvs", [TT, 256])
    o_kvw_p = dout("o_kvw_p", [512, 256])
    o_kvw_s = dout("o_kvw_s", [NS, 512, 256])
    o_shift = dout("o_shift", [NSEQ, RW_SHIFT])
    o_wkv = dout("o_wkv", [NSEQ, 8, 64, 64])

    x1T = (dout if DEBUG else dscr)("x1T", [D, TT])
    x2T = (dout if DEBUG else dscr)("x2T", [D, TT])

    RWN = ("nkk", "w", "b", "kp", "r", "v", "g", "bonus")
    scr = {k: (dout if DEBUG else dscr)("scr_" + k, [128, 4, TT]) for k in RWN}
    y_fm = (dout if DEBUG else dscr)("y_fm", [128, 4, TT])
    yb_fm = (dout if DEBUG else dscr)("yb_fm", [128, 4, TT])
    nsaT = dscr("nsaT", [64, 16, TT])
    kv_tm = dscr("kv_tm", [TT, 768])
    gates_tm = dscr("gates_tm", [TT, 24])

    arena = sb("arena", [128, 3 * DC * DFF], BF16)
    stage = [sb("stage%d" % i, [128, DFF], F32) for i in range(2)]
    xt = [sb("xt%d" % i, [128, DC, NT], F32) for i in range(2)]
    ub = sb("ub", [128, DC, NT], BF16)
    hT = sb("hT", [128, FC, NT], BF16)
    tmpa = [sb("tmpa%d" % i, [128, NT], F32) for i in range(2)]
    tmpb = [sb("tmpb%d" % i, [128, NT], F32) for i in range(2)]
    lnt = {k: sb("ln_" + k, [128, NT], F32) for k in ("mu", "musq", "var", "rstd")}
    ones = sb("ones", [128, 128], F32)
    modS = sb("modS", [128, NSEQ, 9, DC], F32)
    ct_sb = sb("ct_sb", [128, DC, NSEQ], F32)
    bada_sb = sb("bada_sb", [128, 72], F32)
    lng_sb = sb("lng_sb", [128, 3, DC], F32)
    lnb_sb = sb("lnb_sb", [128, 3, DC], F32)
    def carve(off_bytes, ncols):
        a = off_bytes // 2
        return arena[:, a:a + 2 * ncols].bitcast(F32)

    CV0 = DC * 3096 * 2
    kvt = [carve(CV0 + i * 792 * 4, 792) for i in range(2)]
    rwt = carve(CV0 + 2 * 792 * 4, RW_SHIFT)
    bias_tm = carve(CV0 + 2 * 792 * 4 + RW_SHIFT * 4, 792 + RW_SHIFT)

    rwp_sb = sb("rwp_sb", [128, 15, 2])
    rwq_sb = sb("rwq_sb", [128, 4, 7])
    w2_sb = sb("w2_sb", [128, 512])
    a2_sb = sb("a2_sb", [128, 512])
    g2_sb = sb("g2_sb", [128, 512])
    nsab_sb = sb("nsab_sb", [64, 16])
    bm_sb = sb("bm_sb", [128, 16])
    blk1 = sb("blk1_sb", [128, 128])
    istk = sb("istk_sb", [128, 64])
    ps = [es.enter_context(nc.psum_tensor("ps%d" % i, [128, 512], F32)) for i in range(8)]

    P.op("pool", lambda e: e.memset(ones[:], 1.0), writes=["ones"])
    P.dma("sp", ct_sb[:], cT, writes=["ct"])
    P.dma("sp", bada_sb[:], b_ada, writes=["bada"])
    P.dma("sp", lng_sb[:], ln_g, writes=["lng"])
    P.dma("sp", lnb_sb[:], ln_b, writes=["lnb"])
    P.dma("sp", rwp_sb[:], rwp, writes=["rwc"])
    P.dma("sp", rwq_sb[:], rwq, writes=["rwc"])
    P.dma("sp", w2_sb[0:64, :], w2_d, writes=["rwc"])
    P.dma("sp", a2_sb[0:64, :], a2_d, writes=["rwc"])
    P.dma("sp", g2_sb[:], g2_d, writes=["rwc"])
    P.dma("sp", nsab_sb[:], nsab, writes=["rwc"])
    P.dma("sp", bm_sb[:], bm_d, writes=["rwc"])
    P.dma("sp", blk1[:], blk1_d, writes=["rwc"])
    P.dma("sp", istk[:], istk_d, writes=["rwc"])

    P.op("act", lambda e: e.activation(out=ct_sb[:], in_=ct_sb[:], func=AF.Silu), reads=["ct"], writes=["ct"])
    wada_v = w_ada.rearrange("(c p) n -> p c n", p=128)
    mod_ps = ps[0]
    GW = 256
    for gidx in range(9216 // GW):
        st = stage[gidx % 2]
        sk = "stage%d" % (gidx % 2)
        stv = st[:, 0:DC * GW].rearrange("p (c n) -> p c n", c=DC)
        P.dma("sp", stv, wada_v[:, :, gidx * GW:(gidx + 1) * GW], writes=[sk])
        for j in range(GW // 128):
            ch = gidx * (GW // 128) + j
            for c in range(DC):
                P.op("pe", lambda e, stv=stv, j=j, ch=ch, c=c: e.matmul(
                    mod_ps[:, ch * NSEQ:(ch + 1) * NSEQ], lhsT=stv[:, c, j * 128:(j + 1) * 128], rhs=ct_sb[:, c, :],
                    start=(c == 0), stop=(c == DC - 1)),
                    reads=[sk, "ct"], writes=["ps0"], skip_own=True)
    P.op("dve", lambda e: e.tensor_tensor(
        out=modS[:].rearrange("p s k c -> p (k c) s"),
        in0=mod_ps[:, 0:72 * NSEQ].rearrange("p (ch s) -> p ch s", s=NSEQ),
        in1=bada_sb[:].unsqueeze(2).to_broadcast([128, 72, NSEQ]), op=ALU.add),
        reads=["ps0", "bada"], writes=["modS"])
    for k, (mulv, addv) in {1: (1.0, 1.0), 4: (1.0, 1.0), 7: (1.0, 1.0), 2: (0.5, 0.5), 8: (0.5, 0.5), 5: (1.0, 1.0)}.items():
        P.op("dve", lambda e, k=k, mulv=mulv, addv=addv: e.tensor_scalar(
            out=modS[:, :, k, :], in0=modS[:, :, k, :], scalar1=mulv, scalar2=addv, op0=ALU.mult, op1=ALU.add),
            reads=["modS"], writes=["modS"])

    cast_rr = [0]

    gate_sb = sb("gate_sb", [128, 4], F32)

    def arena_gate():
        for gi_, eng in enumerate(("pool", "dve", "act")):
            if eng == "act":
                P.op(eng, lambda e, gi_=gi_: e.memzero(gate_sb[:, gi_:gi_ + 1]), writes=["arena", "hT", "stage0", "stage1", "xt0", "xt1", ("gate", gi_)])
            else:
                P.op(eng, lambda e, gi_=gi_: e.memset(gate_sb[:, gi_:gi_ + 1], 0.0), writes=["arena", "hT", "stage0", "stage1", "xt0", "xt1", ("gate", gi_)])

    def load_weight_bf16(dst_ap, src_ap, ncols, dkey):
        i = cast_rr[0]
        cast_rr[0] += 1
        st = stage[i % 2]
        sk = "stage%d" % (i % 2)
        P.dma("sp", st[:, 0:ncols], src_ap, writes=[sk])
        eng = ("pool", "dve", "act")[i % 3]
        if eng == "act":
            P.op("act", lambda e: e.copy(out=dst_ap, in_=st[:, 0:ncols]), reads=[sk], writes=[dkey])
        else:
            P.op(eng, lambda e: e.tensor_copy(out=dst_ap, in_=st[:, 0:ncols]), reads=[sk], writes=[dkey])

    def layer_norm_tile(z, zk, n, gi, out, outk):
        s1, s2 = ps[6], ps[7]
        for c in range(DC):
            tq = tmpa[c % 2]
            P.op("act", lambda e, c=c, tq=tq: e.activation(out=tq[:, 0:n], in_=z[:, c, 0:n], func=AF.Square),
                 reads=[zk], writes=["tmpa%d" % (c % 2)])
            P.op("pe", lambda e, c=c: e.matmul(s1[:, 0:n], lhsT=ones[:], rhs=z[:, c, 0:n], start=(c == 0), stop=(c == DC - 1)),
                 reads=[zk, "ones"], writes=["ps6"], skip_own=True)
            P.op("pe", lambda e, c=c, tq=tq: e.matmul(s2[:, 0:n], lhsT=ones[:], rhs=tq[:, 0:n], start=(c == 0), stop=(c == DC - 1)),
                 reads=["tmpa%d" % (c % 2), "ones"], writes=["ps7"], skip_own=True)
        mu, musq, var, rstd = lnt["mu"], lnt["musq"], lnt["var"], lnt["rstd"]
        P.op("act", lambda e: e.mul(out=mu[:, 0:n], in_=s1[:, 0:n], mul=1.0 / D), reads=["ps6"], writes=["ln_mu"])
        P.op("dve", lambda e: e.tensor_tensor(out=musq[:, 0:n], in0=mu[:, 0:n], in1=mu[:, 0:n], op=ALU.mult),
             reads=["ln_mu"], writes=["ln_musq"])
        P.op("dve", lambda e: e.scalar_tensor_tensor(out=var[:, 0:n], in0=s2[:, 0:n], scalar=1.0 / D, in1=musq[:, 0:n],
                                                     op0=ALU.mult, op1=ALU.subtract),
             reads=["ps7", "ln_musq"], writes=["ln_var"])
        P.op("dve", lambda e: e.tensor_scalar(out=var[:, 0:n], in0=var[:, 0:n], scalar1=LN_EPS, scalar2=None,
                                              op0=ALU.add), reads=["ln_var"], writes=["ln_var"])
        P.op("act", lambda e: e.sqrt(out=var[:, 0:n], in_=var[:, 0:n]), reads=["ln_var"], writes=["ln_var"])
        P.op("dve", lambda e: e.reciprocal(out=rstd[:, 0:n], in_=var[:, 0:n]), reads=["ln_var"], writes=["ln_rstd"])
        for c in range(DC):
            ta = tmpb[c % 2]
            tk = "tmpb%d" % (c % 2)
            P.op("dve", lambda e, c=c, ta=ta: e.tensor_tensor(out=ta[:, 0:n], in0=z[:, c, 0:n], in1=mu[:, 0:n], op=ALU.subtract),
                 reads=[zk, "ln_mu"], writes=[tk])
            P.op("pool", lambda e, c=c, ta=ta: e.tensor_tensor(out=ta[:, 0:n], in0=ta[:, 0:n], in1=rstd[:, 0:n], op=ALU.mult),
                 reads=[tk, "ln_rstd"], writes=[tk])
            P.op("act", lambda e, c=c, ta=ta: e.activation(out=out[:, c, 0:n], in_=ta[:, 0:n], func=AF.Identity,
                                                           scale=lng_sb[:, gi, c:c + 1], bias=lnb_sb[:, gi, c:c + 1]),
                 reads=[tk, "lng", "lnb"], writes=[outk])

    def modulate(x, xk, n, segs, kshift, dst, dk):
        for c in range(DC):
            for (lo, hi, s) in segs:
                P.op("act", lambda e, c=c, lo=lo, hi=hi, s=s: e.activation(
                    out=dst[:, c, lo:hi], in_=x[:, c, lo:hi], func=AF.Identity,
                    scale=modS[:, s, kshift + 1, c:c + 1], bias=modS[:, s, kshift, c:c + 1]),
                    reads=[xk, "modS"], writes=[dk])

    def ffn_phase(fi, src, dst, gi, kmod, dst_is_output):
        wg = arena[:, 0:DC * DFF].rearrange("p (c f) -> p c f", c=DC)
        wu = arena[:, DC * DFF:2 * DC * DFF].rearrange("p (c f) -> p c f", c=DC)
        wd = arena[:, 2 * DC * DFF:3 * DC * DFF].rearrange("p (f d) -> p f d", f=FC)
        arena_gate()
        for c in range(DC):
            load_weight_bf16(wg[:, c, :], w_gate[fi, c * 128:(c + 1) * 128, :], DFF, ("wg", fi, c))
            load_weight_bf16(wu[:, c, :], w_up[fi, c * 128:(c + 1) * 128, :], DFF, ("wu", fi, c))
        for f2 in range(FC // 2):
            i = cast_rr[0]
            cast_rr[0] += 1
            st = stage[i % 2]
            sk = "stage%d" % (i % 2)
            P.dma("sp", st[:, 0:2 * D].rearrange("p (a d) -> p a d", a=2),
                  w_down[fi, f2 * 256:(f2 + 1) * 256, :].rearrange("(a p) d -> p a d", p=128), writes=[sk])
            eng = ("pool", "dve")[i % 2]
            P.op(eng, lambda e, st=st, f2=f2: e.tensor_copy(
                out=wd[:, 2 * f2:2 * f2 + 2, :], in_=st[:, 0:2 * D].rearrange("p (a d) -> p a d", a=2)),
                reads=[sk], writes=[("wd", fi, f2)])
        src_v = src.rearrange("(c p) n -> p c n", p=128)
        dst_v = dst.rearrange("(c p) n -> p c n", p=128)
        tl = tiles()
        P.dma("sp", xt[0][:, :, 0:tl[0][1]], src_v[:, :, tl[0][0]:tl[0][0] + tl[0][1]], writes=["xt0"])
        for ti, (c0, n, segs) in enumerate(tl):
            x = xt[ti % 2]
            xk = "xt%d" % (ti % 2)
            if ti + 1 < len(tl):
                c0n, nn, _ = tl[ti + 1]
                P.dma("sp", xt[(ti + 1) % 2][:, :, 0:nn], src_v[:, :, c0n:c0n + nn], writes=["xt%d" % ((ti + 1) % 2)])
            modulate(x, xk, n, segs, kmod, ub, "ub")
            P.op("pool", lambda e, x=x, n=n: e.tensor_scalar(out=x[:, :, 0:n], in0=x[:, :, 0:n], scalar1=ALPHA, scalar2=None,
                                                            op0=ALU.mult), reads=[xk], writes=[xk])
            for f in range(FC):
                pg, pu = ps[(2 * f) % 4], ps[(2 * f + 1) % 4]
                kg, ku = "ps%d" % ((2 * f) % 4), "ps%d" % ((2 * f + 1) % 4)
                for c in range(DC):
                    P.op("pe", lambda e, c=c, f=f, pg=pg: e.matmul(pg[:, 0:n], lhsT=wg[:, c, f * 128:(f + 1) * 128], rhs=ub[:, c, 0:n],
                                                                 start=(c == 0), stop=(c == DC - 1)),
                         reads=["arena", ("wg", fi, c), "ub"], writes=[kg], skip_own=True)
                for c in range(DC):
                    P.op("pe", lambda e, c=c, f=f, pu=pu: e.matmul(pu[:, 0:n], lhsT=wu[:, c, f * 128:(f + 1) * 128], rhs=ub[:, c, 0:n],
                                                                 start=(c == 0), stop=(c == DC - 1)),
                         reads=["arena", ("wu", fi, c), "ub"], writes=[ku], skip_own=True)
                tq = tmpa[f % 2]
                tk = "tmpa%d" % (f % 2)
                P.op("act", lambda e, pg=pg, tq=tq: e.activation(out=tq[:, 0:n], in_=pg[:, 0:n], func=AF.Silu),
                     reads=[kg], writes=[tk])
                P.op("dve", lambda e, pu=pu, tq=tq, f=f: e.tensor_tensor(out=hT[:, f, 0:n], in0=tq[:, 0:n], in1=pu[:, 0:n], op=ALU.mult),
                     reads=[tk, ku], writes=["hT"])
            for m in range(DC):
                py = ps[4 + (m % 2)]
                ky = "ps%d" % (4 + (m % 2))
                for f in range(FC):
                    P.op("pe", lambda e, m=m, f=f, py=py: e.matmul(py[:, 0:n], lhsT=wd[:, f, m * 128:(m + 1) * 128], rhs=hT[:, f, 0:n],
                                                                 start=(f == 0), stop=(f == FC - 1)),
                         reads=["arena", ("wd", fi, f // 2), "hT"], writes=[ky], skip_own=True)
                for (lo, hi, s) in segs:
                    P.op("dve", lambda e, m=m, py=py, lo=lo, hi=hi, s=s, x=x: e.scalar_tensor_tensor(
                        out=x[:, m, lo:hi], in0=py[:, lo:hi], scalar=modS[:, s, kmod + 2, m:m + 1], in1=x[:, m, lo:hi],
                        op0=ALU.mult, op1=ALU.add),
                        reads=[ky, xk, "modS"], writes=[xk])
            layer_norm_tile(x, xk, n, gi, x, xk)
            P.dma("pool", dst_v[:, :, c0:c0 + n], x[:, :, 0:n], reads=[xk], writes=["dram_" + dst.name], is_output=dst_is_output)

    ffn_phase(0, xT, x1T, 0, 0, False)

    NC1 = 3096
    win_sb = arena[:, 0:DC * NC1].rearrange("p (c f) -> p c f", c=DC)
    arena_gate()
    for c in range(DC):
        for hi_, (a, b) in enumerate(((0, 1548), (1548, 3096))):
            load_weight_bf16(win_sb[:, c, a:b], w_in[c * 128:(c + 1) * 128, a:b], b - a, ("win", c, hi_))
    P.dma("sp", bias_tm[:, 0:792], b_in_bc[:, 2304:3096], reads=[("gate", 0), "arena"], writes=["bias_tm"])
    P.dma("sp", bias_tm[:, 792:792 + RW_SHIFT], b_in_bc[:, 0:RW_SHIFT], reads=[("gate", 0), "arena"], writes=["bias_tm"])
    CV1 = CV0 + (2 * 792 + RW_SHIFT + 792 + RW_SHIFT) * 4
    pbuf = carve(CV1, 15 * 260)
    xsb = carve(CV1 + 15 * 260 * 4, 15 * NT).rearrange("p (c n) -> p c n", c=15)
    DER0 = CV1 + 15 * 260 * 4 + 15 * NT * 4
    DERN = ("w", "a", "g", "kk", "nkk", "kp", "b", "bonus", "t1", "t2", "tw", "sgd")
    der = {k: carve(DER0 + i * NT * 4, NT) for i, k in enumerate(DERN)}
    CHT = [(j * 128, 128) for j in range(12)] + [(1536, 64), (1600, 64), (1664, 128)]
    GK = [("gate", 0), ("gate", 1), ("gate", 2), "arena"]

    def rw_pre(ti, c0, n, segs):
        S_ = len(segs)
        L = n // S_
        pb = pbuf[:, 0:15 * S_ * (L + 1)].rearrange("p (c s l) -> p c s l", c=15, s=S_)
        if S_ > 1:
            P.dma("sp", pb[:, :, :, 0], shs, reads=GK, writes=["pbuf"], allow_slow_non_contiguous=True)
        for ci, (col0, w) in enumerate(CHT):
            pp = ps[2 + ci % 2]
            pk = "ps%d" % (2 + ci % 2)
            for c in range(DC):
                P.op("pe", lambda e: e.matmul(pp[0:w, 0:n], lhsT=win_sb[:, c, col0:col0 + w], rhs=ub[:, c, 0:n],
                                              start=(c == 0), stop=(c == DC - 1)),
                     reads=["arena", ("win", c, 0), ("win", c, 1), "ub"], writes=[pk], skip_own=True)
            P.op("act", lambda e: e.activation(out=pb[0:w, ci, :, 1:L + 1], in_=pp[0:w, 0:n].rearrange("p (s l) -> p s l", s=S_),
                                               func=AF.Identity, bias=rwp_sb[0:w, ci, 0:1], scale=1.0),
                 reads=[pk, "rwc"] + GK, writes=["pbuf"])
        xs4 = xsb[:, :, 0:n].rearrange("p c (s l) -> p c s l", s=S_)
        P.op("dve", lambda e: e.tensor_tensor(out=xs4, in0=pb[:, :, :, 0:L], in1=pb[:, :, :, 1:L + 1], op=ALU.subtract),
             reads=["pbuf"] + GK, writes=["xs"])
        P.op("dve", lambda e: e.tensor_tensor(out=xsb[:, :, 0:n], in0=xsb[:, :, 0:n],
                                              in1=rwp_sb[:, :, 1:2].to_broadcast([128, 15, n]), op=ALU.mult),
             reads=["xs", "rwc"] + GK, writes=["xs"])
        P.op("dve", lambda e: e.tensor_tensor(out=xs4, in0=xs4, in1=pb[:, :, :, 1:L + 1], op=ALU.add),
             reads=["xs", "pbuf"] + GK, writes=["xs"])
        if S_ == 1:
            P.op("dve", lambda e: e.tensor_copy(out=pb[:, :, :, 0:1], in_=pb[:, :, :, L:L + 1]), reads=["pbuf"] + GK, writes=["pbuf"])
        tw, sgd = der["tw"], der["sgd"]
        P.op("act", lambda e: e.activation(out=tw[0:64, 0:n], in_=xsb[0:64, 12, 0:n], func=AF.Tanh), reads=["xs"] + GK, writes=["tw"])
        P.op("act", lambda e: e.activation(out=sgd[:, 0:n], in_=xsb[:, 14, 0:n], func=AF.Sigmoid), reads=["xs"] + GK, writes=["sgd"])
        NEG = -float(np.exp(-0.5))
        for j in range(4):
            jc = slice(j * 128, (j + 1) * 128)
            q = lambda k: rwq_sb[:, j, k:k + 1]
            r_j, k_j, v_j = xsb[:, j, 0:n], xsb[:, 4 + j, 0:n], xsb[:, 8 + j, 0:n]
            dw, da, dg, dkk, dnkk, dkp, db, dbo, t1, t2 = (der[k][:, 0:n] for k in ("w", "a", "g", "kk", "nkk", "kp", "b", "bonus", "t1", "t2"))
            p4, p5 = ps[4][:, 0:n], ps[5][:, 0:n]
            P.op("pe", lambda e: e.matmul(p4, lhsT=w2_sb[0:64, jc], rhs=tw[0:64, 0:n], start=True, stop=True),
                 reads=["rwc", "tw"] + GK, writes=["ps4"])
            P.op("act", lambda e: e.activation(out=dw, in_=p4, func=AF.Sigmoid, bias=q(0), scale=1.0), reads=["ps4", "rwc"] + GK, writes=["d_w"])
            P.op("act", lambda e: e.activation(out=dw, in_=dw, func=AF.Exp, scale=NEG), reads=["d_w"] + GK, writes=["d_w"])
            P.op("pe", lambda e: e.matmul(p5, lhsT=a2_sb[0:64, jc], rhs=xsb[0:64, 13, 0:n], start=True, stop=True),
                 reads=["rwc", "xs"] + GK, writes=["ps5"])
            P.op("act", lambda e: e.activation(out=da, in_=p5, func=AF.Sigmoid, bias=q(1), scale=1.0), reads=["ps5", "rwc"] + GK, writes=["d_a"])
            P.op("pe", lambda e: e.matmul(p4, lhsT=g2_sb[:, jc], rhs=sgd[:, 0:n], start=True, stop=True),
                 reads=["rwc", "sgd"] + GK, writes=["ps4"])
            P.op("act", lambda e: e.copy(out=dg, in_=p4), reads=["ps4"] + GK, writes=["d_g"])
            P.op("dve", lambda e: e.tensor_scalar(out=dkk, in0=k_j, scalar1=q(2), scalar2=None, op0=ALU.mult), reads=["xs", "rwc"] + GK, writes=["d_kk"])
            P.op("act", lambda e: e.activation(out=t1, in_=dkk, func=AF.Square), reads=["d_kk"] + GK, writes=["d_t1"])
            P.op("pe", lambda e: e.matmul(p5, lhsT=blk1[:], rhs=t1, start=True, stop=True), reads=["rwc", "d_t1"] + GK, writes=["ps5"])
            P.op("dve", lambda e: e.tensor_scalar(out=t2, in0=p5, scalar1=1e-24, scalar2=None, op0=ALU.max), reads=["ps5"] + GK, writes=["d_t2"])
            P.op("act", lambda e: e.sqrt(out=t2, in_=t2), reads=["d_t2"] + GK, writes=["d_t2"])
            P.op("dve", lambda e: e.reciprocal(out=t2, in_=t2), reads=["d_t2"] + GK, writes=["d_t2"])
            P.op("dve", lambda e: e.scalar_tensor_tensor(out=dnkk, in0=dkk, scalar=-1.0, in1=t2, op0=ALU.mult, op1=ALU.mult),
                 reads=["d_kk", "d_t2"] + GK, writes=["d_nkk"])
            P.op("dve", lambda e: e.tensor_scalar(out=t1, in0=da, scalar1=-1.0, scalar2=q(3), op0=ALU.add, op1=ALU.mult),
                 reads=["d_a", "rwc", "d_t1"] + GK, writes=["d_t1"])
            P.op("dve", lambda e: e.scalar_tensor_tensor(out=dkp, in0=t1, scalar=1.0, in1=k_j, op0=ALU.add, op1=ALU.mult),
                 reads=["d_t1", "xs"] + GK, writes=["d_kp"])
            P.op("dve", lambda e: e.scalar_tensor_tensor(out=db, in0=dnkk, scalar=-1.0, in1=da, op0=ALU.mult, op1=ALU.mult),
                 reads=["d_nkk", "d_a"] + GK, writes=["d_b"])
            P.op("dve", lambda e: e.scalar_tensor_tensor(out=t1, in0=r_j, scalar=q(4), in1=dkp, op0=ALU.mult, op1=ALU.mult),
                 reads=["xs", "rwc", "d_kp", "d_t1"] + GK, writes=["d_t1"])
            P.op("pe", lambda e: e.matmul(p4, lhsT=blk1[:], rhs=t1, start=True, stop=True), reads=["rwc", "d_t1"] + GK, writes=["ps4"])
            P.op("dve", lambda e: e.tensor_tensor(out=dbo, in0=p4, in1=v_j, op=ALU.mult), reads=["ps4", "xs"] + GK, writes=["d_bonus"])
            for nm, src_, key in (("nkk", dnkk, "d_nkk"), ("w", dw, "d_w"), ("b", db, "d_b"), ("kp", dkp, "d_kp"),
                                  ("g", dg, "d_g"), ("bonus", dbo, "d_bonus"), ("r", r_j, "xs"), ("v", v_j, "xs")):
                P.dma("pool", scr[nm][:, j, c0:c0 + n], src_, reads=[key] + GK, writes=["dram_scr"])

    nst = carve(DER0 + len(DERN) * NT * 4, 16 * NT).rearrange("p (c n) -> p c n", c=16)
    NCH = [1792 + 64 * h for h in range(8)] + [2304 + br * 256 + g * 64 for br in range(3) for g in range(2)] \
        + [2304 + 128 + g * 64 for g in range(2)]

    def nsa_pre(c0, n):
        for ci, col0 in enumerate(NCH):
            pp = ps[4 + ci % 2]
            pk = "ps%d" % (4 + ci % 2)
            for c in range(DC):
                P.op("pe", lambda e: e.matmul(pp[0:64, 0:n], lhsT=win_sb[:, c, col0:col0 + 64], rhs=ub[:, c, 0:n],
                                              start=(c == 0), stop=(c == DC - 1)),
                     reads=["arena", ("win", c, 0), ("win", c, 1), "ub"], writes=[pk], skip_own=True)
            P.op("act", lambda e: e.activation(out=nst[0:64, ci, 0:n], in_=pp[0:64, 0:n], func=AF.Identity,
                                               bias=nsab_sb[0:64, ci:ci + 1], scale=(0.125 if ci < 8 else 1.0)),
                 reads=[pk, "rwc"] + GK, writes=["nst"])
        P.dma("pool", nsaT[:, :, c0:c0 + n], nst[0:64, :, 0:n], reads=["nst"] + GK, writes=["dram_nsaT"])

    P.op("pool", lambda e: e.memset(pbuf, 0.0), reads=GK, writes=["pbuf"])
    P.op("pool", lambda e: e.memset(xsb, 0.0), reads=GK, writes=["xs"])
    x1_v = x1T.rearrange("(c p) n -> p c n", p=128)
    tl = tiles()
    for ti, (c0, n, segs) in enumerate(tl):
        x = xt[ti % 2]
        xk = "xt%d" % (ti % 2)
        P.dma("sp", x[:, :, 0:n], x1_v[:, :, c0:c0 + n], reads=["dram_x1T"], writes=[xk])
        modulate(x, xk, n, segs, 3, ub, "ub")
        rw_pre(ti, c0, n, segs)
        nsa_pre(c0, n)
        nblk = max(1, n // 128)
        for tb in range(nblk):
            m = min(128, n)
            t0 = c0 + tb * 128
            kv = kvt[tb % 2]
            kk_ = "kvt%d" % (tb % 2)
            for (ca, cb, pi) in ((0, 512, 0), (512, 792, 1)):
                pp = ps[pi]
                for c in range(DC):
                    P.op("pe", lambda e, c=c, tb=tb, m=m, ca=ca, cb=cb, pp=pp: e.matmul(
                        pp[0:m, 0:cb - ca], lhsT=ub[:, c, tb * 128:tb * 128 + m], rhs=win_sb[:, c, 2304 + ca:2304 + cb],
                        start=(c == 0), stop=(c == DC - 1)),
                        reads=["arena", ("win", c, 1), "ub"], writes=["ps%d" % pi], skip_own=True)
                P.op("dve", lambda e, m=m, ca=ca, cb=cb, pp=pp, kv=kv: e.tensor_tensor(
                    out=kv[0:m, ca:cb], in0=pp[0:m, 0:cb - ca], in1=bias_tm[0:m, ca:cb], op=ALU.add),
                    reads=["ps%d" % pi, "bias_tm", "arena"], writes=[kk_])
            P.dma("pool", kv_tm[t0:t0 + m, :], kv[0:m, 0:768], reads=[kk_, "arena"], writes=["dram_kvtm"])
            P.op("act", lambda e: e.activation(out=kv[0:m, 768:792], in_=kv[0:m, 768:792], func=AF.Sigmoid), reads=[kk_, "arena"], writes=[kk_])
            P.dma("pool", gates_tm[t0:t0 + m, :], kv[0:m, 768:792], reads=[kk_, "arena"], writes=["dram_gates"])
            P.dma("pool", o_kvc[t0:t0 + m, :], kv[0:m, 0:256], reads=[kk_, "arena"], is_output=True)
            P.dma("pool", o_kvs[t0:t0 + m, :], kv[0:m, 256:512], reads=[kk_, "arena"], is_output=True)
            if t0 >= SEQ - 512 and t0 < SEQ:
                P.dma("pool", o_kvw_p[t0 - (SEQ - 512):t0 - (SEQ - 512) + m, :], kv[0:m, 512:768], reads=[kk_, "arena"], is_output=True)
            if t0 >= SEQ:
                for s in range(NS):
                    P.dma("pool", o_kvw_s[s, 512 - ST:512, :], kv[s * ST:(s + 1) * ST, 512:768], reads=[kk_, "arena"], is_output=True)
            last_blk = (t0 + m == SEQ) or (t0 >= SEQ)
            if last_blk:
                for q4 in range(4):
                    ca, cb = q4 * 448, (q4 + 1) * 448
                    pp = ps[2 + (q4 % 2)]
                    pk = "ps%d" % (2 + (q4 % 2))
                    for c in range(DC):
                        P.op("pe", lambda e, c=c, tb=tb, m=m, ca=ca, cb=cb, pp=pp: e.matmul(
                            pp[0:m, 0:448], lhsT=ub[:, c, tb * 128:tb * 128 + m], rhs=win_sb[:, c, ca:cb],
                            start=(c == 0), stop=(c == DC - 1)),
                            reads=["arena", ("win", c, 0), ("win", c, 1), "ub"], writes=[pk], skip_own=True)
                    P.op("dve", lambda e, m=m, ca=ca, cb=cb, pp=pp: e.tensor_tensor(
                        out=rwt[0:m, ca:cb], in0=pp[0:m, 0:448], in1=bias_tm[0:m, 792 + ca:792 + cb], op=ALU.add),
                        reads=[pk, "bias_tm", "arena"], writes=["rwt"])
                if t0 < SEQ:
                    P.dma("pool", o_shift[0:1, :], rwt[127:128, :], reads=["rwt", "arena"], is_output=True)
                else:
                    for s in range(NS):
                        P.dma("pool", o_shift[1 + s:2 + s, :], rwt[s * ST + ST - 1:s * ST + ST, :], reads=["rwt", "arena"], is_output=True)
    for s in range(NS):
        P.dma("sp", o_kvw_s[s, 0:512 - ST, :], win_state[s, ST:512, :], is_output=True)

    arena_gate()
    TC = 8
    BN = ("nkk", "w", "b", "kp", "r")
    off = [0]

    def cv(ncols):
        v = carve(off[0], ncols)
        off[0] += ncols * 4
        return v

    xB = {k: [cv(TC * 256) for _ in range(2)] for k in BN}
    bld = [cv(TC * 256) for _ in range(2)]
    m3buf = cv(TC * 256)
    Shist = [cv(TC * 256) for _ in range(2)]
    xin = {k: [cv(4 * TC).rearrange("p (h t) -> p h t", h=4) for _ in range(2)] for k in BN + ("v",)}
    Sst, Sw, m1 = (cv(256) for _ in range(3))
    sa = cv(4)
    hT32 = hT[:].rearrange("p f n -> p (f n)").bitcast(F32)
    ybuf = [hT32[:, i_ * 1024:(i_ + 1) * 1024].rearrange("p (h t) -> p h t", h=4) for i_ in range(2)]
    v3 = lambda ap: ap.rearrange("p (h j) -> p h j", h=4)
    ycnt = [0]
    bcnt = [0]
    BLD_ENG = {"nkk": "dve", "w": "dve", "b": "pool", "kp": "pool", "r": "pool"}

    def scan_seq(seq_i, col0, T):
        if seq_i == 0:
            P.op("pool", lambda e: e.memset(Sst, 0.0), reads=GK, writes=["S0"])
        else:
            src = wkv0[seq_i - 1].rearrange("(hf hp) i j -> hp i hf j", hp=2)
            for hp in range(2):
                P.dma("sp", v3(Sst)[hp * 64:(hp + 1) * 64], src[hp], reads=GK, writes=["S0"])
        Sprev, Spk = Sst, ["S0"]
        yb_i = ycnt[0] % 2
        ycnt[0] += 1
        yb, ybk = ybuf[yb_i], "ybuf%d" % yb_i
        ycol0 = col0
        for t0 in range(0, T, TC):
            tc = min(TC, T - t0)
            bi = bcnt[0] % 2
            bcnt[0] += 1
            for k in BN + ("v",):
                P.dma("sp", xin[k][bi][:, :, 0:tc], scr[k][:, :, col0 + t0:col0 + t0 + tc], reads=["dram_scr"] + GK, writes=[("xin", k, bi)])
            for ki, k in enumerate(BN):
                bl, blk_ = bld[ki % 2], "bld%d" % (ki % 2)
                P.op(BLD_ENG[k], lambda e: e.tensor_tensor(
                    out=bl[:, 0:tc * 256].rearrange("p (t h j) -> p t h j", t=tc, h=4),
                    in0=istk[:].unsqueeze(1).unsqueeze(1).to_broadcast([128, tc, 4, 64]),
                    in1=xin[k][bi][:, :, 0:tc].rearrange("p h t -> p t h").unsqueeze(3).to_broadcast([128, tc, 4, 64]),
                    op=ALU.mult), reads=[("xin", k, bi), "rwc"] + GK, writes=[blk_])
                for t2 in range(0, tc, 2):
                    w_ = min(2, tc - t2) * 256
                    pi = (t2 // 2) % 4
                    P.op("pe", lambda e: e.matmul(ps[pi][:, 0:w_], lhsT=blk1[:], rhs=bl[:, t2 * 256:t2 * 256 + w_], start=True, stop=True),
                         reads=[blk_, "rwc"] + GK, writes=["ps%d" % pi])
                    P.op("act", lambda e: e.copy(out=xB[k][bi][:, t2 * 256:t2 * 256 + w_], in_=ps[pi][:, 0:w_]),
                         reads=["ps%d" % pi] + GK, writes=[("xB", k, bi)])
            Sh = Shist[bi]
            shkeys = lambda tl__: [("Sh", bi, tl__, hf_) for hf_ in range(4)]
            for tl_ in range(tc):
                sl = slice(tl_ * 256, (tl_ + 1) * 256)
                P.op("dve", lambda e: e.tensor_tensor(out=m1, in0=Sprev, in1=xB["nkk"][bi][:, sl], op=ALU.mult),
                     reads=Spk + [("xB", "nkk", bi)] + GK, writes=["m1"])
                P.op("pool", lambda e: e.tensor_tensor(out=Sw, in0=Sprev, in1=xB["w"][bi][:, sl], op=ALU.mult),
                     reads=Spk + [("xB", "w", bi)] + GK, writes=[("Sw", hf_) for hf_ in range(4)])
                P.op("dve", lambda e: e.tensor_reduce(out=sa, in_=v3(m1), axis=AX.X, op=ALU.add), reads=["m1"] + GK, writes=["sa"])
                for hf in range(4):
                    hs = slice(hf * 64, (hf + 1) * 64)
                    P.op("dve", lambda e: e.scalar_tensor_tensor(out=Sw[:, hs], in0=xB["kp"][bi][:, sl][:, hs], scalar=xin["v"][bi][:, hf, tl_:tl_ + 1],
                                                                 in1=Sw[:, hs], op0=ALU.mult, op1=ALU.add),
                         reads=[("Sw", hf), ("xB", "kp", bi), ("xin", "v", bi)] + GK, writes=[("Sw", hf)])
                for hf in range(4):
                    hs = slice(hf * 64, (hf + 1) * 64)
                    P.op("dve", lambda e: e.scalar_tensor_tensor(out=Sh[:, sl][:, hs], in0=xB["b"][bi][:, sl][:, hs], scalar=sa[:, hf:hf + 1],
                                                                 in1=Sw[:, hs], op0=ALU.mult, op1=ALU.add),
                         reads=[("Sw", hf), "sa", ("xB", "b", bi)] + GK, writes=[("Sh", bi, tl_, hf)])
                Sprev, Spk = Sh[:, sl], shkeys(tl_)
            allkeys = [k_ for tl__ in range(tc) for k_ in shkeys(tl__)]
            yc = t0 - (ycol0 - col0)
            P.op("pool", lambda e: e.tensor_tensor(out=m3buf[:, 0:tc * 256], in0=Sh[:, 0:tc * 256], in1=xB["r"][bi][:, 0:tc * 256], op=ALU.mult),
                 reads=allkeys + [("xB", "r", bi)] + GK, writes=["m3"])
            P.op("dve", lambda e: e.tensor_reduce(out=yb[:, :, yc:yc + tc].rearrange("p h t -> p t h"),
                                                  in_=m3buf[:, 0:tc * 256].rearrange("p (t h j) -> p t h j", t=tc, h=4), axis=AX.X, op=ALU.add),
                 reads=["m3", "hT"] + GK, writes=[ybk])
            if yc + tc == 256 or t0 + tc == T:
                P.dma("sp", y_fm[:, :, ycol0:ycol0 + yc + tc], yb[:, :, 0:yc + tc], reads=[ybk, "hT"] + GK, writes=["dram_yfm"])
                ycol0 += yc + tc
                yb_i = ycnt[0] % 2
                ycnt[0] += 1
                yb, ybk = ybuf[yb_i], "ybuf%d" % yb_i
        dst = o_wkv[seq_i].rearrange("(hf hp) i j -> hp i hf j", hp=2)
        for hp in range(2):
            P.dma("sp", dst[hp], v3(Sprev)[hp * 64:(hp + 1) * 64], reads=Spk + GK, is_output=True)

    scan_seq(0, 0, SEQ_SCAN)
    for s_ in range(NS):
        scan_seq(1 + s_, SEQ + s_ * ST, ST)

    arena_gate()
    off[0] = 0
    ktile = [cv(SEQ) for _ in range(3)]
    vaug = [cv(32 * 65).rearrange("p (c f) -> p c f", c=32) for _ in range(2)]
    vcaug = [cv(2 * 129).rearrange("p (c f) -> p c f", c=2) for _ in range(2)]
    kcT = [cv(256) for _ in range(2)]
    w1_sb = cv(32 * 128).rearrange("p (j e) -> p j e", j=32)
    w2a_sb = cv(64)
    peT_sb = cv(64).rearrange("p (k j) -> p k j", k=2)
    hx, ht_, hs_ = cv(256), cv(256), cv(256)
    tri_sb, ident_sb = cv(128), cv(128)
    cvalT_sb = cv(256).rearrange("p (c q) -> p c q", c=2)
    jidx_sb, f0_sb, idx16_sb, curb_sb = cv(64), cv(64), cv(16), cv(1)
    eall_sb = ktile[0]
    ones_c = cv(512)
    qs4, qsq, e_sb = cv(512), cv(512), [cv(512), cv(512)]
    mrow = cv(3 * 512).rearrange("p (b n) -> p b n", b=3)
    kmx = cv(8)
    mask_sb = [cv(128), cv(128)]
    gat = cv(24)
    ob = cv(256).rearrange("p (h d) -> p h d", h=4)
    imp, Am, Fm, F2m, NFm, nfm, nf2m, selm = (cv(64) for _ in range(8))
    t16, oh16 = cv(16), cv(16)
    cur_, curm1, nF_, thr_, rc_ = (cv(1) for _ in range(5))
    rc4 = cv(4)
    selT_sb = cv(128)
    ybT = cv(2 * 128).rearrange("p (c q) -> p c q", c=2)
    pe_sb = cv(1)

    def cvh(ncols):
        a_ = off[0] // 2
        off[0] += ncols * 2
        return arena[:, a_:a_ + ncols]

    GK = GK + ["xt0", "xt1", "hT", "stage0", "stage1"]
    kb16 = [xt[i_][:].rearrange("p c n -> p (c n)").bitcast(BF16) for i_ in range(2)]
    v16 = [stage[i_][:].bitcast(BF16)[:, 0:32 * 65].rearrange("p (c f) -> p c f", c=32) for i_ in range(2)]
    eall16 = hT[:].rearrange("p f n -> p (f n)")[:, 0:SEQ]
    q16, ones16 = cvh(512), cvh(128)
    mrow16 = cvh(2 * 512).rearrange("p (b n) -> p b n", b=2)
    e16 = [cvh(512), cvh(512)]
    mask16 = [cvh(128), cvh(128)]
    tri16p, ntri16p, selT16 = cvh(128), cvh(128), cvh(128)
    assert off[0] <= 135168, off[0]

    def A(eng, fn, reads=(), writes=(), **kw):
        return P.op(eng, fn, reads=list(reads) + GK, writes=writes, **kw)

    def Dm(out, in_, reads=(), writes=(), **kw):
        return P.dma("sp", out, in_, reads=list(reads) + GK, writes=writes, **kw)

    for dst_, src_ in ((tri_sb, tri_d), (ident_sb, ident_d), (cvalT_sb, cvalT_d), (jidx_sb, jidx_d), (eall_sb[0:64, :], eall_d),
                       (idx16_sb, idx16_d), (curb_sb, curb_d)):
        Dm(dst_, src_, writes=["ncst", "kt0"] if dst_ is not tri_sb and False else ["ncst"])
    A("pool", lambda e: e.memset(ones_c, 1.0), writes=["ncst"])
    A("act", lambda e: e.copy(out=eall16[0:64, :], in_=eall_sb[0:64, :]), reads=["ncst"], writes=["ncst16", "kt0"])
    A("pool", lambda e: e.memset(ones16, 1.0), writes=["ncst16"])
    A("dve", lambda e: e.tensor_copy(out=tri16p, in_=tri_sb), reads=["ncst"], writes=["ncst16"])
    A("dve", lambda e: e.tensor_scalar(out=f0_sb, in0=jidx_sb, scalar1=0.0, scalar2=None, op0=ALU.is_equal), reads=["ncst"], writes=["ncst2"])
    for g in range(2):
        A("pool", lambda e: e.memset(vcaug[g][:, :, 64:65], 1.0), writes=[("vcaug", g)])
        Dm(vcaug[g][:, :, 65:129], bimp_d, writes=[("vcaug", g)])

    for kvi in range(2):
        Dm(w1_sb[0:64], phi_w1[kvi].rearrange("j d e -> d j e"), writes=["w1"])
        Dm(w2a_sb, phi_w2[kvi], writes=["w2a"])
        if kvi == 0:
            Dm(peT_sb[0:64], peT_d, writes=["peT"])
        for j in range(32):
            A("pe", lambda e: e.matmul(ps[3][:, 0:1], lhsT=w1_sb[0:64, j, :], rhs=peT_sb[0:64, kvi, j:j + 1], start=(j == 0), stop=(j == 31)),
              reads=["w1", "peT"], writes=["ps3"], skip_own=True)
        A("act", lambda e: e.copy(out=pe_sb, in_=ps[3][:, 0:1]), reads=["ps3"], writes=["pe_sb"])
        for g in range(2):
            xc = ktile[0]
            Dm(xc[0:64, :], nsaT[:, (8 + g) if kvi == 0 else (14 + g), 0:SEQ], reads=["dram_nsaT"], writes=["kt0"])
            for j in range(32):
                A("pe", lambda e: e.matmul(ps[0][:, 0:255], lhsT=w1_sb[0:64, j, :], rhs=xc[0:64, j:j + 16 * 254 + 1:16],
                                           start=(j == 0), stop=(j == 31)), reads=["w1", "kt0"], writes=["ps0"], skip_own=True)
            A("act", lambda e: e.activation(out=hx[:, 0:255], in_=ps[0][:, 0:255], func=AF.Identity, bias=pe_sb[:, 0:1], scale=1.0),
              reads=["ps0", "pe_sb"], writes=["hx"])
            A("dve", lambda e: e.tensor_tensor(out=ht_[:, 0:255], in0=hx[:, 0:255], in1=hx[:, 0:255], op=ALU.mult), reads=["hx"], writes=["ht"])
            A("dve", lambda e: e.tensor_scalar(out=ht_[:, 0:255], in0=ht_[:, 0:255], scalar1=0.044715, scalar2=1.0, op0=ALU.mult, op1=ALU.add),
              reads=["ht"], writes=["ht"])
            A("dve", lambda e: e.tensor_tensor(out=ht_[:, 0:255], in0=ht_[:, 0:255], in1=hx[:, 0:255], op=ALU.mult), reads=["ht", "hx"], writes=["ht"])
            A("act", lambda e: e.activation(out=hs_[:, 0:255], in_=ht_[:, 0:255], func=AF.Sigmoid, scale=1.5957691216057308), reads=["ht"], writes=["hs"])
            A("dve", lambda e: e.tensor_tensor(out=hx[:, 0:255], in0=hx[:, 0:255], in1=hs_[:, 0:255], op=ALU.mult), reads=["hx", "hs"], writes=["hx"])
            if kvi == 0:
                A("pe", lambda e: e.matmul(ps[1][0:64, 0:255], lhsT=w2a_sb[:, 0:64], rhs=hx[:, 0:255], start=True, stop=True),
                  reads=["w2a", "hx"], writes=["ps1"])
                A("act", lambda e: e.copy(out=kcT[g][0:64, 0:255], in_=ps[1][0:64, 0:255]), reads=["ps1"], writes=[("kcT", g)])
            else:
                for ch, nk in ((0, 128), (1, 127)):
                    A("pe", lambda e: e.matmul(ps[1][0:nk, 0:64], lhsT=hx[:, ch * 128:ch * 128 + nk], rhs=w2a_sb[:, 0:64], start=True, stop=True),
                      reads=["w2a", "hx"], writes=["ps1"])
                    A("act", lambda e: e.copy(out=vcaug[g][0:nk, ch, 0:64], in_=ps[1][0:nk, 0:64]), reads=["ps1"], writes=[("vcaug", g)])

    def key_max(kt, nkeys, slot, ktk="kt*"):
        nchunk = (nkeys + 511) // 512
        for c in range(nchunk):
            w_ = min(512, nkeys - c * 512)
            A("act", lambda e: e.activation(out=qsq[0:64, 0:w_], in_=kt[0:64, c * 512:c * 512 + w_], func=AF.Square), reads=[ktk], writes=["qsq"])
            A("pe", lambda e: e.matmul(ps[3][0:1, 0:w_], lhsT=ones_c[0:64, 0:1], rhs=qsq[0:64, 0:w_], start=True, stop=True),
              reads=["qsq", "ncst"], writes=["ps3"])
            A("dve", lambda e: e.tensor_reduce(out=t16[0:1, c:c + 1], in_=ps[3][0:1, 0:w_], axis=AX.X, op=ALU.max), reads=["ps3"], writes=["t16"])
        A("dve", lambda e: e.tensor_reduce(out=kmx[0:1, slot:slot + 1], in_=t16[0:1, 0:nchunk], axis=AX.X, op=ALU.max), reads=["t16"], writes=["kmx"])
        A("act", lambda e: e.sqrt(out=kmx[0:1, slot:slot + 1], in_=kmx[0:1, slot:slot + 1]), reads=["kmx"], writes=["kmx"])
        A("dve", lambda e: e.tensor_scalar(out=kmx[0:1, slot:slot + 1], in0=kmx[0:1, slot:slot + 1], scalar1=-1.0, scalar2=None, op0=ALU.mult),
          reads=["kmx"], writes=["kmx"])

    ecnt = [0]

    def attend(kt_chunk, nk, br, mask_ap, mask_key, vaug_chunk, W, first, last, ktk="kt*", vk="vaug*", lowp=False):
        i = ecnt[0] % 2
        ecnt[0] += 1
        sc, sk = ps[i], "ps%d" % i
        es_, ek = (e16[i], "e16_%d" % i) if lowp else (e_sb[i], "e%d" % i)
        if lowp:
            A("pe", lambda e: e.matmul(sc[0:nk, 0:512], lhsT=kt_chunk, rhs=q16[0:64, :], start=True, stop=False),
              reads=[ktk, "q16"], writes=[sk], skip_own=True)
            A("pe", lambda e: e.matmul(sc[0:nk, 0:512], lhsT=ones16[0:1, 0:nk], rhs=mrow16[0:1, br - 1, :], start=False, stop=True),
              reads=["mrow16", "ncst16"], writes=[sk], skip_own=True)
        else:
            A("pe", lambda e: e.matmul(sc[0:nk, 0:512], lhsT=kt_chunk, rhs=qs4[0:64, :], start=True, stop=False),
              reads=[ktk, "qs4"], writes=[sk], skip_own=True)
            A("pe", lambda e: e.matmul(sc[0:nk, 0:512], lhsT=ones_c[0:1, 0:nk], rhs=mrow[0:1, br, :], start=False, stop=True),
              reads=["mrow", "ncst"], writes=[sk], skip_own=True)
        A("act", lambda e: e.activation(out=es_[0:nk, :], in_=sc[0:nk, 0:512], func=AF.Exp), reads=[sk], writes=[ek])
        if mask_ap is not None:
            A("dve", lambda e: e.tensor_tensor(out=es_[0:nk, :].rearrange("p (h q) -> p h q", h=4), in0=es_[0:nk, :].rearrange("p (h q) -> p h q", h=4),
                                               in1=mask_ap.unsqueeze(1).to_broadcast([nk, 4, 128]), op=ALU.mult),
              reads=[ek, mask_key], writes=[ek])
        for h in range(4):
            A("pe", lambda e: e.matmul(ps[4 + h][:, 0:W], lhsT=es_[0:nk, h * 128:(h + 1) * 128], rhs=vaug_chunk, start=first, stop=last),
              reads=[ek, vk], writes=["ps%d" % (4 + h)], skip_own=True)

    def finish_branch(g, br, first_branch):
        for h in range(4):
            o = ps[4 + h]
            ok = "ps%d" % (4 + h)
            A("dve", lambda e: e.tensor_scalar(out=rc_, in0=o[:, 64:65], scalar1=1e-30, scalar2=None, op0=ALU.max), reads=[ok], writes=["rc"])
            A("dve", lambda e: e.reciprocal(out=rc_, in_=rc_), reads=["rc"], writes=["rc"])
            if br == 0:
                if h == 0:
                    A("dve", lambda e: e.tensor_scalar(out=imp, in0=o[:, 65:129], scalar1=rc_[:, 0:1], scalar2=None, op0=ALU.mult),
                      reads=[ok, "rc"], writes=["imp"])
                else:
                    A("dve", lambda e: e.scalar_tensor_tensor(out=imp, in0=o[:, 65:129], scalar=rc_[:, 0:1], in1=imp, op0=ALU.mult, op1=ALU.add),
                      reads=[ok, "rc", "imp"], writes=["imp"])
            gi_ = (4 * g + h) * 3 + br
            A("dve", lambda e: e.tensor_tensor(out=rc_, in0=rc_, in1=gat[:, gi_:gi_ + 1], op=ALU.mult), reads=["rc", "gat"], writes=["rc"])
            if first_branch:
                A("dve", lambda e: e.tensor_scalar(out=ob[:, h, :], in0=o[:, 0:64], scalar1=rc_[:, 0:1], scalar2=None, op0=ALU.mult),
                  reads=[ok, "rc"], writes=["ob"])
            else:
                A("dve", lambda e: e.scalar_tensor_tensor(out=ob[:, h, :], in0=o[:, 0:64], scalar=rc_[:, 0:1], in1=ob[:, h, :], op0=ALU.mult, op1=ALU.add),
                  reads=[ok, "rc", "ob"], writes=["ob"])

    def select_blocks(curval):
        A("dve", lambda e: e.tensor_scalar(out=cur_, in0=curb_sb, scalar1=float(curval), scalar2=None, op0=ALU.add), reads=["ncst"], writes=["cur"])
        A("dve", lambda e: e.tensor_scalar(out=curm1, in0=cur_, scalar1=-1.0, scalar2=None, op0=ALU.add), reads=["cur"], writes=["curm1"])
        A("dve", lambda e: e.tensor_scalar(out=Am, in0=jidx_sb, scalar1=cur_[:, 0:1], scalar2=None, op0=ALU.is_le), reads=["cur", "ncst"], writes=["Am"])
        A("dve", lambda e: e.tensor_scalar(out=Fm, in0=jidx_sb, scalar1=cur_[:, 0:1], scalar2=None, op0=ALU.is_equal), reads=["cur", "ncst"], writes=["Fm"])
        A("dve", lambda e: e.tensor_scalar(out=F2m, in0=jidx_sb, scalar1=curm1[:, 0:1], scalar2=None, op0=ALU.is_equal), reads=["curm1", "ncst"], writes=["F2m"])
        A("dve", lambda e: e.tensor_tensor(out=Fm, in0=Fm, in1=F2m, op=ALU.max), reads=["Fm", "F2m"], writes=["Fm"])
        A("dve", lambda e: e.tensor_tensor(out=Fm, in0=Fm, in1=f0_sb, op=ALU.max), reads=["Fm", "ncst2"], writes=["Fm"])
        A("dve", lambda e: e.tensor_tensor(out=NFm, in0=Am, in1=Fm, op=ALU.subtract), reads=["Am", "Fm"], writes=["NFm"])
        A("dve", lambda e: e.scalar_tensor_tensor(out=nfm, in0=imp, scalar=1.0, in1=NFm, op0=ALU.add, op1=ALU.mult), reads=["imp", "NFm"], writes=["nfm"])
        A("dve", lambda e: e.tensor_scalar(out=nfm, in0=nfm, scalar1=-1.0, scalar2=None, op0=ALU.add), reads=["nfm"], writes=["nfm"])
        A("dve", lambda e: e.tensor_reduce(out=nF_, in_=Fm, axis=AX.X, op=ALU.add), reads=["Fm"], writes=["nF"])
        A("dve", lambda e: e.tensor_scalar(out=nF_, in0=nF_, scalar1=-1.0, scalar2=15.0, op0=ALU.mult, op1=ALU.add), reads=["nF"], writes=["nF"])
        A("dve", lambda e: e.tensor_scalar(out=oh16, in0=idx16_sb, scalar1=nF_[:, 0:1], scalar2=None, op0=ALU.is_equal), reads=["nF", "ncst"], writes=["oh16"])
        A("dve", lambda e: e.max(out=t16[:, 0:8], in_=nfm), reads=["nfm"], writes=["t16"])
        A("dve", lambda e: e.match_replace(out=nf2m, in_to_replace=t16[:, 0:8], in_values=nfm, imm_value=-2.0), reads=["nfm", "t16"], writes=["nf2m"])
        A("dve", lambda e: e.max(out=t16[:, 8:16], in_=nf2m), reads=["nf2m"], writes=["t16"])
        A("dve", lambda e: e.tensor_tensor(out=t16, in0=t16, in1=oh16, op=ALU.mult), reads=["t16", "oh16"], writes=["t16"])
        A("dve", lambda e: e.tensor_reduce(out=thr_, in_=t16, axis=AX.X, op=ALU.add), reads=["t16"], writes=["thr"])
        A("dve", lambda e: e.tensor_scalar(out=selm, in0=nfm, scalar1=thr_[:, 0:1], scalar2=None, op0=ALU.is_ge), reads=["nfm", "thr"], writes=["selm"])
        A("dve", lambda e: e.tensor_tensor(out=selm, in0=selm, in1=NFm, op=ALU.mult), reads=["selm", "NFm"], writes=["selm"])
        A("dve", lambda e: e.tensor_tensor(out=selm, in0=selm, in1=Fm, op=ALU.add), reads=["selm", "Fm"], writes=["selm"])
        A("pe", lambda e: e.transpose(out=ps[3][0:64, 0:128], in_=selm, identity=ident_sb), reads=["selm", "ncst"], writes=["ps3"])
        A("act", lambda e: e.copy(out=selT16[0:64, :], in_=ps[3][0:64, 0:128]), reads=["ps3"], writes=["selT"])

    mcnt = [0]

    def sel_mask(chunk, diag):
        i = mcnt[0] % 2
        mcnt[0] += 1
        A("pe", lambda e: e.matmul(ps[2][:, 0:128], lhsT=eall16[0:64, chunk * 128:(chunk + 1) * 128], rhs=selT16[0:64, :], start=True, stop=True),
          reads=["selT", "ncst16"], writes=["ps2"])
        if diag:
            A("dve", lambda e: e.tensor_tensor(out=mask16[i], in0=ps[2][:, 0:128], in1=tri_sb, op=ALU.mult), reads=["ps2", "ncst"], writes=[("mask16", i)])
        else:
            A("act", lambda e: e.copy(out=mask16[i], in_=ps[2][:, 0:128]), reads=["ps2"], writes=[("mask16", i)])
        return mask16[i], ("mask16", i)

    ntri_sb = hs_[:, 0:128]
    A("dve", lambda e: e.tensor_scalar(out=ntri_sb, in0=tri_sb, scalar1=-1.0, scalar2=1.0, op0=ALU.mult, op1=ALU.add), reads=["ncst", "hs"], writes=["ntri"])
    A("dve", lambda e: e.tensor_copy(out=ntri16p, in_=ntri_sb), reads=["ntri"], writes=["ntri16"])

    NQB = SEQ_NSA // 128
    A("pool", lambda e: e.memset(qsq, 0.0), writes=["qsq", "qsq0"])
    Dm(yb_fm[:, :, SEQ:TT], ybT[:, :, 0:NS * ST].rearrange("p c q -> p (c q)")[:, 0:4 * NS * ST].rearrange("p (c q) -> p c q", c=4)
       if False else qsq[:, 0:4 * NS * ST].rearrange("p (c q) -> p c q", c=4), reads=["qsq0"], writes=["dram_ybfm"])
    for g in range(2):
        Dm(ktile[1][0:64, :], nsaT[:, 10 + g, 0:SEQ], reads=["dram_nsaT"], writes=["kt*"])
        Dm(ktile[2][0:64, :], nsaT[:, 12 + g, 0:SEQ], reads=["dram_nsaT"], writes=["kt*"])
        for bi_, col in ((0, 256 + 128 + g * 64), (1, 512 + 128 + g * 64)):
            Dm(vaug[bi_][:, :, 0:64], kv_tm[0:SEQ, col:col + 64].rearrange("(c p) f -> p c f", p=128), reads=["dram_kvtm"], writes=["vaug*"])
            A("pool", lambda e: e.memset(vaug[bi_][:, :, 64:65], 1.0), writes=["vaug*"])
        key_max(kcT[g], 255, 0, ("kcT", g))
        key_max(ktile[1], SEQ, 1)
        key_max(ktile[2], SEQ, 2)
        A("act", lambda e: e.copy(out=kb16[0][0:64, :], in_=ktile[1][0:64, :]), reads=["kt*"], writes=["kb16"])
        A("dve", lambda e: e.tensor_copy(out=kb16[1][0:64, :], in_=ktile[2][0:64, :]), reads=["kt*"], writes=["kb16"])
        A("pool", lambda e: e.tensor_copy(out=v16[0], in_=vaug[0]), reads=["vaug*"], writes=["v16"])
        A("pool", lambda e: e.tensor_copy(out=v16[1], in_=vaug[1]), reads=["vaug*"], writes=["v16"])
        for qb in range(NQB):
            q0 = qb * 128
            Dm(qs4[0:64, :].rearrange("p (h q) -> p h q", h=4), nsaT[:, 4 * g:4 * g + 4, q0:q0 + 128], reads=["dram_nsaT"], writes=["qs4"])
            Dm(gat, gates_tm[q0:q0 + 128, :], reads=["dram_gates"], writes=["gat"])
            A("act", lambda e: e.activation(out=qsq[0:64, :], in_=qs4[0:64, :], func=AF.Square), reads=["qs4"], writes=["qsq"])
            A("pe", lambda e: e.matmul(ps[3][0:1, 0:512], lhsT=ones_c[0:64, 0:1], rhs=qsq[0:64, :], start=True, stop=True),
              reads=["qsq", "ncst"], writes=["ps3"])
            A("act", lambda e: e.sqrt(out=mrow[0:1, 0, :], in_=ps[3][0:1, 0:512]), reads=["ps3"], writes=["mrow"])
            for br in (2, 1, 0):
                A("dve", lambda e: e.tensor_scalar(out=mrow[0:1, br, :], in0=mrow[0:1, 0, :], scalar1=kmx[0:1, br:br + 1], scalar2=None, op0=ALU.mult),
                  reads=["mrow", "kmx"], writes=["mrow"])
            A("dve", lambda e: e.tensor_copy(out=mrow16[0:1, :, :], in_=mrow[0:1, 1:3, :]), reads=["mrow"], writes=["mrow16"])
            A("dve", lambda e: e.tensor_copy(out=q16[0:64, :], in_=qs4[0:64, :]), reads=["qs4"], writes=["q16"])
            for ch, nk in ((0, 128), (1, 127)):
                i = mcnt[0] % 2
                mcnt[0] += 1
                A("dve", lambda e: e.tensor_scalar(out=mask_sb[i][0:nk, :], in0=cvalT_sb[0:nk, ch, :], scalar1=float(q0), scalar2=None, op0=ALU.is_le),
                  reads=["ncst"], writes=[("mask", i)])
                attend(kcT[g][0:64, ch * 128:ch * 128 + nk], nk, 0, mask_sb[i][0:nk, :], ("mask", i), vcaug[g][0:nk, ch, :], 129, ch == 0, ch == 1, ktk=("kcT", g), vk=("vcaug", g))
            finish_branch(g, 0, True)
            select_blocks(2 * qb)
            for ch in range(qb + 1):
                mk, mkk = sel_mask(ch, ch == qb)
                attend(kb16[0][0:64, ch * 128:(ch + 1) * 128], 128, 1, mk, mkk, v16[0][:, ch, :], 65, ch == 0, ch == qb, ktk="kb16", vk="v16", lowp=True)
            finish_branch(g, 1, False)
            lo = max(0, qb - 4)
            for ch in range(lo, qb + 1):
                if ch == qb:
                    mk, mkk = tri16p, "ncst16"
                elif ch == qb - 4:
                    mk, mkk = ntri16p, "ntri16"
                else:
                    mk, mkk = None, None
                attend(kb16[1][0:64, ch * 128:(ch + 1) * 128], 128, 2, mk, mkk, v16[1][:, ch, :], 65, ch == lo, ch == qb, ktk="kb16", vk="v16", lowp=True)
            finish_branch(g, 2, False)
            for c2 in range(2):
                A("pe", lambda e: e.transpose(out=ps[3][:, 0:128], in_=ob[:, 2 * c2:2 * c2 + 2, :].rearrange("p h d -> p (h d)"), identity=ident_sb),
                  reads=["ob", "ncst"], writes=["ps3"])
                A("act", lambda e: e.copy(out=ybT[:, c2, :], in_=ps[3][:, 0:128]), reads=["ps3"], writes=["ybT"])
            Dm(yb_fm[:, 2 * g:2 * g + 2, q0:q0 + 128], ybT, reads=["ybT"], writes=["dram_ybfm"])

    if DO_SAMPLE:
        arena_gate()
        off[0] = 0
        PAST = 16384
        NPG = NPG_DBG
        GK = GK + ["xt0", "xt1"]
        xoff = [0, 0]

        def cvx(i, ncols):
            v = xt[i][:].rearrange("p c n -> p (c n)")[:, xoff[i]:xoff[i] + ncols]
            xoff[i] += ncols
            assert xoff[i] <= DC * NT
            return v
        xTs = cv(PAST + 4)
        regB = cv(8192)
        w1pad = regB.rearrange("p (g j e) -> p g j e", g=2, j=32)
        pg = [cv(128) for _ in range(3)]
        vpg = [cv(2 * 65).rearrange("p (g f) -> p g f", g=2) for _ in range(2)]
        ptf, idxf = cv(NS * 128), cv(NS * 128)
        idx_i = [cv(NS * 128).bitcast(I32) for _ in range(2)]
        pt_i = cv(NS * 128).bitcast(I32)
        pcol_sb, hsel_sb = cv(1), cv(4)
        tri16, ntri16, ident_s, ones_s = cv(16), cv(16), cv(128), cv(512)
        onesg = cv(2)
        jidx2, f02 = cv(264), cv(264)
        qg = [cv(16) for _ in range(2)]
        qpad = [cv(16) for _ in range(2)]
        qsqs, qn_s = cv(16), cv(16)
        mrow_s = cv(6 * 16).rearrange("p (b n) -> p b n", b=6)
        kmx_s = cv(8)
        es2 = [cv(16), cv(16)]
        msk2 = [cv(16), cv(16)]
        gat_s = [cv(3), cv(3)]
        oacc = [cv(64), cv(64)]
        o322 = cv(322)
        imp2, Am2, Fm2, F2m2, NFm2, nfm2, nf2m2 = (cvx(1, 264) for _ in range(7))
        selm2 = cv(264)
        t16b, oh16b, idx16_s = cv(16), cv(16), cv(16)
        cur2, curm2, nF2, thr2, rc2, zero1 = (cv(1) for _ in range(6))
        selT2 = [cv(2 * 16).rearrange("p (k q) -> p k q", k=2) for _ in range(2)]
        vnew = [cv(2 * 65).rearrange("p (g f) -> p g f", g=2) for _ in range(2)]
        wv = cv(4 * 2 * 65).rearrange("p (c g f) -> p c g f", c=4, g=2)
        wk = cv(4 * 256).rearrange("p (c f) -> p c f", c=4)
        kTw = cv(516)
        obT = cv(16)
        sqt = cvx(0, 512)
        hx2, ht2, hs2 = cvx(0, 512), cvx(0, 512), cvx(0, 512)
        pe2, w2b = cv(1), cv(64)
        peT2 = cv(64).rearrange("p (k j) -> p k j", k=2)
        assert off[0] <= 135168, off[0]
        vcs = [stage[g_][:, 0:8 * 322].rearrange("p (c f) -> p c f", c=8) for g_ in range(2)]
        kcs = hT[:].rearrange("p f n -> p (f n)").bitcast(F32)[:, 0:2048].rearrange("p (g n) -> p g n", g=2)
        SK = ["stage0", "stage1", "hT"]

        def Gq(out, cache, half, col, reads=(), writes=()):
            return P.dma("pool", out, cache, reads=list(reads) + GK, writes=writes, gather_idx=idx_i[half][:, col:col + 1])

        for dst_, src_ in ((tri16, tri16_d), (ident_s, ident_d), (pcol_sb, pcol_d), (hsel_sb[0:16, :], hsel_d), (jidx2, jidx2_d), (idx16_s, idx16_d)):
            Dm(dst_, src_, writes=["scst"])
        A("pool", lambda e: e.memset(ones_s, 1.0), writes=["scst"])
        A("pool", lambda e: e.memset(zero1, 0.0), writes=["scst"])
        A("pool", lambda e: e.memset(onesg, 0.0), writes=["onesg"])
        A("pool", lambda e: e.memset(onesg[0:64, 0:1], 1.0), reads=["onesg"], writes=["onesg"])
        A("pool", lambda e: e.memset(onesg[64:128, 1:2], 1.0), reads=["onesg"], writes=["onesg"])
        A("dve", lambda e: e.tensor_scalar(out=ntri16, in0=tri16, scalar1=-1.0, scalar2=1.0, op0=ALU.mult, op1=ALU.add), reads=["scst"], writes=["scst2"])
        A("dve", lambda e: e.tensor_scalar(out=f02, in0=jidx2, scalar1=0.0, scalar2=None, op0=ALU.is_equal), reads=["scst"], writes=["scst2"])
        for g in range(2):
            A("pool", lambda e: e.memset(vcs[g][:, :, 64:65], 1.0), reads=SK, writes=[("vcs", g)])
            Dm(vcs[g][:, :, 65:322], bimps_d, reads=SK, writes=[("vcs", g)])
            A("pool", lambda e: e.memset(vpg[g][:, :, 64:65], 1.0), writes=[("vpg", g)])
            A("pool", lambda e: e.memset(vnew[g][:, :, 64:65], 1.0), writes=["vnew"])
            A("pool", lambda e: e.memset(qpad[g], 0.0), writes=["qpad"])
        A("pool", lambda e: e.memset(wv[:, :, :, 64:65], 1.0), writes=["wv"])
        Dm(pt_i, ptab.partition_broadcast(128), writes=["pt"])
        A("dve", lambda e: e.tensor_copy(out=ptf, in_=pt_i), reads=["pt"], writes=["ptf"])
        A("dve", lambda e: e.tensor_scalar(out=idxf, in0=ptf, scalar1=128.0, scalar2=pcol_sb[:, 0:1], op0=ALU.mult, op1=ALU.add), reads=["ptf", "scst"], writes=["idxf"])
        A("dve", lambda e: e.tensor_scalar(out=idxf, in0=idxf, scalar1=2.0, scalar2=None, op0=ALU.mult), reads=["idxf"], writes=["idxf"])
        A("dve", lambda e: e.tensor_copy(out=idx_i[0], in_=idxf), reads=["idxf"], writes=["idx"])
        A("dve", lambda e: e.tensor_scalar(out=ptf, in0=idxf, scalar1=1.0, scalar2=None, op0=ALU.add), reads=["idxf", "ptf"], writes=["ptf"])
        A("dve", lambda e: e.tensor_copy(out=idx_i[1], in_=ptf), reads=["ptf"], writes=["idx"])
        Dm(peT2[0:64], peT_d, writes=["peT2"])

        pcnt = [0]

        def gather_T(cache, s_, j, half, dst, dst_cols, dkey):
            i = pcnt[0] % 3
            pcnt[0] += 1
            Gq(pg[i], cache, half, s_ * 128 + j, reads=["idx"], writes=[("pg", i)])
            pb_ = ps[6 + i % 2]
            pbk = "ps%d" % (6 + i % 2)
            A("pe", lambda e: e.transpose(out=pb_[:, 0:128], in_=pg[i], identity=ident_s), reads=[("pg", i), "scst"], writes=[pbk])
            if i % 2 == 0:
                A("act", lambda e: e.copy(out=dst[:, dst_cols], in_=pb_[:, 0:128]), reads=[pbk], writes=[dkey])
            else:
                A("dve", lambda e: e.tensor_copy(out=dst[:, dst_cols], in_=pb_[:, 0:128]), reads=[pbk], writes=[dkey])

        def gelu_to(hx_, n_):
            A("dve", lambda e: e.tensor_tensor(out=ht2[:, 0:n_], in0=hx_, in1=hx_, op=ALU.mult), reads=["hx2"], writes=["ht2"])
            A("dve", lambda e: e.tensor_scalar(out=ht2[:, 0:n_], in0=ht2[:, 0:n_], scalar1=0.044715, scalar2=1.0, op0=ALU.mult, op1=ALU.add), reads=["ht2"], writes=["ht2"])
            A("dve", lambda e: e.tensor_tensor(out=ht2[:, 0:n_], in0=ht2[:, 0:n_], in1=hx_, op=ALU.mult), reads=["ht2", "hx2"], writes=["ht2"])
            A("act", lambda e: e.activation(out=hs2[:, 0:n_], in_=ht2[:, 0:n_], func=AF.Sigmoid, scale=1.5957691216057308), reads=["ht2"], writes=["hs2"])
            A("dve", lambda e: e.tensor_tensor(out=hx_, in0=hx_, in1=hs2[:, 0:n_], op=ALU.mult), reads=["hx2", "hs2"], writes=["hx2"])

        def key_max_s(kt, K_, ncols, slot, ktk, lhs1):
            nchunk = (ncols + 511) // 512
            for c in range(nchunk):
                w_ = min(512, ncols - c * 512)
                A("act", lambda e: e.activation(out=sqt[0:K_, 0:w_], in_=kt[0:K_, c * 512:c * 512 + w_], func=AF.Square), reads=[ktk], writes=["sqt"])
                A("pe", lambda e: e.matmul(ps[3][0:1, 0:w_], lhsT=lhs1, rhs=sqt[0:K_, 0:w_], start=True, stop=True), reads=["sqt", "scst", "onesg"], writes=["ps3"])
                if c == 0:
                    A("dve", lambda e: e.tensor_reduce(out=kmx_s[0:1, slot:slot + 1], in_=ps[3][0:1, 0:w_], axis=AX.X, op=ALU.max), reads=["ps3"], writes=["kmxs"])
                else:
                    A("dve", lambda e: e.tensor_reduce(out=kmx_s[0:1, 7:8], in_=ps[3][0:1, 0:w_], axis=AX.X, op=ALU.max), reads=["ps3"], writes=["kmxs"])
                    A("dve", lambda e: e.tensor_tensor(out=kmx_s[0:1, slot:slot + 1], in0=kmx_s[0:1, slot:slot + 1], in1=kmx_s[0:1, 7:8], op=ALU.max),
                      reads=["kmxs"], writes=["kmxs"])
            A("act", lambda e: e.sqrt(out=kmx_s[0:1, slot:slot + 1], in_=kmx_s[0:1, slot:slot + 1]), reads=["kmxs"], writes=["kmxs"])
            A("dve", lambda e: e.tensor_scalar(out=mrow_s[0:1, slot, :], in0=qn_s[0:1, :] if False else mrow_s[0:1, slot, :], scalar1=1.0, scalar2=None, op0=ALU.mult),
              reads=["mrows"], writes=["mrows"]) if False else None

        def set_mrow(slot, g):
            A("act", lambda e: e.activation(out=qsqs[0:64, :], in_=qg[g][0:64, :], func=AF.Square), reads=["qg"], writes=["qsqs"])
            A("pe", lambda e: e.matmul(ps[3][0:1, 0:16], lhsT=ones_s[0:64, 0:1], rhs=qsqs[0:64, :], start=True, stop=True), reads=["qsqs", "scst"], writes=["ps3"])
            A("act", lambda e: e.sqrt(out=qn_s[0:1, :], in_=ps[3][0:1, 0:16]), reads=["ps3"], writes=["qn"])
            A("dve", lambda e: e.tensor_scalar(out=mrow_s[0:1, slot, :], in0=qn_s[0:1, :], scalar1=kmx_s[0:1, slot:slot + 1], scalar2=-1.0, op0=ALU.mult, op1=ALU.mult),
              reads=["qn", "kmxs"], writes=["mrows"])

        e2cnt = [0]

        def attend_s(kt_chunk, K_, nk, qtile, slot, mask_ap, mask_key, v_chunk, W, g, first, last, ktk, vk):
            i = e2cnt[0] % 2
            e2cnt[0] += 1
            sc, sk = ps[i], "ps%d" % i
            es_, ek = es2[i], "es%d" % i
            A("pe", lambda e: e.matmul(sc[0:nk, 0:16], lhsT=kt_chunk, rhs=qtile[0:K_, :], start=True, stop=False), reads=[ktk, "qg", "qpad"], writes=[sk], skip_own=True)
            A("pe", lambda e: e.matmul(sc[0:nk, 0:16], lhsT=ones_s[0:1, 0:nk], rhs=mrow_s[0:1, slot, :], start=False, stop=True), reads=["mrows", "scst"], writes=[sk], skip_own=True)
            A("act", lambda e: e.activation(out=es_[0:nk, :], in_=sc[0:nk, 0:16], func=AF.Exp), reads=[sk], writes=[ek])
            if mask_ap is not None:
                A("dve", lambda e: e.tensor_tensor(out=es_[0:nk, :], in0=es_[0:nk, :], in1=mask_ap, op=ALU.mult), reads=[ek, mask_key], writes=[ek])
            A("pe", lambda e: e.matmul(ps[4 + g][0:16, 0:W], lhsT=es_[0:nk, :], rhs=v_chunk, start=first, stop=last), reads=[ek, vk], writes=["ps%d" % (4 + g)], skip_own=True)

        def finish_s(g, br, first_branch):
            o = ps[4 + g]
            ok = "ps%d" % (4 + g)
            A("dve", lambda e: e.tensor_scalar(out=rc2[0:16], in0=o[0:16, 64:65], scalar1=1e-30, scalar2=None, op0=ALU.max), reads=[ok], writes=["rc2"])
            A("dve", lambda e: e.reciprocal(out=rc2[0:16], in_=rc2[0:16]), reads=["rc2"], writes=["rc2"])
            if br == 0:
                A("dve", lambda e: e.tensor_scalar(out=o322[0:16, 0:257], in0=o[0:16, 65:322], scalar1=rc2[0:16, 0:1], scalar2=None, op0=ALU.mult), reads=[ok, "rc2"], writes=["o322"])
                A("pe", lambda e: e.matmul(ps[3][0:4, 0:257], lhsT=hsel_sb[0:16, 0:4], rhs=o322[0:16, 0:257], start=True, stop=True), reads=["o322", "scst"], writes=["ps3"])
                A("act", lambda e: e.copy(out=imp2[0:4, 0:257], in_=ps[3][0:4, 0:257]), reads=["ps3"], writes=["imp2"])
            A("dve", lambda e: e.tensor_tensor(out=rc2[0:16], in0=rc2[0:16], in1=gat_s[g][0:16, br:br + 1], op=ALU.mult), reads=["rc2", "gats"], writes=["rc2"])
            if first_branch:
                A("dve", lambda e: e.tensor_scalar(out=oacc[g][0:16, :], in0=o[0:16, 0:64], scalar1=rc2[0:16, 0:1], scalar2=None, op0=ALU.mult), reads=[ok, "rc2"], writes=[("oacc", g)])
            else:
                A("dve", lambda e: e.scalar_tensor_tensor(out=oacc[g][0:16, :], in0=o[0:16, 0:64], scalar=rc2[0:16, 0:1], in1=oacc[g][0:16, :], op0=ALU.mult, op1=ALU.add),
                  reads=[ok, "rc2", ("oacc", g)], writes=[("oacc", g)])

        def select_s(g):
            R4 = slice(0, 4)
            NB = 257
            v = lambda t_: t_[R4, 0:NB]
            A("dve", lambda e: e.tensor_scalar(out=cur2[R4], in0=zero1[R4], scalar1=256.0, scalar2=None, op0=ALU.add), reads=["scst"], writes=["cur2"])
            A("dve", lambda e: e.tensor_scalar(out=curm2[R4], in0=zero1[R4], scalar1=255.0, scalar2=None, op0=ALU.add), reads=["scst"], writes=["curm2"])
            A("dve", lambda e: e.tensor_scalar(out=v(Am2), in0=v(jidx2), scalar1=cur2[R4, 0:1], scalar2=None, op0=ALU.is_le), reads=["cur2", "scst"], writes=["Am2"])
            A("dve", lambda e: e.tensor_scalar(out=v(Fm2), in0=v(jidx2), scalar1=cur2[R4, 0:1], scalar2=None, op0=ALU.is_equal), reads=["cur2", "scst"], writes=["Fm2"])
            A("dve", lambda e: e.tensor_scalar(out=v(F2m2), in0=v(jidx2), scalar1=curm2[R4, 0:1], scalar2=None, op0=ALU.is_equal), reads=["curm2", "scst"], writes=["F2m2"])
            A("dve", lambda e: e.tensor_tensor(out=v(Fm2), in0=v(Fm2), in1=v(F2m2), op=ALU.max), reads=["Fm2", "F2m2"], writes=["Fm2"])
            A("dve", lambda e: e.tensor_tensor(out=v(Fm2), in0=v(Fm2), in1=v(f02), op=ALU.max), reads=["Fm2", "scst2"], writes=["Fm2"])
            A("dve", lambda e: e.tensor_tensor(out=v(NFm2), in0=v(Am2), in1=v(Fm2), op=ALU.subtract), reads=["Am2", "Fm2"], writes=["NFm2"])
            A("dve", lambda e: e.scalar_tensor_tensor(out=v(nfm2), in0=v(imp2), scalar=1.0, in1=v(NFm2), op0=ALU.add, op1=ALU.mult), reads=["imp2", "NFm2"], writes=["nfm2"])
            A("dve", lambda e: e.tensor_scalar(out=v(nfm2), in0=v(nfm2), scalar1=-1.0, scalar2=None, op0=ALU.add), reads=["nfm2"], writes=["nfm2"])
            A("dve", lambda e: e.tensor_reduce(out=nF2[R4], in_=v(Fm2), axis=AX.X, op=ALU.add), reads=["Fm2"], writes=["nF2"])
            A("dve", lambda e: e.tensor_scalar(out=nF2[R4], in0=nF2[R4], scalar1=-1.0, scalar2=15.0, op0=ALU.mult, op1=ALU.add), reads=["nF2"], writes=["nF2"])
            A("dve", lambda e: e.tensor_scalar(out=oh16b[R4], in0=idx16_s[R4], scalar1=nF2[R4, 0:1], scalar2=None, op0=ALU.is_equal), reads=["nF2", "scst"], writes=["oh16b"])
            A("dve", lambda e: e.max(out=t16b[R4, 0:8], in_=v(nfm2)), reads=["nfm2"], writes=["t16b"])
            A("dve", lambda e: e.match_replace(out=v(nf2m2), in_to_replace=t16b[R4, 0:8], in_values=v(nfm2), imm_value=-2.0), reads=["nfm2", "t16b"], writes=["nf2m2"])
            A("dve", lambda e: e.max(out=t16b[R4, 8:16], in_=v(nf2m2)), reads=["nf2m2"], writes=["t16b"])
            A("dve", lambda e: e.tensor_tensor(out=t16b[R4], in0=t16b[R4], in1=oh16b[R4], op=ALU.mult), reads=["t16b", "oh16b"], writes=["t16b"])
            A("dve", lambda e: e.tensor_reduce(out=thr2[R4], in_=t16b[R4], axis=AX.X, op=ALU.add), reads=["t16b"], writes=["thr2"])
            A("dve", lambda e: e.tensor_scalar(out=v(selm2), in0=v(nfm2), scalar1=thr2[R4, 0:1], scalar2=None, op0=ALU.is_ge), reads=["nfm2", "thr2"], writes=["selm2"])
            A("dve", lambda e: e.tensor_tensor(out=v(selm2), in0=v(selm2), in1=v(NFm2), op=ALU.mult), reads=["selm2", "NFm2"], writes=["selm2"])
            A("dve", lambda e: e.tensor_tensor(out=v(selm2), in0=v(selm2), in1=v(Fm2), op=ALU.add), reads=["selm2", "Fm2"], writes=["selm2"])
            for kch in range(2):
                A("pe", lambda e: e.transpose(out=ps[3][0:128, 0:4], in_=selm2[R4, kch * 128:kch * 128 + 128], identity=ident_s[0:4, 0:4]), reads=["selm2", "scst"], writes=["ps3"])
                for h in range(4):
                    A("act", lambda e: e.copy(out=selT2[g][:, kch, h * 4:(h + 1) * 4], in_=ps[3][0:128, 0:4]), reads=["ps3"], writes=[("selT2", g)])

        def dbg_dump(name, ap, key):
            if DEBUG:
                t = dout("dbg_" + name, list(ap.shape))
                Dm(t, ap, reads=[key], is_output=True)

        for s_ in range(NS_DBG):
            col_s = SEQ + s_ * ST
            for g in range(2):
                Dm(qg[g][0:64, :].rearrange("p (h q) -> p h q", h=4), nsaT[:, 4 * g:4 * g + 4, col_s:col_s + ST], reads=["dram_nsaT"], writes=["qg"])
                Dm(qpad[g][g * 64:(g + 1) * 64, :].rearrange("p (h q) -> p h q", h=4), nsaT[:, 4 * g:4 * g + 4, col_s:col_s + ST], reads=["dram_nsaT"], writes=["qpad"])
                for h in range(4):
                    Dm(gat_s[g][h * 4:(h + 1) * 4, :], gates_tm[col_s:col_s + ST, (4 * g + h) * 3:(4 * g + h) * 3 + 3], reads=["dram_gates"], writes=["gats"])
            for kvi in range(2):
                for g in range(2):
                    A("pool", lambda e: e.memset(w1pad[:, g], 0.0), writes=["regB"])
                    Dm(w1pad[g * 64:(g + 1) * 64, g], phi_w1[kvi].rearrange("j d e -> d j e"), reads=["regB"], writes=["regB"])
                Dm(w2b, phi_w2[kvi], writes=["w2b"])
                for j in range(32):
                    A("pe", lambda e: e.matmul(ps[3][:, 0:1], lhsT=w1pad[0:64, 0, j, :], rhs=peT2[0:64, kvi, j:j + 1], start=(j == 0), stop=(j == 31)),
                      reads=["regB", "peT2"], writes=["ps3"], skip_own=True)
                A("act", lambda e: e.copy(out=pe2, in_=ps[3][:, 0:1]), reads=["ps3"], writes=["pe2"])
                for j in range(NPG):
                    gather_T(cache_cmp, s_, j, kvi, xTs, slice(j * 128, (j + 1) * 128), "xTs")
                for g in range(2):
                    for (n0, ncol) in ((0, 512), (512, 511)):
                        for j in range(32):
                            A("pe", lambda e: e.matmul(ps[2][:, 0:ncol], lhsT=w1pad[:, g, j, :], rhs=xTs[:, 16 * n0 + j:16 * n0 + j + 16 * (ncol - 1) + 1:16],
                                                       start=(j == 0), stop=(j == 31)), reads=["regB", "xTs"], writes=["ps2"], skip_own=True)
                        A("act", lambda e: e.activation(out=hx2[:, 0:ncol], in_=ps[2][:, 0:ncol], func=AF.Identity, bias=pe2[:, 0:1], scale=1.0), reads=["ps2", "pe2"], writes=["hx2"])
                        gelu_to(hx2[:, 0:ncol], ncol)
                        if kvi == 0:
                            A("pe", lambda e: e.matmul(ps[3][0:64, 0:ncol], lhsT=w2b[:, 0:64], rhs=hx2[:, 0:ncol], start=True, stop=True), reads=["w2b", "hx2"], writes=["ps3"])
                            A("act", lambda e: e.copy(out=kcs[0:64, g, n0:n0 + ncol], in_=ps[3][0:64, 0:ncol]), reads=["ps3"] + SK, writes=[("kcs", g)])
                        else:
                            for c4 in range(4):
                                nk = min(128, ncol - c4 * 128)
                                A("pe", lambda e: e.matmul(ps[3][0:nk, 0:64], lhsT=hx2[:, c4 * 128:c4 * 128 + nk], rhs=w2b[:, 0:64], start=True, stop=True), reads=["w2b", "hx2"], writes=["ps3"])
                                A("act", lambda e: e.copy(out=vcs[g][0:nk, n0 // 128 + c4, 0:64], in_=ps[3][0:nk, 0:64]), reads=["ps3"] + SK, writes=[("vcs", g)])
            for g in range(2):
                key_max_s(kcs[:, g, :], 64, 1023, g, ("kcs", g), ones_s[0:64, 0:1])
                set_mrow(g, g)
                for ch in range(8):
                    nk = 128 if ch < 7 else 127
                    attend_s(kcs[0:64, g, ch * 128:ch * 128 + nk], 64, nk, qg[g], g, None, None, vcs[g][0:nk, ch, :], 322, g, ch == 0, ch == 7, ("kcs", g), ("vcs", g))
                finish_s(g, 0, True)
                select_s(g)
            Dm(regB, e2_d, reads=["regB"], writes=["regB"])
            for j in range(NPG):
                gather_T(cache_sel, s_, j, 0, xTs, slice(j * 128, (j + 1) * 128), "xTs")
            for g in range(2):
                Dm(xTs[g * 64:(g + 1) * 64, PAST:PAST + ST], nsaT[:, 10 + g, col_s:col_s + ST], reads=["dram_nsaT"], writes=["xTs"])
            Dm(vnew[0][0:ST, :, 0:64], kv_tm[col_s:col_s + ST, 384:512].rearrange("r (g f) -> r g f", g=2), reads=["dram_kvtm"], writes=["vnew"])
            Dm(vnew[1][0:ST, :, 0:64], kv_tm[col_s:col_s + ST, 640:768].rearrange("r (g f) -> r g f", g=2), reads=["dram_kvtm"], writes=["vnew"])
            for g in range(2):
                key_max_s(xTs, 128, PAST + ST, 2 + g, "xTs", onesg[:, g:g + 1])
                set_mrow(2 + g, g)
            for j in range(NPG):
                i = pcnt[0] % 3
                pcnt[0] += 1
                vi = j % 2
                Gq(pg[i], cache_sel, 1, s_ * 128 + j, reads=["idx"], writes=[("pg", i)])
                A("act", lambda e: e.copy(out=vpg[vi][:, :, 0:64], in_=pg[i].rearrange("p (g f) -> p g f", g=2)), reads=[("pg", i)], writes=[("vpg", vi)])
                kch, cl = j // 64, j % 64
                for g in range(2):
                    mi = e2cnt[0] % 2
                    A("pe", lambda e: e.matmul(ps[2][:, 0:16], lhsT=regB[:, cl * 128:(cl + 1) * 128], rhs=selT2[g][:, kch, :], start=True, stop=True),
                      reads=["regB", ("selT2", g)], writes=["ps2"])
                    A("dve", lambda e: e.tensor_copy(out=msk2[mi], in_=ps[2][:, 0:16]), reads=["ps2"], writes=[("msk2", mi)])
                    attend_s(xTs[:, j * 128:(j + 1) * 128], 128, 128, qpad[g], 2 + g, msk2[mi], ("msk2", mi), vpg[vi][:, g, :], 65, g, j == 0, False, "xTs", ("vpg", vi))
            if s_ == 0:
                dbg_dump("selm2", selm2[0:4, 0:257], "selm2")
                dbg_dump("imp2", imp2[0:4, 0:257], "imp2")
                dbg_dump("kmx", kmx_s[0:1, :], "kmxs")
                dbg_dump("mrow", mrow_s[0:1, :, :], "mrows")
                dbg_dump("selT2", selT2[1][:, :, :], ("selT2", 1))
                dbg_dump("msk", msk2[0][:, :], ("msk2", 0))
                dbg_dump("xTs", xTs[:, PAST - 256:PAST + ST], "xTs")
            for g in range(2):
                attend_s(xTs[:, PAST:PAST + ST], 128, ST, qpad[g], 2 + g, tri16[0:ST, :], "scst", vnew[0][0:ST, g, :], 65, g, NPG == 0, True, "xTs", "vnew")
                finish_s(g, 1, False)
            Dm(wk, win_state[s_].rearrange("(c p) f -> p c f", p=128), writes=["wk"])
            for c4 in range(4):
                A("act", lambda e: e.copy(out=wv[:, c4, :, 0:64], in_=wk[:, c4, 128:256].rearrange("p (g f) -> p g f", g=2)), reads=["wk"], writes=["wv"])
                pb_ = ps[6 + c4 % 2]
                pbk = "ps%d" % (6 + c4 % 2)
                A("pe", lambda e: e.transpose(out=pb_[:, 0:128], in_=wk[:, c4, 0:128], identity=ident_s), reads=["wk", "scst"], writes=[pbk])
                A("dve", lambda e: e.tensor_copy(out=kTw[:, c4 * 128:(c4 + 1) * 128], in_=pb_[:, 0:128]), reads=[pbk], writes=["kTw"])
            for g in range(2):
                Dm(kTw[g * 64:(g + 1) * 64, 512:512 + ST], nsaT[:, 12 + g, col_s:col_s + ST], reads=["dram_nsaT"], writes=["kTw"])
            for g in range(2):
                key_max_s(kTw, 128, 512 + ST, 4 + g, "kTw", onesg[:, g:g + 1])
                set_mrow(4 + g, g)
                for c4 in range(4):
                    attend_s(kTw[:, c4 * 128:(c4 + 1) * 128], 128, 128, qpad[g], 4 + g, ntri16 if c4 == 0 else None, "scst2", wv[:, c4, g, :], 65, g, c4 == 0, False, "kTw", "wv")
                attend_s(kTw[:, 512:512 + ST], 128, ST, qpad[g], 4 + g, tri16[0:ST, :], "scst", vnew[1][0:ST, g, :], 65, g, False, True, "kTw", "vnew")
                finish_s(g, 2, False)
                A("pe", lambda e: e.transpose(out=ps[3][0:64, 0:16], in_=oacc[g][0:16, :], identity=ident_s[0:16, 0:16]), reads=[("oacc", g), "scst"], writes=["ps3"])
                A("act", lambda e: e.copy(out=obT[0:64, :], in_=ps[3][0:64, 0:16]), reads=["ps3"], writes=["obT"])
                for h in range(4):
                    hh = 4 * g + h
                    Dm(yb_fm[(hh % 2) * 64:(hh % 2) * 64 + 64, hh // 2, col_s:col_s + ST], obT[0:64, h * 4:(h + 1) * 4], reads=["obT"], writes=["dram_ybfm"])

    arena_gate()
    wm = arena[:, 0:DC * 2048].rearrange("p (c f) -> p c f", c=DC)
    woa = arena[:, 16384:16384 + 4096].rearrange("p (c f) -> p c f", c=4)
    wob = arena[:, 20480:20480 + 4096].rearrange("p (c f) -> p c f", c=4)
    wo = arena[:, 24576:24576 + 8192].rearrange("p (c f) -> p c f", c=DC)
    for c in range(DC):
        load_weight_bf16(wm[:, c, :], w_in[c * 128:(c + 1) * 128, 3096:5144], 2048, ("wm", c))
        load_weight_bf16(wo[:, c, :], w_o[c * 128:(c + 1) * 128, :], 1024, ("wo", c))
    for c in range(4):
        load_weight_bf16(woa[:, c, :], w_out_a[c * 128:(c + 1) * 128, :], 1024, ("woa", c))
        load_weight_bf16(wob[:, c, :], w_out_b[c * 128:(c + 1) * 128, :], 1024, ("wob", c))
    off[0] = 32768 * 2
    yt_, gt_, bt_, ybt_ = (cv(4 * NT).rearrange("p (c n) -> p c n", c=4) for _ in range(4))
    yab = arena[:, off[0] // 2:off[0] // 2 + 4 * NT].rearrange("p (c n) -> p c n", c=4)
    ybb = arena[:, off[0] // 2 + 4 * NT:off[0] // 2 + 8 * NT].rearrange("p (c n) -> p c n", c=4)
    mmb = arena[:, off[0] // 2 + 8 * NT:off[0] // 2 + 16 * NT].rearrange("p (c n) -> p c n", c=8)
    off[0] += 16 * NT * 2
    gn1, gn2, gn3, gab, gbb = (cv(NT) for _ in range(5))
    x2_v = x2T.rearrange("(c p) n -> p c n", p=128)
    for ti, (c0, n, segs) in enumerate(tl):
        x = xt[ti % 2]
        xk = "xt%d" % (ti % 2)
        P.dma("sp", x[:, :, 0:n], x1_v[:, :, c0:c0 + n], reads=["dram_x1T"], writes=[xk])
        Dm(yt_[:, :, 0:n], y_fm[:, :, c0:c0 + n], reads=["dram_yfm"], writes=["yt"])
        Dm(gt_[:, :, 0:n], scr["g"][:, :, c0:c0 + n], reads=["dram_scr"], writes=["gt"])
        Dm(bt_[:, :, 0:n], scr["bonus"][:, :, c0:c0 + n], reads=["dram_scr"], writes=["bt"])
        Dm(ybt_[:, :, 0:n], yb_fm[:, :, c0:c0 + n], reads=["dram_ybfm"], writes=["ybt"])
        modulate(x, xk, n, segs, 3, ub, "ub")
        P.op("pool", lambda e: e.tensor_scalar(out=x[:, :, 0:n], in0=x[:, :, 0:n], scalar1=ALPHA, scalar2=None, op0=ALU.mult),
             reads=[xk], writes=[xk])
        for j in range(4):
            yj = yt_[:, j, 0:n]
            A("act", lambda e: e.activation(out=gn1[:, 0:n], in_=yj, func=AF.Square), reads=["yt"], writes=["gn1"])
            A("pe", lambda e: e.matmul(ps[0][:, 0:n], lhsT=blk1[:], rhs=yj, start=True, stop=True), reads=["yt", "rwc"], writes=["ps0"])
            A("pe", lambda e: e.matmul(ps[1][:, 0:n], lhsT=blk1[:], rhs=gn1[:, 0:n], start=True, stop=True), reads=["gn1", "rwc"], writes=["ps1"])
            A("act", lambda e: e.mul(out=gn2[:, 0:n], in_=ps[0][:, 0:n], mul=1.0 / 64), reads=["ps0"], writes=["gn2"])
            A("dve", lambda e: e.tensor_tensor(out=gn3[:, 0:n], in0=gn2[:, 0:n], in1=gn2[:, 0:n], op=ALU.mult), reads=["gn2"], writes=["gn3"])
            A("dve", lambda e: e.scalar_tensor_tensor(out=gn3[:, 0:n], in0=ps[1][:, 0:n], scalar=1.0 / 64, in1=gn3[:, 0:n], op0=ALU.mult, op1=ALU.subtract),
              reads=["ps1", "gn3"], writes=["gn3"])
            A("dve", lambda e: e.tensor_scalar(out=gn3[:, 0:n], in0=gn3[:, 0:n], scalar1=64e-5, scalar2=None, op0=ALU.add), reads=["gn3"], writes=["gn3"])
            A("act", lambda e: e.sqrt(out=gn3[:, 0:n], in_=gn3[:, 0:n]), reads=["gn3"], writes=["gn3"])
            A("dve", lambda e: e.reciprocal(out=gn3[:, 0:n], in_=gn3[:, 0:n]), reads=["gn3"], writes=["gn3"])
            A("dve", lambda e: e.tensor_tensor(out=gn1[:, 0:n], in0=yj, in1=gn2[:, 0:n], op=ALU.subtract), reads=["yt", "gn2", "gn1"], writes=["gn1"])
            A("dve", lambda e: e.tensor_tensor(out=gn1[:, 0:n], in0=gn1[:, 0:n], in1=gn3[:, 0:n], op=ALU.mult), reads=["gn1", "gn3"], writes=["gn1"])
            A("act", lambda e: e.activation(out=gn1[:, 0:n], in_=gn1[:, 0:n], func=AF.Identity, scale=rwq_sb[:, j, 5:6], bias=rwq_sb[:, j, 6:7]),
              reads=["gn1", "rwc"], writes=["gn1"])
            A("dve", lambda e: e.tensor_tensor(out=gn1[:, 0:n], in0=gn1[:, 0:n], in1=bt_[:, j, 0:n], op=ALU.add), reads=["gn1", "bt"], writes=["gn1"])
            A("dve", lambda e: e.tensor_tensor(out=yab[:, j, 0:n], in0=gn1[:, 0:n], in1=gt_[:, j, 0:n], op=ALU.mult), reads=["gn1", "gt"], writes=["yab"])
            A("act", lambda e: e.copy(out=ybb[:, j, 0:n], in_=ybt_[:, j, 0:n]), reads=["ybt"], writes=["ybb"])
        for m in range(DC):
            mc = slice(m * 128, (m + 1) * 128)
            for j in range(4):
                A("pe", lambda e: e.matmul(ps[0][:, 0:n], lhsT=woa[:, j, mc], rhs=yab[:, j, 0:n], start=(j == 0), stop=(j == 3)),
                  reads=["yab", ("woa", j), "arena"], writes=["ps0"], skip_own=True)
            for j in range(4):
                A("pe", lambda e: e.matmul(ps[1][:, 0:n], lhsT=wob[:, j, mc], rhs=ybb[:, j, 0:n], start=(j == 0), stop=(j == 3)),
                  reads=["ybb", ("wob", j), "arena"], writes=["ps1"], skip_own=True)
            for c in range(DC):
                A("pe", lambda e: e.matmul(ps[2][:, 0:n], lhsT=wm[:, c, mc], rhs=ub[:, c, 0:n], start=(c == 0), stop=(c == DC - 1)),
                  reads=["ub", ("wm", c), "arena"], writes=["ps2"], skip_own=True)
            for c in range(DC):
                A("pe", lambda e: e.matmul(ps[3][:, 0:n], lhsT=wm[:, c, 1024 + m * 128:1024 + (m + 1) * 128], rhs=ub[:, c, 0:n], start=(c == 0), stop=(c == DC - 1)),
                  reads=["ub", ("wm", c), "arena"], writes=["ps3"], skip_own=True)
            A("act", lambda e: e.activation(out=gab[:, 0:n], in_=ps[2][:, 0:n], func=AF.Sigmoid, bias=bm_sb[:, m:m + 1], scale=1.0), reads=["ps2", "rwc"], writes=["gab"])
            A("act", lambda e: e.activation(out=gbb[:, 0:n], in_=ps[3][:, 0:n], func=AF.Sigmoid, bias=bm_sb[:, 8 + m:9 + m], scale=1.0), reads=["ps3", "rwc"], writes=["gbb"])
            A("dve", lambda e: e.tensor_tensor(out=gab[:, 0:n], in0=gab[:, 0:n], in1=ps[0][:, 0:n], op=ALU.mult), reads=["gab", "ps0"], writes=["gab"])
            A("dve", lambda e: e.tensor_tensor(out=gbb[:, 0:n], in0=gbb[:, 0:n], in1=ps[1][:, 0:n], op=ALU.mult), reads=["gbb", "ps1"], writes=["gbb"])
            A("dve", lambda e: e.tensor_tensor(out=mmb[:, m, 0:n], in0=gab[:, 0:n], in1=gbb[:, 0:n], op=ALU.add), reads=["gab", "gbb"], writes=["mmb"])
        for m in range(DC):
            py = ps[4 + (m % 2)]
            ky = "ps%d" % (4 + (m % 2))
            for c in range(DC):
                A("pe", lambda e: e.matmul(py[:, 0:n], lhsT=wo[:, c, m * 128:(m + 1) * 128], rhs=mmb[:, c, 0:n], start=(c == 0), stop=(c == DC - 1)),
                  reads=["mmb", ("wo", c), "arena"], writes=[ky], skip_own=True)
            for (lo, hi, s_) in segs:
                P.op("dve", lambda e: e.scalar_tensor_tensor(out=x[:, m, lo:hi], in0=py[:, lo:hi], scalar=modS[:, s_, 5, m:m + 1], in1=x[:, m, lo:hi],
                                                             op0=ALU.mult, op1=ALU.add), reads=[ky, xk, "modS"], writes=[xk])
        layer_norm_tile(x, xk, n, 1, x, xk)
        P.dma("pool", x2_v[:, :, c0:c0 + n], x[:, :, 0:n], reads=[xk], writes=["dram_x2T"])

    ffn_phase(1, x2T, yT, 2, 6, True)

    P.emit()
    es.close()
    return nc


_NC_CACHE = {}


def kernel(**inp):
    f = lambda a: np.ascontiguousarray(np.asarray(a), dtype=np.float32)
    x_prompt, x_sample = f(inp["x_prompt"]), f(inp["x_sample"])
    c_prompt, c_sample = f(inp["c_prompt"]), f(inp["c_sample"])
    w_ada = f(inp["w_ada"])[0]
    b_ada = f(inp["b_ada"])[0].reshape(72, 128).T.copy()
    ln_g = f(inp["ln_g"])[0].reshape(3, DC, 128).transpose(2, 0, 1).copy()
    ln_b = f(inp["ln_b"])[0].reshape(3, DC, 128).transpose(2, 0, 1).copy()
    w_gate, w_up, w_down = f(inp["ffn_w_gate"])[0], f(inp["ffn_w_up"])[0], f(inp["ffn_w_down"])[0]
    w_in = f(inp["w_in"])[0]
    b_in_bc = np.ascontiguousarray(np.broadcast_to(f(inp["b_in"])[0][None, :], (128, N_IN)))
    state_kv_win = f(inp["state_kv_win"])[0].reshape(32, 512, 256)

    CHT = [(j * 128, 128) for j in range(12)] + [(1536, 64), (1600, 64), (1664, 128)]

    def chunked(vec):
        vec = np.asarray(vec)
        lead = vec.shape[:-1]
        out = np.zeros((128, 15) + lead, np.float32)
        for ci, (c0_, w_) in enumerate(CHT):
            out[:w_, ci] = np.moveaxis(vec[..., c0_:c0_ + w_], -1, 0)
        return out

    b_in = f(inp["b_in"])[0]
    rwp = np.ascontiguousarray(np.stack([chunked(b_in[:RW_SHIFT]), chunked(f(inp["rw_mu"])[0])], axis=-1))
    q7 = [f(inp[k])[0].reshape(512) for k in ("rw_w0", "rw_a0", "rw_k_k", "rw_k_a", "rw_r_k", "rw_ln_w", "rw_ln_b")]
    rwq = np.ascontiguousarray(np.stack([v.reshape(4, 128).T for v in q7], axis=-1))
    rw_w2, rw_a2, rw_g2 = f(inp["rw_w2"])[0], f(inp["rw_a2"])[0], f(inp["rw_g2"])[0]
    state_shift = f(inp["state_shift"])[0]
    state_wkv = f(inp["state_wkv"])[0]
    pidx = np.arange(128)
    blk1 = (pidx[:, None] // 64 == pidx[None, :] // 64).astype(np.float32)
    istk = (pidx[:, None] % 64 == np.arange(64)[None, :]).astype(np.float32)

    nsab_cols = [1792 + 64 * h for h in range(8)] + [2304 + br * 256 + g * 64 for br in range(3) for g in range(2)] \
        + [2304 + 128 + g * 64 for g in range(2)]
    nsab = np.ascontiguousarray(np.stack([b_in[c0_:c0_ + 64] * (0.125 if ci < 8 else 1.0) for ci, c0_ in enumerate(nsab_cols)], axis=1))
    phi_w1, phi_w2 = f(inp["nsa_phi_w1"])[0], f(inp["nsa_phi_w2"])[0]
    peT = np.ascontiguousarray(f(inp["nsa_phi_pe"])[0].transpose(2, 0, 1))
    tri = (pidx[:, None] <= pidx[None, :]).astype(np.float32)
    nn = np.arange(256).reshape(2, 128)
    cvalT = np.ascontiguousarray((16.0 * nn.T[:, :, None] + 31.0 - pidx[None, None, :]).astype(np.float32))
    jidx = np.ascontiguousarray(np.broadcast_to(np.arange(64, dtype=np.float32)[None, :], (128, 64)))
    curb = (pidx[:, None] >= 64).astype(np.float32)
    eall = (np.arange(SEQ)[None, :] // 64 == np.arange(64)[:, None]).astype(np.float32)
    nidx = np.arange(256)[:, None]
    jj = np.arange(64)[None, :]
    bimp_full = ((nidx >= 4 * jj - 1) & (nidx <= 4 * jj + 3) & (nidx < 255)).astype(np.float32)
    bimp = np.ascontiguousarray(bimp_full.reshape(2, 128, 64).transpose(1, 0, 2))
    idx16 = np.ascontiguousarray(np.broadcast_to(np.arange(16, dtype=np.float32)[None, :], (128, 16)))
    ident = np.eye(128, dtype=np.float32)
    bm = np.ascontiguousarray(b_in[3096:5144].reshape(16, 128).T)
    w_out_a, w_out_b, w_o = f(inp["w_out_a"])[0], f(inp["w_out_b"])[0], f(inp["w_o"])[0]

    cache_cmp = f(inp["cache_kv_cmp"])[0].reshape(-1, 128)
    cache_sel = f(inp["cache_kv_sel"])[0].reshape(-1, 128)
    page_table = np.ascontiguousarray(np.asarray(inp["page_table"]), dtype=np.int32)
    n1 = np.arange(1024)[:, None]
    j1 = np.arange(257)[None, :]
    bimps_full = ((n1 >= 4 * j1 - 1) & (n1 <= 4 * j1 + 3) & (n1 < 1023)).astype(np.float32)
    bimps = np.ascontiguousarray(bimps_full.reshape(8, 128, 257).transpose(1, 0, 2))
    bb = np.arange(128)[:, None]
    kk8 = np.arange(8192)[None, :]
    e2 = (bb == 2 * (kk8 // 128) + (kk8 % 128) // 64).astype(np.float32)
    hsel = (np.arange(16)[:, None] % 4 == np.arange(4)[None, :]).astype(np.float32)
    pcol = pidx[:, None].astype(np.float32)
    jidx2 = np.ascontiguousarray(np.broadcast_to(np.arange(264, dtype=np.float32)[None, :], (128, 264)))
    tri16 = (pidx[:, None] <= (np.arange(16)[None, :] % 4)).astype(np.float32)

    if "nc" not in _NC_CACHE:
        _NC_CACHE["nc"] = build()
    nc = _NC_CACHE["nc"]

    in_maps = []
    for i in range(8):
        b = i // 2
        xs = x_sample[4 * i:4 * i + 4].reshape(NS * ST, D)
        xT = np.ascontiguousarray(np.concatenate([x_prompt[b], xs], axis=0).T)
        cv = np.concatenate([c_prompt[b:b + 1], c_sample[4 * i:4 * i + 4]], axis=0)
        cT = np.ascontiguousarray(cv.T.reshape(DC, 128, NSEQ).transpose(1, 0, 2))
        in_maps.append(dict(xT=xT, cT=cT, w_ada=w_ada, b_ada=b_ada, ln_g=ln_g, ln_b=ln_b, w_gate=w_gate, w_up=w_up,
                            w_down=w_down, w_in=w_in, b_in_bc=b_in_bc,
                            win_state=np.ascontiguousarray(state_kv_win[4 * i:4 * i + 4]),
                            rwp=rwp, rwq=rwq, rw_w2=rw_w2, rw_a2=rw_a2, rw_g2=rw_g2,
                            shs=np.ascontiguousarray(chunked(state_shift[4 * i:4 * i + 4])),
                            wkv0=np.ascontiguousarray(state_wkv[4 * i:4 * i + 4]), blk1=blk1, istk=istk,
                            nsab=nsab, phi_w1=phi_w1, phi_w2=phi_w2, peT=peT, tri=tri, cvalT=cvalT, jidx=jidx, curb=curb,
                            eall=eall, bimp=bimp, idx16=idx16, ident=ident, bm=bm, w_out_a=w_out_a, w_out_b=w_out_b, w_o=w_o,
                            cache_cmp=cache_cmp, cache_sel=cache_sel,
                            ptab=np.ascontiguousarray(page_table[4 * i:4 * i + 4].reshape(1, NS * 128)),
                            bimps=bimps, e2=e2, hsel=hsel, pcol=pcol, jidx2=jidx2, tri16=tri16))
    res = run_bass_kernel_spmd(nc, in_maps, core_ids=list(range(8)))
    R = res.results

    y_prompt = np.stack([R[2 * b]["yT"][:, :SEQ].T for b in range(4)])
    y_sample = np.concatenate([R[i]["yT"][:, SEQ:].T.reshape(NS, ST, D) for i in range(8)])
    kvshape = lambda a: a.reshape(a.shape[0], 2, 2, 64)
    kvc_p = np.stack([kvshape(R[2 * b]["o_kvc"][:SEQ]) for b in range(4)])[None]
    kvs_p = np.stack([kvshape(R[2 * b]["o_kvs"][:SEQ]) for b in range(4)])[None]
    kvw_p = np.stack([kvshape(R[2 * b]["o_kvw_p"]) for b in range(4)])[None]
    wkv_p = np.stack([R[2 * b]["o_wkv"][0] for b in range(4)])[None]
    sh_p = np.stack([R[2 * b]["o_shift"][0] for b in range(4)])[None]
    kvc_s = np.concatenate([R[i]["o_kvc"][SEQ:].reshape(NS, ST, 2, 2, 64) for i in range(8)])[None]
    kvs_s = np.concatenate([R[i]["o_kvs"][SEQ:].reshape(NS, ST, 2, 2, 64) for i in range(8)])[None]
    kvw_s = np.concatenate([R[i]["o_kvw_s"].reshape(NS, 512, 2, 2, 64) for i in range(8)])[None]
    wkv_s = np.concatenate([R[i]["o_wkv"][1:] for i in range(8)])[None]
    sh_s = np.concatenate([R[i]["o_shift"][1:] for i in range(8)])[None]
    outs = (y_prompt, y_sample, kvc_p, kvs_p, kvw_p, wkv_p, sh_p, kvc_s, kvs_s, kvw_s, wkv_s, sh_s)
    return tuple(np.ascontiguousarray(o, dtype=np.float32) for o in outs)
```

```python
import contextlib
import numpy as np
import concourse.bass as bass
import concourse.mybir as mybir
from concourse.bass_utils import run_bass_kernel_spmd

F32 = mybir.dt.float32
BF16 = mybir.dt.bfloat16
I32 = mybir.dt.int32
AF = mybir.ActivationFunctionType
ALU = mybir.AluOpType
AX = mybir.AxisListType

D = 1024
DC = 8
DFF = 2816
FC = 22
SEQ = 4096
NS = 4
ST = 4
TT = SEQ + NS * ST
NSEQ = 1 + NS
RW_SHIFT = 1792
N_IN = 5144
ALPHA = 2 ** 0.25
LN_EPS = 1e-5
NT = 256
DEBUG = False
SEQ_SCAN = SEQ
SEQ_NSA = SEQ
DO_SAMPLE = True
NPG_DBG = 128
NS_DBG = NS

ENGS = ("pe", "act", "dve", "pool", "sp")


class _Rec:
    def __getattr__(self, name):
        def f(*a, **k):
            self.call = (name, a, k)
            return self
        return f


class Prog:
    def __init__(self, nc, n_dma_sems=32):
        self.nc = nc
        self.ops = {e: [] for e in ENGS}
        self.cnt = {e: 0 for e in ENGS}
        self.last_w = {}
        self.readers = {}
        self.seen = {e: {} for e in ENGS}
        self.n_dma = n_dma_sems
        self.dma_i = 0
        self.dma_cnt = [0] * n_dma_sems
        self.out_tokens = []

    def _deps(self, reads, writes):
        deps = []
        for k in reads:
            t = self.last_w.get(k)
            if t is not None:
                deps.append(t)
        for k in writes:
            t = self.last_w.get(k)
            if t is not None:
                deps.append(t)
            deps.extend(self.readers.get(k, ()))
        return deps

    def _commit(self, tok, reads, writes):
        for k in reads:
            self.readers.setdefault(k, []).append(tok)
        for k in writes:
            self.last_w[k] = tok
            self.readers[k] = []

    def _waits(self, eng, deps, skip_own=False):
        need = {}
        for (s, v) in deps:
            if skip_own and s == ("c", eng):
                continue
            if self.seen[eng].get(s, 0) >= v:
                continue
            if need.get(s, 0) < v:
                need[s] = v
        for s, v in need.items():
            self.seen[eng][s] = v
        return list(need.items())

    def op(self, eng, fn, reads=(), writes=(), skip_own=False):
        rec = _Rec()
        fn(rec)
        fn = rec.call
        deps = self._deps(reads, writes)
        waits = self._waits(eng, deps, skip_own)
        self.cnt[eng] += 1
        tok = (("c", eng), self.cnt[eng])
        self.ops[eng].append(("c", fn, waits, tok))
        self._commit(tok, reads, writes)
        return tok

    def dma(self, eng, out, in_, reads=(), writes=(), is_output=False, **kw):
        deps = self._deps(reads, writes)
        si = self.dma_i % self.n_dma
        self.dma_i += 1
        sem = ("d", si)
        prev = self.dma_cnt[si]
        if prev > 0:
            deps.append((sem, 16 * prev))
        waits = self._waits(eng, deps)
        self.dma_cnt[si] += 1
        tok = (sem, 16 * self.dma_cnt[si])
        self.ops[eng].append(("d", (out, in_, kw), waits, tok))
        self._commit(tok, reads, writes)
        if is_output:
            self.out_tokens.append(tok)
        return tok

    def emit(self):
        nc = self.nc
        with contextlib.ExitStack() as es:
            sems = {}
            for e in ENGS:
                sems[("c", e)] = es.enter_context(nc.semaphore("c_" + e))
            for i in range(self.n_dma):
                sems[("d", i)] = es.enter_context(nc.semaphore("d_%d" % i))
            final = {}
            for (s, v) in self.out_tokens:
                final[s] = max(final.get(s, 0), v)
            block = es.enter_context(nc.Block())
            engobj = {"pe": "tensor", "act": "scalar", "dve": "vector", "pool": "gpsimd", "sp": "sync"}

            def run(e, eng):
                for kind, payload, waits, tok in self.ops[e]:
                    for (s, v) in waits:
                        eng.wait_ge(sems[s], v)
                    if kind == "c":
                        name, a, k = payload
                        getattr(eng, name)(*a, **k).then_inc(sems[tok[0]], 1)
                    else:
                        out, in_, kw = payload
                        if "gather_idx" in kw:
                            eng.indirect_dma_start(out=out, out_offset=None, in_=in_,
                                                   in_offset=bass.IndirectOffsetOnAxis(ap=kw["gather_idx"], axis=0),
                                                   element_offset=kw.get("element_offset", 0)
                                                   ).then_inc(sems[tok[0]], 16)
                        else:
                            eng.dma_start(out=out, in_=in_, **kw).then_inc(sems[tok[0]], 16)
                if e == "sp":
                    for s, v in final.items():
                        eng.wait_ge(sems[s], v)

            for e in ENGS:
                getattr(block, engobj[e])(lambda eng, e=e: run(e, eng))


def tiles():
    out = []
    for i in range(SEQ // NT):
        out.append((i * NT, NT, [(0, NT, 0)]))
    out.append((SEQ, NS * ST, [(s * ST, (s + 1) * ST, 1 + s) for s in range(NS)]))
    return out


def build():
    nc = bass.Bass("TRN2", target_bir_lowering=False)
    P = Prog(nc)
    es = contextlib.ExitStack()

    def din(name, shape, dt=F32):
        return nc.dram_tensor(name, list(shape), dt, kind="ExternalInput").ap()

    def dout(name, shape, dt=F32):
        return nc.dram_tensor(name, list(shape), dt, kind="ExternalOutput").ap()

    def dscr(name, shape, dt=F32):
        return nc.dram_tensor(name, list(shape), dt, kind="Internal").ap()

    def sb(name, shape, dt=F32):
        return es.enter_context(nc.sbuf_tensor(name, list(shape), dt))

    xT = din("xT", [D, TT])
    cT = din("cT", [128, DC, NSEQ])
    w_ada = din("w_ada", [D, 9 * D])
    b_ada = din("b_ada", [128, 72])
    ln_g = din("ln_g", [128, 3, DC])
    ln_b = din("ln_b", [128, 3, DC])
    w_gate = din("w_gate", [2, D, DFF])
    w_up = din("w_up", [2, D, DFF])
    w_down = din("w_down", [2, DFF, D])
    w_in = din("w_in", [D, N_IN])
    b_in_bc = din("b_in_bc", [128, N_IN])
    win_state = din("win_state", [NS, 512, 256])

    rwp = din("rwp", [128, 15, 2])
    rwq = din("rwq", [128, 4, 7])
    w2_d = din("rw_w2", [64, 512])
    a2_d = din("rw_a2", [64, 512])
    g2_d = din("rw_g2", [128, 512])
    shs = din("shs", [128, 15, NS])
    wkv0 = din("wkv0", [NS, 8, 64, 64])
    blk1_d = din("blk1", [128, 128])
    istk_d = din("istk", [128, 64])

    nsab = din("nsab", [64, 16])
    phi_w1 = din("phi_w1", [2, 32, 64, 128])
    phi_w2 = din("phi_w2", [2, 128, 64])
    peT_d = din("peT", [64, 2, 32])
    tri_d = din("tri", [128, 128])
    cvalT_d = din("cvalT", [128, 2, 128])
    jidx_d = din("jidx", [128, 64])
    curb_d = din("curb", [128, 1])
    eall_d = din("eall", [64, SEQ])
    bimp_d = din("bimp", [128, 2, 64])
    idx16_d = din("idx16", [128, 16])
    ident_d = din("ident", [128, 128])
    bm_d = din("bm", [128, 16])
    w_out_a = din("w_out_a", [512, D])
    w_out_b = din("w_out_b", [512, D])
    w_o = din("w_o", [D, D])

    NPHYS = 5120
    cache_cmp = din("cache_cmp", [NPHYS * 256, 128])
    cache_sel = din("cache_sel", [NPHYS * 256, 128])
    ptab = din("ptab", [1, NS * 128], I32)
    bimps_d = din("bimps", [128, 8, 257])
    e2_d = din("e2", [128, 8192])
    hsel_d = din("hsel", [16, 4])
    pcol_d = din("pcol", [128, 1])
    jidx2_d = din("jidx2", [128, 264])
    tri16_d = din("tri16", [128, 16])

    yT = dout("yT", [D, TT])
    o_kvc = dout("o_kvc", [TT, 256])
    o_kvs = dout("o_kvs", [TT, 256])
    o_kvw_p = dout("o_kvw_p", [512, 256])
    o_kvw_s = dout("o_kvw_s", [NS, 512, 256])
    o_shift = dout("o_shift", [NSEQ, RW_SHIFT])
    o_wkv = dout("o_wkv", [NSEQ, 8, 64, 64])

    x1T = (dout if DEBUG else dscr)("x1T", [D, TT])
    x2T = (dout if DEBUG else dscr)("x2T", [D, TT])

    RWN = ("nkk", "w", "b", "kp", "r", "v", "g", "bonus")
    scr = {k: (dout if DEBUG else dscr)("scr_" + k, [128, 4, TT]) for k in RWN}
    y_fm = (dout if DEBUG else dscr)("y_fm", [128, 4, TT])
    yb_fm = (dout if DEBUG else dscr)("yb_fm", [128, 4, TT])
    nsaT = dscr("nsaT", [64, 16, TT])
    kv_tm = dscr("kv_tm", [TT, 768])
    gates_tm = dscr("gates_tm", [TT, 24])

    arena = sb("arena", [128, 3 * DC * DFF], BF16)
    stage = [sb("stage%d" % i, [128, DFF], F32) for i in range(2)]
    xt = [sb("xt%d" % i, [128, DC, NT], F32) for i in range(2)]
    ub = sb("ub", [128, DC, NT], BF16)
    hT = sb("hT", [128, FC, NT], BF16)
    tmpa = [sb("tmpa%d" % i, [128, NT], F32) for i in range(2)]
    tmpb = [sb("tmpb%d" % i, [128, NT], F32) for i in range(2)]
    lnt = {k: sb("ln_" + k, [128, NT], F32) for k in ("mu", "musq", "var", "rstd")}
    ones = sb("ones", [128, 128], F32)
    modS = sb("modS", [128, NSEQ, 9, DC], F32)
    ct_sb = sb("ct_sb", [128, DC, NSEQ], F32)
    bada_sb = sb("bada_sb", [128, 72], F32)
    lng_sb = sb("lng_sb", [128, 3, DC], F32)
    lnb_sb = sb("lnb_sb", [128, 3, DC], F32)
    def carve(off_bytes, ncols):
        a = off_bytes // 2
        return arena[:, a:a + 2 * ncols].bitcast(F32)

    CV0 = DC * 3096 * 2
    kvt = [carve(CV0 + i * 792 * 4, 792) for i in range(2)]
    rwt = carve(CV0 + 2 * 792 * 4, RW_SHIFT)
    bias_tm = carve(CV0 + 2 * 792 * 4 + RW_SHIFT * 4, 792 + RW_SHIFT)

    rwp_sb = sb("rwp_sb", [128, 15, 2])
    rwq_sb = sb("rwq_sb", [128, 4, 7])
    w2_sb = sb("w2_sb", [128, 512])
    a2_sb = sb("a2_sb", [128, 512])
    g2_sb = sb("g2_sb", [128, 512])
    nsab_sb = sb("nsab_sb", [64, 16])
    bm_sb = sb("bm_sb", [128, 16])
    blk1 = sb("blk1_sb", [128, 128])
    istk = sb("istk_sb", [128, 64])
    ps = [es.enter_context(nc.psum_tensor("ps%d" % i, [128, 512], F32)) for i in range(8)]

    P.op("pool", lambda e: e.memset(ones[:], 1.0), writes=["ones"])
    P.dma("sp", ct_sb[:], cT, writes=["ct"])
    P.dma("sp", bada_sb[:], b_ada, writes=["bada"])
    P.dma("sp", lng_sb[:], ln_g, writes=["lng"])
    P.dma("sp", lnb_sb[:], ln_b, writes=["lnb"])
    P.dma("sp", rwp_sb[:], rwp, writes=["rwc"])
    P.dma("sp", rwq_sb[:], rwq, writes=["rwc"])
    P.dma("sp", w2_sb[0:64, :], w2_d, writes=["rwc"])
    P.dma("sp", a2_sb[0:64, :], a2_d, writes=["rwc"])
    P.dma("sp", g2_sb[:], g2_d, writes=["rwc"])
    P.dma("sp", nsab_sb[:], nsab, writes=["rwc"])
    P.dma("sp", bm_sb[:], bm_d, writes=["rwc"])
    P.dma("sp", blk1[:], blk1_d, writes=["rwc"])
    P.dma("sp", istk[:], istk_d, writes=["rwc"])

    P.op("act", lambda e: e.activation(out=ct_sb[:], in_=ct_sb[:], func=AF.Silu), reads=["ct"], writes=["ct"])
    wada_v = w_ada.rearrange("(c p) n -> p c n", p=128)
    mod_ps = ps[0]
    GW = 256
    for gidx in range(9216 // GW):
        st = stage[gidx % 2]
        sk = "stage%d" % (gidx % 2)
        stv = st[:, 0:DC * GW].rearrange("p (c n) -> p c n", c=DC)
        P.dma("sp", stv, wada_v[:, :, gidx * GW:(gidx + 1) * GW], writes=[sk])
        for j in range(GW // 128):
            ch = gidx * (GW // 128) + j
            for c in range(DC):
                P.op("pe", lambda e, stv=stv, j=j, ch=ch, c=c: e.matmul(
                    mod_ps[:, ch * NSEQ:(ch + 1) * NSEQ], lhsT=stv[:, c, j * 128:(j + 1) * 128], rhs=ct_sb[:, c, :],
                    start=(c == 0), stop=(c == DC - 1)),
                    reads=[sk, "ct"], writes=["ps0"], skip_own=True)
    P.op("dve", lambda e: e.tensor_tensor(
        out=modS[:].rearrange("p s k c -> p (k c) s"),
        in0=mod_ps[:, 0:72 * NSEQ].rearrange("p (ch s) -> p ch s", s=NSEQ),
        in1=bada_sb[:].unsqueeze(2).to_broadcast([128, 72, NSEQ]), op=ALU.add),
        reads=["ps0", "bada"], writes=["modS"])
    for k, (mulv, addv) in {1: (1.0, 1.0), 4: (1.0, 1.0), 7: (1.0, 1.0), 2: (0.5, 0.5), 8: (0.5, 0.5), 5: (1.0, 1.0)}.items():
        P.op("dve", lambda e, k=k, mulv=mulv, addv=addv: e.tensor_scalar(
            out=modS[:, :, k, :], in0=modS[:, :, k, :], scalar1=mulv, scalar2=addv, op0=ALU.mult, op1=ALU.add),
            reads=["modS"], writes=["modS"])

    cast_rr = [0]

    gate_sb = sb("gate_sb", [128, 4], F32)

    def arena_gate():
        for gi_, eng in enumerate(("pool", "dve", "act")):
            if eng == "act":
                P.op(eng, lambda e, gi_=gi_: e.memzero(gate_sb[:, gi_:gi_ + 1]), writes=["arena", "hT", "stage0", "stage1", "xt0", "xt1", ("gate", gi_)])
            else:
                P.op(eng, lambda e, gi_=gi_: e.memset(gate_sb[:, gi_:gi_ + 1], 0.0), writes=["arena", "hT", "stage0", "stage1", "xt0", "xt1", ("gate", gi_)])

    def load_weight_bf16(dst_ap, src_ap, ncols, dkey):
        i = cast_rr[0]
        cast_rr[0] += 1
        st = stage[i % 2]
        sk = "stage%d" % (i % 2)
        P.dma("sp", st[:, 0:ncols], src_ap, writes=[sk])
        eng = ("pool", "dve", "act")[i % 3]
        if eng == "act":
            P.op("act", lambda e: e.copy(out=dst_ap, in_=st[:, 0:ncols]), reads=[sk], writes=[dkey])
        else:
            P.op(eng, lambda e: e.tensor_copy(out=dst_ap, in_=st[:, 0:ncols]), reads=[sk], writes=[dkey])

    def layer_norm_tile(z, zk, n, gi, out, outk):
        s1, s2 = ps[6], ps[7]
        for c in range(DC):
            tq = tmpa[c % 2]
            P.op("act", lambda e, c=c, tq=tq: e.activation(out=tq[:, 0:n], in_=z[:, c, 0:n], func=AF.Square),
                 reads=[zk], writes=["tmpa%d" % (c % 2)])
            P.op("pe", lambda e, c=c: e.matmul(s1[:, 0:n], lhsT=ones[:], rhs=z[:, c, 0:n], start=(c == 0), stop=(c == DC - 1)),
                 reads=[zk, "ones"], writes=["ps6"], skip_own=True)
            P.op("pe", lambda e, c=c, tq=tq: e.matmul(s2[:, 0:n], lhsT=ones[:], rhs=tq[:, 0:n], start=(c == 0), stop=(c == DC - 1)),
                 reads=["tmpa%d" % (c % 2), "ones"], writes=["ps7"], skip_own=True)
        mu, musq, var, rstd = lnt["mu"], lnt["musq"], lnt["var"], lnt["rstd"]
        P.op("act", lambda e: e.mul(out=mu[:, 0:n], in_=s1[:, 0:n], mul=1.0 / D), reads=["ps6"], writes=["ln_mu"])
        P.op("dve", lambda e: e.tensor_tensor(out=musq[:, 0:n], in0=mu[:, 0:n], in1=mu[:, 0:n], op=ALU.mult),
             reads=["ln_mu"], writes=["ln_musq"])
        P.op("dve", lambda e: e.scalar_tensor_tensor(out=var[:, 0:n], in0=s2[:, 0:n], scalar=1.0 / D, in1=musq[:, 0:n],
                                                     op0=ALU.mult, op1=ALU.subtract),
             reads=["ps7", "ln_musq"], writes=["ln_var"])
        P.op("dve", lambda e: e.tensor_scalar(out=var[:, 0:n], in0=var[:, 0:n], scalar1=LN_EPS, scalar2=None,
                                              op0=ALU.add), reads=["ln_var"], writes=["ln_var"])
        P.op("act", lambda e: e.sqrt(out=var[:, 0:n], in_=var[:, 0:n]), reads=["ln_var"], writes=["ln_var"])
        P.op("dve", lambda e: e.reciprocal(out=rstd[:, 0:n], in_=var[:, 0:n]), reads=["ln_var"], writes=["ln_rstd"])
        for c in range(DC):
            ta = tmpb[c % 2]
            tk = "tmpb%d" % (c % 2)
            P.op("dve", lambda e, c=c, ta=ta: e.tensor_tensor(out=ta[:, 0:n], in0=z[:, c, 0:n], in1=mu[:, 0:n], op=ALU.subtract),
                 reads=[zk, "ln_mu"], writes=[tk])
            P.op("pool", lambda e, c=c, ta=ta: e.tensor_tensor(out=ta[:, 0:n], in0=ta[:, 0:n], in1=rstd[:, 0:n], op=ALU.mult),
                 reads=[tk, "ln_rstd"], writes=[tk])
            P.op("act", lambda e, c=c, ta=ta: e.activation(out=out[:, c, 0:n], in_=ta[:, 0:n], func=AF.Identity,
                                                           scale=lng_sb[:, gi, c:c + 1], bias=lnb_sb[:, gi, c:c + 1]),
                 reads=[tk, "lng", "lnb"], writes=[outk])

    def modulate(x, xk, n, segs, kshift, dst, dk):
        for c in range(DC):
            for (lo, hi, s) in segs:
                P.op("act", lambda e, c=c, lo=lo, hi=hi, s=s: e.activation(
                    out=dst[:, c, lo:hi], in_=x[:, c, lo:hi], func=AF.Identity,
                    scale=modS[:, s, kshift + 1, c:c + 1], bias=modS[:, s, kshift, c:c + 1]),
                    reads=[xk, "modS"], writes=[dk])

    def ffn_phase(fi, src, dst, gi, kmod, dst_is_output):
        wg = arena[:, 0:DC * DFF].rearrange("p (c f) -> p c f", c=DC)
        wu = arena[:, DC * DFF:2 * DC * DFF].rearrange("p (c f) -> p c f", c=DC)
        wd = arena[:, 2 * DC * DFF:3 * DC * DFF].rearrange("p (f d) -> p f d", f=FC)
        arena_gate()
        for c in range(DC):
            load_weight_bf16(wg[:, c, :], w_gate[fi, c * 128:(c + 1) * 128, :], DFF, ("wg", fi, c))
            load_weight_bf16(wu[:, c, :], w_up[fi, c * 128:(c + 1) * 128, :], DFF, ("wu", fi, c))
        for f2 in range(FC // 2):
            i = cast_rr[0]
            cast_rr[0] += 1
            st = stage[i % 2]
            sk = "stage%d" % (i % 2)
            P.dma("sp", st[:, 0:2 * D].rearrange("p (a d) -> p a d", a=2),
                  w_down[fi, f2 * 256:(f2 + 1) * 256, :].rearrange("(a p) d -> p a d", p=128), writes=[sk])
            eng = ("pool", "dve")[i % 2]
            P.op(eng, lambda e, st=st, f2=f2: e.tensor_copy(
                out=wd[:, 2 * f2:2 * f2 + 2, :], in_=st[:, 0:2 * D].rearrange("p (a d) -> p a d", a=2)),
                reads=[sk], writes=[("wd", fi, f2)])
        src_v = src.rearrange("(c p) n -> p c n", p=128)
        dst_v = dst.rearrange("(c p) n -> p c n", p=128)
        tl = tiles()
        P.dma("sp", xt[0][:, :, 0:tl[0][1]], src_v[:, :, tl[0][0]:tl[0][0] + tl[0][1]], writes=["xt0"])
        for ti, (c0, n, segs) in enumerate(tl):
            x = xt[ti % 2]
            xk = "xt%d" % (ti % 2)
            if ti + 1 < len(tl):
                c0n, nn, _ = tl[ti + 1]
                P.dma("sp", xt[(ti + 1) % 2][:, :, 0:nn], src_v[:, :, c0n:c0n + nn], writes=["xt%d" % ((ti + 1) % 2)])
            modulate(x, xk, n, segs, kmod, ub, "ub")
            P.op("pool", lambda e, x=x, n=n: e.tensor_scalar(out=x[:, :, 0:n], in0=x[:, :, 0:n], scalar1=ALPHA, scalar2=None,
                                                            op0=ALU.mult), reads=[xk], writes=[xk])
            for f in range(FC):
                pg, pu = ps[(2 * f) % 4], ps[(2 * f + 1) % 4]
                kg, ku = "ps%d" % ((2 * f) % 4), "ps%d" % ((2 * f + 1) % 4)
                for c in range(DC):
                    P.op("pe", lambda e, c=c, f=f, pg=pg: e.matmul(pg[:, 0:n], lhsT=wg[:, c, f * 128:(f + 1) * 128], rhs=ub[:, c, 0:n],
                                                                 start=(c == 0), stop=(c == DC - 1)),
                         reads=["arena", ("wg", fi, c), "ub"], writes=[kg], skip_own=True)
                for c in range(DC):
                    P.op("pe", lambda e, c=c, f=f, pu=pu: e.matmul(pu[:, 0:n], lhsT=wu[:, c, f * 128:(f + 1) * 128], rhs=ub[:, c, 0:n],
                                                                 start=(c == 0), stop=(c == DC - 1)),
                         reads=["arena", ("wu", fi, c), "ub"], writes=[ku], skip_own=True)
                tq = tmpa[f % 2]
                tk = "tmpa%d" % (f % 2)
                P.op("act", lambda e, pg=pg, tq=tq: e.activation(out=tq[:, 0:n], in_=pg[:, 0:n], func=AF.Silu),
                     reads=[kg], writes=[tk])
                P.op("dve", lambda e, pu=pu, tq=tq, f=f: e.tensor_tensor(out=hT[:, f, 0:n], in0=tq[:, 0:n], in1=pu[:, 0:n], op=ALU.mult),
                     reads=[tk, ku], writes=["hT"])
            for m in range(DC):
                py = ps[4 + (m % 2)]
                ky = "ps%d" % (4 + (m % 2))
                for f in range(FC):
                    P.op("pe", lambda e, m=m, f=f, py=py: e.matmul(py[:, 0:n], lhsT=wd[:, f, m * 128:(m + 1) * 128], rhs=hT[:, f, 0:n],
                                                                 start=(f == 0), stop=(f == FC - 1)),
                         reads=["arena", ("wd", fi, f // 2), "hT"], writes=[ky], skip_own=True)
                for (lo, hi, s) in segs:
                    P.op("dve", lambda e, m=m, py=py, lo=lo, hi=hi, s=s, x=x: e.scalar_tensor_tensor(
                        out=x[:, m, lo:hi], in0=py[:, lo:hi], scalar=modS[:, s, kmod + 2, m:m + 1], in1=x[:, m, lo:hi],
                        op0=ALU.mult, op1=ALU.add),
                        reads=[ky, xk, "modS"], writes=[xk])
            layer_norm_tile(x, xk, n, gi, x, xk)
            P.dma("pool", dst_v[:, :, c0:c0 + n], x[:, :, 0:n], reads=[xk], writes=["dram_" + dst.name], is_output=dst_is_output)

    ffn_phase(0, xT, x1T, 0, 0, False)

    NC1 = 3096
    win_sb = arena[:, 0:DC * NC1].rearrange("p (c f) -> p c f", c=DC)
    arena_gate()
    for c in range(DC):
        for hi_, (a, b) in enumerate(((0, 1548), (1548, 3096))):
            load_weight_bf16(win_sb[:, c, a:b], w_in[c * 128:(c + 1) * 128, a:b], b - a, ("win", c, hi_))
    P.dma("sp", bias_tm[:, 0:792], b_in_bc[:, 2304:3096], reads=[("gate", 0), "arena"], writes=["bias_tm"])
    P.dma("sp", bias_tm[:, 792:792 + RW_SHIFT], b_in_bc[:, 0:RW_SHIFT], reads=[("gate", 0), "arena"], writes=["bias_tm"])
    CV1 = CV0 + (2 * 792 + RW_SHIFT + 792 + RW_SHIFT) * 4
    pbuf = carve(CV1, 15 * 260)
    xsb = carve(CV1 + 15 * 260 * 4, 15 * NT).rearrange("p (c n) -> p c n", c=15)
    DER0 = CV1 + 15 * 260 * 4 + 15 * NT * 4
    DERN = ("w", "a", "g", "kk", "nkk", "kp", "b", "bonus", "t1", "t2", "tw", "sgd")
    der = {k: carve(DER0 + i * NT * 4, NT) for i, k in enumerate(DERN)}
    CHT = [(j * 128, 128) for j in range(12)] + [(1536, 64), (1600, 64), (1664, 128)]
    GK = [("gate", 0), ("gate", 1), ("gate", 2), "arena"]

    def rw_pre(ti, c0, n, segs):
        S_ = len(segs)
        L = n // S_
        pb = pbuf[:, 0:15 * S_ * (L + 1)].rearrange("p (c s l) -> p c s l", c=15, s=S_)
        if S_ > 1:
            P.dma("sp", pb[:, :, :, 0], shs, reads=GK, writes=["pbuf"], allow_slow_non_contiguous=True)
        for ci, (col0, w) in enumerate(CHT):
            pp = ps[2 + ci % 2]
            pk = "ps%d" % (2 + ci % 2)
            for c in range(DC):
                P.op("pe", lambda e: e.matmul(pp[0:w, 0:n], lhsT=win_sb[:, c, col0:col0 + w], rhs=ub[:, c, 0:n],
                                              start=(c == 0), stop=(c == DC - 1)),
                     reads=["arena", ("win", c, 0), ("win", c, 1), "ub"], writes=[pk], skip_own=True)
            P.op("act", lambda e: e.activation(out=pb[0:w, ci, :, 1:L + 1], in_=pp[0:w, 0:n].rearrange("p (s l) -> p s l", s=S_),
                                               func=AF.Identity, bias=rwp_sb[0:w, ci, 0:1], scale=1.0),
                 reads=[pk, "rwc"] + GK, writes=["pbuf"])
        xs4 = xsb[:, :, 0:n].rearrange("p c (s l) -> p c s l", s=S_)
        P.op("dve", lambda e: e.tensor_tensor(out=xs4, in0=pb[:, :, :, 0:L], in1=pb[:, :, :, 1:L + 1], op=ALU.subtract),
             reads=["pbuf"] + GK, writes=["xs"])
        P.op("dve", lambda e: e.tensor_tensor(out=xsb[:, :, 0:n], in0=xsb[:, :, 0:n],
                                              in1=rwp_sb[:, :, 1:2].to_broadcast([128, 15, n]), op=ALU.mult),
             reads=["xs", "rwc"] + GK, writes=["xs"])
        P.op("dve", lambda e: e.tensor_tensor(out=xs4, in0=xs4, in1=pb[:, :, :, 1:L + 1], op=ALU.add),
             reads=["xs", "pbuf"] + GK, writes=["xs"])
        if S_ == 1:
            P.op("dve", lambda e: e.tensor_copy(out=pb[:, :, :, 0:1], in_=pb[:, :, :, L:L + 1]), reads=["pbuf"] + GK, writes=["pbuf"])
        tw, sgd = der["tw"], der["sgd"]
        P.op("act", lambda e: e.activation(out=tw[0:64, 0:n], in_=xsb[0:64, 12, 0:n], func=AF.Tanh), reads=["xs"] + GK, writes=["tw"])
        P.op("act", lambda e: e.activation(out=sgd[:, 0:n], in_=xsb[:, 14, 0:n], func=AF.Sigmoid), reads=["xs"] + GK, writes=["sgd"])
        NEG = -float(np.exp(-0.5))
        for j in range(4):
            jc = slice(j * 128, (j + 1) * 128)
            q = lambda k: rwq_sb[:, j, k:k + 1]
            r_j, k_j, v_j = xsb[:, j, 0:n], xsb[:, 4 + j, 0:n], xsb[:, 8 + j, 0:n]
            dw, da, dg, dkk, dnkk, dkp, db, dbo, t1, t2 = (der[k][:, 0:n] for k in ("w", "a", "g", "kk", "nkk", "kp", "b", "bonus", "t1", "t2"))
            p4, p5 = ps[4][:, 0:n], ps[5][:, 0:n]
            P.op("pe", lambda e: e.matmul(p4, lhsT=w2_sb[0:64, jc], rhs=tw[0:64, 0:n], start=True, stop=True),
                 reads=["rwc", "tw"] + GK, writes=["ps4"])
            P.op("act", lambda e: e.activation(out=dw, in_=p4, func=AF.Sigmoid, bias=q(0), scale=1.0), reads=["ps4", "rwc"] + GK, writes=["d_w"])
            P.op("act", lambda e: e.activation(out=dw, in_=dw, func=AF.Exp, scale=NEG), reads=["d_w"] + GK, writes=["d_w"])
            P.op("pe", lambda e: e.matmul(p5, lhsT=a2_sb[0:64, jc], rhs=xsb[0:64, 13, 0:n], start=True, stop=True),
                 reads=["rwc", "xs"] + GK, writes=["ps5"])
            P.op("act", lambda e: e.activation(out=da, in_=p5, func=AF.Sigmoid, bias=q(1), scale=1.0), reads=["ps5", "rwc"] + GK, writes=["d_a"])
            P.op("pe", lambda e: e.matmul(p4, lhsT=g2_sb[:, jc], rhs=sgd[:, 0:n], start=True, stop=True),
                 reads=["rwc", "sgd"] + GK, writes=["ps4"])
            P.op("act", lambda e: e.copy(out=dg, in_=p4), reads=["ps4"] + GK, writes=["d_g"])
            P.op("dve", lambda e: e.tensor_scalar(out=dkk, in0=k_j, scalar1=q(2), scalar2=None, op0=ALU.mult), reads=["xs", "rwc"] + GK, writes=["d_kk"])
            P.op("act", lambda e: e.activation(out=t1, in_=dkk, func=AF.Square), reads=["d_kk"] + GK, writes=["d_t1"])
            P.op("pe", lambda e: e.matmul(p5, lhsT=blk1[:], rhs=t1, start=True, stop=True), reads=["rwc", "d_t1"] + GK, writes=["ps5"])
            P.op("dve", lambda e: e.tensor_scalar(out=t2, in0=p5, scalar1=1e-24, scalar2=None, op0=ALU.max), reads=["ps5"] + GK, writes=["d_t2"])
            P.op("act", lambda e: e.sqrt(out=t2, in_=t2), reads=["d_t2"] + GK, writes=["d_t2"])
            P.op("dve", lambda e: e.reciprocal(out=t2, in_=t2), reads=["d_t2"] + GK, writes=["d_t2"])
            P.op("dve", lambda e: e.scalar_tensor_tensor(out=dnkk, in0=dkk, scalar=-1.0, in1=t2, op0=ALU.mult, op1=ALU.mult),
                 reads=["d_kk", "d_t2"] + GK, writes=["d_nkk"])
            P.op("dve", lambda e: e.tensor_scalar(out=t1, in0=da, scalar1=-1.0, scalar2=q(3), op0=ALU.add, op1=ALU.mult),
                 reads=["d_a", "rwc", "d_t1"] + GK, writes=["d_t1"])
            P.op("dve", lambda e: e.scalar_tensor_tensor(out=dkp, in0=t1, scalar=1.0, in1=k_j, op0=ALU.add, op1=ALU.mult),
                 reads=["d_t1", "xs"] + GK, writes=["d_kp"])
            P.op("dve", lambda e: e.scalar_tensor_tensor(out=db, in0=dnkk, scalar=-1.0, in1=da, op0=ALU.mult, op1=ALU.mult),
                 reads=["d_nkk", "d_a"] + GK, writes=["d_b"])
            P.op("dve", lambda e: e.scalar_tensor_tensor(out=t1, in0=r_j, scalar=q(4), in1=dkp, op0=ALU.mult, op1=ALU.mult),
                 reads=["xs", "rwc", "d_kp", "d_t1"] + GK, writes=["d_t1"])
            P.op("pe", lambda e: e.matmul(p4, lhsT=blk1[:], rhs=t1, start=True, stop=True), reads=["rwc", "d_t1"] + GK, writes=["ps4"])
            P.op("dve", lambda e: e.tensor_tensor(out=dbo, in0=p4, in1=v_j, op=ALU.mult), reads=["ps4", "xs"] + GK, writes=["d_bonus"])
            for nm, src_, key in (("nkk", dnkk, "d_nkk"), ("w", dw, "d_w"), ("b", db, "d_b"), ("kp", dkp, "d_kp"),
                                  ("g", dg, "d_g"), ("bonus", dbo, "d_bonus"), ("r", r_j, "xs"), ("v", v_j, "xs")):
                P.dma("pool", scr[nm][:, j, c0:c0 + n], src_, reads=[key] + GK, writes=["dram_scr"])

    nst = carve(DER0 + len(DERN) * NT * 4, 16 * NT).rearrange("p (c n) -> p c n", c=16)
    NCH = [1792 + 64 * h for h in range(8)] + [2304 + br * 256 + g * 64 for br in range(3) for g in range(2)] \
        + [2304 + 128 + g * 64 for g in range(2)]

    def nsa_pre(c0, n):
        for ci, col0 in enumerate(NCH):
            pp = ps[4 + ci % 2]
            pk = "ps%d" % (4 + ci % 2)
            for c in range(DC):
                P.op("pe", lambda e: e.matmul(pp[0:64, 0:n], lhsT=win_sb[:, c, col0:col0 + 64], rhs=ub[:, c, 0:n],
                                              start=(c == 0), stop=(c == DC - 1)),
                     reads=["arena", ("win", c, 0), ("win", c, 1), "ub"], writes=[pk], skip_own=True)
            P.op("act", lambda e: e.activation(out=nst[0:64, ci, 0:n], in_=pp[0:64, 0:n], func=AF.Identity,
                                               bias=nsab_sb[0:64, ci:ci + 1], scale=(0.125 if ci < 8 else 1.0)),
                 reads=[pk, "rwc"] + GK, writes=["nst"])
        P.dma("pool", nsaT[:, :, c0:c0 + n], nst[0:64, :, 0:n], reads=["nst"] + GK, writes=["dram_nsaT"])

    P.op("pool", lambda e: e.memset(pbuf, 0.0), reads=GK, writes=["pbuf"])
    P.op("pool", lambda e: e.memset(xsb, 0.0), reads=GK, writes=["xs"])
    x1_v = x1T.rearrange("(c p) n -> p c n", p=128)
    tl = tiles()
    for ti, (c0, n, segs) in enumerate(tl):
        x = xt[ti % 2]
        xk = "xt%d" % (ti % 2)
        P.dma("sp", x[:, :, 0:n], x1_v[:, :, c0:c0 + n], reads=["dram_x1T"], writes=[xk])
        modulate(x, xk, n, segs, 3, ub, "ub")
        rw_pre(ti, c0, n, segs)
        nsa_pre(c0, n)
        nblk = max(1, n // 128)
        for tb in range(nblk):
            m = min(128, n)
            t0 = c0 + tb * 128
            kv = kvt[tb % 2]
            kk_ = "kvt%d" % (tb % 2)
            for (ca, cb, pi) in ((0, 512, 0), (512, 792, 1)):
                pp = ps[pi]
                for c in range(DC):
                    P.op("pe", lambda e, c=c, tb=tb, m=m, ca=ca, cb=cb, pp=pp: e.matmul(
                        pp[0:m, 0:cb - ca], lhsT=ub[:, c, tb * 128:tb * 128 + m], rhs=win_sb[:, c, 2304 + ca:2304 + cb],
                        start=(c == 0), stop=(c == DC - 1)),
                        reads=["arena", ("win", c, 1), "ub"], writes=["ps%d" % pi], skip_own=True)
                P.op("dve", lambda e, m=m, ca=ca, cb=cb, pp=pp, kv=kv: e.tensor_tensor(
                    out=kv[0:m, ca:cb], in0=pp[0:m, 0:cb - ca], in1=bias_tm[0:m, ca:cb], op=ALU.add),
                    reads=["ps%d" % pi, "bias_tm", "arena"], writes=[kk_])
            P.dma("pool", kv_tm[t0:t0 + m, :], kv[0:m, 0:768], reads=[kk_, "arena"], writes=["dram_kvtm"])
            P.op("act", lambda e: e.activation(out=kv[0:m, 768:792], in_=kv[0:m, 768:792], func=AF.Sigmoid), reads=[kk_, "arena"], writes=[kk_])
            P.dma("pool", gates_tm[t0:t0 + m, :], kv[0:m, 768:792], reads=[kk_, "arena"], writes=["dram_gates"])
            P.dma("pool", o_kvc[t0:t0 + m, :], kv[0:m, 0:256], reads=[kk_, "arena"], is_output=True)
            P.dma("pool", o_kvs[t0:t0 + m, :], kv[0:m, 256:512], reads=[kk_, "arena"], is_output=True)
            if t0 >= SEQ - 512 and t0 < SEQ:
                P.dma("pool", o_kvw_p[t0 - (SEQ - 512):t0 - (SEQ - 512) + m, :], kv[0:m, 512:768], reads=[kk_, "arena"], is_output=True)
            if t0 >= SEQ:
                for s in range(NS):
                    P.dma("pool", o_kvw_s[s, 512 - ST:512, :], kv[s * ST:(s + 1) * ST, 512:768], reads=[kk_, "arena"], is_output=True)
            last_blk = (t0 + m == SEQ) or (t0 >= SEQ)
            if last_blk:
                for q4 in range(4):
                    ca, cb = q4 * 448, (q4 + 1) * 448
                    pp = ps[2 + (q4 % 2)]
                    pk = "ps%d" % (2 + (q4 % 2))
                    for c in range(DC):
                        P.op("pe", lambda e, c=c, tb=tb, m=m, ca=ca, cb=cb, pp=pp: e.matmul(
                            pp[0:m, 0:448], lhsT=ub[:, c, tb * 128:tb * 128 + m], rhs=win_sb[:, c, ca:cb],
                            start=(c == 0), stop=(c == DC - 1)),
                            reads=["arena", ("win", c, 0), ("win", c, 1), "ub"], writes=[pk], skip_own=True)
                    P.op("dve", lambda e, m=m, ca=ca, cb=cb, pp=pp: e.tensor_tensor(
                        out=rwt[0:m, ca:cb], in0=pp[0:m, 0:448], in1=bias_tm[0:m, 792 + ca:792 + cb], op=ALU.add),
                        reads=[pk, "bias_tm", "arena"], writes=["rwt"])
                if t0 < SEQ:
                    P.dma("pool", o_shift[0:1, :], rwt[127:128, :], reads=["rwt", "arena"], is_output=True)
                else:
                    for s in range(NS):
                        P.dma("pool", o_shift[1 + s:2 + s, :], rwt[s * ST + ST - 1:s * ST + ST, :], reads=["rwt", "arena"], is_output=True)
    for s in range(NS):
        P.dma("sp", o_kvw_s[s, 0:512 - ST, :], win_state[s, ST:512, :], is_output=True)

    arena_gate()
    TC = 8
    BN = ("nkk", "w", "b", "kp", "r")
    off = [0]

    def cv(ncols):
        v = carve(off[0], ncols)
        off[0] += ncols * 4
        return v

    xB = {k: [cv(TC * 256) for _ in range(2)] for k in BN}
    bld = [cv(TC * 256) for _ in range(2)]
    m3buf = cv(TC * 256)
    Shist = [cv(TC * 256) for _ in range(2)]
    xin = {k: [cv(4 * TC).rearrange("p (h t) -> p h t", h=4) for _ in range(2)] for k in BN + ("v",)}
    Sst, Sw, m1 = (cv(256) for _ in range(3))
    sa = cv(4)
    hT32 = hT[:].rearrange("p f n -> p (f n)").bitcast(F32)
    ybuf = [hT32[:, i_ * 1024:(i_ + 1) * 1024].rearrange("p (h t) -> p h t", h=4) for i_ in range(2)]
    v3 = lambda ap: ap.rearrange("p (h j) -> p h j", h=4)
    ycnt = [0]
    bcnt = [0]
    BLD_ENG = {"nkk": "pool", "w": "pool", "b": "pool", "kp": "pool", "r": "pool"}

    def scan_seq(seq_i, col0, T):
        if seq_i == 0:
            P.op("pool", lambda e: e.memset(Sst, 0.0), reads=GK, writes=["S0"])
        else:
            src = wkv0[seq_i - 1].rearrange("(hf hp) i j -> hp i hf j", hp=2)
            for hp in range(2):
                P.dma("sp", v3(Sst)[hp * 64:(hp + 1) * 64], src[hp], reads=GK, writes=["S0"])
        Sprev, Spk = Sst, ["S0"]
        yb_i = ycnt[0] % 2
        ycnt[0] += 1
        yb, ybk = ybuf[yb_i], "ybuf%d" % yb_i
        ycol0 = col0
        for t0 in range(0, T, TC):
            tc = min(TC, T - t0)
            bi = bcnt[0] % 2
            bcnt[0] += 1
            for k in BN + ("v",):
                P.dma("sp", xin[k][bi][:, :, 0:tc], scr[k][:, :, col0 + t0:col0 + t0 + tc], reads=["dram_scr"] + GK, writes=[("xin", k, bi)])
            for ki, k in enumerate(BN):
                bl, blk_ = bld[ki % 2], "bld%d" % (ki % 2)
                P.op(BLD_ENG[k], lambda e: e.tensor_tensor(
                    out=bl[:, 0:tc * 256].rearrange("p (t h j) -> p t h j", t=tc, h=4),
                    in0=istk[:].unsqueeze(1).unsqueeze(1).to_broadcast([128, tc, 4, 64]),
                    in1=xin[k][bi][:, :, 0:tc].rearrange("p h t -> p t h").unsqueeze(3).to_broadcast([128, tc, 4, 64]),
                    op=ALU.mult), reads=[("xin", k, bi), "rwc"] + GK, writes=[blk_])
                for t2 in range(0, tc, 2):
                    w_ = min(2, tc - t2) * 256
                    pi = (t2 // 2) % 4
                    P.op("pe", lambda e: e.matmul(ps[pi][:, 0:w_], lhsT=blk1[:], rhs=bl[:, t2 * 256:t2 * 256 + w_], start=True, stop=True),
                         reads=[blk_, "rwc"] + GK, writes=["ps%d" % pi])
                    P.op("act", lambda e: e.copy(out=xB[k][bi][:, t2 * 256:t2 * 256 + w_], in_=ps[pi][:, 0:w_]),
                         reads=["ps%d" % pi] + GK, writes=[("xB", k, bi)])
            Sh = Shist[bi]
            shkeys = lambda tl__: [("Sh", bi, tl__, hf_) for hf_ in range(4)]
            for tl_ in range(tc):
                sl = slice(tl_ * 256, (tl_ + 1) * 256)
                P.op("dve", lambda e: e.tensor_tensor(out=m1, in0=Sprev, in1=xB["nkk"][bi][:, sl], op=ALU.mult),
                     reads=Spk + [("xB", "nkk", bi)] + GK, writes=["m1"])
                P.op("dve", lambda e: e.tensor_tensor(out=Sw, in0=Sprev, in1=xB["w"][bi][:, sl], op=ALU.mult),
                     reads=Spk + [("xB", "w", bi)] + GK, writes=[("Sw", hf_) for hf_ in range(4)])
                P.op("dve", lambda e: e.tensor_reduce(out=sa, in_=v3(m1), axis=AX.X, op=ALU.add), reads=["m1"] + GK, writes=["sa"])
                for hf in range(4):
                    hs = slice(hf * 64, (hf + 1) * 64)
                    P.op("dve", lambda e: e.scalar_tensor_tensor(out=Sw[:, hs], in0=xB["kp"][bi][:, sl][:, hs], scalar=xin["v"][bi][:, hf, tl_:tl_ + 1],
                                                                 in1=Sw[:, hs], op0=ALU.mult, op1=ALU.add),
                         reads=[("Sw", hf), ("xB", "kp", bi), ("xin", "v", bi)] + GK, writes=[("Sw", hf)])
                for hf in range(4):
                    hs = slice(hf * 64, (hf + 1) * 64)
                    P.op("dve", lambda e: e.scalar_tensor_tensor(out=Sh[:, sl][:, hs], in0=xB["b"][bi][:, sl][:, hs], scalar=sa[:, hf:hf + 1],
                                                                 in1=Sw[:, hs], op0=ALU.mult, op1=ALU.add),
                         reads=[("Sw", hf), "sa", ("xB", "b", bi)] + GK, writes=[("Sh", bi, tl_, hf)])
                Sprev, Spk = Sh[:, sl], shkeys(tl_)
            allkeys = [k_ for tl__ in range(tc) for k_ in shkeys(tl__)]
            yc = t0 - (ycol0 - col0)
            P.op("pool", lambda e: e.tensor_tensor(out=m3buf[:, 0:tc * 256], in0=Sh[:, 0:tc * 256], in1=xB["r"][bi][:, 0:tc * 256], op=ALU.mult),
                 reads=allkeys + [("xB", "r", bi)] + GK, writes=["m3"])
            P.op("dve", lambda e: e.tensor_reduce(out=yb[:, :, yc:yc + tc].rearrange("p h t -> p t h"),
                                                  in_=m3buf[:, 0:tc * 256].rearrange("p (t h j) -> p t h j", t=tc, h=4), axis=AX.X, op=ALU.add),
                 reads=["m3", "hT"] + GK, writes=[ybk])
            if yc + tc == 256 or t0 + tc == T:
                P.dma("sp", y_fm[:, :, ycol0:ycol0 + yc + tc], yb[:, :, 0:yc + tc], reads=[ybk, "hT"] + GK, writes=["dram_yfm"])
                ycol0 += yc + tc
                yb_i = ycnt[0] % 2
                ycnt[0] += 1
                yb, ybk = ybuf[yb_i], "ybuf%d" % yb_i
        dst = o_wkv[seq_i].rearrange("(hf hp) i j -> hp i hf j", hp=2)
        for hp in range(2):
            P.dma("sp", dst[hp], v3(Sprev)[hp * 64:(hp + 1) * 64], reads=Spk + GK, is_output=True)

    scan_seq(0, 0, SEQ_SCAN)
    for s_ in range(NS):
        scan_seq(1 + s_, SEQ + s_ * ST, ST)

    arena_gate()
    off[0] = 0
    ktile = [cv(SEQ) for _ in range(3)]
    vaug = [cv(32 * 65).rearrange("p (c f) -> p c f", c=32) for _ in range(2)]
    vcaug = [cv(2 * 129).rearrange("p (c f) -> p c f", c=2) for _ in range(2)]
    kcT = [cv(256) for _ in range(2)]
    w1_sb = cv(32 * 128).rearrange("p (j e) -> p j e", j=32)
    w2a_sb = cv(64)
    peT_sb = cv(64).rearrange("p (k j) -> p k j", k=2)
    hx, ht_, hs_ = cv(256), cv(256), cv(256)
    tri_sb, ident_sb = cv(128), cv(128)
    cvalT_sb = cv(256).rearrange("p (c q) -> p c q", c=2)
    jidx_sb, f0_sb, idx16_sb, curb_sb = cv(64), cv(64), cv(16), cv(1)
    eall_sb = ktile[0]
    ones_c = cv(512)
    qs4, qsq, e_sb = cv(512), cv(512), [cv(512), cv(512)]
    mrow = cv(3 * 512).rearrange("p (b n) -> p b n", b=3)
    kmx = cv(8)
    mask_sb = [cv(128), cv(128)]
    gat = cv(24)
    ob = cv(256).rearrange("p (h d) -> p h d", h=4)
    imp, Am, Fm, F2m, NFm, nfm, nf2m, selm = (cv(64) for _ in range(8))
    t16, oh16 = cv(16), cv(16)
    cur_, curm1, nF_, thr_, rc_ = (cv(1) for _ in range(5))
    rc4 = cv(4)
    selT_sb = cv(128)
    ybT = cv(2 * 128).rearrange("p (c q) -> p c q", c=2)
    pe_sb = cv(1)

    def cvh(ncols):
        a_ = off[0] // 2
        off[0] += ncols * 2
        return arena[:, a_:a_ + ncols]

    GK = GK + ["xt0", "xt1", "hT", "stage0", "stage1"]
    kb16 = [xt[i_][:].rearrange("p c n -> p (c n)").bitcast(BF16) for i_ in range(2)]
    v16 = [stage[i_][:].bitcast(BF16)[:, 0:32 * 65].rearrange("p (c f) -> p c f", c=32) for i_ in range(2)]
    eall16 = hT[:].rearrange("p f n -> p (f n)")[:, 0:SEQ]
    q16, ones16 = cvh(512), cvh(128)
    mrow16 = cvh(2 * 512).rearrange("p (b n) -> p b n", b=2)
    e16 = [cvh(512), cvh(512)]
    mask16 = [cvh(128), cvh(128)]
    tri16p, ntri16p, selT16 = cvh(128), cvh(128), cvh(128)
    assert off[0] <= 135168, off[0]

    def A(eng, fn, reads=(), writes=(), **kw):
        return P.op(eng, fn, reads=list(reads) + GK, writes=writes, **kw)

    def Dm(out, in_, reads=(), writes=(), **kw):
        return P.dma("sp", out, in_, reads=list(reads) + GK, writes=writes, **kw)

    for dst_, src_ in ((tri_sb, tri_d), (ident_sb, ident_d), (cvalT_sb, cvalT_d), (jidx_sb, jidx_d), (eall_sb[0:64, :], eall_d),
                       (idx16_sb, idx16_d), (curb_sb, curb_d)):
        Dm(dst_, src_, writes=["ncst", "kt0"] if dst_ is not tri_sb and False else ["ncst"])
    A("pool", lambda e: e.memset(ones_c, 1.0), writes=["ncst"])
    A("act", lambda e: e.copy(out=eall16[0:64, :], in_=eall_sb[0:64, :]), reads=["ncst"], writes=["ncst16", "kt0"])
    A("pool", lambda e: e.memset(ones16, 1.0), writes=["ncst16"])
    A("dve", lambda e: e.tensor_copy(out=tri16p, in_=tri_sb), reads=["ncst"], writes=["ncst16"])
    A("dve", lambda e: e.tensor_scalar(out=f0_sb, in0=jidx_sb, scalar1=0.0, scalar2=None, op0=ALU.is_equal), reads=["ncst"], writes=["ncst2"])
    for g in range(2):
        A("pool", lambda e: e.memset(vcaug[g][:, :, 64:65], 1.0), writes=[("vcaug", g)])
        Dm(vcaug[g][:, :, 65:129], bimp_d, writes=[("vcaug", g)])

    for kvi in range(2):
        Dm(w1_sb[0:64], phi_w1[kvi].rearrange("j d e -> d j e"), writes=["w1"])
        Dm(w2a_sb, phi_w2[kvi], writes=["w2a"])
        if kvi == 0:
            Dm(peT_sb[0:64], peT_d, writes=["peT"])
        for j in range(32):
            A("pe", lambda e: e.matmul(ps[3][:, 0:1], lhsT=w1_sb[0:64, j, :], rhs=peT_sb[0:64, kvi, j:j + 1], start=(j == 0), stop=(j == 31)),
              reads=["w1", "peT"], writes=["ps3"], skip_own=True)
        A("act", lambda e: e.copy(out=pe_sb, in_=ps[3][:, 0:1]), reads=["ps3"], writes=["pe_sb"])
        for g in range(2):
            xc = ktile[0]
            Dm(xc[0:64, :], nsaT[:, (8 + g) if kvi == 0 else (14 + g), 0:SEQ], reads=["dram_nsaT"], writes=["kt0"])
            for j in range(32):
                A("pe", lambda e: e.matmul(ps[0][:, 0:255], lhsT=w1_sb[0:64, j, :], rhs=xc[0:64, j:j + 16 * 254 + 1:16],
                                           start=(j == 0), stop=(j == 31)), reads=["w1", "kt0"], writes=["ps0"], skip_own=True)
            A("act", lambda e: e.activation(out=hx[:, 0:255], in_=ps[0][:, 0:255], func=AF.Identity, bias=pe_sb[:, 0:1], scale=1.0),
              reads=["ps0", "pe_sb"], writes=["hx"])
            A("dve", lambda e: e.tensor_tensor(out=ht_[:, 0:255], in0=hx[:, 0:255], in1=hx[:, 0:255], op=ALU.mult), reads=["hx"], writes=["ht"])
            A("dve", lambda e: e.tensor_scalar(out=ht_[:, 0:255], in0=ht_[:, 0:255], scalar1=0.044715, scalar2=1.0, op0=ALU.mult, op1=ALU.add),
              reads=["ht"], writes=["ht"])
            A("dve", lambda e: e.tensor_tensor(out=ht_[:, 0:255], in0=ht_[:, 0:255], in1=hx[:, 0:255], op=ALU.mult), reads=["ht", "hx"], writes=["ht"])
            A("act", lambda e: e.activation(out=hs_[:, 0:255], in_=ht_[:, 0:255], func=AF.Sigmoid, scale=1.5957691216057308), reads=["ht"], writes=["hs"])
            A("dve", lambda e: e.tensor_tensor(out=hx[:, 0:255], in0=hx[:, 0:255], in1=hs_[:, 0:255], op=ALU.mult), reads=["hx", "hs"], writes=["hx"])
            if kvi == 0:
                A("pe", lambda e: e.matmul(ps[1][0:64, 0:255], lhsT=w2a_sb[:, 0:64], rhs=hx[:, 0:255], start=True, stop=True),
                  reads=["w2a", "hx"], writes=["ps1"])
                A("act", lambda e: e.copy(out=kcT[g][0:64, 0:255], in_=ps[1][0:64, 0:255]), reads=["ps1"], writes=[("kcT", g)])
            else:
                for ch, nk in ((0, 128), (1, 127)):
                    A("pe", lambda e: e.matmul(ps[1][0:nk, 0:64], lhsT=hx[:, ch * 128:ch * 128 + nk], rhs=w2a_sb[:, 0:64], start=True, stop=True),
                      reads=["w2a", "hx"], writes=["ps1"])
                    A("act", lambda e: e.copy(out=vcaug[g][0:nk, ch, 0:64], in_=ps[1][0:nk, 0:64]), reads=["ps1"], writes=[("vcaug", g)])

    def key_max(kt, nkeys, slot, ktk="kt*"):
        nchunk = (nkeys + 511) // 512
        for c in range(nchunk):
            w_ = min(512, nkeys - c * 512)
            A("act", lambda e: e.activation(out=qsq[0:64, 0:w_], in_=kt[0:64, c * 512:c * 512 + w_], func=AF.Square), reads=[ktk], writes=["qsq"])
            A("pe", lambda e: e.matmul(ps[3][0:1, 0:w_], lhsT=ones_c[0:64, 0:1], rhs=qsq[0:64, 0:w_], start=True, stop=True),
              reads=["qsq", "ncst"], writes=["ps3"])
            A("dve", lambda e: e.tensor_reduce(out=t16[0:1, c:c + 1], in_=ps[3][0:1, 0:w_], axis=AX.X, op=ALU.max), reads=["ps3"], writes=["t16"])
        A("dve", lambda e: e.tensor_reduce(out=kmx[0:1, slot:slot + 1], in_=t16[0:1, 0:nchunk], axis=AX.X, op=ALU.max), reads=["t16"], writes=["kmx"])
        A("act", lambda e: e.sqrt(out=kmx[0:1, slot:slot + 1], in_=kmx[0:1, slot:slot + 1]), reads=["kmx"], writes=["kmx"])
        A("dve", lambda e: e.tensor_scalar(out=kmx[0:1, slot:slot + 1], in0=kmx[0:1, slot:slot + 1], scalar1=-1.0, scalar2=None, op0=ALU.mult),
          reads=["kmx"], writes=["kmx"])

    ecnt = [0]

    def attend(kt_chunk, nk, br, mask_ap, mask_key, vaug_chunk, W, first, last, ktk="kt*", vk="vaug*", lowp=False):
        i = ecnt[0] % 2
        ecnt[0] += 1
        sc, sk = ps[i], "ps%d" % i
        es_, ek = (e16[i], "e16_%d" % i) if lowp else (e_sb[i], "e%d" % i)
        if lowp:
            A("pe", lambda e: e.matmul(sc[0:nk, 0:512], lhsT=kt_chunk, rhs=q16[0:64, :], start=True, stop=False),
              reads=[ktk, "q16"], writes=[sk], skip_own=True)
            A("pe", lambda e: e.matmul(sc[0:nk, 0:512], lhsT=ones16[0:1, 0:nk], rhs=mrow16[0:1, br - 1, :], start=False, stop=True),
              reads=["mrow16", "ncst16"], writes=[sk], skip_own=True)
        else:
            A("pe", lambda e: e.matmul(sc[0:nk, 0:512], lhsT=kt_chunk, rhs=qs4[0:64, :], start=True, stop=False),
              reads=[ktk, "qs4"], writes=[sk], skip_own=True)
            A("pe", lambda e: e.matmul(sc[0:nk, 0:512], lhsT=ones_c[0:1, 0:nk], rhs=mrow[0:1, br, :], start=False, stop=True),
              reads=["mrow", "ncst"], writes=[sk], skip_own=True)
        A("act", lambda e: e.activation(out=es_[0:nk, :], in_=sc[0:nk, 0:512], func=AF.Exp), reads=[sk], writes=[ek])
        if mask_ap is not None:
            A("dve", lambda e: e.tensor_tensor(out=es_[0:nk, :].rearrange("p (h q) -> p h q", h=4), in0=es_[0:nk, :].rearrange("p (h q) -> p h q", h=4),
                                               in1=mask_ap.unsqueeze(1).to_broadcast([nk, 4, 128]), op=ALU.mult),
              reads=[ek, mask_key], writes=[ek])
        for h in range(4):
            A("pe", lambda e: e.matmul(ps[4 + h][:, 0:W], lhsT=es_[0:nk, h * 128:(h + 1) * 128], rhs=vaug_chunk, start=first, stop=last),
              reads=[ek, vk], writes=["ps%d" % (4 + h)], skip_own=True)

    def finish_branch(g, br, first_branch):
        for h in range(4):
            o = ps[4 + h]
            ok = "ps%d" % (4 + h)
            A("dve", lambda e: e.tensor_scalar(out=rc_, in0=o[:, 64:65], scalar1=1e-30, scalar2=None, op0=ALU.max), reads=[ok], writes=["rc"])
            A("dve", lambda e: e.reciprocal(out=rc_, in_=rc_), reads=["rc"], writes=["rc"])
            if br == 0:
                if h == 0:
                    A("dve", lambda e: e.tensor_scalar(out=imp, in0=o[:, 65:129], scalar1=rc_[:, 0:1], scalar2=None, op0=ALU.mult),
                      reads=[ok, "rc"], writes=["imp"])
                else:
                    A("dve", lambda e: e.scalar_tensor_tensor(out=imp, in0=o[:, 65:129], scalar=rc_[:, 0:1], in1=imp, op0=ALU.mult, op1=ALU.add),
                      reads=[ok, "rc", "imp"], writes=["imp"])
            gi_ = (4 * g + h) * 3 + br
            A("dve", lambda e: e.tensor_tensor(out=rc_, in0=rc_, in1=gat[:, gi_:gi_ + 1], op=ALU.mult), reads=["rc", "gat"], writes=["rc"])
            if first_branch:
                A("dve", lambda e: e.tensor_scalar(out=ob[:, h, :], in0=o[:, 0:64], scalar1=rc_[:, 0:1], scalar2=None, op0=ALU.mult),
                  reads=[ok, "rc"], writes=["ob"])
            else:
                A("dve", lambda e: e.scalar_tensor_tensor(out=ob[:, h, :], in0=o[:, 0:64], scalar=rc_[:, 0:1], in1=ob[:, h, :], op0=ALU.mult, op1=ALU.add),
                  reads=[ok, "rc", "ob"], writes=["ob"])

    def select_blocks(curval):
        A("dve", lambda e: e.tensor_scalar(out=cur_, in0=curb_sb, scalar1=float(curval), scalar2=None, op0=ALU.add), reads=["ncst"], writes=["cur"])
        A("dve", lambda e: e.tensor_scalar(out=curm1, in0=cur_, scalar1=-1.0, scalar2=None, op0=ALU.add), reads=["cur"], writes=["curm1"])
        A("dve", lambda e: e.tensor_scalar(out=Am, in0=jidx_sb, scalar1=cur_[:, 0:1], scalar2=None, op0=ALU.is_le), reads=["cur", "ncst"], writes=["Am"])
        A("dve", lambda e: e.tensor_scalar(out=Fm, in0=jidx_sb, scalar1=cur_[:, 0:1], scalar2=None, op0=ALU.is_equal), reads=["cur", "ncst"], writes=["Fm"])
        A("dve", lambda e: e.tensor_scalar(out=F2m, in0=jidx_sb, scalar1=curm1[:, 0:1], scalar2=None, op0=ALU.is_equal), reads=["curm1", "ncst"], writes=["F2m"])
        A("dve", lambda e: e.tensor_tensor(out=Fm, in0=Fm, in1=F2m, op=ALU.max), reads=["Fm", "F2m"], writes=["Fm"])
        A("dve", lambda e: e.tensor_tensor(out=Fm, in0=Fm, in1=f0_sb, op=ALU.max), reads=["Fm", "ncst2"], writes=["Fm"])
        A("dve", lambda e: e.tensor_tensor(out=NFm, in0=Am, in1=Fm, op=ALU.subtract), reads=["Am", "Fm"], writes=["NFm"])
        A("dve", lambda e: e.scalar_tensor_tensor(out=nfm, in0=imp, scalar=1.0, in1=NFm, op0=ALU.add, op1=ALU.mult), reads=["imp", "NFm"], writes=["nfm"])
        A("dve", lambda e: e.tensor_scalar(out=nfm, in0=nfm, scalar1=-1.0, scalar2=None, op0=ALU.add), reads=["nfm"], writes=["nfm"])
        A("dve", lambda e: e.tensor_reduce(out=nF_, in_=Fm, axis=AX.X, op=ALU.add), reads=["Fm"], writes=["nF"])
        A("dve", lambda e: e.tensor_scalar(out=nF_, in0=nF_, scalar1=-1.0, scalar2=15.0, op0=ALU.mult, op1=ALU.add), reads=["nF"], writes=["nF"])
        A("dve", lambda e: e.tensor_scalar(out=oh16, in0=idx16_sb, scalar1=nF_[:, 0:1], scalar2=None, op0=ALU.is_equal), reads=["nF", "ncst"], writes=["oh16"])
        A("dve", lambda e: e.max(out=t16[:, 0:8], in_=nfm), reads=["nfm"], writes=["t16"])
        A("dve", lambda e: e.match_replace(out=nf2m, in_to_replace=t16[:, 0:8], in_values=nfm, imm_value=-2.0), reads=["nfm", "t16"], writes=["nf2m"])
        A("dve", lambda e: e.max(out=t16[:, 8:16], in_=nf2m), reads=["nf2m"], writes=["t16"])
        A("dve", lambda e: e.tensor_tensor(out=t16, in0=t16, in1=oh16, op=ALU.mult), reads=["t16", "oh16"], writes=["t16"])
        A("dve", lambda e: e.tensor_reduce(out=thr_, in_=t16, axis=AX.X, op=ALU.add), reads=["t16"], writes=["thr"])
        A("dve", lambda e: e.tensor_scalar(out=selm, in0=nfm, scalar1=thr_[:, 0:1], scalar2=None, op0=ALU.is_ge), reads=["nfm", "thr"], writes=["selm"])
        A("dve", lambda e: e.tensor_tensor(out=selm, in0=selm, in1=NFm, op=ALU.mult), reads=["selm", "NFm"], writes=["selm"])
        A("dve", lambda e: e.tensor_tensor(out=selm, in0=selm, in1=Fm, op=ALU.add), reads=["selm", "Fm"], writes=["selm"])
        A("pe", lambda e: e.transpose(out=ps[3][0:64, 0:128], in_=selm, identity=ident_sb), reads=["selm", "ncst"], writes=["ps3"])
        A("act", lambda e: e.copy(out=selT16[0:64, :], in_=ps[3][0:64, 0:128]), reads=["ps3"], writes=["selT"])

    mcnt = [0]

    def sel_mask(chunk, diag):
        i = mcnt[0] % 2
        mcnt[0] += 1
        A("pe", lambda e: e.matmul(ps[2][:, 0:128], lhsT=eall16[0:64, chunk * 128:(chunk + 1) * 128], rhs=selT16[0:64, :], start=True, stop=True),
          reads=["selT", "ncst16"], writes=["ps2"])
        if diag:
            A("dve", lambda e: e.tensor_tensor(out=mask16[i], in0=ps[2][:, 0:128], in1=tri_sb, op=ALU.mult), reads=["ps2", "ncst"], writes=[("mask16", i)])
        else:
            A("act", lambda e: e.copy(out=mask16[i], in_=ps[2][:, 0:128]), reads=["ps2"], writes=[("mask16", i)])
        return mask16[i], ("mask16", i)

    ntri_sb = hs_[:, 0:128]
    A("dve", lambda e: e.tensor_scalar(out=ntri_sb, in0=tri_sb, scalar1=-1.0, scalar2=1.0, op0=ALU.mult, op1=ALU.add), reads=["ncst", "hs"], writes=["ntri"])
    A("dve", lambda e: e.tensor_copy(out=ntri16p, in_=ntri_sb), reads=["ntri"], writes=["ntri16"])

    NQB = SEQ_NSA // 128
    A("pool", lambda e: e.memset(qsq, 0.0), writes=["qsq", "qsq0"])
    Dm(yb_fm[:, :, SEQ:TT], ybT[:, :, 0:NS * ST].rearrange("p c q -> p (c q)")[:, 0:4 * NS * ST].rearrange("p (c q) -> p c q", c=4)
       if False else qsq[:, 0:4 * NS * ST].rearrange("p (c q) -> p c q", c=4), reads=["qsq0"], writes=["dram_ybfm"])
    for g in range(2):
        Dm(ktile[1][0:64, :], nsaT[:, 10 + g, 0:SEQ], reads=["dram_nsaT"], writes=["kt*"])
        Dm(ktile[2][0:64, :], nsaT[:, 12 + g, 0:SEQ], reads=["dram_nsaT"], writes=["kt*"])
        for bi_, col in ((0, 256 + 128 + g * 64), (1, 512 + 128 + g * 64)):
            Dm(vaug[bi_][:, :, 0:64], kv_tm[0:SEQ, col:col + 64].rearrange("(c p) f -> p c f", p=128), reads=["dram_kvtm"], writes=["vaug*"])
            A("pool", lambda e: e.memset(vaug[bi_][:, :, 64:65], 1.0), writes=["vaug*"])
        key_max(kcT[g], 255, 0, ("kcT", g))
        key_max(ktile[1], SEQ, 1)
        key_max(ktile[2], SEQ, 2)
        A("act", lambda e: e.copy(out=kb16[0][0:64, :], in_=ktile[1][0:64, :]), reads=["kt*"], writes=["kb16"])
        A("dve", lambda e: e.tensor_copy(out=kb16[1][0:64, :], in_=ktile[2][0:64, :]), reads=["kt*"], writes=["kb16"])
        A("pool", lambda e: e.tensor_copy(out=v16[0], in_=vaug[0]), reads=["vaug*"], writes=["v16"])
        A("pool", lambda e: e.tensor_copy(out=v16[1], in_=vaug[1]), reads=["vaug*"], writes=["v16"])
        for qb in range(NQB):
            q0 = qb * 128
            Dm(qs4[0:64, :].rearrange("p (h q) -> p h q", h=4), nsaT[:, 4 * g:4 * g + 4, q0:q0 + 128], reads=["dram_nsaT"], writes=["qs4"])
            Dm(gat, gates_tm[q0:q0 + 128, :], reads=["dram_gates"], writes=["gat"])
            A("act", lambda e: e.activation(out=qsq[0:64, :], in_=qs4[0:64, :], func=AF.Square), reads=["qs4"], writes=["qsq"])
            A("pe", lambda e: e.matmul(ps[3][0:1, 0:512], lhsT=ones_c[0:64, 0:1], rhs=qsq[0:64, :], start=True, stop=True),
              reads=["qsq", "ncst"], writes=["ps3"])
            A("act", lambda e: e.sqrt(out=mrow[0:1, 0, :], in_=ps[3][0:1, 0:512]), reads=["ps3"], writes=["mrow"])
            for br in (2, 1, 0):
                A("dve", lambda e: e.tensor_scalar(out=mrow[0:1, br, :], in0=mrow[0:1, 0, :], scalar1=kmx[0:1, br:br + 1], scalar2=None, op0=ALU.mult),
                  reads=["mrow", "kmx"], writes=["mrow"])
            A("dve", lambda e: e.tensor_copy(out=mrow16[0:1, :, :], in_=mrow[0:1, 1:3, :]), reads=["mrow"], writes=["mrow16"])
            A("dve", lambda e: e.tensor_copy(out=q16[0:64, :], in_=qs4[0:64, :]), reads=["qs4"], writes=["q16"])
            for ch, nk in ((0, 128), (1, 127)):
                i = mcnt[0] % 2
                mcnt[0] += 1
                A("dve", lambda e: e.tensor_scalar(out=mask_sb[i][0:nk, :], in0=cvalT_sb[0:nk, ch, :], scalar1=float(q0), scalar2=None, op0=ALU.is_le),
                  reads=["ncst"], writes=[("mask", i)])
                attend(kcT[g][0:64, ch * 128:ch * 128 + nk], nk, 0, mask_sb[i][0:nk, :], ("mask", i), vcaug[g][0:nk, ch, :], 129, ch == 0, ch == 1, ktk=("kcT", g), vk=("vcaug", g))
            finish_branch(g, 0, True)
            select_blocks(2 * qb)
            for ch in range(qb + 1):
                mk, mkk = sel_mask(ch, ch == qb)
                attend(kb16[0][0:64, ch * 128:(ch + 1) * 128], 128, 1, mk, mkk, v16[0][:, ch, :], 65, ch == 0, ch == qb, ktk="kb16", vk="v16", lowp=True)
            finish_branch(g, 1, False)
            lo = max(0, qb - 4)
            for ch in range(lo, qb + 1):
                if ch == qb:
                    mk, mkk = tri16p, "ncst16"
                elif ch == qb - 4:
                    mk, mkk = ntri16p, "ntri16"
                else:
                    mk, mkk = None, None
                attend(kb16[1][0:64, ch * 128:(ch + 1) * 128], 128, 2, mk, mkk, v16[1][:, ch, :], 65, ch == lo, ch == qb, ktk="kb16", vk="v16", lowp=True)
            finish_branch(g, 2, False)
            for c2 in range(2):
                A("pe", lambda e: e.transpose(out=ps[3][:, 0:128], in_=ob[:, 2 * c2:2 * c2 + 2, :].rearrange("p h d -> p (h d)"), identity=ident_sb),
                  reads=["ob", "ncst"], writes=["ps3"])
                A("act", lambda e: e.copy(out=ybT[:, c2, :], in_=ps[3][:, 0:128]), reads=["ps3"], writes=["ybT"])
            Dm(yb_fm[:, 2 * g:2 * g + 2, q0:q0 + 128], ybT, reads=["ybT"], writes=["dram_ybfm"])

    if DO_SAMPLE:
        arena_gate()
        off[0] = 0
        PAST = 16384
        NPG = NPG_DBG
        GK = GK + ["xt0", "xt1"]
        xoff = [0, 0]

        def cvx(i, ncols):
            v = xt[i][:].rearrange("p c n -> p (c n)")[:, xoff[i]:xoff[i] + ncols]
            xoff[i] += ncols
            assert xoff[i] <= DC * NT
            return v
        xTs = cv(PAST + 4)
        regB = cv(8192)
        w1pad = regB.rearrange("p (g j e) -> p g j e", g=2, j=32)
        pg = [cv(128) for _ in range(3)]
        vpg = [cv(2 * 65).rearrange("p (g f) -> p g f", g=2) for _ in range(2)]
        ptf, idxf = cv(NS * 128), cv(NS * 128)
        idx_i = [cv(NS * 128).bitcast(I32) for _ in range(2)]
        pt_i = cv(NS * 128).bitcast(I32)
        pcol_sb, hsel_sb = cv(1), cv(4)
        tri16, ntri16, ident_s, ones_s = cv(16), cv(16), cv(128), cv(512)
        onesg = cv(2)
        jidx2, f02 = cv(264), cv(264)
        qg = [cv(16) for _ in range(2)]
        qpad = [cv(16) for _ in range(2)]
        qsqs, qn_s = cv(16), cv(16)
        mrow_s = cv(6 * 16).rearrange("p (b n) -> p b n", b=6)
        kmx_s = cv(8)
        es2 = [cv(16), cv(16)]
        msk2 = [cv(16), cv(16)]
        gat_s = [cv(3), cv(3)]
        oacc = [cv(64), cv(64)]
        o322 = cv(322)
        imp2, Am2, Fm2, F2m2, NFm2, nfm2, nf2m2 = (cvx(1, 264) for _ in range(7))
        selm2 = cv(264)
        t16b, oh16b, idx16_s = cv(16), cv(16), cv(16)
        cur2, curm2, nF2, thr2, rc2, zero1 = (cv(1) for _ in range(6))
        selT2 = [cv(2 * 16).rearrange("p (k q) -> p k q", k=2) for _ in range(2)]
        vnew = [cv(2 * 65).rearrange("p (g f) -> p g f", g=2) for _ in range(2)]
        wv = cv(4 * 2 * 65).rearrange("p (c g f) -> p c g f", c=4, g=2)
        wk = cv(4 * 256).rearrange("p (c f) -> p c f", c=4)
        kTw = cv(516)
        obT = cv(16)
        sqt = cvx(0, 512)
        hx2, ht2, hs2 = cvx(0, 512), cvx(0, 512), cvx(0, 512)
        pe2, w2b = cv(1), cv(64)
        peT2 = cv(64).rearrange("p (k j) -> p k j", k=2)
        assert off[0] <= 135168, off[0]
        vcs = [stage[g_][:, 0:8 * 322].rearrange("p (c f) -> p c f", c=8) for g_ in range(2)]
        kcs = hT[:].rearrange("p f n -> p (f n)").bitcast(F32)[:, 0:2048].rearrange("p (g n) -> p g n", g=2)
        SK = ["stage0", "stage1", "hT"]

        def Gq(out, cache, half, col, reads=(), writes=()):
            return P.dma("pool", out, cache, reads=list(reads) + GK, writes=writes, gather_idx=idx_i[half][:, col:col + 1])

        for dst_, src_ in ((tri16, tri16_d), (ident_s, ident_d), (pcol_sb, pcol_d), (hsel_sb[0:16, :], hsel_d), (jidx2, jidx2_d), (idx16_s, idx16_d)):
            Dm(dst_, src_, writes=["scst"])
        A("pool", lambda e: e.memset(ones_s, 1.0), writes=["scst"])
        A("pool", lambda e: e.memset(zero1, 0.0), writes=["scst"])
        A("pool", lambda e: e.memset(onesg, 0.0), writes=["onesg"])
        A("pool", lambda e: e.memset(onesg[0:64, 0:1], 1.0), reads=["onesg"], writes=["onesg"])
        A("pool", lambda e: e.memset(onesg[64:128, 1:2], 1.0), reads=["onesg"], writes=["onesg"])
        A("dve", lambda e: e.tensor_scalar(out=ntri16, in0=tri16, scalar1=-1.0, scalar2=1.0, op0=ALU.mult, op1=ALU.add), reads=["scst"], writes=["scst2"])
        A("dve", lambda e: e.tensor_scalar(out=f02, in0=jidx2, scalar1=0.0, scalar2=None, op0=ALU.is_equal), reads=["scst"], writes=["scst2"])
        for g in range(2):
            A("pool", lambda e: e.memset(vcs[g][:, :, 64:65], 1.0), reads=SK, writes=[("vcs", g)])
            Dm(vcs[g][:, :, 65:322], bimps_d, reads=SK, writes=[("vcs", g)])
            A("pool", lambda e: e.memset(vpg[g][:, :, 64:65], 1.0), writes=[("vpg", g)])
            A("pool", lambda e: e.memset(vnew[g][:, :, 64:65], 1.0), writes=["vnew"])
            A("pool", lambda e: e.memset(qpad[g], 0.0), writes=["qpad"])
        A("pool", lambda e: e.memset(wv[:, :, :, 64:65], 1.0), writes=["wv"])
        Dm(pt_i, ptab.partition_broadcast(128), writes=["pt"])
        A("dve", lambda e: e.tensor_copy(out=ptf, in_=pt_i), reads=["pt"], writes=["ptf"])
        A("dve", lambda e: e.tensor_scalar(out=idxf, in0=ptf, scalar1=128.0, scalar2=pcol_sb[:, 0:1], op0=ALU.mult, op1=ALU.add), reads=["ptf", "scst"], writes=["idxf"])
        A("dve", lambda e: e.tensor_scalar(out=idxf, in0=idxf, scalar1=2.0, scalar2=None, op0=ALU.mult), reads=["idxf"], writes=["idxf"])
        A("dve", lambda e: e.tensor_copy(out=idx_i[0], in_=idxf), reads=["idxf"], writes=["idx"])
        A("dve", lambda e: e.tensor_scalar(out=ptf, in0=idxf, scalar1=1.0, scalar2=None, op0=ALU.add), reads=["idxf", "ptf"], writes=["ptf"])
        A("dve", lambda e: e.tensor_copy(out=idx_i[1], in_=ptf), reads=["ptf"], writes=["idx"])
        Dm(peT2[0:64], peT_d, writes=["peT2"])

        pcnt = [0]

        def gather_T(cache, s_, j, half, dst, dst_cols, dkey):
            i = pcnt[0] % 3
            pcnt[0] += 1
            Gq(pg[i], cache, half, s_ * 128 + j, reads=["idx"], writes=[("pg", i)])
            pb_ = ps[6 + i % 2]
            pbk = "ps%d" % (6 + i % 2)
            A("pe", lambda e: e.transpose(out=pb_[:, 0:128], in_=pg[i], identity=ident_s), reads=[("pg", i), "scst"], writes=[pbk])
            if i % 2 == 0:
                A("act", lambda e: e.copy(out=dst[:, dst_cols], in_=pb_[:, 0:128]), reads=[pbk], writes=[dkey])
            else:
                A("dve", lambda e: e.tensor_copy(out=dst[:, dst_cols], in_=pb_[:, 0:128]), reads=[pbk], writes=[dkey])

        def gelu_to(hx_, n_):
            A("dve", lambda e: e.tensor_tensor(out=ht2[:, 0:n_], in0=hx_, in1=hx_, op=ALU.mult), reads=["hx2"], writes=["ht2"])
            A("dve", lambda e: e.tensor_scalar(out=ht2[:, 0:n_], in0=ht2[:, 0:n_], scalar1=0.044715, scalar2=1.0, op0=ALU.mult, op1=ALU.add), reads=["ht2"], writes=["ht2"])
            A("dve", lambda e: e.tensor_tensor(out=ht2[:, 0:n_], in0=ht2[:, 0:n_], in1=hx_, op=ALU.mult), reads=["ht2", "hx2"], writes=["ht2"])
            A("act", lambda e: e.activation(out=hs2[:, 0:n_], in_=ht2[:, 0:n_], func=AF.Sigmoid, scale=1.5957691216057308), reads=["ht2"], writes=["hs2"])
            A("dve", lambda e: e.tensor_tensor(out=hx_, in0=hx_, in1=hs2[:, 0:n_], op=ALU.mult), reads=["hx2", "hs2"], writes=["hx2"])

        def key_max_s(kt, K_, ncols, slot, ktk, lhs1):
            nchunk = (ncols + 511) // 512
            for c in range(nchunk):
                w_ = min(512, ncols - c * 512)
                A("act", lambda e: e.activation(out=sqt[0:K_, 0:w_], in_=kt[0:K_, c * 512:c * 512 + w_], func=AF.Square), reads=[ktk], writes=["sqt"])
                A("pe", lambda e: e.matmul(ps[3][0:1, 0:w_], lhsT=lhs1, rhs=sqt[0:K_, 0:w_], start=True, stop=True), reads=["sqt", "scst", "onesg"], writes=["ps3"])
                if c == 0:
                    A("dve", lambda e: e.tensor_reduce(out=kmx_s[0:1, slot:slot + 1], in_=ps[3][0:1, 0:w_], axis=AX.X, op=ALU.max), reads=["ps3"], writes=["kmxs"])
                else:
                    A("dve", lambda e: e.tensor_reduce(out=kmx_s[0:1, 7:8], in_=ps[3][0:1, 0:w_], axis=AX.X, op=ALU.max), reads=["ps3"], writes=["kmxs"])
                    A("dve", lambda e: e.tensor_tensor(out=kmx_s[0:1, slot:slot + 1], in0=kmx_s[0:1, slot:slot + 1], in1=kmx_s[0:1, 7:8], op=ALU.max),
                      reads=["kmxs"], writes=["kmxs"])
            A("act", lambda e: e.sqrt(out=kmx_s[0:1, slot:slot + 1], in_=kmx_s[0:1, slot:slot + 1]), reads=["kmxs"], writes=["kmxs"])
            A("dve", lambda e: e.tensor_scalar(out=mrow_s[0:1, slot, :], in0=qn_s[0:1, :] if False else mrow_s[0:1, slot, :], scalar1=1.0, scalar2=None, op0=ALU.mult),
              reads=["mrows"], writes=["mrows"]) if False else None

        def set_mrow(slot, g):
            A("act", lambda e: e.activation(out=qsqs[0:64, :], in_=qg[g][0:64, :], func=AF.Square), reads=["qg"], writes=["qsqs"])
            A("pe", lambda e: e.matmul(ps[3][0:1, 0:16], lhsT=ones_s[0:64, 0:1], rhs=qsqs[0:64, :], start=True, stop=True), reads=["qsqs", "scst"], writes=["ps3"])
            A("act", lambda e: e.sqrt(out=qn_s[0:1, :], in_=ps[3][0:1, 0:16]), reads=["ps3"], writes=["qn"])
            A("dve", lambda e: e.tensor_scalar(out=mrow_s[0:1, slot, :], in0=qn_s[0:1, :], scalar1=kmx_s[0:1, slot:slot + 1], scalar2=-1.0, op0=ALU.mult, op1=ALU.mult),
              reads=["qn", "kmxs"], writes=["mrows"])

        e2cnt = [0]

        def attend_s(kt_chunk, K_, nk, qtile, slot, mask_ap, mask_key, v_chunk, W, g, first, last, ktk, vk):
            i = e2cnt[0] % 2
            e2cnt[0] += 1
            sc, sk = ps[i], "ps%d" % i
            es_, ek = es2[i], "es%d" % i
            A("pe", lambda e: e.matmul(sc[0:nk, 0:16], lhsT=kt_chunk, rhs=qtile[0:K_, :], start=True, stop=False), reads=[ktk, "qg", "qpad"], writes=[sk], skip_own=True)
            A("pe", lambda e: e.matmul(sc[0:nk, 0:16], lhsT=ones_s[0:1, 0:nk], rhs=mrow_s[0:1, slot, :], start=False, stop=True), reads=["mrows", "scst"], writes=[sk], skip_own=True)
            A("act", lambda e: e.activation(out=es_[0:nk, :], in_=sc[0:nk, 0:16], func=AF.Exp), reads=[sk], writes=[ek])
            if mask_ap is not None:
                A("dve", lambda e: e.tensor_tensor(out=es_[0:nk, :], in0=es_[0:nk, :], in1=mask_ap, op=ALU.mult), reads=[ek, mask_key], writes=[ek])
            A("pe", lambda e: e.matmul(ps[4 + g][0:16, 0:W], lhsT=es_[0:nk, :], rhs=v_chunk, start=first, stop=last), reads=[ek, vk], writes=["ps%d" % (4 + g)], skip_own=True)

        def finish_s(g, br, first_branch):
            o = ps[4 + g]
            ok = "ps%d" % (4 + g)
            A("dve", lambda e: e.tensor_scalar(out=rc2[0:16], in0=o[0:16, 64:65], scalar1=1e-30, scalar2=None, op0=ALU.max), reads=[ok], writes=["rc2"])
            A("dve", lambda e: e.reciprocal(out=rc2[0:16], in_=rc2[0:16]), reads=["rc2"], writes=["rc2"])
            if br == 0:
                A("dve", lambda e: e.tensor_scalar(out=o322[0:16, 0:257], in0=o[0:16, 65:322], scalar1=rc2[0:16, 0:1], scalar2=None, op0=ALU.mult), reads=[ok, "rc2"], writes=["o322"])
                A("pe", lambda e: e.matmul(ps[3][0:4, 0:257], lhsT=hsel_sb[0:16, 0:4], rhs=o322[0:16, 0:257], start=True, stop=True), reads=["o322", "scst"], writes=["ps3"])
                A("act", lambda e: e.copy(out=imp2[0:4, 0:257], in_=ps[3][0:4, 0:257]), reads=["ps3"], writes=["imp2"])
            A("dve", lambda e: e.tensor_tensor(out=rc2[0:16], in0=rc2[0:16], in1=gat_s[g][0:16, br:br + 1], op=ALU.mult), reads=["rc2", "gats"], writes=["rc2"])
            if first_branch:
                A("dve", lambda e: e.tensor_scalar(out=oacc[g][0:16, :], in0=o[0:16, 0:64], scalar1=rc2[0:16, 0:1], scalar2=None, op0=ALU.mult), reads=[ok, "rc2"], writes=[("oacc", g)])
            else:
                A("dve", lambda e: e.scalar_tensor_tensor(out=oacc[g][0:16, :], in0=o[0:16, 0:64], scalar=rc2[0:16, 0:1], in1=oacc[g][0:16, :], op0=ALU.mult, op1=ALU.add),
                  reads=[ok, "rc2", ("oacc", g)], writes=[("oacc", g)])

        def select_s(g):
            R4 = slice(0, 4)
            NB = 257
            v = lambda t_: t_[R4, 0:NB]
            A("dve", lambda e: e.tensor_scalar(out=cur2[R4], in0=zero1[R4], scalar1=256.0, scalar2=None, op0=ALU.add), reads=["scst"], writes=["cur2"])
            A("dve", lambda e: e.tensor_scalar(out=curm2[R4], in0=zero1[R4], scalar1=255.0, scalar2=None, op0=ALU.add), reads=["scst"], writes=["curm2"])
            A("dve", lambda e: e.tensor_scalar(out=v(Am2), in0=v(jidx2), scalar1=cur2[R4, 0:1], scalar2=None, op0=ALU.is_le), reads=["cur2", "scst"], writes=["Am2"])
            A("dve", lambda e: e.tensor_scalar(out=v(Fm2), in0=v(jidx2), scalar1=cur2[R4, 0:1], scalar2=None, op0=ALU.is_equal), reads=["cur2", "scst"], writes=["Fm2"])
            A("dve", lambda e: e.tensor_scalar(out=v(F2m2), in0=v(jidx2), scalar1=curm2[R4, 0:1], scalar2=None, op0=ALU.is_equal), reads=["curm2", "scst"], writes=["F2m2"])
            A("dve", lambda e: e.tensor_tensor(out=v(Fm2), in0=v(Fm2), in1=v(F2m2), op=ALU.max), reads=["Fm2", "F2m2"], writes=["Fm2"])
            A("dve", lambda e: e.tensor_tensor(out=v(Fm2), in0=v(Fm2), in1=v(f02), op=ALU.max), reads=["Fm2", "scst2"], writes=["Fm2"])
            A("dve", lambda e: e.tensor_tensor(out=v(NFm2), in0=v(Am2), in1=v(Fm2), op=ALU.subtract), reads=["Am2", "Fm2"], writes=["NFm2"])
            A("dve", lambda e: e.scalar_tensor_tensor(out=v(nfm2), in0=v(imp2), scalar=1.0, in1=v(NFm2), op0=ALU.add, op1=ALU.mult), reads=["imp2", "NFm2"], writes=["nfm2"])
            A("dve", lambda e: e.tensor_scalar(out=v(nfm2), in0=v(nfm2), scalar1=-1.0, scalar2=None, op0=ALU.add), reads=["nfm2"], writes=["nfm2"])
            A("dve", lambda e: e.tensor_reduce(out=nF2[R4], in_=v(Fm2), axis=AX.X, op=ALU.add), reads=["Fm2"], writes=["nF2"])
            A("dve", lambda e: e.tensor_scalar(out=nF2[R4], in0=nF2[R4], scalar1=-1.0, scalar2=15.0, op0=ALU.mult, op1=ALU.add), reads=["nF2"], writes=["nF2"])
            A("dve", lambda e: e.tensor_scalar(out=oh16b[R4], in0=idx16_s[R4], scalar1=nF2[R4, 0:1], scalar2=None, op0=ALU.is_equal), reads=["nF2", "scst"], writes=["oh16b"])
            A("dve", lambda e: e.max(out=t16b[R4, 0:8], in_=v(nfm2)), reads=["nfm2"], writes=["t16b"])
            A("dve", lambda e: e.match_replace(out=v(nf2m2), in_to_replace=t16b[R4, 0:8], in_values=v(nfm2), imm_value=-2.0), reads=["nfm2", "t16b"], writes=["nf2m2"])
            A("dve", lambda e: e.max(out=t16b[R4, 8:16], in_=v(nf2m2)), reads=["nf2m2"], writes=["t16b"])
            A("dve", lambda e: e.tensor_tensor(out=t16b[R4], in0=t16b[R4], in1=oh16b[R4], op=ALU.mult), reads=["t16b", "oh16b"], writes=["t16b"])
            A("dve", lambda e: e.tensor_reduce(out=thr2[R4], in_=t16b[R4], axis=AX.X, op=ALU.add), reads=["t16b"], writes=["thr2"])
            A("dve", lambda e: e.tensor_scalar(out=v(selm2), in0=v(nfm2), scalar1=thr2[R4, 0:1], scalar2=None, op0=ALU.is_ge), reads=["nfm2", "thr2"], writes=["selm2"])
            A("dve", lambda e: e.tensor_tensor(out=v(selm2), in0=v(selm2), in1=v(NFm2), op=ALU.mult), reads=["selm2", "NFm2"], writes=["selm2"])
            A("dve", lambda e: e.tensor_tensor(out=v(selm2), in0=v(selm2), in1=v(Fm2), op=ALU.add), reads=["selm2", "Fm2"], writes=["selm2"])
            for kch in range(2):
                A("pe", lambda e: e.transpose(out=ps[3][0:128, 0:4], in_=selm2[R4, kch * 128:kch * 128 + 128], identity=ident_s[0:4, 0:4]), reads=["selm2", "scst"], writes=["ps3"])
                for h in range(4):
                    A("act", lambda e: e.copy(out=selT2[g][:, kch, h * 4:(h + 1) * 4], in_=ps[3][0:128, 0:4]), reads=["ps3"], writes=[("selT2", g)])

        def dbg_dump(name, ap, key):
            if DEBUG:
                t = dout("dbg_" + name, list(ap.shape))
                Dm(t, ap, reads=[key], is_output=True)

        for s_ in range(NS_DBG):
            col_s = SEQ + s_ * ST
            for g in range(2):
                Dm(qg[g][0:64, :].rearrange("p (h q) -> p h q", h=4), nsaT[:, 4 * g:4 * g + 4, col_s:col_s + ST], reads=["dram_nsaT"], writes=["qg"])
                Dm(qpad[g][g * 64:(g + 1) * 64, :].rearrange("p (h q) -> p h q", h=4), nsaT[:, 4 * g:4 * g + 4, col_s:col_s + ST], reads=["dram_nsaT"], writes=["qpad"])
                for h in range(4):
                    Dm(gat_s[g][h * 4:(h + 1) * 4, :], gates_tm[col_s:col_s + ST, (4 * g + h) * 3:(4 * g + h) * 3 + 3], reads=["dram_gates"], writes=["gats"])
            for kvi in range(2):
                for g in range(2):
                    A("pool", lambda e: e.memset(w1pad[:, g], 0.0), writes=["regB"])
                    Dm(w1pad[g * 64:(g + 1) * 64, g], phi_w1[kvi].rearrange("j d e -> d j e"), reads=["regB"], writes=["regB"])
                Dm(w2b, phi_w2[kvi], writes=["w2b"])
                for j in range(32):
                    A("pe", lambda e: e.matmul(ps[3][:, 0:1], lhsT=w1pad[0:64, 0, j, :], rhs=peT2[0:64, kvi, j:j + 1], start=(j == 0), stop=(j == 31)),
                      reads=["regB", "peT2"], writes=["ps3"], skip_own=True)
                A("act", lambda e: e.copy(out=pe2, in_=ps[3][:, 0:1]), reads=["ps3"], writes=["pe2"])
                for j in range(NPG):
                    gather_T(cache_cmp, s_, j, kvi, xTs, slice(j * 128, (j + 1) * 128), "xTs")
                for g in range(2):
                    for (n0, ncol) in ((0, 512), (512, 511)):
                        for j in range(32):
                            A("pe", lambda e: e.matmul(ps[2][:, 0:ncol], lhsT=w1pad[:, g, j, :], rhs=xTs[:, 16 * n0 + j:16 * n0 + j + 16 * (ncol - 1) + 1:16],
                                                       start=(j == 0), stop=(j == 31)), reads=["regB", "xTs"], writes=["ps2"], skip_own=True)
                        A("act", lambda e: e.activation(out=hx2[:, 0:ncol], in_=ps[2][:, 0:ncol], func=AF.Identity, bias=pe2[:, 0:1], scale=1.0), reads=["ps2", "pe2"], writes=["hx2"])
                        gelu_to(hx2[:, 0:ncol], ncol)
                        if kvi == 0:
                            A("pe", lambda e: e.matmul(ps[3][0:64, 0:ncol], lhsT=w2b[:, 0:64], rhs=hx2[:, 0:ncol], start=True, stop=True), reads=["w2b", "hx2"], writes=["ps3"])
                            A("act", lambda e: e.copy(out=kcs[0:64, g, n0:n0 + ncol], in_=ps[3][0:64, 0:ncol]), reads=["ps3"] + SK, writes=[("kcs", g)])
                        else:
                            for c4 in range(4):
                                nk = min(128, ncol - c4 * 128)
                                A("pe", lambda e: e.matmul(ps[3][0:nk, 0:64], lhsT=hx2[:, c4 * 128:c4 * 128 + nk], rhs=w2b[:, 0:64], start=True, stop=True), reads=["w2b", "hx2"], writes=["ps3"])
                                A("act", lambda e: e.copy(out=vcs[g][0:nk, n0 // 128 + c4, 0:64], in_=ps[3][0:nk, 0:64]), reads=["ps3"] + SK, writes=[("vcs", g)])
            for g in range(2):
                key_max_s(kcs[:, g, :], 64, 1023, g, ("kcs", g), ones_s[0:64, 0:1])
                set_mrow(g, g)
                for ch in range(8):
                    nk = 128 if ch < 7 else 127
                    attend_s(kcs[0:64, g, ch * 128:ch * 128 + nk], 64, nk, qg[g], g, None, None, vcs[g][0:nk, ch, :], 322, g, ch == 0, ch == 7, ("kcs", g), ("vcs", g))
                finish_s(g, 0, True)
                select_s(g)
            Dm(regB, e2_d, reads=["regB"], writes=["regB"])
            for j in range(NPG):
                gather_T(cache_sel, s_, j, 0, xTs, slice(j * 128, (j + 1) * 128), "xTs")
            for g in range(2):
                Dm(xTs[g * 64:(g + 1) * 64, PAST:PAST + ST], nsaT[:, 10 + g, col_s:col_s + ST], reads=["dram_nsaT"], writes=["xTs"])
            Dm(vnew[0][0:ST, :, 0:64], kv_tm[col_s:col_s + ST, 384:512].rearrange("r (g f) -> r g f", g=2), reads=["dram_kvtm"], writes=["vnew"])
            Dm(vnew[1][0:ST, :, 0:64], kv_tm[col_s:col_s + ST, 640:768].rearrange("r (g f) -> r g f", g=2), reads=["dram_kvtm"], writes=["vnew"])
            for g in range(2):
                key_max_s(xTs, 128, PAST + ST, 2 + g, "xTs", onesg[:, g:g + 1])
                set_mrow(2 + g, g)
            for j in range(NPG):
                i = pcnt[0] % 3
                pcnt[0] += 1
                vi = j % 2
                Gq(pg[i], cache_sel, 1, s_ * 128 + j, reads=["idx"], writes=[("pg", i)])
                A("act", lambda e: e.copy(out=vpg[vi][:, :, 0:64], in_=pg[i].rearrange("p (g f) -> p g f", g=2)), reads=[("pg", i)], writes=[("vpg", vi)])
                kch, cl = j // 64, j % 64
                for g in range(2):
                    mi = e2cnt[0] % 2
                    A("pe", lambda e: e.matmul(ps[2][:, 0:16], lhsT=regB[:, cl * 128:(cl + 1) * 128], rhs=selT2[g][:, kch, :], start=True, stop=True),
                      reads=["regB", ("selT2", g)], writes=["ps2"])
                    A("dve", lambda e: e.tensor_copy(out=msk2[mi], in_=ps[2][:, 0:16]), reads=["ps2"], writes=[("msk2", mi)])
                    attend_s(xTs[:, j * 128:(j + 1) * 128], 128, 128, qpad[g], 2 + g, msk2[mi], ("msk2", mi), vpg[vi][:, g, :], 65, g, j == 0, False, "xTs", ("vpg", vi))
            if s_ == 0:
                dbg_dump("selm2", selm2[0:4, 0:257], "selm2")
                dbg_dump("imp2", imp2[0:4, 0:257], "imp2")
                dbg_dump("kmx", kmx_s[0:1, :], "kmxs")
                dbg_dump("mrow", mrow_s[0:1, :, :], "mrows")
                dbg_dump("selT2", selT2[1][:, :, :], ("selT2", 1))
                dbg_dump("msk", msk2[0][:, :], ("msk2", 0))
                dbg_dump("xTs", xTs[:, PAST - 256:PAST + ST], "xTs")
            for g in range(2):
                attend_s(xTs[:, PAST:PAST + ST], 128, ST, qpad[g], 2 + g, tri16[0:ST, :], "scst", vnew[0][0:ST, g, :], 65, g, NPG == 0, True, "xTs", "vnew")
                finish_s(g, 1, False)
            Dm(wk, win_state[s_].rearrange("(c p) f -> p c f", p=128), writes=["wk"])
            for c4 in range(4):
                A("act", lambda e: e.copy(out=wv[:, c4, :, 0:64], in_=wk[:, c4, 128:256].rearrange("p (g f) -> p g f", g=2)), reads=["wk"], writes=["wv"])
                pb_ = ps[6 + c4 % 2]
                pbk = "ps%d" % (6 + c4 % 2)
                A("pe", lambda e: e.transpose(out=pb_[:, 0:128], in_=wk[:, c4, 0:128], identity=ident_s), reads=["wk", "scst"], writes=[pbk])
                A("dve", lambda e: e.tensor_copy(out=kTw[:, c4 * 128:(c4 + 1) * 128], in_=pb_[:, 0:128]), reads=[pbk], writes=["kTw"])
            for g in range(2):
                Dm(kTw[g * 64:(g + 1) * 64, 512:512 + ST], nsaT[:, 12 + g, col_s:col_s + ST], reads=["dram_nsaT"], writes=["kTw"])
            for g in range(2):
                key_max_s(kTw, 128, 512 + ST, 4 + g, "kTw", onesg[:, g:g + 1])
                set_mrow(4 + g, g)
                for c4 in range(4):
                    attend_s(kTw[:, c4 * 128:(c4 + 1) * 128], 128, 128, qpad[g], 4 + g, ntri16 if c4 == 0 else None, "scst2", wv[:, c4, g, :], 65, g, c4 == 0, False, "kTw", "wv")
                attend_s(kTw[:, 512:512 + ST], 128, ST, qpad[g], 4 + g, tri16[0:ST, :], "scst", vnew[1][0:ST, g, :], 65, g, False, True, "kTw", "vnew")
                finish_s(g, 2, False)
                A("pe", lambda e: e.transpose(out=ps[3][0:64, 0:16], in_=oacc[g][0:16, :], identity=ident_s[0:16, 0:16]), reads=[("oacc", g), "scst"], writes=["ps3"])
                A("act", lambda e: e.copy(out=obT[0:64, :], in_=ps[3][0:64, 0:16]), reads=["ps3"], writes=["obT"])
                for h in range(4):
                    hh = 4 * g + h
                    Dm(yb_fm[(hh % 2) * 64:(hh % 2) * 64 + 64, hh // 2, col_s:col_s + ST], obT[0:64, h * 4:(h + 1) * 4], reads=["obT"], writes=["dram_ybfm"])

    arena_gate()
    wm = arena[:, 0:DC * 2048].rearrange("p (c f) -> p c f", c=DC)
    woa = arena[:, 16384:16384 + 4096].rearrange("p (c f) -> p c f", c=4)
    wob = arena[:, 20480:20480 + 4096].rearrange("p (c f) -> p c f", c=4)
    wo = arena[:, 24576:24576 + 8192].rearrange("p (c f) -> p c f", c=DC)
    for c in range(DC):
        load_weight_bf16(wm[:, c, :], w_in[c * 128:(c + 1) * 128, 3096:5144], 2048, ("wm", c))
        load_weight_bf16(wo[:, c, :], w_o[c * 128:(c + 1) * 128, :], 1024, ("wo", c))
    for c in range(4):
        load_weight_bf16(woa[:, c, :], w_out_a[c * 128:(c + 1) * 128, :], 1024, ("woa", c))
        load_weight_bf16(wob[:, c, :], w_out_b[c * 128:(c + 1) * 128, :], 1024, ("wob", c))
    off[0] = 32768 * 2
    yt_, gt_, bt_, ybt_ = (cv(4 * NT).rearrange("p (c n) -> p c n", c=4) for _ in range(4))
    yab = arena[:, off[0] // 2:off[0] // 2 + 4 * NT].rearrange("p (c n) -> p c n", c=4)
    ybb = arena[:, off[0] // 2 + 4 * NT:off[0] // 2 + 8 * NT].rearrange("p (c n) -> p c n", c=4)
    mmb = arena[:, off[0] // 2 + 8 * NT:off[0] // 2 + 16 * NT].rearrange("p (c n) -> p c n", c=8)
    off[0] += 16 * NT * 2
    gn1, gn2, gn3, gab, gbb = (cv(NT) for _ in range(5))
    x2_v = x2T.rearrange("(c p) n -> p c n", p=128)
    for ti, (c0, n, segs) in enumerate(tl):
        x = xt[ti % 2]
        xk = "xt%d" % (ti % 2)
        P.dma("sp", x[:, :, 0:n], x1_v[:, :, c0:c0 + n], reads=["dram_x1T"], writes=[xk])
        Dm(yt_[:, :, 0:n], y_fm[:, :, c0:c0 + n], reads=["dram_yfm"], writes=["yt"])
        Dm(gt_[:, :, 0:n], scr["g"][:, :, c0:c0 + n], reads=["dram_scr"], writes=["gt"])
        Dm(bt_[:, :, 0:n], scr["bonus"][:, :, c0:c0 + n], reads=["dram_scr"], writes=["bt"])
        Dm(ybt_[:, :, 0:n], yb_fm[:, :, c0:c0 + n], reads=["dram_ybfm"], writes=["ybt"])
        modulate(x, xk, n, segs, 3, ub, "ub")
        P.op("pool", lambda e: e.tensor_scalar(out=x[:, :, 0:n], in0=x[:, :, 0:n], scalar1=ALPHA, scalar2=None, op0=ALU.mult),
             reads=[xk], writes=[xk])
        for j in range(4):
            yj = yt_[:, j, 0:n]
            A("act", lambda e: e.activation(out=gn1[:, 0:n], in_=yj, func=AF.Square), reads=["yt"], writes=["gn1"])
            A("pe", lambda e: e.matmul(ps[0][:, 0:n], lhsT=blk1[:], rhs=yj, start=True, stop=True), reads=["yt", "rwc"], writes=["ps0"])
            A("pe", lambda e: e.matmul(ps[1][:, 0:n], lhsT=blk1[:], rhs=gn1[:, 0:n], start=True, stop=True), reads=["gn1", "rwc"], writes=["ps1"])
            A("act", lambda e: e.mul(out=gn2[:, 0:n], in_=ps[0][:, 0:n], mul=1.0 / 64), reads=["ps0"], writes=["gn2"])
            A("dve", lambda e: e.tensor_tensor(out=gn3[:, 0:n], in0=gn2[:, 0:n], in1=gn2[:, 0:n], op=ALU.mult), reads=["gn2"], writes=["gn3"])
            A("dve", lambda e: e.scalar_tensor_tensor(out=gn3[:, 0:n], in0=ps[1][:, 0:n], scalar=1.0 / 64, in1=gn3[:, 0:n], op0=ALU.mult, op1=ALU.subtract),
              reads=["ps1", "gn3"], writes=["gn3"])
            A("dve", lambda e: e.tensor_scalar(out=gn3[:, 0:n], in0=gn3[:, 0:n], scalar1=64e-5, scalar2=None, op0=ALU.add), reads=["gn3"], writes=["gn3"])
            A("act", lambda e: e.sqrt(out=gn3[:, 0:n], in_=gn3[:, 0:n]), reads=["gn3"], writes=["gn3"])
            A("dve", lambda e: e.reciprocal(out=gn3[:, 0:n], in_=gn3[:, 0:n]), reads=["gn3"], writes=["gn3"])
            A("dve", lambda e: e.tensor_tensor(out=gn1[:, 0:n], in0=yj, in1=gn2[:, 0:n], op=ALU.subtract), reads=["yt", "gn2", "gn1"], writes=["gn1"])
            A("dve", lambda e: e.tensor_tensor(out=gn1[:, 0:n], in0=gn1[:, 0:n], in1=gn3[:, 0:n], op=ALU.mult), reads=["gn1", "gn3"], writes=["gn1"])
            A("act", lambda e: e.activation(out=gn1[:, 0:n], in_=gn1[:, 0:n], func=AF.Identity, scale=rwq_sb[:, j, 5:6], bias=rwq_sb[:, j, 6:7]),
              reads=["gn1", "rwc"], writes=["gn1"])
            A("dve", lambda e: e.tensor_tensor(out=gn1[:, 0:n], in0=gn1[:, 0:n], in1=bt_[:, j, 0:n], op=ALU.add), reads=["gn1", "bt"], writes=["gn1"])
            A("dve", lambda e: e.tensor_tensor(out=yab[:, j, 0:n], in0=gn1[:, 0:n], in1=gt_[:, j, 0:n], op=ALU.mult), reads=["gn1", "gt"], writes=["yab"])
            A("act", lambda e: e.copy(out=ybb[:, j, 0:n], in_=ybt_[:, j, 0:n]), reads=["ybt"], writes=["ybb"])
        for m in range(DC):
            mc = slice(m * 128, (m + 1) * 128)
            for j in range(4):
                A("pe", lambda e: e.matmul(ps[0][:, 0:n], lhsT=woa[:, j, mc], rhs=yab[:, j, 0:n], start=(j == 0), stop=(j == 3)),
                  reads=["yab", ("woa", j), "arena"], writes=["ps0"], skip_own=True)
            for j in range(4):
                A("pe", lambda e: e.matmul(ps[1][:, 0:n], lhsT=wob[:, j, mc], rhs=ybb[:, j, 0:n], start=(j == 0), stop=(j == 3)),
                  reads=["ybb", ("wob", j), "arena"], writes=["ps1"], skip_own=True)
            for c in range(DC):
                A("pe", lambda e: e.matmul(ps[2][:, 0:n], lhsT=wm[:, c, mc], rhs=ub[:, c, 0:n], start=(c == 0), stop=(c == DC - 1)),
                  reads=["ub", ("wm", c), "arena"], writes=["ps2"], skip_own=True)
            for c in range(DC):
                A("pe", lambda e: e.matmul(ps[3][:, 0:n], lhsT=wm[:, c, 1024 + m * 128:1024 + (m + 1) * 128], rhs=ub[:, c, 0:n], start=(c == 0), stop=(c == DC - 1)),
                  reads=["ub", ("wm", c), "arena"], writes=["ps3"], skip_own=True)
            A("act", lambda e: e.activation(out=gab[:, 0:n], in_=ps[2][:, 0:n], func=AF.Sigmoid, bias=bm_sb[:, m:m + 1], scale=1.0), reads=["ps2", "rwc"], writes=["gab"])
            A("act", lambda e: e.activation(out=gbb[:, 0:n], in_=ps[3][:, 0:n], func=AF.Sigmoid, bias=bm_sb[:, 8 + m:9 + m], scale=1.0), reads=["ps3", "rwc"], writes=["gbb"])
            A("dve", lambda e: e.tensor_tensor(out=gab[:, 0:n], in0=gab[:, 0:n], in1=ps[0][:, 0:n], op=ALU.mult), reads=["gab", "ps0"], writes=["gab"])
            A("dve", lambda e: e.tensor_tensor(out=gbb[:, 0:n], in0=gbb[:, 0:n], in1=ps[1][:, 0:n], op=ALU.mult), reads=["gbb", "ps1"], writes=["gbb"])
            A("dve", lambda e: e.tensor_tensor(out=mmb[:, m, 0:n], in0=gab[:, 0:n], in1=gbb[:, 0:n], op=ALU.add), reads=["gab", "gbb"], writes=["mmb"])
        for m in range(DC):
            py = ps[4 + (m % 2)]
            ky = "ps%d" % (4 + (m % 2))
            for c in range(DC):
                A("pe", lambda e: e.matmul(py[:, 0:n], lhsT=wo[:, c, m * 128:(m + 1) * 128], rhs=mmb[:, c, 0:n], start=(c == 0), stop=(c == DC - 1)),
                  reads=["mmb", ("wo", c), "arena"], writes=[ky], skip_own=True)
            for (lo, hi, s_) in segs:
                P.op("dve", lambda e: e.scalar_tensor_tensor(out=x[:, m, lo:hi], in0=py[:, lo:hi], scalar=modS[:, s_, 5, m:m + 1], in1=x[:, m, lo:hi],
                                                             op0=ALU.mult, op1=ALU.add), reads=[ky, xk, "modS"], writes=[xk])
        layer_norm_tile(x, xk, n, 1, x, xk)
        P.dma("pool", x2_v[:, :, c0:c0 + n], x[:, :, 0:n], reads=[xk], writes=["dram_x2T"])

    ffn_phase(1, x2T, yT, 2, 6, True)

    P.emit()
    es.close()
    return nc


_NC_CACHE = {}


def kernel(**inp):
    f = lambda a: np.ascontiguousarray(np.asarray(a), dtype=np.float32)
    x_prompt, x_sample = f(inp["x_prompt"]), f(inp["x_sample"])
    c_prompt, c_sample = f(inp["c_prompt"]), f(inp["c_sample"])
    w_ada = f(inp["w_ada"])[0]
    b_ada = f(inp["b_ada"])[0].reshape(72, 128).T.copy()
    ln_g = f(inp["ln_g"])[0].reshape(3, DC, 128).transpose(2, 0, 1).copy()
    ln_b = f(inp["ln_b"])[0].reshape(3, DC, 128).transpose(2, 0, 1).copy()
    w_gate, w_up, w_down = f(inp["ffn_w_gate"])[0], f(inp["ffn_w_up"])[0], f(inp["ffn_w_down"])[0]
    w_in = f(inp["w_in"])[0]
    b_in_bc = np.ascontiguousarray(np.broadcast_to(f(inp["b_in"])[0][None, :], (128, N_IN)))
    state_kv_win = f(inp["state_kv_win"])[0].reshape(32, 512, 256)

    CHT = [(j * 128, 128) for j in range(12)] + [(1536, 64), (1600, 64), (1664, 128)]

    def chunked(vec):
        vec = np.asarray(vec)
        lead = vec.shape[:-1]
        out = np.zeros((128, 15) + lead, np.float32)
        for ci, (c0_, w_) in enumerate(CHT):
            out[:w_, ci] = np.moveaxis(vec[..., c0_:c0_ + w_], -1, 0)
        return out

    b_in = f(inp["b_in"])[0]
    rwp = np.ascontiguousarray(np.stack([chunked(b_in[:RW_SHIFT]), chunked(f(inp["rw_mu"])[0])], axis=-1))
    q7 = [f(inp[k])[0].reshape(512) for k in ("rw_w0", "rw_a0", "rw_k_k", "rw_k_a", "rw_r_k", "rw_ln_w", "rw_ln_b")]
    rwq = np.ascontiguousarray(np.stack([v.reshape(4, 128).T for v in q7], axis=-1))
    rw_w2, rw_a2, rw_g2 = f(inp["rw_w2"])[0], f(inp["rw_a2"])[0], f(inp["rw_g2"])[0]
    state_shift = f(inp["state_shift"])[0]
    state_wkv = f(inp["state_wkv"])[0]
    pidx = np.arange(128)
    blk1 = (pidx[:, None] // 64 == pidx[None, :] // 64).astype(np.float32)
    istk = (pidx[:, None] % 64 == np.arange(64)[None, :]).astype(np.float32)

    nsab_cols = [1792 + 64 * h for h in range(8)] + [2304 + br * 256 + g * 64 for br in range(3) for g in range(2)] \
        + [2304 + 128 + g * 64 for g in range(2)]
    nsab = np.ascontiguousarray(np.stack([b_in[c0_:c0_ + 64] * (0.125 if ci < 8 else 1.0) for ci, c0_ in enumerate(nsab_cols)], axis=1))
    phi_w1, phi_w2 = f(inp["nsa_phi_w1"])[0], f(inp["nsa_phi_w2"])[0]
    peT = np.ascontiguousarray(f(inp["nsa_phi_pe"])[0].transpose(2, 0, 1))
    tri = (pidx[:, None] <= pidx[None, :]).astype(np.float32)
    nn = np.arange(256).reshape(2, 128)
    cvalT = np.ascontiguousarray((16.0 * nn.T[:, :, None] + 31.0 - pidx[None, None, :]).astype(np.float32))
    jidx = np.ascontiguousarray(np.broadcast_to(np.arange(64, dtype=np.float32)[None, :], (128, 64)))
    curb = (pidx[:, None] >= 64).astype(np.float32)
    eall = (np.arange(SEQ)[None, :] // 64 == np.arange(64)[:, None]).astype(np.float32)
    nidx = np.arange(256)[:, None]
    jj = np.arange(64)[None, :]
    bimp_full = ((nidx >= 4 * jj - 1) & (nidx <= 4 * jj + 3) & (nidx < 255)).astype(np.float32)
    bimp = np.ascontiguousarray(bimp_full.reshape(2, 128, 64).transpose(1, 0, 2))
    idx16 = np.ascontiguousarray(np.broadcast_to(np.arange(16, dtype=np.float32)[None, :], (128, 16)))
    ident = np.eye(128, dtype=np.float32)
    bm = np.ascontiguousarray(b_in[3096:5144].reshape(16, 128).T)
    w_out_a, w_out_b, w_o = f(inp["w_out_a"])[0], f(inp["w_out_b"])[0], f(inp["w_o"])[0]

    cache_cmp = f(inp["cache_kv_cmp"])[0].reshape(-1, 128)
    cache_sel = f(inp["cache_kv_sel"])[0].reshape(-1, 128)
    page_table = np.ascontiguousarray(np.asarray(inp["page_table"]), dtype=np.int32)
    n1 = np.arange(1024)[:, None]
    j1 = np.arange(257)[None, :]
    bimps_full = ((n1 >= 4 * j1 - 1) & (n1 <= 4 * j1 + 3) & (n1 < 1023)).astype(np.float32)
    bimps = np.ascontiguousarray(bimps_full.reshape(8, 128, 257).transpose(1, 0, 2))
    bb = np.arange(128)[:, None]
    kk8 = np.arange(8192)[None, :]
    e2 = (bb == 2 * (kk8 // 128) + (kk8 % 128) // 64).astype(np.float32)
    hsel = (np.arange(16)[:, None] % 4 == np.arange(4)[None, :]).astype(np.float32)
    pcol = pidx[:, None].astype(np.float32)
    jidx2 = np.ascontiguousarray(np.broadcast_to(np.arange(264, dtype=np.float32)[None, :], (128, 264)))
    tri16 = (pidx[:, None] <= (np.arange(16)[None, :] % 4)).astype(np.float32)

    if "nc" not in _NC_CACHE:
        _NC_CACHE["nc"] = build()
    nc = _NC_CACHE["nc"]

    in_maps = []
    for i in range(8):
        b = i // 2
        xs = x_sample[4 * i:4 * i + 4].reshape(NS * ST, D)
        xT = np.ascontiguousarray(np.concatenate([x_prompt[b], xs], axis=0).T)
        cv = np.concatenate([c_prompt[b:b + 1], c_sample[4 * i:4 * i + 4]], axis=0)
        cT = np.ascontiguousarray(cv.T.reshape(DC, 128, NSEQ).transpose(1, 0, 2))
        in_maps.append(dict(xT=xT, cT=cT, w_ada=w_ada, b_ada=b_ada, ln_g=ln_g, ln_b=ln_b, w_gate=w_gate, w_up=w_up,
                            w_down=w_down, w_in=w_in, b_in_bc=b_in_bc,
                            win_state=np.ascontiguousarray(state_kv_win[4 * i:4 * i + 4]),
                            rwp=rwp, rwq=rwq, rw_w2=rw_w2, rw_a2=rw_a2, rw_g2=rw_g2,
                            shs=np.ascontiguousarray(chunked(state_shift[4 * i:4 * i + 4])),
                            wkv0=np.ascontiguousarray(state_wkv[4 * i:4 * i + 4]), blk1=blk1, istk=istk,
                            nsab=nsab, phi_w1=phi_w1, phi_w2=phi_w2, peT=peT, tri=tri, cvalT=cvalT, jidx=jidx, curb=curb,
                            eall=eall, bimp=bimp, idx16=idx16, ident=ident, bm=bm, w_out_a=w_out_a, w_out_b=w_out_b, w_o=w_o,
                            cache_cmp=cache_cmp, cache_sel=cache_sel,
                            ptab=np.ascontiguousarray(page_table[4 * i:4 * i + 4].reshape(1, NS * 128)),
                            bimps=bimps, e2=e2, hsel=hsel, pcol=pcol, jidx2=jidx2, tri16=tri16))
    res = run_bass_kernel_spmd(nc, in_maps, core_ids=list(range(8)))
    R = res.results

    y_prompt = np.stack([R[2 * b]["yT"][:, :SEQ].T for b in range(4)])
    y_sample = np.concatenate([R[i]["yT"][:, SEQ:].T.reshape(NS, ST, D) for i in range(8)])
    kvshape = lambda a: a.reshape(a.shape[0], 2, 2, 64)
    kvc_p = np.stack([kvshape(R[2 * b]["o_kvc"][:SEQ]) for b in range(4)])[None]
    kvs_p = np.stack([kvshape(R[2 * b]["o_kvs"][:SEQ]) for b in range(4)])[None]
    kvw_p = np.stack([kvshape(R[2 * b]["o_kvw_p"]) for b in range(4)])[None]
    wkv_p = np.stack([R[2 * b]["o_wkv"][0] for b in range(4)])[None]
    sh_p = np.stack([R[2 * b]["o_shift"][0] for b in range(4)])[None]
    kvc_s = np.concatenate([R[i]["o_kvc"][SEQ:].reshape(NS, ST, 2, 2, 64) for i in range(8)])[None]
    kvs_s = np.concatenate([R[i]["o_kvs"][SEQ:].reshape(NS, ST, 2, 2, 64) for i in range(8)])[None]
    kvw_s = np.concatenate([R[i]["o_kvw_s"].reshape(NS, 512, 2, 2, 64) for i in range(8)])[None]
    wkv_s = np.concatenate([R[i]["o_wkv"][1:] for i in range(8)])[None]
    sh_s = np.concatenate([R[i]["o_shift"][1:] for i in range(8)])[None]
    outs = (y_prompt, y_sample, kvc_p, kvs_p, kvw_p, wkv_p, sh_p, kvc_s, kvs_s, kvw_s, wkv_s, sh_s)
    return tuple(np.ascontiguousarray(o, dtype=np.float32) for o in outs)
```
